# Optimizing a Trainium2 kernel written in Bass

```python
import jax
import jax.numpy as jnp
from jax import lax
import numpy as np

D_MODEL = 1024
BATCH = 8
SEQ = 2048
DEPTH = 2

GRID_W = 64
CTX_LEN = 256
N_EVEN = (DEPTH + 1) // 2
N_ODD = DEPTH // 2
N_MOD = 6
EPS = 1e-6
ROPE_THETA = 10000.0
NEG_INF = -1e30

A_HEADS = 8
A_KV_HEADS = 2
A_HEAD_DIM = 64
WINDOW = 128
B_HEADS = 8
B_Q_RANK = 256
B_KV_RANK = 256
B_NOPE = 64
B_ROPE = 32
B_V_DIM = 64
Q_BLOCK = 128
C_HEADS = 8
C_HEAD = 64
C_DIM = C_HEADS * C_HEAD
C_DECAY_LORA = 64
C_AAA_LORA = 64
C_GATE_LORA = 128
C_GN_EPS = 64e-5
D_HEADS = 4
D_HEAD_DIM = 128
D_DIM = D_HEADS * D_HEAD_DIM
D_CONV = 5
D_CHUNK = 64
D_FF = 2816
FFN_CONV = 3

MIX_WIDTH = A_HEADS * A_HEAD_DIM + B_HEADS * B_V_DIM
AB_SIZES = (A_HEADS * A_HEAD_DIM, A_KV_HEADS * A_HEAD_DIM, A_KV_HEADS * A_HEAD_DIM, B_Q_RANK, B_KV_RANK, B_ROPE)
IN_AB = sum(AB_SIZES)
C_SIZES = (C_DIM, C_DIM, C_DIM, C_DECAY_LORA, C_DECAY_LORA, C_AAA_LORA, C_AAA_LORA, C_GATE_LORA)
IN_C = sum(C_SIZES)
D_SIZES = (3 * D_DIM, D_DIM, D_HEADS, D_HEADS, D_HEADS, D_HEADS)
IN_CD = IN_C + sum(D_SIZES)

kernel_name = 'hybrid_dit_swa_mla_rwkv7_gdn'

F32 = jnp.float32


def split_cols(p, sizes):
    return jnp.split(p, [int(s) for s in np.cumsum(sizes)[:-1]], axis=-1)


def rms_norm(x, gain=None, eps=EPS):
    xf = x.astype(F32)
    y = xf * lax.rsqrt(jnp.mean(xf * xf, axis=-1, keepdims=True) + eps)
    if gain is not None:
        y = y * gain.astype(F32)
    return y.astype(x.dtype)


def l2_normalize(x, eps=1e-6):
    xf = x.astype(F32)
    return (xf * lax.rsqrt(jnp.sum(xf * xf, axis=-1, keepdims=True) + eps)).astype(x.dtype)


def modulate(x, shift, scale):
    return rms_norm(x) * (1.0 + scale) + shift


def rope_1d(x, pos):
    half = x.shape[-1] // 2
    inv = jnp.power(ROPE_THETA, -jnp.arange(half, dtype=F32) / half)
    ang = pos.astype(F32)[:, None] * inv[None, :]
    cos = jnp.cos(ang)[None, :, None, :]
    sin = jnp.sin(ang)[None, :, None, :]
    x1, x2 = x[..., :half], x[..., half:]
    return jnp.concatenate([x1 * cos - x2 * sin, x1 * sin + x2 * cos], axis=-1).astype(x.dtype)


def rope_2d(x, row, col):
    h = x.shape[-1] // 2
    return jnp.concatenate([rope_1d(x[..., :h], row), rope_1d(x[..., h:], col)], axis=-1)


def dwconv_centred(x, w):
    k, ch = w.shape
    p = k // 2
    return lax.conv_general_dilated(x, w[:, None, :].astype(x.dtype), (1,), [(p, p)],
                                    dimension_numbers=('NWC', 'WIO', 'NWC'), feature_group_count=ch)


def flip_time(t, rev):
    return jnp.flip(t, axis=1) if rev else t


def window_attention(q, k, v, k_ctx, v_ctx, sink):
    b, s, h, d = q.shape
    kv = k.shape[2]
    g = h // kv
    nb = s // WINDOW
    n_ctx = k_ctx.shape[1]
    qb = q.reshape(b, nb, WINDOW, kv, g, d)

    def band(t):
        tb = t.reshape(b, nb, WINDOW, kv, d)
        zero = jnp.zeros_like(tb[:, :1])
        prev = jnp.concatenate([zero, tb[:, :-1]], axis=1)
        nxt = jnp.concatenate([tb[:, 1:], zero], axis=1)
        return jnp.concatenate([prev, tb, nxt], axis=2)

    kb, vb = band(k), band(v)
    scale = d ** -0.5
    s_loc = jnp.einsum('bnqhgd,bnjhd->bnhgqj', qb, kb).astype(F32) * scale
    blk = jnp.arange(nb)[:, None, None]
    qpos = blk * WINDOW + jnp.arange(WINDOW)[None, :, None]
    kpos = (blk - 1) * WINDOW + jnp.arange(3 * WINDOW)[None, None, :]
    valid = (jnp.abs(qpos - kpos) <= WINDOW) & (kpos >= 0) & (kpos < s)
    s_loc = jnp.where(valid[None, :, None, None], s_loc, NEG_INF)
    s_ctx = jnp.einsum('bnqhgd,bjhd->bnhgqj', qb, k_ctx).astype(F32) * scale
    s_sink = jnp.broadcast_to(sink.astype(F32).reshape(1, 1, kv, g, 1, 1), s_loc.shape[:-1] + (1,))
    p = jax.nn.softmax(jnp.concatenate([s_loc, s_ctx, s_sink], axis=-1), axis=-1)
    p_loc = p[..., :3 * WINDOW].astype(v.dtype)
    p_ctx = p[..., 3 * WINDOW:3 * WINDOW + n_ctx].astype(v.dtype)
    o = jnp.einsum('bnhgqj,bnjhd->bnqhgd', p_loc, vb) + jnp.einsum('bnhgqj,bjhd->bnqhgd', p_ctx, v_ctx)
    return o.reshape(b, s, h * d)


def context_gqa(q, k, v, sink):
    b, n, h, d = q.shape
    kv = k.shape[2]
    g = h // kv
    qg = q.reshape(b, n, kv, g, d)
    s = jnp.einsum('bqhgd,bjhd->bhgqj', qg, k).astype(F32) * d ** -0.5
    s_sink = jnp.broadcast_to(sink.astype(F32).reshape(1, kv, g, 1, 1), s.shape[:-1] + (1,))
    p = jax.nn.softmax(jnp.concatenate([s, s_sink], axis=-1), axis=-1)[..., :n].astype(v.dtype)
    return jnp.einsum('bhgqj,bjhd->bqhgd', p, v).reshape(b, n, h * d)


def mla_attend(qn, qr, kn, kr, v):
    s = (jnp.einsum('bqhd,bkhd->bhqk', qn, kn) + jnp.einsum('bqhd,bkd->bhqk', qr, kr)).astype(F32)
    p = jax.nn.softmax(s * (B_NOPE + B_ROPE) ** -0.5, axis=-1).astype(v.dtype)
    return jnp.einsum('bhqk,bkhd->bqhd', p, v)


def mla_latent(qn, qr, kn_all, kr_all, v_all):
    b, s = qn.shape[:2]
    nq = s // Q_BLOCK

    def blocks(t):
        return jnp.moveaxis(t.reshape(b, nq, Q_BLOCK, *t.shape[2:]), 1, 0)

    o = lax.map(lambda qs: mla_attend(qs[0], qs[1], kn_all, kr_all, v_all), (blocks(qn), blocks(qr)))
    return jnp.moveaxis(o, 0, 1).reshape(b, s, -1)


def ab_prep(p, a_q_norm, a_k_norm, b_cq_norm, b_ckv_norm, b_w_uq, b_w_uk, b_w_uv,
            b_qn_norm, b_qr_norm, b_kn_norm, b_kr_norm):
    b, t = p.shape[:2]
    qa, ka, va, cq, ckv, kr = split_cols(p, AB_SIZES)
    qa = rms_norm(qa.reshape(b, t, A_HEADS, A_HEAD_DIM), a_q_norm)
    ka = rms_norm(ka.reshape(b, t, A_KV_HEADS, A_HEAD_DIM), a_k_norm)
    va = va.reshape(b, t, A_KV_HEADS, A_HEAD_DIM)
    qb = (rms_norm(cq, b_cq_norm) @ b_w_uq).reshape(b, t, B_HEADS, B_NOPE + B_ROPE)
    qn = rms_norm(qb[..., :B_NOPE], b_qn_norm)
    qr = rms_norm(qb[..., B_NOPE:], b_qr_norm)
    ckv = rms_norm(ckv, b_ckv_norm)
    kn = rms_norm((ckv @ b_w_uk).reshape(b, t, B_HEADS, B_NOPE), b_kn_norm)
    vb = (ckv @ b_w_uv).reshape(b, t, B_HEADS, B_V_DIM)
    kr = rms_norm(kr, b_kr_norm)
    return qa, ka, va, qn, qr, kn, kr, vb


def mixer_ab(p_lat, p_ctx, row, col, a_q_norm, a_k_norm, a_sink, b_cq_norm, b_ckv_norm, b_w_uq, b_w_uk,
             b_w_uv, b_qn_norm, b_qr_norm, b_kn_norm, b_kr_norm, ctx_out):
    prm = (a_q_norm, a_k_norm, b_cq_norm, b_ckv_norm, b_w_uq, b_w_uk, b_w_uv,
           b_qn_norm, b_qr_norm, b_kn_norm, b_kr_norm)
    qa_c, ka_c, va_c, qn_c, qr_c, kn_c, kr_c, vb_c = ab_prep(p_ctx, *prm)
    qa_l, ka_l, va_l, qn_l, qr_l, kn_l, kr_l, vb_l = ab_prep(p_lat, *prm)
    qa_l = rope_2d(qa_l, row, col)
    ka_l = rope_2d(ka_l, row, col)
    qr_l = rope_2d(qr_l, row, col)
    kr_l = rope_2d(kr_l[:, :, None, :], row, col)[:, :, 0, :]
    o_a = window_attention(qa_l, ka_l, va_l, ka_c, va_c, a_sink)
    o_b = mla_latent(qn_l, qr_l, jnp.concatenate([kn_c, kn_l], axis=1),
                     jnp.concatenate([kr_c, kr_l], axis=1), jnp.concatenate([vb_c, vb_l], axis=1))
    y_lat = jnp.concatenate([o_a, o_b], axis=-1)
    y_ctx = None
    if ctx_out:
        b, n = p_ctx.shape[:2]
        y_ctx = jnp.concatenate([context_gqa(qa_c, ka_c, va_c, a_sink),
                                 mla_attend(qn_c, qr_c, kn_c, kr_c, vb_c).reshape(b, n, -1)], axis=-1)
    return y_lat, y_ctx


def token_shift_centred(p, mu_prev, mu_next):
    prev = jnp.pad(p, ((0, 0), (1, 0), (0, 0)))[:, :-1]
    nxt = jnp.pad(p, ((0, 0), (0, 1), (0, 0)))[:, 1:]
    return p + mu_prev * (prev - p) + mu_next * (nxt - p)


def rwkv_prep(pc, c_mu_prev, c_mu_next, c_w0, c_w2, c_a0, c_a2, c_g2, c_k_k, c_k_a):
    b, t = pc.shape[:2]

    def heads(z):
        return z.reshape(b, t, C_HEADS, C_HEAD)

    xs = token_shift_centred(pc, c_mu_prev, c_mu_next)
    r, k, v, wl_f, wl_b, al_f, al_b, gl = split_cols(xs, C_SIZES)

    def decay(wl, w0, w2):
        w = -jax.nn.softplus(-(w0 + jnp.tanh(wl) @ w2)) - 0.5
        return heads(jnp.exp(-jnp.exp(w.astype(F32))))

    decays = (decay(wl_f, c_w0[0], c_w2[0]), decay(wl_b, c_w0[1], c_w2[1]))
    iclr = (jax.nn.sigmoid(c_a0[0] + al_f @ c_a2[0]), jax.nn.sigmoid(c_a0[1] + al_b @ c_a2[1]))
    g = jax.nn.sigmoid(gl) @ c_g2
    kk = l2_normalize(heads(k * c_k_k))
    ks = tuple(heads(k * (1.0 + (a - 1.0) * c_k_a)) for a in iclr)
    return heads(r), heads(v), kk, g, decays, tuple(heads(a) for a in iclr), ks


def rwkv_scan(r, w, k, v, kk, a, s0, reverse):
    def step(state, inp):
        rt, wt, kt, vt, kkt, at = inp
        sa = jnp.einsum('bhvk,bhk->bhv', state, -kkt)
        state = state * wt[:, :, None, :] + sa[..., None] * (kkt * at)[:, :, None, :] + vt[..., None] * kt[:, :, None, :]
        return state, jnp.einsum('bhvk,bhk->bhv', state, rt)

    xs = tuple(jnp.moveaxis(z.astype(F32), 1, 0) for z in (r, w, k, v, kk, a))
    state, y = lax.scan(step, s0, xs, reverse=reverse)
    return jnp.moveaxis(y, 0, 1), state


def head_group_norm(y, w, b_):
    mean = jnp.mean(y, axis=-1, keepdims=True)
    var = jnp.mean(jnp.square(y - mean), axis=-1, keepdims=True)
    yn = (y - mean) * lax.rsqrt(var + C_GN_EPS)
    return yn * w.astype(F32).reshape(C_HEADS, C_HEAD) + b_.astype(F32).reshape(C_HEADS, C_HEAD)


def rwkv_output(y, r, ks, v, g, c_r_k, c_ln_w, c_ln_b):
    b, t = y.shape[:2]
    yn = head_group_norm(y, c_ln_w, c_ln_b)
    bonus = sum(jnp.sum(r * kd * c_r_k, axis=-1, keepdims=True) * v for kd in ks)
    return (yn + bonus).reshape(b, t, C_DIM) * g


def gdn_prep(pd, d_conv_w, d_A_log, d_dt_bias):
    b, t = pd.shape[:2]
    qkv, z, bf, bb, af, ab = split_cols(pd, D_SIZES)
    qkv = jax.nn.silu(dwconv_centred(qkv, d_conv_w))
    q, k, v = jnp.split(qkv, 3, axis=-1)
    q = l2_normalize(q.reshape(b, t, D_HEADS, D_HEAD_DIM)) * D_HEAD_DIM ** -0.5
    k = l2_normalize(k.reshape(b, t, D_HEADS, D_HEAD_DIM))
    v = v.reshape(b, t, D_HEADS, D_HEAD_DIM)
    betas = (jax.nn.sigmoid(bf), jax.nn.sigmoid(bb))
    gs = tuple(-jnp.exp(d_A_log[i].astype(F32)) * jax.nn.softplus((al + d_dt_bias[i]).astype(F32))
               for i, al in enumerate((af, ab)))
    return q, k, v, z, betas, gs


def chunk_gated_delta(q, k, v, beta, g, s0):
    b, t, h, dk = q.shape
    dv = v.shape[-1]
    n = t // D_CHUNK

    def to_chunks(z):
        z = z.astype(F32).reshape(b, n, D_CHUNK, h, *z.shape[3:])
        return jnp.moveaxis(z, (1, 3), (0, 2))

    qc, kc, vc, bc, gc = (to_chunks(z) for z in (q, k, v, beta, g))
    gcum = jnp.cumsum(gc, axis=-1)
    idx = jnp.arange(D_CHUNK)
    causal = idx[:, None] >= idx[None, :]
    strict = idx[:, None] > idx[None, :]
    diff = gcum[..., :, None] - gcum[..., None, :]
    decay = jnp.where(causal, jnp.exp(jnp.where(causal, diff, 0.0)), 0.0)
    kb = kc * bc[..., None]
    vb = vc * bc[..., None]
    lmat = jnp.where(strict, jnp.einsum('nbhid,nbhjd->nbhij', kb, kc) * decay, 0.0)
    eye = jnp.broadcast_to(jnp.eye(D_CHUNK, dtype=F32), lmat.shape)
    tinv = lax.linalg.triangular_solve(lmat + eye, eye, left_side=True, lower=True, unit_diagonal=True)
    u = tinv @ vb
    wk = tinv @ (kb * jnp.exp(gcum)[..., None])
    a_intra = jnp.einsum('nbhid,nbhjd->nbhij', qc, kc) * decay

    def step(state, inp):
        qi, ki, ui, wi, ai, gi = inp
        v_new = ui - wi @ state
        o = (qi * jnp.exp(gi)[..., None]) @ state + ai @ v_new
        g_last = gi[..., -1:]
        state = state * jnp.exp(g_last)[..., None] + jnp.einsum(
            'bhcd,bhce->bhde', ki * jnp.exp(g_last - gi)[..., None], v_new)
        return state, o

    state, o = lax.scan(step, s0, (qc, kc, u, wk, a_intra, gcum))
    return jnp.moveaxis(o, (0, 2), (1, 3)).reshape(b, t, h, dv), state


def gdn_output(o, z, d_o_norm):
    b, t = o.shape[:2]
    gate = jax.nn.silu(z.reshape(b, t, D_HEADS, D_HEAD_DIM).astype(F32))
    return (rms_norm(o, d_o_norm) * gate).reshape(b, t, D_DIM)


def mixer_cd(p_lat, p_ctx, c_mu_prev, c_mu_next, c_w0, c_w2, c_a0, c_a2, c_g2, c_k_k, c_k_a, c_r_k,
             c_ln_w, c_ln_b, d_conv_w, d_A_log, d_dt_bias, d_o_norm, ctx_out):
    b = p_lat.shape[0]
    rw = (c_mu_prev, c_mu_next, c_w0, c_w2, c_a0, c_a2, c_g2, c_k_k, c_k_a)
    r_c, v_c, kk_c, g_c, dec_c, a_c, k_c = rwkv_prep(p_ctx[..., :IN_C], *rw)
    r_l, v_l, kk_l, g_l, dec_l, a_l, k_l = rwkv_prep(p_lat[..., :IN_C], *rw)
    s0 = jnp.zeros((b, C_HEADS, C_HEAD, C_HEAD), F32)
    y_c = 0.0
    y_l = 0.0
    for i, rev in enumerate((False, True)):
        yc, st = rwkv_scan(r_c, dec_c[i], k_c[i], v_c, kk_c, a_c[i], s0, rev)
        yl, _ = rwkv_scan(r_l, dec_l[i], k_l[i], v_l, kk_l, a_l[i], st, rev)
        y_c = y_c + yc
        y_l = y_l + yl

    q_c, kd_c, vd_c, z_c, beta_c, gd_c = gdn_prep(p_ctx[..., IN_C:], d_conv_w, d_A_log, d_dt_bias)
    q_l, kd_l, vd_l, z_l, beta_l, gd_l = gdn_prep(p_lat[..., IN_C:], d_conv_w, d_A_log, d_dt_bias)
    h0 = jnp.zeros((b, D_HEADS, D_HEAD_DIM, D_HEAD_DIM), F32)
    o_c = 0.0
    o_l = 0.0
    for i, rev in enumerate((False, True)):
        oc, st = chunk_gated_delta(flip_time(q_c, rev), flip_time(kd_c, rev), flip_time(vd_c, rev),
                                   flip_time(beta_c[i], rev), flip_time(gd_c[i], rev), h0)
        ol, _ = chunk_gated_delta(flip_time(q_l, rev), flip_time(kd_l, rev), flip_time(vd_l, rev),
                                  flip_time(beta_l[i], rev), flip_time(gd_l[i], rev), st)
        o_c = o_c + flip_time(oc, rev)
        o_l = o_l + flip_time(ol, rev)

    y_lat = jnp.concatenate([rwkv_output(y_l, r_l, k_l, v_l, g_l, c_r_k, c_ln_w, c_ln_b),
                             gdn_output(o_l, z_l, d_o_norm)], axis=-1).astype(p_lat.dtype)
    y_ctx = None
    if ctx_out:
        y_ctx = jnp.concatenate([rwkv_output(y_c, r_c, k_c, v_c, g_c, c_r_k, c_ln_w, c_ln_b),
                                 gdn_output(o_c, z_c, d_o_norm)], axis=-1).astype(p_ctx.dtype)
    return y_lat, y_ctx


def conv_ffn(h, w_up, conv_w, conv_b, w_down):
    u = dwconv_centred(h @ w_up, conv_w) + conv_b
    val, gate = jnp.split(u, 2, axis=-1)
    return (jax.nn.silu(gate) * val) @ w_down


def setup_inputs(seed: int = 0) -> dict:
    key = jax.random.key(seed)
    keys = iter(jax.random.split(key, 64))
    d = D_MODEL

    def nrm(shape, scale):
        return jax.random.normal(next(keys), shape, F32) * scale

    def gain(shape):
        return 1.0 + 0.05 * jax.random.normal(next(keys), shape, F32)

    def uni(shape, lo, hi):
        return jax.random.uniform(next(keys), shape, F32, lo, hi)

    dt = jnp.exp(uni((N_ODD, 2, D_HEADS), float(np.log(1e-3)), float(np.log(1e-1))))
    return {
        'x': nrm((BATCH, SEQ, d), 1.0),
        'c': nrm((BATCH, d), 1.0),
        'ctx': nrm((BATCH, CTX_LEN, d), 1.0),
        'c_ctx': nrm((d,), 1.0),
        'ada_w': nrm((DEPTH, d, N_MOD * d), 0.5 * d ** -0.5),
        'ada_b': nrm((DEPTH, N_MOD * d), 0.02),
        'ffn_w_up': nrm((DEPTH, d, 2 * D_FF), d ** -0.5),
        'ffn_conv_w': nrm((DEPTH, FFN_CONV, 2 * D_FF), FFN_CONV ** -0.5),
        'ffn_conv_b': nrm((DEPTH, 2 * D_FF), 0.02),
        'ffn_w_down': nrm((DEPTH, D_FF, d), D_FF ** -0.5),
        'ab_w_in': nrm((N_EVEN, d, IN_AB), d ** -0.5),
        'ab_w_out': nrm((N_EVEN, MIX_WIDTH, d), MIX_WIDTH ** -0.5),
        'a_q_norm': gain((N_EVEN, A_HEAD_DIM)),
        'a_k_norm': gain((N_EVEN, A_HEAD_DIM)),
        'a_sink': nrm((N_EVEN, A_HEADS), 1.0),
        'b_cq_norm': gain((N_EVEN, B_Q_RANK)),
        'b_ckv_norm': gain((N_EVEN, B_KV_RANK)),
        'b_w_uq': nrm((N_EVEN, B_Q_RANK, B_HEADS * (B_NOPE + B_ROPE)), B_Q_RANK ** -0.5),
        'b_w_uk': nrm((N_EVEN, B_KV_RANK, B_HEADS * B_NOPE), B_KV_RANK ** -0.5),
        'b_w_uv': nrm((N_EVEN, B_KV_RANK, B_HEADS * B_V_DIM), B_KV_RANK ** -0.5),
        'b_qn_norm': gain((N_EVEN, B_NOPE)),
        'b_qr_norm': gain((N_EVEN, B_ROPE)),
        'b_kn_norm': gain((N_EVEN, B_NOPE)),
        'b_kr_norm': gain((N_EVEN, B_ROPE)),
        'cd_w_in': nrm((N_ODD, d, IN_CD), d ** -0.5),
        'cd_w_out': nrm((N_ODD, MIX_WIDTH, d), MIX_WIDTH ** -0.5),
        'c_mu_prev': uni((N_ODD, IN_C), 0.0, 0.5),
        'c_mu_next': uni((N_ODD, IN_C), 0.0, 0.5),
        'c_w0': uni((N_ODD, 2, C_DIM), -6.0, -1.0),
        'c_w2': nrm((N_ODD, 2, C_DECAY_LORA, C_DIM), 0.1),
        'c_a0': nrm((N_ODD, 2, C_DIM), 0.1),
        'c_a2': nrm((N_ODD, 2, C_AAA_LORA, C_DIM), C_AAA_LORA ** -0.5),
        'c_g2': nrm((N_ODD, C_GATE_LORA, C_DIM), C_GATE_LORA ** -0.5),
        'c_k_k': 0.85 + nrm((N_ODD, C_DIM), 0.05),
        'c_k_a': gain((N_ODD, C_DIM)),
        'c_r_k': nrm((N_ODD, C_HEADS, C_HEAD), 0.1),
        'c_ln_w': gain((N_ODD, C_DIM)),
        'c_ln_b': nrm((N_ODD, C_DIM), 0.02),
        'd_conv_w': nrm((N_ODD, D_CONV, 3 * D_DIM), D_CONV ** -0.5),
        'd_A_log': jnp.log(uni((N_ODD, 2, D_HEADS), 1.0, 16.0)),
        'd_dt_bias': dt + jnp.log(-jnp.expm1(-dt)),
        'd_o_norm': gain((N_ODD, D_HEAD_DIM)),
    }


def reference(x, c, ctx, c_ctx, ada_w, ada_b, ffn_w_up, ffn_conv_w, ffn_conv_b, ffn_w_down,
              ab_w_in, ab_w_out, a_q_norm, a_k_norm, a_sink, b_cq_norm, b_ckv_norm, b_w_uq, b_w_uk, b_w_uv,
              b_qn_norm, b_qr_norm, b_kn_norm, b_kr_norm, cd_w_in, cd_w_out, c_mu_prev, c_mu_next, c_w0, c_w2,
              c_a0, c_a2, c_g2, c_k_k, c_k_a, c_r_k, c_ln_w, c_ln_b, d_conv_w, d_A_log, d_dt_bias, d_o_norm):
    seq = x.shape[1]
    rows = seq // GRID_W
    row = jnp.repeat(jnp.arange(rows, dtype=jnp.int32), GRID_W)
    col = jnp.tile(jnp.arange(GRID_W, dtype=jnp.int32), rows)
    silu_c = jax.nn.silu(c)
    silu_cc = jax.nn.silu(c_ctx)
    for l in range(DEPTH):
        last = l == DEPTH - 1
        i = l // 2
        mod_l = jnp.split((silu_c @ ada_w[l] + ada_b[l])[:, None, :], N_MOD, axis=-1)
        mod_c = jnp.split(silu_cc @ ada_w[l] + ada_b[l], N_MOD, axis=-1)
        h_l = modulate(x, mod_l[0], mod_l[1])
        h_c = modulate(ctx, mod_c[0], mod_c[1])
        if l % 2 == 0:
            y_l, y_c = mixer_ab(h_l @ ab_w_in[i], h_c @ ab_w_in[i], row, col, a_q_norm[i], a_k_norm[i],
                                a_sink[i], b_cq_norm[i], b_ckv_norm[i], b_w_uq[i], b_w_uk[i], b_w_uv[i],
                                b_qn_norm[i], b_qr_norm[i], b_kn_norm[i], b_kr_norm[i], not last)
            w_out = ab_w_out[i]
        else:
            y_l, y_c = mixer_cd(h_l @ cd_w_in[i], h_c @ cd_w_in[i], c_mu_prev[i], c_mu_next[i], c_w0[i],
                                c_w2[i], c_a0[i], c_a2[i], c_g2[i], c_k_k[i], c_k_a[i], c_r_k[i], c_ln_w[i],
                                c_ln_b[i], d_conv_w[i], d_A_log[i], d_dt_bias[i], d_o_norm[i], not last)
            w_out = cd_w_out[i]
        x = x + mod_l[2] * (y_l @ w_out)
        x = x + mod_l[5] * conv_ffn(modulate(x, mod_l[3], mod_l[4]), ffn_w_up[l], ffn_conv_w[l],
                                    ffn_conv_b[l], ffn_w_down[l])
        if not last:
            ctx = ctx + mod_c[2] * (y_c @ w_out)
            ctx = ctx + mod_c[5] * conv_ffn(modulate(ctx, mod_c[3], mod_c[4]), ffn_w_up[l], ffn_conv_w[l],
                                            ffn_conv_b[l], ffn_w_down[l])
    return x
```

```python
import numpy as np
from contextlib import ExitStack
import concourse.bass as bass
import concourse.mybir as mybir
from concourse.bass_utils import run_bass_kernel_spmd

F32 = mybir.dt.float32
BF16 = mybir.dt.bfloat16
AF = mybir.ActivationFunctionType
ALU = mybir.AluOpType

D = 1024
T = 2304
NCTX = 256
NT = 18
EPS = 1e-6
CHUNKS = [(0, 256), (256, 512), (768, 512), (1280, 512), (1792, 512)]
DFF = 2816
DBG = {'lora': 1, 'ng': 4, 'prep': 1, 'nstep': 36, 'rwout': 1, 'gdn': 4, 'gprep': 3, 'cs': 5, 'invbf': 0, 'invlv': 5, 'invev': 5, 'invp': 1}


class Sched:
    def __init__(self, nc, stack, n_dma_sems=12):
        self.nc = nc
        self.engs = {"pe": nc.tensor, "dve": nc.vector, "act": nc.scalar, "pool": nc.gpsimd, "sp": nc.sync}
        self.sem = {k: stack.enter_context(nc.semaphore("s_" + k)) for k in self.engs}
        self.cnt = {k: 0 for k in self.engs}
        self.dsem = [stack.enter_context(nc.semaphore(f"dq{i}")) for i in range(n_dma_sems)]
        self.dcnt = [0] * n_dma_sems
        self.dnext = 0
        self.seen = {k: {} for k in self.engs}
        self.lastw = {}
        self.readers = {}
        self.ninst = 0

    def _semobj(self, sk):
        return self.sem[sk] if isinstance(sk, str) else self.dsem[sk]

    def _wait(self, e, sk, val):
        if val <= 0 or self.seen[e].get(sk, 0) >= val:
            return
        self.engs[e].wait_ge(self._semobj(sk), val)
        self.seen[e][sk] = val

    def _deps(self, e, reads, writes):
        deps = {}

        def add(p):
            if p is not None and deps.get(p[0], 0) < p[1]:
                deps[p[0]] = p[1]

        for k in reads:
            add(self.lastw.get(k))
        for k in writes:
            add(self.lastw.get(k))
            for sk, v in self.readers.get(k, {}).items():
                add((sk, v))
        for sk, v in deps.items():
            if sk == "pe" and e == "pe":
                continue
            self._wait(e, sk, v)

    def _commit(self, tok, reads, writes):
        sk, v = tok
        for k in reads:
            self.readers.setdefault(k, {})[sk] = v
        for k in writes:
            self.lastw[k] = tok
            self.readers[k] = {}

    def op(self, e, ins_fn, reads=(), writes=()):
        self._deps(e, reads, writes)
        ins = ins_fn()
        self.cnt[e] += 1
        self.ninst += 1
        ins.then_inc(self.sem[e], 1)
        self._commit((e, self.cnt[e]), reads, writes)
        return ins

    def dma(self, e, out, in_, reads=(), writes=(), **kw):
        i = self.dnext
        self.dnext = (self.dnext + 1) % len(self.dsem)
        self._wait(e, i, self.dcnt[i])
        self._deps(e, reads, writes)
        ins = self.engs[e].dma_start(out=out, in_=in_, **kw)
        self.dcnt[i] += 16
        self.ninst += 1
        ins.then_inc(self.dsem[i], 16)
        self._commit((i, self.dcnt[i]), reads, writes)
        return ins

    def barrier(self):
        for e in self.engs:
            for o in self.engs:
                if o != e:
                    self._wait(e, o, self.cnt[o])
            for i in range(len(self.dsem)):
                self._wait(e, i, self.dcnt[i])
        self.lastw = {}
        self.readers = {}

    def finish(self):
        for o in self.engs:
            if o != "sp":
                self._wait("sp", o, self.cnt[o])
        for i in range(len(self.dsem)):
            self._wait("sp", i, self.dcnt[i])


def _rope_tables():
    theta = 10000.0
    s = np.arange(2048)
    row = (s // 64).astype(np.float64)
    col = (s % 64).astype(np.float64)

    def tab(nd):
        h = nd // 2
        half = h // 2
        inv = theta ** (-np.arange(half, dtype=np.float64) / half)
        cos = np.ones((nd, T)); sin = np.zeros((nd, T))
        for d_ in range(nd):
            b = d_ // h
            i = (d_ % h) % half
            pos = row if b == 0 else col
            ang = (pos.astype(np.float32)[:, None] * inv.astype(np.float32)[None, :])[:, i]
            cos[d_, NCTX:] = np.cos(ang.astype(np.float32))
            sin[d_, NCTX:] = np.sin(ang.astype(np.float32))
        R = np.zeros((nd, nd))
        for d_ in range(nd):
            e = d_ % h
            if e < half:
                R[d_, d_ + half] = -1.0
            else:
                R[d_, d_ - half] = 1.0
        return cos.astype(np.float32), sin.astype(np.float32), R.astype(np.float32)

    cA, sA, RA = tab(64)
    cB, sB, RB = tab(32)
    cosA = np.concatenate([cA, cA], 0); sinA = np.concatenate([sA, sA], 0)
    RA2 = np.zeros((128, 128), np.float32); RA2[:64, :64] = RA; RA2[64:, 64:] = RA
    cosB = np.ones((96, T), np.float32); sinB = np.zeros((96, T), np.float32)
    cosB[64:] = cB; sinB[64:] = sB
    RB96 = np.zeros((96, 96), np.float32); RB96[64:, 64:] = RB
    return cosA, sinA, RA2.T.copy(), cosB, sinB, RB96.T.copy()


def _consts():
    c = {}
    cosA, sinA, RAT, cosB, sinB, RBT = _rope_tables()
    c["cosA"] = cosA; c["sinA"] = sinA; c["cosB"] = cosB; c["sinB"] = sinB
    mats = {}
    mats["ident"] = np.eye(128, dtype=np.float32)
    mats["ones1024"] = np.full((128, 128), 1.0 / 1024, np.float32)
    mats["ones256"] = np.full((128, 128), 1.0 / 256, np.float32)
    o = np.zeros((128, 128), np.float32); o[:64, :64] = 1 / 64; o[64:, 64:] = 1 / 64
    mats["onesA"] = o
    o = np.zeros((128, 128), np.float32); o[:64, :64] = 1 / 64; o[64:96, 64:96] = 1 / 32
    mats["onesB"] = o
    mats["RAT"] = RAT
    r = np.zeros((128, 128), np.float32); r[:96, :96] = RBT
    mats["RBT"] = r
    a = np.arange(128)[:, None]; b = np.arange(128)[None, :]
    mats["maskPrevT"] = np.where(a <= b, 0.0, -30000.0).astype(np.float32)
    mats["maskNextT"] = np.where(b <= a, 0.0, -30000.0).astype(np.float32)
    names = list(mats.keys())
    c["cmats"] = np.stack([mats[n] for n in names], 1).astype(np.float32)
    return c, names


def _fm(v, p=128):
    v = np.asarray(v, np.float32)
    return np.ascontiguousarray(v.reshape(-1, p).T)


class Ctx:
    pass


def build_program(dbg=None, nlayers=2):
    nc = bass.Bass("TRN2", target_bir_lowering=False)
    st = ExitStack()
    K = Ctx()
    with st:
        S = Sched(nc, st)

        def din(name, shape, dt=F32):
            return nc.dram_tensor(name, list(shape), dt, kind="ExternalInput").ap()

        uid = [0]

        def sb(stack, name, shape, dt=F32):
            uid[0] += 1
            return stack.enter_context(nc.sbuf_tensor(f"{name}_s{uid[0]}", list(shape), dt))

        def ps(stack, name, shape, dt=F32):
            uid[0] += 1
            return stack.enter_context(nc.psum_tensor(f"{name}_p{uid[0]}", list(shape), dt))

        consts, cnames = _consts()
        NCM = len(cnames)
        x_d = din("x", [2048, D]); ctx_d = din("ctx", [NCTX, D])
        cv_d = din("cv", [128, 8, 2])
        adaw_d = din("ada_w", [2, D, 6 * D]); adab_d = din("ada_b_fm", [2, 128, 48])
        wup_d = din("ffn_w_up", [2, D, 2 * DFF]); wdn_d = din("ffn_w_down", [2, DFF, D])
        fconv_d = din("ffn_conv_fm", [2, 128, 44, 4])
        abin_d = din("ab_w_in", [D, 1312]); about_d = din("ab_w_out", [D, D])
        wuq_d = din("b_w_uq", [256, 768]); wuk_d = din("b_w_uk", [256, 512]); wuv_d = din("b_w_uv", [256, 512])
        vecs_d = din("vecs0", [128, 16])
        sink_d = din("sink_b", [64, 8])
        cmats_d = din("cmats", [128, NCM, 128])
        cosA_d = din("cosA", [128, T]); sinA_d = din("sinA", [128, T])
        cosB_d = din("cosB", [96, T]); sinB_d = din("sinB", [96, T])
        cdin_d = din("cd_w_in", [D, 3984]); cdout_d = din("cd_w_out", [D, D])
        cw2_d = din("c_w2r", [128, 512]); ca2_d = din("c_a2r", [128, 512]); cg2_d = din("c_g2", [128, 512])
        v1_d = din("vecs1", [128, 160])
        gsm_d = din("gsm", [16, 2])
        cm2_d = din("cm2", [64, 320])
        sel_d = din("sel16", [16, 16 * 128])
        xpark_d = nc.dram_tensor("xpark", [128, 8 * T], F32, kind="Internal").ap()
        out_d = nc.dram_tensor("out", [2048, D], F32, kind="ExternalOutput").ap()
        dbg_d = None
        if dbg is not None:
            dbg_d = nc.dram_tensor("dbg", [T, D], F32, kind="ExternalOutput").ap()

        X = sb(st, "X", [128, 8, T])
        identf = sb(st, "identf", [128, 128])
        cm_b = sb(st, "cm_b", [128, NCM, 128], BF16)
        id4 = sb(st, "id4", [128, 4, 128], BF16)
        modv = sb(st, "modv", [128, 2, 48, 2])
        onep = sb(st, "onep", [128, 2, 2, 8, 2])
        adab = sb(st, "adab", [128, 2, 48])
        cv = sb(st, "cv", [128, 8, 2])
        scv = sb(st, "scv", [128, 8, 2])
        vecs = sb(st, "vecs", [128, 16])
        zeros = sb(st, "zeros", [128, 128])

        def CM(name, bf=True):
            if not bf:
                assert name == "ident"
                return identf[:]
            i = cnames.index(name)
            return cm_b[:, i, :]

        rr = {"cast": 0, "ev": 0}

        def evac_eng():
            rr["ev"] ^= 1
            return "act" if rr["ev"] else "dve"

        def copy_op(e, out, in_):
            if e == "act":
                return lambda: nc.scalar.copy(out=out, in_=in_)
            if e == "dve":
                return lambda: nc.vector.tensor_copy(out=out, in_=in_)
            return lambda: nc.gpsimd.tensor_copy(out=out, in_=in_)

        with ExitStack() as ph:
            cm_f = sb(ph, "cm_f", [128, NCM, 128])
            S.dma("sp", cm_f[:], cmats_d, writes=["cm_f0"])
            S.op("dve", lambda: nc.vector.tensor_copy(out=cm_b[:], in_=cm_f[:]), reads=["cm_f0"], writes=["cm_b"])
            S.op("act", lambda: nc.scalar.copy(out=identf[:], in_=cm_f[:, 0, :]), reads=["cm_f0"], writes=["cm_f"])
            for r in range(4):
                S.op("pool", lambda: nc.gpsimd.tensor_copy(out=id4[:, r, :], in_=cm_f[:, 0, :]), reads=["cm_f0"], writes=["id4"])
            S.barrier()
        S.dma("sp", cv[:], cv_d, writes=["cv"])
        S.dma("sp", adab[:], adab_d.rearrange("l p m -> p l m"), writes=["adab"])
        S.dma("sp", vecs[:], vecs_d, writes=["vecs"])
        S.op("pool", lambda: nc.gpsimd.memset(zeros[:], 0.0), writes=["zeros"])
        S.op("act", lambda: nc.scalar.activation(out=scv[:], in_=cv[:], func=AF.Silu), reads=["cv"], writes=["scv"])

        with ExitStack() as ph:
            xin = [sb(ph, f"xin{i}", [128, D]) for i in range(2)]
            pT = [ps(ph, f"pT{i}", [128, 512]) for i in range(4)]
            for t in range(NT):
                src = ctx_d[t * 128:(t + 1) * 128, :] if t < 2 else x_d[(t - 2) * 128:(t - 1) * 128, :]
                xi = xin[t % 2]
                S.dma("sp" if t % 2 == 0 else "pool", xi[:], src, writes=[f"xin{t%2}"])
                for hh in range(2):
                    pt = pT[(t % 2) * 2 + hh]
                    pk = f"pT{(t%2)*2+hh}"
                    for c4 in range(4):
                        c = hh * 4 + c4
                        S.op("pe", lambda: nc.tensor.transpose(out=pt[:, c4 * 128:(c4 + 1) * 128],
                                                               in_=xi[:, c * 128:(c + 1) * 128],
                                                               identity=CM("ident", False)),
                             reads=[f"xin{t%2}", "cm_f"], writes=[pk])
                    e = evac_eng()
                    S.op(e, copy_op(e, X[:, hh * 4:(hh + 1) * 4, t * 128:(t + 1) * 128],
                                    pt[:].rearrange("p (c n) -> p c n", c=4)),
                         reads=[pk], writes=[f"X{t}"])
        S.barrier()

        with ExitStack() as ph:
            astg = [sb(ph, f"astg{i}", [128, 8, 768]) for i in range(2)]
            pm = ps(ph, "pm", [128, 48, 2])
            adv = adaw_d.rearrange("l (kc p) n -> l p kc n", p=128)
            it = 0
            for l in range(nlayers):
                for g in range(8):
                    a = astg[it % 2]
                    S.dma("sp" if it % 2 == 0 else "pool", a[:], adv[l, :, :, g * 768:(g + 1) * 768],
                          writes=[f"astg{it%2}"])
                    for mm in range(6):
                        m = g * 6 + mm
                        for k in range(8):
                            S.op("pe", lambda: nc.tensor.matmul(pm[:, m, :], lhsT=a[:, k, mm * 128:(mm + 1) * 128],
                                                                rhs=scv[:, k, :], start=(k == 0), stop=(k == 7)),
                                 reads=[f"astg{it%2}", "scv"], writes=["pm"])
                    it += 1
                for w in range(2):
                    S.op("dve", lambda: nc.vector.tensor_tensor(out=modv[:, l, :, w], in0=pm[:, :, w], in1=adab[:, l, :],
                                                                op=ALU.add),
                         reads=["pm", "adab"], writes=["modv"])
                for ji, j in enumerate((1, 4)):
                    S.op("dve", lambda: nc.vector.tensor_scalar_add(out=onep[:, l, ji, :, :],
                                                                    in0=modv[:, l, j * 8:(j + 1) * 8, :], scalar1=1.0),
                         reads=["modv"], writes=["onep"])
        S.barrier()

        def mod(l, j, c, w):
            return modv[:, l, j * 8 + c, w:w + 1]

        def modulate_chunk(ph_bufs, l, jshift, ji_scale, t0, n, w, dst_fn, dst_keys):
            sq, pms, sd, rstd, tmp = ph_bufs
            for c in range(8):
                S.op("act", lambda: nc.scalar.activation(out=sq[c % 2][:, :n], in_=X[:, c, t0:t0 + n], func=AF.Square),
                     reads=[f"X{tt}" for tt in range(t0 // 128, (t0 + n) // 128)], writes=[f"msq{c%2}"])
                S.op("pe", lambda: nc.tensor.matmul(pms[:, :n], lhsT=CM("ones1024"), rhs=sq[c % 2][:, :n],
                                                    start=(c == 0), stop=(c == 7)),
                     reads=[f"msq{c%2}", "cm_b"], writes=["pms"])
            S.op("act", lambda: nc.scalar.activation(out=sd[:, :n], in_=pms[:, :n], func=AF.Sqrt, bias=eps_t[:, 0:1]),
                 reads=["pms", "eps"], writes=["msd"])
            S.op("dve", lambda: nc.vector.reciprocal(out=rstd[:, :n], in_=sd[:, :n]), reads=["msd"], writes=["mrstd"])
            for c in range(8):
                S.op("dve", lambda: nc.vector.scalar_tensor_tensor(out=tmp[c % 2][:, :n], in0=X[:, c, t0:t0 + n],
                                                                   scalar=onep[:, l, ji_scale, c, w:w + 1],
                                                                   in1=rstd[:, :n], op0=ALU.mult, op1=ALU.mult),
                     reads=[f"X{tt}" for tt in range(t0 // 128, (t0 + n) // 128)] + ["mrstd", "onep"],
                     writes=[f"mtmp{c%2}"])
                S.op("act", lambda: nc.scalar.activation(out=dst_fn(c), in_=tmp[c % 2][:, :n], func=AF.Identity,
                                                         bias=mod(l, jshift, c, w), scale=1.0),
                     reads=[f"mtmp{c%2}", "modv"], writes=dst_keys)

        eps_t = sb(st, "eps_t", [128, 1])
        S.op("pool", lambda: nc.gpsimd.memset(eps_t[:], EPS), writes=["eps"])

        def mod_bufs(ph):
            sq = [sb(ph, f"msq{i}", [128, 512], BF16) for i in range(2)]
            pms = ps(ph, "pms", [128, 512])
            sd = sb(ph, "msd", [128, 512])
            rstd = sb(ph, "mrstd", [128, 512])
            tmp = [sb(ph, f"mtmp{i}", [128, 512]) for i in range(2)]
            return (sq, pms, sd, rstd, tmp)

        def wload(stg, stg_key, dst_ap, dst_keys, dram_ap, shape, q):
            view = stg[:, :int(np.prod(shape[1:]))]
            if len(shape) == 3:
                view = view.rearrange("p (a b) -> p a b", a=shape[1])
            view = view[:shape[0]] if shape[0] < 128 else view
            S.dma(q, view, dram_ap, writes=[stg_key])
            rr["cast"] = (rr["cast"] + 1) % 2
            e = ("dve", "pool")[rr["cast"]]
            S.op(e, copy_op(e, dst_ap, view), reads=[stg_key], writes=dst_keys)

        def xkeys(t0, n):
            return [f"X{tt}" for tt in range(t0 // 128, (t0 + n + 127) // 128)]

        def layer0():
            l = 0
            with ExitStack() as L:
                cqn = sb(L, "cqn", [128, 2, T], BF16)
                ckvn = sb(L, "ckvn", [128, 2, T], BF16)
                KR = sb(L, "KR", [96, T], BF16)
                LA = ExitStack()
                QA = sb(LA, "QA", [128, 4, T], BF16)
                KA = sb(LA, "KA", [128, 2, T], BF16)
                VA = sb(LA, "VA", [128, NT, 2, 128], BF16)
                S.op("pool", lambda: nc.gpsimd.memset(VA[:], 1.0), writes=["VA"])
                with ExitStack() as ph:
                    win = sb(ph, "win", [128, 8, 1312], BF16)
                    wkd = sb(ph, "wkd", [128, 8, 256], BF16)
                    wkr = sb(ph, "wkr", [128, 8, 96], BF16)
                    abv = abin_d.rearrange("(kc p) n -> p kc n", p=128)
                    with ExitStack() as phs:
                        stg = [sb(phs, f"stg{i}", [128, 4096]) for i in range(2)]
                        for i, (c0, c1) in enumerate([(0, 512), (512, 1024), (1024, 1312)]):
                            wload(stg[i % 2], f"stg{i%2}", win[:, :, c0:c1], ["win"], abv[:, :, c0:c1], [128, 8, c1 - c0],
                                  "sp" if i % 2 == 0 else "pool")
                        S.barrier()
                    for g in range(2):
                        for hf in range(2):
                            S.op("pool", lambda: nc.gpsimd.tensor_copy(out=wkd[:, :, g * 128 + hf * 64:g * 128 + hf * 64 + 64],
                                                                       in_=win[:, :, 512 + g * 64:512 + g * 64 + 64]),
                                 reads=["win"], writes=["wkd"])
                    S.op("pool", lambda: nc.gpsimd.memset(wkr[:], 0.0), writes=["wkr"])
                    S.op("pool", lambda: nc.gpsimd.tensor_copy(out=wkr[:, :, 64:96], in_=win[:, :, 1280:1312]),
                         reads=["win"], writes=["wkr"])
                    hbuf = sb(ph, "hbuf", [128, 8, 512], BF16)
                    mb = mod_bufs(ph)
                    tabs1 = [sb(ph, f"tab{j}", [128, 512]) for j in range(4)]
                    tabs = [tabs1, tabs1]
                    pin = [ps(ph, f"pin{i}", [128, 512]) for i in range(2)]
                    pms2 = ps(ph, "pms2", [128, 512])
                    prot = ps(ph, "prot", [128, 512])
                    sq2 = [sb(ph, f"sq2{i}", [128, 512], BF16) for i in range(2)]
                    sd2 = sb(ph, "sd2", [128, 512])
                    rs2 = sb(ph, "rs2", [128, 512])
                    qn = sb(ph, "qn", [128, 512], BF16)
                    t1 = sb(ph, "t1", [128, 512])
                    t2 = sb(ph, "t2", [128, 512])
                    cnt = {"pin": 0}

                    def pipeline(mm_list, M, n, ones_name, gain_ap, rope, dst_ap, dst_keys, tb):
                        pi = cnt["pin"] % 2
                        cnt["pin"] += 1
                        p_in = pin[pi]
                        for i, (lh, rh, rk) in enumerate(mm_list):
                            S.op("pe", lambda: nc.tensor.matmul(p_in[:M, :n], lhsT=lh, rhs=rh, start=(i == 0),
                                                                stop=(i == len(mm_list) - 1)),
                                 reads=rk, writes=[f"pin{pi}"])
                        S.op("act", lambda: nc.scalar.activation(out=sq2[pi][:M, :n], in_=p_in[:M, :n], func=AF.Square),
                             reads=[f"pin{pi}"], writes=[f"sq2{pi}"])
                        S.op("pe", lambda: nc.tensor.matmul(pms2[:M, :n], lhsT=CM(ones_name)[:M, :M], rhs=sq2[pi][:M, :n],
                                                            start=True, stop=True),
                             reads=[f"sq2{pi}", "cm_b"], writes=["pms2"])
                        S.op("act", lambda: nc.scalar.activation(out=sd2[:M, :n], in_=pms2[:M, :n], func=AF.Sqrt,
                                                                 bias=eps_t[:M, 0:1]),
                             reads=["pms2", "eps"], writes=["sd2"])
                        S.op("dve", lambda: nc.vector.reciprocal(out=rs2[:M, :n], in_=sd2[:M, :n]), reads=["sd2"], writes=["rs2"])
                        o1 = qn[:M, :n] if rope else dst_ap
                        S.op("dve", lambda: nc.vector.scalar_tensor_tensor(out=o1, in0=p_in[:M, :n], scalar=gain_ap,
                                                                           in1=rs2[:M, :n], op0=ALU.mult, op1=ALU.mult),
                             reads=[f"pin{pi}", "rs2", "vecs"], writes=(["qn"] if rope else dst_keys))
                        if rope:
                            rname, ci, si = rope
                            S.op("pe", lambda: nc.tensor.matmul(prot[:M, :n], lhsT=CM(rname)[:M, :M], rhs=qn[:M, :n],
                                                                start=True, stop=True),
                                 reads=["qn", "cm_b"], writes=["prot"])
                            S.op("dve", lambda: nc.vector.tensor_tensor(out=t1[:M, :n], in0=qn[:M, :n], in1=tb[ci][:M, :n],
                                                                        op=ALU.mult),
                                 reads=["qn", f"tab{ci}"], writes=["t1"])
                            S.op("dve", lambda: nc.vector.tensor_tensor(out=t2[:M, :n], in0=prot[:M, :n], in1=tb[si][:M, :n],
                                                                        op=ALU.mult),
                                 reads=["prot", f"tab{si}"], writes=["t2"])
                            S.op("pool", lambda: nc.gpsimd.tensor_tensor(out=dst_ap, in0=t1[:M, :n], in1=t2[:M, :n], op=ALU.add),
                                 reads=["t1", "t2"], writes=dst_keys)

                    K.pipeline = pipeline
                    for ci_, (t0, n) in enumerate(CHUNKS):
                        w = 1 if ci_ == 0 else 0
                        tb = tabs[ci_ % 2]
                        for j, src in enumerate((cosA_d, sinA_d, cosB_d, sinB_d)):
                            np_ = 128 if j < 2 else 96
                            S.dma("pool", tb[j][:np_, :n], src[:, t0:t0 + n], writes=[f"tab{j}"])
                        modulate_chunk(mb, l, 0, 0, t0, n, w, lambda c: hbuf[:, c, :n], ["hbuf"])
                        ck = [f"tk{tt}" for tt in range(t0 // 128, (t0 + n) // 128)]
                        for i in range(4):
                            pipeline([(win[:, k, i * 128:(i + 1) * 128], hbuf[:, k, :n], ["win", "hbuf"]) for k in range(8)],
                                     128, n, "onesA", vecs[:, 0:1], ("RAT", 0, 1), QA[:, i, t0:t0 + n],
                                     [f"QA{i}.{tt}" for tt in range(t0 // 128, (t0 + n) // 128)], tb)
                        for g in range(2):
                            pipeline([(wkd[:, k, g * 128:(g + 1) * 128], hbuf[:, k, :n], ["wkd", "hbuf"]) for k in range(8)],
                                     128, n, "onesA", vecs[:, 1:2], ("RAT", 0, 1), KA[:, g, t0:t0 + n], ["KA"], tb)
                        for tt in range(n // 128):
                            pi = cnt["pin"] % 2
                            cnt["pin"] += 1
                            for k in range(8):
                                S.op("pe", lambda: nc.tensor.matmul(pin[pi][:, :128], lhsT=hbuf[:, k, tt * 128:(tt + 1) * 128],
                                                                    rhs=win[:, k, 640:768], start=(k == 0), stop=(k == 7)),
                                     reads=["win", "hbuf"], writes=[f"pin{pi}"])
                            e = evac_eng()
                            S.op(e, copy_op(e, VA[:, t0 // 128 + tt, :, 0:64], pin[pi][:, :128].rearrange("p (g d) -> p g d", g=2)),
                                 reads=[f"pin{pi}"], writes=["VA"])
                        for (dst, c0, gcol, dk) in ((cqn, 768, 2, "cqn"), (ckvn, 1024, 4, "ckvn")):
                            for i in range(2):
                                for k in range(8):
                                    S.op("pe", lambda: nc.tensor.matmul(pin[i][:, :n], lhsT=win[:, k, c0 + i * 128:c0 + (i + 1) * 128],
                                                                        rhs=hbuf[:, k, :n], start=(k == 0), stop=(k == 7)),
                                         reads=["win", "hbuf"], writes=[f"pin{i}"])
                                S.op("act", lambda: nc.scalar.activation(out=sq2[i][:, :n], in_=pin[i][:, :n], func=AF.Square),
                                     reads=[f"pin{i}"], writes=[f"sq2{i}"])
                            for i in range(2):
                                S.op("pe", lambda: nc.tensor.matmul(pms2[:, :n], lhsT=CM("ones256"), rhs=sq2[i][:, :n],
                                                                    start=(i == 0), stop=(i == 1)),
                                     reads=[f"sq2{i}", "cm_b"], writes=["pms2"])
                            S.op("act", lambda: nc.scalar.activation(out=sd2[:, :n], in_=pms2[:, :n], func=AF.Sqrt, bias=eps_t[:, 0:1]),
                                 reads=["pms2", "eps"], writes=["sd2"])
                            S.op("dve", lambda: nc.vector.reciprocal(out=rs2[:, :n], in_=sd2[:, :n]), reads=["sd2"], writes=["rs2"])
                            for i in range(2):
                                S.op("dve", lambda: nc.vector.scalar_tensor_tensor(out=dst[:, i, t0:t0 + n], in0=pin[i][:, :n],
                                                                                   scalar=vecs[:, gcol + i:gcol + i + 1], in1=rs2[:, :n],
                                                                                   op0=ALU.mult, op1=ALU.mult),
                                     reads=[f"pin{i}", "rs2", "vecs"], writes=[dk])
                        pipeline([(wkr[:, k, :], hbuf[:, k, :n], ["wkr", "hbuf"]) for k in range(8)],
                                 96, n, "onesB", vecs[:96, 6:7], ("RBT", 2, 3), KR[:96, t0:t0 + n], ["KR"], tb)
                S.barrier()
                if dbg == "qa":
                    dump_fm(QA, 4, BF16)
                    LA.close()
                    return
                with ExitStack() as ph:
                    woA = sb(ph, "woA", [128, 4, D], BF16)
                    aov = about_d.rearrange("(kc p) n -> p kc n", p=128)
                    with ExitStack() as phs:
                        stg = [sb(phs, f"stg{i}", [128, 4096]) for i in range(2)]
                        for i in range(2):
                            wload(stg[i], f"stg{i}", woA[:, 2 * i:2 * i + 2, :], ["woA"], aov[:, 2 * i:2 * i + 2, :], [128, 2, D],
                                  "sp" if i == 0 else "pool")
                        S.barrier()
                    sk_raw = sb(ph, "sk_raw", [64, 8])
                    sk_exp = sb(ph, "sk_exp", [64, 8])
                    SE = sb(ph, "SE", [64, 2, 512])
                    S.dma("sp", sk_raw[:], sink_d, writes=["sk_raw"])
                    S.op("act", lambda: nc.scalar.activation(out=sk_exp[:], in_=sk_raw[:], func=AF.Exp), reads=["sk_raw"], writes=["sk_exp"])
                    for g in range(2):
                        for hb in range(4):
                            hf, j = hb // 2, hb % 2
                            hd = 4 * g + 2 * j + hf
                            S.op("dve", lambda: nc.vector.tensor_scalar(out=SE[:, g, hb * 128:(hb + 1) * 128], in0=zeros[:64, :],
                                                                        scalar1=sk_exp[:, hd:hd + 1], scalar2=None, op0=ALU.add),
                                 reads=["zeros", "sk_exp"], writes=["SE"])
                    pS = [[ps(ph, f"pS{i}{hf}", [128, 512]) for hf in range(2)] for i in range(2)]
                    pO = [ps(ph, f"pO{i}", [128, 512]) for i in range(2)]
                    PT = [sb(ph, f"PT{i}", [128, 512], BF16) for i in range(3)]
                    den = sb(ph, "den", [64, 512])
                    rden = sb(ph, "rden", [64, 512])
                    it = 0
                    nit = 0
                    for qb in range(NT):
                        q0 = qb * 128
                        if qb < 2:
                            kts = [(0, None), (1, None)]
                        else:
                            kts = [(0, None), (1, None)]
                            if qb - 1 >= 2:
                                kts.append((qb - 1, "maskPrevT"))
                            kts.append((qb, None))
                            if qb + 1 < NT:
                                kts.append((qb + 1, "maskNextT"))
                        for g in range(2):
                            po = pO[nit % 2]
                            pok = f"pO{nit%2}"
                            nit += 1
                            for ki, (kt, mk) in enumerate(kts):
                                ptb = PT[it % 3]; ptk = f"PT{it%3}"
                                pss = pS[it % 2]
                                it += 1
                                for hf in range(2):
                                    psb = pss[hf]; psk = f"pS{(it-1)%2}{hf}"
                                    pr = slice(hf * 64, (hf + 1) * 64)
                                    if mk is not None:
                                        S.op("pe", lambda: nc.tensor.matmul(psb[:, :256], lhsT=CM(mk),
                                                                            rhs=id4[:, 0:2, :].rearrange("p r n -> p (r n)"),
                                                                            start=True, stop=False),
                                             reads=["cm_b", "id4"], writes=[psk])
                                    S.op("pe", lambda: nc.tensor.matmul(psb[:, :256].rearrange("p (r n) -> p r n", r=2),
                                                                        lhsT=KA[pr, g, kt * 128:(kt + 1) * 128],
                                                                        rhs=QA[pr, 2 * g:2 * g + 2, q0:q0 + 128],
                                                                        start=(mk is None), stop=True),
                                         reads=["KA", f"QA{2*g}.{qb}", f"QA{2*g+1}.{qb}"], writes=[psk])
                                    S.op("act", lambda: nc.scalar.activation(out=ptb[:, hf * 256:(hf + 1) * 256], in_=psb[:, :256],
                                                                             func=AF.Exp, scale=0.125),
                                         reads=[psk], writes=[ptk])
                                S.op("pe", lambda: nc.tensor.matmul(po[:], lhsT=VA[:, kt, g, :], rhs=ptb[:], start=(ki == 0),
                                                                    stop=(ki == len(kts) - 1)),
                                     reads=["VA", ptk], writes=[pok])
                            S.op("dve", lambda: nc.vector.tensor_tensor(out=den[:], in0=po[64:128, :], in1=SE[:, g, :], op=ALU.add),
                                 reads=[pok, "SE"], writes=["den"])
                            S.op("dve", lambda: nc.vector.reciprocal(out=rden[:], in_=den[:]), reads=["den"], writes=["rden"])
                            for hf in range(2):
                                S.op("dve", lambda: nc.vector.tensor_tensor(
                                    out=QA[hf * 64:(hf + 1) * 64, 2 * g:2 * g + 2, q0:q0 + 128],
                                    in0=po[0:64, hf * 256:(hf + 1) * 256].rearrange("p (r n) -> p r n", r=2),
                                    in1=rden[:, hf * 256:(hf + 1) * 256].rearrange("p (r n) -> p r n", r=2), op=ALU.mult),
                                     reads=[pok, "rden"], writes=[f"QA{2*g}.{qb}", f"QA{2*g+1}.{qb}"])
                    if dbg == "oa":
                        S.barrier()
                        dump_fm(QA, 4, BF16)
                        K.stop = True
                    pd = [ps(ph, f"pd{i}", [128, 512]) for i in range(2)]
                    it = 0
                    for ci_, (t0, n) in enumerate([] if K.stop else CHUNKS):
                        w = 1 if ci_ == 0 else 0
                        for m in range(8):
                            p_ = pd[it % 2]; pk = f"pd{it%2}"; it += 1
                            for i in range(4):
                                S.op("pe", lambda: nc.tensor.matmul(p_[:, :n], lhsT=woA[:, i, m * 128:(m + 1) * 128], rhs=QA[:, i, t0:t0 + n],
                                                                    start=(i == 0), stop=(i == 3)),
                                     reads=["woA"] + [f"QA{i}.{tt}" for tt in range(t0 // 128, (t0 + n) // 128)], writes=[pk])
                            S.op("dve", lambda: nc.vector.scalar_tensor_tensor(out=X[:, m, t0:t0 + n], in0=p_[:, :n], scalar=mod(l, 2, m, w),
                                                                               in1=X[:, m, t0:t0 + n], op0=ALU.mult, op1=ALU.add),
                                 reads=[pk, "modv"] + xkeys(t0, n), writes=xkeys(t0, n))
                S.barrier()
                LA.close()
                if K.stop:
                    return
                with ExitStack() as ph:
                    wuq = sb(ph, "wuq", [128, 2, 768], BF16)
                    wuk = sb(ph, "wuk", [128, 2, 512], BF16)
                    wuv = sb(ph, "wuv", [128, 2, 512], BF16)
                    woB = sb(ph, "woB", [128, 4, D], BF16)
                    with ExitStack() as phs:
                        stg = [sb(phs, f"stg{i}", [128, 4096]) for i in range(2)]
                        wload(stg[0], "stg0", wuq[:], ["wuq"], wuq_d.rearrange("(kc p) n -> p kc n", p=128), [128, 2, 768], "sp")
                        wload(stg[1], "stg1", wuk[:], ["wuk"], wuk_d.rearrange("(kc p) n -> p kc n", p=128), [128, 2, 512], "pool")
                        wload(stg[0], "stg0", wuv[:], ["wuv"], wuv_d.rearrange("(kc p) n -> p kc n", p=128), [128, 2, 512], "sp")
                        aov = about_d.rearrange("(kc p) n -> p kc n", p=128)
                        for i in range(2):
                            wload(stg[(i + 1) % 2], f"stg{(i+1)%2}", woB[:, 2 * i:2 * i + 2, :], ["woB"], aov[:, 4 + 2 * i:4 + 2 * i + 2, :],
                                  [128, 2, D], "pool" if i == 0 else "sp")
                        S.barrier()
                    KB = sb(ph, "KB", [96, 4, T], BF16)
                    VB = sb(ph, "VB", [128, NT, 4, 128], BF16)
                    S.op("pool", lambda: nc.gpsimd.memset(VB[:], 1.0), writes=["VB"])
                    tabs1 = [None, None] + [sb(ph, f"tab{j}", [128, 512]) for j in (2, 3)]
                    tabs = [tabs1, tabs1]
                    pin = [ps(ph, f"pin{i}", [128, 512]) for i in range(2)]
                    pms2 = ps(ph, "pms2", [128, 512])
                    prot = ps(ph, "prot", [128, 512])
                    pS = [ps(ph, f"pS{i}", [128, 512]) for i in range(2)]
                    pO = ps(ph, "pO", [128, 512])
                    pd = ps(ph, "pd", [128, 512])
                    sq2 = [sb(ph, f"sq2{i}", [128, 512], BF16) for i in range(2)]
                    sd2 = sb(ph, "sd2", [128, 512])
                    rs2 = sb(ph, "rs2", [128, 512])
                    qn = sb(ph, "qn", [128, 512], BF16)
                    t1 = sb(ph, "t1", [128, 512])
                    t2 = sb(ph, "t2", [128, 512])
                    QBc = sb(ph, "QBc", [96, 4, 512], BF16)
                    Yc = sb(ph, "Yc", [128, 2, 512], BF16)
                    PT = [sb(ph, f"PT{i}", [128, 512], BF16) for i in range(3)]
                    rden = sb(ph, "rden", [64, 512])
                    cnt = {"pin": 0}

                    def pipeline(mm_list, M, n, ones_name, gain_ap, rope, dst_ap, dst_keys, tb):
                        pi = cnt["pin"] % 2
                        cnt["pin"] += 1
                        p_in = pin[pi]
                        for i, (lh, rh, rk) in enumerate(mm_list):
                            S.op("pe", lambda: nc.tensor.matmul(p_in[:M, :n], lhsT=lh, rhs=rh, start=(i == 0),
                                                                stop=(i == len(mm_list) - 1)),
                                 reads=rk, writes=[f"pin{pi}"])
                        S.op("act", lambda: nc.scalar.activation(out=sq2[pi][:M, :n], in_=p_in[:M, :n], func=AF.Square),
                             reads=[f"pin{pi}"], writes=[f"sq2{pi}"])
                        S.op("pe", lambda: nc.tensor.matmul(pms2[:M, :n], lhsT=CM(ones_name)[:M, :M], rhs=sq2[pi][:M, :n],
                                                            start=True, stop=True),
                             reads=[f"sq2{pi}", "cm_b"], writes=["pms2"])
                        S.op("act", lambda: nc.scalar.activation(out=sd2[:M, :n], in_=pms2[:M, :n], func=AF.Sqrt,
                                                                 bias=eps_t[:M, 0:1]),
                             reads=["pms2", "eps"], writes=["sd2"])
                        S.op("dve", lambda: nc.vector.reciprocal(out=rs2[:M, :n], in_=sd2[:M, :n]), reads=["sd2"], writes=["rs2"])
                        o1 = qn[:M, :n] if rope else dst_ap
                        S.op("dve", lambda: nc.vector.scalar_tensor_tensor(out=o1, in0=p_in[:M, :n], scalar=gain_ap,
                                                                           in1=rs2[:M, :n], op0=ALU.mult, op1=ALU.mult),
                             reads=[f"pin{pi}", "rs2", "vecs"], writes=(["qn"] if rope else dst_keys))
                        if rope:
                            rname, ci, si = rope
                            S.op("pe", lambda: nc.tensor.matmul(prot[:M, :n], lhsT=CM(rname)[:M, :M], rhs=qn[:M, :n],
                                                                start=True, stop=True),
                                 reads=["qn", "cm_b"], writes=["prot"])
                            S.op("dve", lambda: nc.vector.tensor_tensor(out=t1[:M, :n], in0=qn[:M, :n], in1=tb[ci][:M, :n],
                                                                        op=ALU.mult),
                                 reads=["qn", f"tab{ci}"], writes=["t1"])
                            S.op("dve", lambda: nc.vector.tensor_tensor(out=t2[:M, :n], in0=prot[:M, :n], in1=tb[si][:M, :n],
                                                                        op=ALU.mult),
                                 reads=["prot", f"tab{si}"], writes=["t2"])
                            S.op("pool", lambda: nc.gpsimd.tensor_tensor(out=dst_ap, in0=t1[:M, :n], in1=t2[:M, :n], op=ALU.add),
                                 reads=["t1", "t2"], writes=dst_keys)

                    it = 0
                    for p in range(2):
                        for hl in range(4):
                            h = 4 * p + hl
                            for (t0, n) in CHUNKS:
                                pipeline([(wuk[:, k, h * 64:(h + 1) * 64], ckvn[:, k, t0:t0 + n], ["wuk", "ckvn"]) for k in range(2)],
                                         64, n, "onesA", vecs[:64, 7:8], None, KB[0:64, hl, t0:t0 + n], ["KB"], None)
                            S.op("pool", lambda: nc.gpsimd.tensor_copy(out=KB[64:96, hl, :], in_=KR[64:96, :]), reads=["KR"], writes=["KB"])
                        for tt in range(NT):
                            pi = cnt["pin"] % 2
                            cnt["pin"] += 1
                            for k in range(2):
                                S.op("pe", lambda: nc.tensor.matmul(pin[pi][:, :256], lhsT=ckvn[:, k, tt * 128:(tt + 1) * 128],
                                                                    rhs=wuv[:, k, p * 256:(p + 1) * 256], start=(k == 0), stop=(k == 1)),
                                     reads=["wuv", "ckvn"], writes=[f"pin{pi}"])
                            e = evac_eng()
                            S.op(e, copy_op(e, VB[:, tt, :, 0:64], pin[pi][:, :256].rearrange("p (g d) -> p g d", g=4)),
                                 reads=[f"pin{pi}"], writes=["VB"])
                        for ci_, (t0, n) in enumerate(CHUNKS):
                            w = 1 if ci_ == 0 else 0
                            tb = tabs[ci_ % 2]
                            for j, src in ((2, cosB_d), (3, sinB_d)):
                                S.dma("pool", tb[j][:96, :n], src[:, t0:t0 + n], writes=[f"tab{j}"])
                            kts = [0, 1] if ci_ == 0 else list(range(NT))
                            for hl in range(4):
                                h = 4 * p + hl
                                pipeline([(wuq[:, k, h * 96:(h + 1) * 96], cqn[:, k, t0:t0 + n], ["wuq", "cqn"]) for k in range(2)],
                                         96, n, "onesB", vecs[:96, 8:9], ("RBT", 2, 3), QBc[:96, hl, :n], [f"QBc{hl}"], tb)
                            for hl in range(4):
                                for ki, kt in enumerate(kts):
                                    psb = pS[it % 2]; psk = f"pS{it%2}"
                                    ptb = PT[it % 3]; ptk = f"PT{it%3}"
                                    it += 1
                                    S.op("pe", lambda: nc.tensor.matmul(psb[:, :n], lhsT=KB[:96, hl, kt * 128:(kt + 1) * 128],
                                                                        rhs=QBc[:96, hl, :n], start=True, stop=True),
                                         reads=["KB", f"QBc{hl}"], writes=[psk])
                                    S.op("act", lambda: nc.scalar.activation(out=ptb[:, :n], in_=psb[:, :n], func=AF.Exp,
                                                                             scale=float(96 ** -0.5)),
                                         reads=[psk], writes=[ptk])
                                    S.op("pe", lambda: nc.tensor.matmul(pO[:, :n], lhsT=VB[:, kt, hl, :], rhs=ptb[:, :n],
                                                                        start=(ki == 0), stop=(ki == len(kts) - 1)),
                                         reads=["VB", ptk], writes=["pO"])
                                S.op("dve", lambda: nc.vector.reciprocal(out=rden[:, :n], in_=pO[64:128, :n]), reads=["pO"], writes=["rden"])
                                S.op("dve", lambda: nc.vector.tensor_tensor(out=Yc[(hl % 2) * 64:(hl % 2) * 64 + 64, hl // 2, :n],
                                                                            in0=pO[0:64, :n], in1=rden[:, :n], op=ALU.mult),
                                     reads=["pO", "rden"], writes=["Yc"])
                            for m in range(8):
                                for i in range(2):
                                    S.op("pe", lambda: nc.tensor.matmul(pd[:, :n], lhsT=woB[:, 2 * p + i, m * 128:(m + 1) * 128],
                                                                        rhs=Yc[:, i, :n], start=(i == 0), stop=(i == 1)),
                                         reads=["woB", "Yc"], writes=["pd"])
                                S.op("dve", lambda: nc.vector.scalar_tensor_tensor(out=X[:, m, t0:t0 + n], in0=pd[:, :n], scalar=mod(l, 2, m, w),
                                                                                   in1=X[:, m, t0:t0 + n], op0=ALU.mult, op1=ALU.add),
                                     reads=["pd", "modv"] + xkeys(t0, n), writes=xkeys(t0, n))
                S.barrier()

        def ffn(l, last):
            G = 3
            groups = [list(range(j, min(j + G, 22))) for j in range(0, 22, G)]
            tch = [(0, 256, 0, 0)]
            for i in range(8):
                tch.append((256 + i * 256, 256, 0 if i == 0 else 1, 0 if i == 7 else 1))
            if last:
                tch = tch[1:]
            with ExitStack() as ph:
                H2 = sb(ph, "H2", [128, 8, T], BF16)
                fcv = sb(ph, "fcv", [128, 44, 4])
                S.dma("sp", fcv[:], fconv_d[l], writes=["fcv"])
                with ExitStack() as ph2:
                    mb = mod_bufs(ph2)
                    for ci_, (t0, n) in enumerate(CHUNKS):
                        w = 1 if ci_ == 0 else 0
                        if last and ci_ == 0:
                            continue
                        modulate_chunk(mb, l, 3, 1, t0, n, w, lambda c: H2[:, c, t0:t0 + n], ["H2"])
                    S.barrier()
                stg = [sb(ph, f"fstg{i}", [128, 4096]) for i in range(2)]
                wup = [sb(ph, f"wup{i}", [128, 8, 2 * G * 128], BF16) for i in range(2)]
                wdn = [sb(ph, f"wdn{i}", [128, G, D], BF16) for i in range(2)]
                pu = [ps(ph, f"pu{i}", [128, 512]) for i in range(4)]
                pdn = [ps(ph, f"pdn{i}", [128, 512]) for i in range(2)]
                uu = [sb(ph, f"uu{i}", [128, 256]) for i in range(4)]
                sg = sb(ph, "sg", [128, 256])
                actb = [sb(ph, f"actb{i}", [128, 256], BF16) for i in range(2 * G)]
                upv = wup_d[l].rearrange("(kc p) n -> p kc n", p=128)
                dnv = wdn_d[l].rearrange("(kc p) n -> p kc n", p=128)
                si = 0
                iu = 0
                ia = 0
                idn = 0
                for gi, js in enumerate(groups):
                    g_n = len(js)
                    wu = wup[gi % 2]; wd = wdn[gi % 2]
                    j0 = js[0]
                    for part in range(2):
                        wload(stg[si % 2], f"fstg{si%2}", wu[:, :, part * G * 128:part * G * 128 + g_n * 128], [f"wup{gi%2}"],
                              upv[:, :, part * DFF + j0 * 128:part * DFF + (j0 + g_n) * 128], [128, 8, g_n * 128],
                              "sp" if si % 2 == 0 else "pool")
                        si += 1
                    wload(stg[si % 2], f"fstg{si%2}", wd[:, :g_n, :], [f"wdn{gi%2}"], dnv[:, j0:j0 + g_n, :], [128, g_n, D],
                          "sp" if si % 2 == 0 else "pool")
                    si += 1
                    for (s0, n, lo, hi) in tch:
                        w = 1 if s0 == 0 else 0
                        e0 = s0 - lo
                        ne = n + lo + hi
                        abufs = []
                        for jj, j in enumerate(js):
                            us = []
                            for part in range(2):
                                p_ = pu[iu % 4]; pk = f"pu{iu%4}"
                                u_ = uu[iu % 4]; uk = f"uu{iu%4}"
                                iu += 1
                                fc = part * 22 + j
                                for k in range(8):
                                    S.op("pe", lambda: nc.tensor.matmul(p_[:, :ne], lhsT=wu[:, k, (part * G + jj) * 128:(part * G + jj + 1) * 128],
                                                                        rhs=H2[:, k, e0:e0 + ne], start=(k == 0), stop=(k == 7)),
                                         reads=[f"wup{gi%2}", "H2"], writes=[pk])
                                S.op("act", lambda: nc.scalar.activation(out=u_[:, :n], in_=p_[:, lo:lo + n], func=AF.Identity,
                                                                         bias=fcv[:, fc, 3:4], scale=fcv[:, fc, 1:2]),
                                     reads=[pk, "fcv"], writes=[uk])
                                a = 1 if lo == 0 else 0
                                S.op("dve", lambda: nc.vector.scalar_tensor_tensor(out=u_[:, a:n], in0=p_[:, lo - 1 + a:lo + n - 1],
                                                                                   scalar=fcv[:, fc, 0:1], in1=u_[:, a:n],
                                                                                   op0=ALU.mult, op1=ALU.add),
                                     reads=[pk, "fcv", uk], writes=[uk])
                                b = 1 if hi == 0 else 0
                                S.op("dve", lambda: nc.vector.scalar_tensor_tensor(out=u_[:, :n - b], in0=p_[:, lo + 1:lo + 1 + n - b],
                                                                                   scalar=fcv[:, fc, 2:3], in1=u_[:, :n - b],
                                                                                   op0=ALU.mult, op1=ALU.add),
                                     reads=[pk, "fcv", uk], writes=[uk])
                                us.append((u_, uk))
                            (uv, uvk), (ug, ugk) = us
                            S.op("act", lambda: nc.scalar.activation(out=sg[:, :n], in_=ug[:, :n], func=AF.Silu), reads=[ugk], writes=["sg"])
                            ab = actb[ia % (2 * G)]; abk = f"actb{ia%(2*G)}"
                            ia += 1
                            S.op("pool", lambda: nc.gpsimd.tensor_tensor(out=ab[:, :n], in0=sg[:, :n], in1=uv[:, :n], op=ALU.mult),
                                 reads=["sg", uvk], writes=[abk])
                            abufs.append((ab, abk))
                        for m in range(8):
                            p_ = pdn[idn % 2]; pk = f"pdn{idn%2}"; idn += 1
                            for jj in range(g_n):
                                S.op("pe", lambda: nc.tensor.matmul(p_[:, :n], lhsT=wd[:, jj, m * 128:(m + 1) * 128], rhs=abufs[jj][0][:, :n],
                                                                    start=(jj == 0), stop=(jj == g_n - 1)),
                                     reads=[f"wdn{gi%2}", abufs[jj][1]], writes=[pk])
                            S.op("dve", lambda: nc.vector.scalar_tensor_tensor(out=X[:, m, s0:s0 + n], in0=p_[:, :n], scalar=mod(l, 5, m, w),
                                                                               in1=X[:, m, s0:s0 + n], op0=ALU.mult, op1=ALU.add),
                                 reads=[pk, "modv"] + xkeys(s0, n), writes=xkeys(s0, n))
            S.barrier()

        def dump_fm(buf, nchunk, dt, ntile=NT):
            with ExitStack() as ph:
                tin = sb(ph, "d_tin", [128, 128])
                pt_ = ps(ph, "d_pt", [128, 128])
                ob = sb(ph, "d_ob", [128, D])
                for t in range(ntile):
                    for c in range(nchunk):
                        S.op("dve", lambda: nc.vector.tensor_copy(out=tin[:], in_=buf[:, c, t * 128:(t + 1) * 128]), reads=["*"], writes=["d_tin"])
                        S.op("pe", lambda: nc.tensor.transpose(out=pt_[:], in_=tin[:], identity=CM("ident", False)),
                             reads=["d_tin", "cm_f"], writes=["d_pt"])
                        S.op("dve", lambda: nc.vector.tensor_copy(out=ob[:, c * 128:(c + 1) * 128], in_=pt_[:]), reads=["d_pt"], writes=["d_ob"])
                    S.dma("sp", dbg_d[t * 128:(t + 1) * 128, :nchunk * 128], ob[:, :nchunk * 128], reads=["d_ob"])
                S.barrier()

        def write_out():
            with ExitStack() as ph:
                pt_ = [ps(ph, f"o_pt{i}", [128, 512]) for i in range(4)]
                ob = [sb(ph, f"o_ob{i}", [128, D]) for i in range(2)]
                for t in range(2, NT):
                    o_ = ob[t % 2]
                    for hh in range(2):
                        p_ = pt_[(t % 2) * 2 + hh]; pk = f"o_pt{(t%2)*2+hh}"
                        for c4 in range(4):
                            c = hh * 4 + c4
                            S.op("pe", lambda: nc.tensor.transpose(out=p_[:, c4 * 128:(c4 + 1) * 128], in_=X[:, c, t * 128:(t + 1) * 128],
                                                                   identity=CM("ident", False)),
                                 reads=[f"X{t}", "cm_f"], writes=[pk])
                        e = evac_eng()
                        S.op(e, copy_op(e, o_[:, hh * 512:(hh + 1) * 512], p_[:]), reads=[pk], writes=[f"o_ob{t%2}"])
                    S.dma("sp" if t % 2 == 0 else "pool", out_d[(t - 2) * 128:(t - 1) * 128, :], o_[:], reads=[f"o_ob{t%2}"])
                if dbg == "x":
                    for t in range(2):
                        o_ = ob[t % 2]
                        for hh in range(2):
                            p_ = pt_[(t % 2) * 2 + hh]; pk = f"o_pt{(t%2)*2+hh}"
                            for c4 in range(4):
                                c = hh * 4 + c4
                                S.op("pe", lambda: nc.tensor.transpose(out=p_[:, c4 * 128:(c4 + 1) * 128], in_=X[:, c, t * 128:(t + 1) * 128],
                                                                       identity=CM("ident", False)),
                                     reads=[f"X{t}", "cm_f"], writes=[pk])
                            e = evac_eng()
                            S.op(e, copy_op(e, o_[:, hh * 512:(hh + 1) * 512], p_[:]), reads=[pk], writes=[f"o_ob{t%2}"])
                        S.dma("sp", dbg_d[t * 128:(t + 1) * 128, :], o_[:], reads=[f"o_ob{t%2}"])


        def layer1():
            l = 1
            INC = 1920
            NCH = T // 64
            fwd_order = list(range(NCH))
            bwd_order = [3, 2, 1, 0] + list(range(NCH - 1, 3, -1))
            PCH = [(0, 256, 0, 256)] + [(256 + i * 256, 256, 256, T) for i in range(8)]
            with ExitStack() as L:
                L0 = L
                H1 = sb(L, "H1", [128, 8, T], BF16)
                Yall = sb(L, "Yall", [128, 8, 2048], BF16)
                v1 = sb(L, "v1", [128, 160])
                cm2 = sb(L, "cm2", [64, 320])
                ones64 = sb(L, "ones64", [128, 64])
                id64b = sb(L, "id64b", [64, 64], BF16)
                S.dma("sp", v1[:], v1_d, writes=["v1"])
                S.dma("sp", cm2[:], cm2_d, writes=["cm2"])
                S.op("pool", lambda: nc.gpsimd.memset(ones64[:], 1.0), writes=["ones64"])
                S.op("dve", lambda: nc.vector.tensor_copy(out=id64b[:], in_=cm2[:, 256:320]), reads=["cm2"], writes=["id64b"])
                S.op("dve", lambda: nc.vector.tensor_tensor(out=v1[:, 30:45], in0=v1[:, 0:15], in1=v1[:, 15:30], op=ALU.add), reads=["v1"], writes=["v1"])
                S.op("dve", lambda: nc.vector.tensor_scalar(out=v1[:, 30:45], in0=v1[:, 30:45], scalar1=-1.0, scalar2=1.0, op0=ALU.mult, op1=ALU.add),
                     reads=["v1"], writes=["v1"])
                with ExitStack() as ph2:
                    mb = mod_bufs(ph2)
                    for ci_, (t0, n) in enumerate(CHUNKS):
                        w = 1 if ci_ == 0 else 0
                        modulate_chunk(mb, l, 0, 0, t0, n, w, lambda c: H1[:, c, t0:t0 + n], ["H1"])
                    S.barrier()
                for c in range(8):
                    S.dma("sp" if c % 2 == 0 else "pool", xpark_d[:, c * T:(c + 1) * T], X[:, c, :], reads=[f"X{t}" for t in range(NT)])
                S.barrier()
                XB = X[:].rearrange("p c t -> p (c t)").bitcast(BF16)
                XF = X[:].rearrange("p c t -> p (c t)")

                def xb(i):
                    return XB[:, i * T:(i + 1) * T]

                def xf(i):
                    return XF[:, i * T:(i + 1) * T]

                LS = ExitStack()
                stg = sb(LS, "l1stg", [128, 1024])
                wg = sb(LS, "l1wg", [128, 8, 512], BF16)
                Yacc = sb(LS, "Yacc", [128, 2048], BF16)
                gneps = sb(LS, "gneps", [128, 1])
                S.op("pool", lambda: nc.gpsimd.memset(gneps[:], 64e-5), writes=["gneps"])
                L = LS
                pin = [ps(L, f"l1pin{i}", [128, 512]) for i in range(2)]
                pA = [ps(L, f"l1pA{i}", [64, 512]) for i in range(2)]
                pSc = [ps(L, f"l1pS{i}", [128, 512]) for i in range(2)]
                pTr = ps(L, "l1pT", [64, 512], BF16)
                uw = [sb(L, f"l1u{i}", [128, 260]) for i in range(2)]
                cnt = {"pin": 0}
                cdv = cdin_d.rearrange("(kc p) n -> p kc n", p=128)

                def load_cols(cols_list):
                    off = 0
                    for (c0, ncol) in cols_list:
                        for b0 in range(0, ncol, 128):
                            nb = min(128, ncol - b0)
                            wload(stg, "l1stg", wg[:, :, off:off + nb], ["l1wg"], cdv[:, :, c0 + b0:c0 + b0 + nb], [128, 8, nb], "sp")
                            off += nb

                def proj_taps(woff, M, taps, func, out_fn, out_keys, bias_ap=None, scale_out=None):
                    hw = max(abs(o) for o, _ in taps)
                    for (s0, n, qlo, qhi) in PCH:
                        e0 = max(s0 - hw, qlo); e1 = min(s0 + n + hw, qhi)
                        ne = e1 - e0
                        pi = cnt["pin"] % 2; cnt["pin"] += 1
                        p_ = pin[pi]; u_ = uw[pi]
                        for k in range(8):
                            S.op("pe", lambda: nc.tensor.matmul(p_[:M, :ne], lhsT=wg[:, k, woff:woff + M], rhs=H1[:, k, e0:e1],
                                                                start=(k == 0), stop=(k == 7)),
                                 reads=["l1wg", "H1"], writes=[f"l1pin{pi}"])
                        src = p_
                        if len(taps) > 1:
                            base = s0 - e0
                            o0, c0_ = taps[0]
                            S.op("act", lambda: nc.scalar.activation(out=u_[:M, :n], in_=p_[:M, base:base + n], func=AF.Identity,
                                                                     scale=c0_),
                                 reads=[f"l1pin{pi}", "v1"], writes=[f"l1u{pi}"])
                            for (o, cf) in taps[1:]:
                                i0 = max(0, e0 - s0 - o); i1 = min(n, e1 - s0 - o)
                                S.op("dve", lambda: nc.vector.scalar_tensor_tensor(out=u_[:M, i0:i1], in0=p_[:M, base + i0 + o:base + i1 + o],
                                                                                   scalar=cf, in1=u_[:M, i0:i1], op0=ALU.mult, op1=ALU.add),
                                     reads=[f"l1pin{pi}", f"l1u{pi}", "v1"], writes=[f"l1u{pi}"])
                            src = u_
                            srck = f"l1u{pi}"
                            sl = slice(0, n)
                        else:
                            srck = f"l1pin{pi}"
                            sl = slice(s0 - e0, s0 - e0 + n)
                        kw = {}
                        if bias_ap is not None:
                            kw["bias"] = bias_ap
                        if scale_out is not None:
                            kw["scale"] = scale_out
                        S.op("act", lambda: nc.scalar.activation(out=out_fn(s0, n), in_=src[:M, sl], func=func, **kw),
                             reads=[srck, "v1"], writes=out_keys)

                def shift_taps(fc):
                    return [(0, v1[:, 30 + fc:31 + fc]), (-1, v1[:, fc:fc + 1]), (1, v1[:, 15 + fc:16 + fc])]

                wk = {}
                for dn in ("f", "b"):
                    wk[dn] = dict(
                        al=sb(L, f"al{dn}", [128, 64]), be=sb(L, f"be{dn}", [128, 64]), kd=sb(L, f"kd{dn}", [128, 64]),
                        rr=sb(L, f"rr{dn}", [128, 64]), lw=sb(L, f"lw{dn}", [128, 64]),
                        pfx=sb(L, f"pfx{dn}", [128, 64]), Gi=sb(L, f"Gi{dn}", [128, 64]), Ge=sb(L, f"Ge{dn}", [128, 64]),
                        E=sb(L, f"E{dn}", [128, 4, 64]), Et=sb(L, f"Et{dn}", [128, 1]), Es=sb(L, f"Es{dn}", [128, 3, 64]),
                        negm=sb(L, f"negm{dn}", [128, 1]), ART=sb(L, f"ART{dn}", [128, 2, 64], BF16),
                        AR=sb(L, f"AR{dn}", [128, 2, 64], BF16), BH=sb(L, f"BH{dn}", [128, 64], BF16), KH=sb(L, f"KH{dn}", [128, 64], BF16),
                        BT=sb(L, f"BT{dn}", [128, 64], BF16), KT=sb(L, f"KT{dn}", [128, 64], BF16),
                        BTt=sb(L, f"BTt{dn}", [64, 128], BF16), KTt=sb(L, f"KTt{dn}", [64, 128], BF16), VT=sb(L, f"VT{dn}", [64, 128], BF16),
                        Hs=sb(L, f"Hs{dn}", [128, 128]), Hb=sb(L, f"Hb{dn}", [128, 128], BF16), ztmp=sb(L, f"ztmp{dn}", [64, 128]),
                    )
                    for j in range(2):
                        idt = BF16 if DBG['invbf'] else F32
                        wk[dn][f"N{j}"] = [sb(L, f"N{dn}{j}{i}", [64, 64], idt) for i in range(2)]
                        wk[dn][f"NT{j}"] = [sb(L, f"NT{dn}{j}{i}", [64, 64], idt) for i in range(2)]
                        wk[dn][f"IN{j}"] = sb(L, f"IN{dn}{j}", [64, 64], idt)
                        wk[dn][f"PT{j}"] = [sb(L, f"PTi{dn}{j}{i}", [64, 64], idt) for i in range(2)]
                        wk[dn][f"TT{j}"] = sb(L, f"TT{dn}{j}", [64, 64], BF16)
                        wk[dn][f"ARB{j}"] = sb(L, f"ARB{dn}{j}", [64, 64], BF16)
                        wk[dn][f"AKRK{j}"] = sb(L, f"AKRK{dn}{j}", [64, 128], BF16)
                        wk[dn][f"Zb{j}"] = sb(L, f"Zb{dn}{j}", [64, 128], BF16)
                        wk[dn][f"Ub{j}"] = sb(L, f"Ub{dn}{j}", [64, 128], BF16)

                def chunk_step(dn, c0, emit, insts, vsrc_ap, vkeys, sdec=False):
                    W = wk[dn]
                    k_ = lambda nm: f"{nm}{dn}"
                    fwd = dn == "f"
                    ms_mi = cm2[:, 0:128] if fwd else cm2[:, 128:256]
                    ms_other = cm2[:, 128:192] if fwd else cm2[:, 0:64]
                    I64 = cm2[:, 256:320]
                    P = 128
                    S.op("dve", lambda: nc.vector.tensor_tensor_scan(out=W["pfx"][:], data0=ones64[:], data1=W["lw"][:], initial=0.0,
                                                                     op0=ALU.mult, op1=ALU.add),
                         reads=[k_("lw"), "ones64"], writes=[k_("pfx")])
                    tot = W["pfx"][:, 63:64]
                    if fwd:
                        S.op("pool", lambda: nc.gpsimd.tensor_copy(out=W["Gi"][:], in_=W["pfx"][:]), reads=[k_("pfx")], writes=[k_("Gi")])
                        S.op("dve", lambda: nc.vector.tensor_tensor(out=W["Ge"][:], in0=W["pfx"][:], in1=W["lw"][:], op=ALU.subtract),
                             reads=[k_("pfx"), k_("lw")], writes=[k_("Ge")])
                    else:
                        S.op("dve", lambda: nc.vector.tensor_scalar(out=W["Ge"][:], in0=W["pfx"][:], scalar1=-1.0, scalar2=tot,
                                                                    op0=ALU.mult, op1=ALU.add),
                             reads=[k_("pfx")], writes=[k_("Ge")])
                        S.op("dve", lambda: nc.vector.tensor_tensor(out=W["Gi"][:], in0=W["Ge"][:], in1=W["lw"][:], op=ALU.add),
                             reads=[k_("Ge"), k_("lw")], writes=[k_("Gi")])
                    E = W["E"]
                    S.op("act", lambda: nc.scalar.activation(out=E[:, 0, :], in_=W["Ge"][:], func=AF.Exp), reads=[k_("Ge")], writes=[k_("E0")])
                    S.op("act", lambda: nc.scalar.activation(out=E[:, 1, :], in_=W["Gi"][:], func=AF.Exp), reads=[k_("Gi")], writes=[k_("E1")])
                    Es = W["Es"]; xt = W["Es"]; negm = W["negm"]
                    mcol = W["Gi"][:, 32:33]
                    if sdec:
                        lwf = K.lw2flat
                        dmw = lwf[:, 0:640].bitcast(F32)[0:64, :]
                        gcol = lwf[:, 640:644].bitcast(F32)[0:64, :]
                        msk = (cm2[:, 0:64], cm2[:, 64:128], cm2[:, 128:192]) if fwd else (cm2[:, 128:192], cm2[:, 192:256], cm2[:, 0:64])
                        for ti, srcn in enumerate(("Ge", "Gi")):
                            S.op("pe", lambda: nc.tensor.transpose(out=pin[1][0:64, ti * 128:(ti + 1) * 128], in_=W[srcn][:], identity=CM("ident", False)),
                                 reads=[k_(srcn), "cm_f"], writes=["l1pin1"])
                            S.op("dve", lambda: nc.vector.tensor_copy(out=gcol[:, ti:ti + 1], in_=pin[1][0:64, ti * 128:ti * 128 + 1]),
                                 reads=["l1pin1"], writes=["gcol"])
                        rowGe = W["Ge"][0:64, :]; rowGi = W["Gi"][0:64, :]
                        for qd, (row, rk_, cidx) in enumerate(((rowGe, "Ge", 0), (rowGe, "Ge", 1), (rowGi, "Gi", 0), (rowGi, "Gi", 1))):
                            S.op("dve", lambda: nc.vector.tensor_scalar(out=dmw[:, qd * 64:(qd + 1) * 64], in0=row, scalar1=gcol[:, cidx:cidx + 1], scalar2=0.0,
                                                                        op0=ALU.subtract, op1=ALU.min),
                                 reads=[k_(rk_), "gcol"], writes=["dmw"])
                        S.op("dve", lambda: nc.vector.tensor_scalar(out=dmw[:, 256:320], in0=rowGe, scalar1=-1.0, scalar2=gcol[:, 0:1], op0=ALU.mult, op1=ALU.add),
                             reads=[k_("Ge"), "gcol"], writes=["dmw"])
                        S.op("dve", lambda: nc.vector.tensor_scalar(out=dmw[:, 256:320], in0=dmw[:, 256:320], scalar1=0.0, scalar2=None, op0=ALU.min),
                             reads=["dmw"], writes=["dmw"])
                        S.op("act", lambda: nc.scalar.activation(out=dmw[:, :], in_=dmw[:, :], func=AF.Exp), reads=["dmw"], writes=["dmw"])
                        for qd, mi in enumerate((0, 0, 1, 1, 2)):
                            S.op("dve", lambda: nc.vector.tensor_tensor(out=dmw[:, qd * 64:(qd + 1) * 64], in0=dmw[:, qd * 64:(qd + 1) * 64], in1=msk[mi], op=ALU.mult),
                                 reads=["dmw", "cm2"], writes=["dmw"])
                    S.op("dve", lambda: nc.vector.tensor_scalar(out=negm[:], in0=mcol, scalar1=(0.0 if sdec else -1.0), scalar2=None, op0=ALU.mult),
                         reads=[k_("Gi")], writes=[k_("negm")])
                    S.op("dve", lambda: nc.vector.tensor_scalar(out=xt[:, 0, :], in0=W["Ge"][:], scalar1=negm[:, 0:1], scalar2=40.0, op0=ALU.add, op1=ALU.min),
                         reads=[k_("Ge"), k_("negm")], writes=[k_("Es0")])
                    S.op("dve", lambda: nc.vector.tensor_scalar(out=xt[:, 1, :], in0=W["Gi"][:], scalar1=negm[:, 0:1], scalar2=40.0, op0=ALU.add, op1=ALU.min),
                         reads=[k_("Gi"), k_("negm")], writes=[k_("Es1")])
                    S.op("dve", lambda: nc.vector.tensor_scalar(out=xt[:, 2, :], in0=W["Gi"][:], scalar1=-1.0, scalar2=mcol, op0=ALU.mult, op1=ALU.add),
                         reads=[k_("Gi")], writes=[k_("Es2")])
                    S.op("dve", lambda: nc.vector.tensor_scalar(out=xt[:, 2, :], in0=xt[:, 2, :], scalar1=40.0, scalar2=None, op0=ALU.min),
                         reads=[k_("Es2")], writes=[k_("Es2")])
                    for q_ in range(3):
                        S.op("act", lambda: nc.scalar.activation(out=Es[:, q_, :], in_=xt[:, q_, :], func=AF.Exp), reads=[k_(f"Es{q_}")], writes=[k_(f"Es{q_}")])
                    S.op("act", lambda: nc.scalar.activation(out=E[:, 3, :], in_=W["Gi"][:], func=AF.Exp, scale=-1.0, bias=tot),
                         reads=[k_("Gi"), k_("pfx")], writes=[k_("E3")])
                    S.op("act", lambda: nc.scalar.activation(out=W["Et"][:], in_=tot, func=AF.Exp), reads=[k_("pfx")], writes=[k_("Et")])
                    S.op("dve", lambda: nc.vector.tensor_tensor(out=W["ART"][:, 0, :], in0=W["al"][:], in1=E[:, 0, :], op=ALU.mult),
                         reads=[k_("al"), k_("E0")], writes=[k_("ART")])
                    S.op("pool", lambda: nc.gpsimd.tensor_tensor(out=W["ART"][:, 1, :], in0=W["rr"][:], in1=E[:, 1, :], op=ALU.mult),
                         reads=[k_("rr"), k_("E1")], writes=[k_("ART")])
                    if sdec:
                        S.op("dve", lambda: nc.vector.tensor_copy(out=W["AR"][:, 0, :], in_=W["al"][:]), reads=[k_("al")], writes=[k_("AR")])
                        S.op("pool", lambda: nc.gpsimd.tensor_copy(out=W["AR"][:, 1, :], in_=W["rr"][:]), reads=[k_("rr")], writes=[k_("AR")])
                        S.op("pool", lambda: nc.gpsimd.tensor_copy(out=W["KH"][:], in_=W["kd"][:]), reads=[k_("kd")], writes=[k_("KH")])
                    else:
                        S.op("dve", lambda: nc.vector.tensor_tensor(out=W["AR"][:, 0, :], in0=W["al"][:], in1=Es[:, 0, :], op=ALU.mult),
                             reads=[k_("al"), k_("Es0")], writes=[k_("AR")])
                        S.op("pool", lambda: nc.gpsimd.tensor_tensor(out=W["AR"][:, 1, :], in0=W["rr"][:], in1=Es[:, 1, :], op=ALU.mult),
                             reads=[k_("rr"), k_("Es1")], writes=[k_("AR")])
                        S.op("dve", lambda: nc.vector.tensor_tensor(out=W["BH"][:], in0=W["be"][:], in1=Es[:, 2, :], op=ALU.mult),
                             reads=[k_("be"), k_("Es2")], writes=[k_("BH")])
                        S.op("pool", lambda: nc.gpsimd.tensor_tensor(out=W["KH"][:], in0=W["kd"][:], in1=Es[:, 2, :], op=ALU.mult),
                             reads=[k_("kd"), k_("Es2")], writes=[k_("KH")])
                    S.op("dve", lambda: nc.vector.tensor_tensor(out=W["BT"][:], in0=W["be"][:], in1=E[:, 3, :], op=ALU.mult),
                         reads=[k_("be"), k_("E3")], writes=[k_("BT")])
                    S.op("pool", lambda: nc.gpsimd.tensor_tensor(out=W["KT"][:], in0=W["kd"][:], in1=E[:, 3, :], op=ALU.mult),
                         reads=[k_("kd"), k_("E3")], writes=[k_("KT")])
                    if DBG['cs'] < 2:
                        return
                    for ti, (src, srk, dst, dsk) in enumerate(((W["BT"][:], k_("BT"), W["BTt"], k_("BTt")), (W["KT"][:], k_("KT"), W["KTt"], k_("KTt")),
                                                               (vsrc_ap, vkeys, W["VT"], k_("VT")))):
                        S.op("pe", lambda: nc.tensor.transpose(out=pTr[:, ti * 128:(ti + 1) * 128], in_=src, identity=CM("ident")),
                             reads=([srk] if isinstance(srk, str) else list(srk)) + ["cm_b"], writes=["l1pT"])
                        e = evac_eng()
                        S.op(e, copy_op(e, dst[:], pTr[:, ti * 128:(ti + 1) * 128]), reads=["l1pT"], writes=[dsk])
                    for j, (pr, dk, dv) in enumerate(insts if DBG['cs'] >= 3 else []):
                        pa = pA[j]; pak = f"l1pA{j}"
                        psx = pSc[j]; psk = f"l1pS{j}"
                        N = W[f"N{j}"]; NTt = W[f"NT{j}"]; IN = W[f"IN{j}"]; PTm = W[f"PT{j}"]; TT = W[f"TT{j}"]
                        ARB = W[f"ARB{j}"]; AKRK = W[f"AKRK{j}"]; Zb = W[f"Zb{j}"]; Ub = W[f"Ub{j}"]
                        kj = lambda nm: f"{nm}{dn}{j}"
                        ar2 = W["AR"][pr, :, :].rearrange("p a n -> p (a n)")
                        if sdec:
                            S.op("pe", lambda: nc.tensor.matmul(pa[:, 0:128], lhsT=W["KH"][pr, :], rhs=ar2, start=True, stop=True),
                                 reads=[k_("KH"), k_("AR")], writes=[pak])
                            S.op("pe", lambda: nc.tensor.matmul(pa[:, 256:320], lhsT=W["AR"][pr, 0, :], rhs=W["KH"][pr, :], start=True, stop=True),
                                 reads=[k_("KH"), k_("AR")], writes=[pak])
                            S.op("dve", lambda: nc.vector.scalar_tensor_tensor(out=NTt[0][:], in0=pa[:, 0:64], scalar=-1.0, in1=dmw[:, 0:64], op0=ALU.mult, op1=ALU.mult),
                                 reads=[pak, "dmw"], writes=[kj("NT0")])
                            S.op("dve", lambda: nc.vector.tensor_tensor(out=AKRK[:, 0:64], in0=pa[:, 0:64], in1=dmw[:, 64:128], op=ALU.mult),
                                 reads=[pak, "dmw"], writes=[kj("AKRK")])
                            S.op("dve", lambda: nc.vector.scalar_tensor_tensor(out=ARB[:], in0=pa[:, 64:128], scalar=-1.0, in1=dmw[:, 128:192], op0=ALU.mult, op1=ALU.mult),
                                 reads=[pak, "dmw"], writes=[kj("ARB")])
                            S.op("dve", lambda: nc.vector.tensor_tensor(out=AKRK[:, 64:128], in0=pa[:, 64:128], in1=dmw[:, 192:256], op=ALU.mult),
                                 reads=[pak, "dmw"], writes=[kj("AKRK")])
                            S.op("dve", lambda: nc.vector.scalar_tensor_tensor(out=N[0][:], in0=pa[:, 256:320], scalar=-1.0, in1=dmw[:, 256:320], op0=ALU.mult, op1=ALU.mult),
                                 reads=[pak, "dmw"], writes=[kj("N0")])
                        else:
                            S.op("pe", lambda: nc.tensor.matmul(pa[:, 0:128], lhsT=W["BH"][pr, :], rhs=ar2, start=True, stop=True),
                                 reads=[k_("BH"), k_("AR")], writes=[pak])
                            S.op("pe", lambda: nc.tensor.matmul(pa[:, 128:256], lhsT=W["KH"][pr, :], rhs=ar2, start=True, stop=True),
                                 reads=[k_("KH"), k_("AR")], writes=[pak])
                            S.op("pe", lambda: nc.tensor.matmul(pa[:, 256:320], lhsT=W["AR"][pr, 0, :], rhs=W["BH"][pr, :], start=True, stop=True),
                                 reads=[k_("BH"), k_("AR")], writes=[pak])
                            S.op("dve", lambda: nc.vector.tensor_tensor(out=NTt[0][:], in0=pa[:, 0:64], in1=ms_mi[:, 0:64], op=ALU.mult),
                                 reads=[pak, "cm2"], writes=[kj("NT0")])
                            S.op("dve", lambda: nc.vector.tensor_tensor(out=ARB[:], in0=pa[:, 64:128], in1=ms_mi[:, 64:128], op=ALU.mult),
                                 reads=[pak, "cm2"], writes=[kj("ARB")])
                            S.op("dve", lambda: nc.vector.tensor_tensor(out=AKRK[:], in0=pa[:, 128:256], in1=ms_mi, op=ALU.mult),
                                 reads=[pak, "cm2"], writes=[kj("AKRK")])
                            S.op("dve", lambda: nc.vector.tensor_tensor(out=N[0][:], in0=pa[:, 256:320], in1=ms_other, op=ALU.mult),
                                 reads=[pak, "cm2"], writes=[kj("N0")])
                        S.op("pool", lambda: nc.gpsimd.tensor_tensor(out=PTm[0][:], in0=NTt[0][:], in1=I64, op=ALU.add),
                             reads=[kj("NT0"), "cm2"], writes=[kj("PT0")])
                        for lv in (range(1, DBG['invlv'] + 1) if DBG['cs'] >= 4 else []):
                            a_, b_ = (lv - 1) % 2, lv % 2
                            S.op("pe", lambda: nc.tensor.matmul(pa[:, 320:384], lhsT=NTt[a_][:], rhs=N[a_][:], start=True, stop=True),
                                 reads=[kj(f"NT{a_}"), kj(f"N{a_}")], writes=[pak])
                            if lv < 5:
                                S.op("pe", lambda: nc.tensor.matmul(pa[:, 384:448], lhsT=N[a_][:], rhs=NTt[a_][:], start=True, stop=True),
                                     reads=[kj(f"NT{a_}"), kj(f"N{a_}")], writes=[pak])
                            if DBG['invev'] & 1:
                                S.op("dve", lambda: nc.vector.tensor_tensor(out=IN[:], in0=pa[:, 320:384], in1=I64, op=ALU.add),
                                     reads=[pak, "cm2"], writes=[kj("IN")])
                            if DBG['invev'] & 2:
                                if lv < 5:
                                    S.op("act", lambda: nc.scalar.copy(out=N[b_][:], in_=pa[:, 320:384]), reads=[pak], writes=[kj(f"N{b_}")])
                                    S.op("act", lambda: nc.scalar.copy(out=NTt[b_][:], in_=pa[:, 384:448]), reads=[pak], writes=[kj(f"NT{b_}")])
                            if DBG['invev'] & 4:
                                if lv < 5:
                                    S.op("dve", lambda: nc.vector.tensor_copy(out=N[b_][:], in_=pa[:, 320:384]), reads=[pak], writes=[kj(f"N{b_}")])
                                    S.op("dve", lambda: nc.vector.tensor_copy(out=NTt[b_][:], in_=pa[:, 384:448]), reads=[pak], writes=[kj(f"NT{b_}")])
                            if DBG['invp']:
                                S.op("pe", lambda: nc.tensor.matmul(pa[:, 448:512], lhsT=IN[:], rhs=PTm[a_][:], start=True, stop=True),
                                     reads=[kj("IN"), kj(f"PT{a_}")], writes=[pak])
                                if lv < 5:
                                    S.op("dve", lambda: nc.vector.tensor_copy(out=PTm[b_][:], in_=pa[:, 448:512]), reads=[pak], writes=[kj(f"PT{b_}")])
                                else:
                                    S.op("dve", lambda: nc.vector.tensor_copy(out=TT[:], in_=pa[:, 448:512]), reads=[pak], writes=[kj("TT")])
                        if DBG['cs'] < 5:
                            continue
                        vt = W["VT"][:, j * dv:(j + 1) * dv] if dv == 64 else W["VT"][:, :]
                        cs = slice(j * 64, (j + 1) * 64) if dk == 64 else slice(0, 128)
                        split = (pr.start == 64)
                        psy = pin[0]; psyk = "l1pin0"
                        if split:
                            S.op("pe", lambda: nc.tensor.matmul(psy[0:64, 0:dv], lhsT=W["ART"][pr, 0, :], rhs=W["Hb"][pr, :dv], start=True, stop=True),
                                 reads=[k_("ART"), kj("Hb")], writes=[psyk])
                            S.op("pe", lambda: nc.tensor.matmul(psx[0:64, 0:dv], lhsT=AKRK[:, 0:64], rhs=vt, start=True, stop=True),
                                 reads=[kj("AKRK"), k_("VT")], writes=[psk])
                            S.op("act", lambda: nc.scalar.copy(out=W["ztmp"][:, :dv], in_=psy[0:64, 0:dv]), reads=[psyk], writes=[k_("ztmp")])
                            S.op("dve", lambda: nc.vector.tensor_tensor(out=Zb[:, :dv], in0=psx[0:64, 0:dv], in1=W["ztmp"][:, :dv], op=ALU.add),
                                 reads=[psk, k_("ztmp")], writes=[kj("Zb")])
                        else:
                            S.op("pe", lambda: nc.tensor.matmul(psx[0:64, 0:dv], lhsT=W["ART"][pr, 0, :], rhs=W["Hb"][pr, :dv], start=True, stop=False),
                                 reads=[k_("ART"), kj("Hb")], writes=[psk])
                            S.op("pe", lambda: nc.tensor.matmul(psx[0:64, 0:dv], lhsT=AKRK[:, 0:64], rhs=vt, start=False, stop=True),
                                 reads=[kj("AKRK"), k_("VT")], writes=[psk])
                            S.op("act", lambda: nc.scalar.copy(out=Zb[:, :dv], in_=psx[0:64, 0:dv]), reads=[psk], writes=[kj("Zb")])
                        S.op("pe", lambda: nc.tensor.matmul(psx[0:64, 128:128 + dv], lhsT=TT[:], rhs=Zb[:, :dv], start=True, stop=True),
                             reads=[kj("TT"), kj("Zb")], writes=[psk])
                        S.op("dve", lambda: nc.vector.tensor_copy(out=Ub[:, :dv], in_=psx[0:64, 128:128 + dv]), reads=[psk], writes=[kj("Ub")])
                        if emit:
                            t_lat = c0 - NCTX
                            yk_ = f"Yacc{t_lat//64}"
                            if split:
                                S.op("pe", lambda: nc.tensor.matmul(psy[pr, 256:320], lhsT=W["Hb"][pr, :dv], rhs=W["ART"][pr, 1, :], start=True, stop=True),
                                     reads=[k_("ART"), kj("Hb")], writes=[psyk])
                                S.op("dve", lambda: nc.vector.tensor_tensor(out=Yacc[pr, t_lat:t_lat + 64], in0=psy[pr, 256:320],
                                                                            in1=Yacc[pr, t_lat:t_lat + 64], op=ALU.add),
                                     reads=[psyk, yk_], writes=[yk_])
                            else:
                                S.op("pe", lambda: nc.tensor.matmul(psx[pr, 256:320], lhsT=W["Hb"][pr, :dv], rhs=W["ART"][pr, 1, :], start=True, stop=False),
                                     reads=[k_("ART"), kj("Hb")], writes=[psk])
                            S.op("pe", lambda: nc.tensor.matmul(psx[pr, 256:320], lhsT=Ub[:, :dv], rhs=ARB[:], start=split, stop=False),
                                 reads=[kj("Ub"), kj("ARB")], writes=[psk])
                            S.op("pe", lambda: nc.tensor.matmul(psx[pr, 256:320], lhsT=vt, rhs=AKRK[:, 64:128], start=False, stop=True),
                                 reads=[kj("AKRK"), k_("VT")], writes=[psk])
                            S.op("dve", lambda: nc.vector.tensor_tensor(out=Yacc[pr, t_lat:t_lat + 64], in0=psx[pr, 256:320],
                                                                        in1=Yacc[pr, t_lat:t_lat + 64], op=ALU.add),
                                 reads=[psk, yk_], writes=[yk_])
                        S.op("pe", lambda: nc.tensor.matmul(psx[pr, 320:320 + dv], lhsT=W["BTt"][:, cs], rhs=Ub[:, :dv], start=True, stop=False),
                             reads=[k_("BTt"), kj("Ub")], writes=[psk])
                        S.op("pe", lambda: nc.tensor.matmul(psx[pr, 320:320 + dv], lhsT=W["KTt"][:, cs], rhs=vt, start=False, stop=True),
                             reads=[k_("KTt"), k_("VT")], writes=[psk])
                        S.op("dve", lambda: nc.vector.scalar_tensor_tensor(out=W["Hs"][pr, :dv], in0=W["Hs"][pr, :dv], scalar=W["Et"][pr, 0:1],
                                                                           in1=psx[pr, 320:320 + dv], op0=ALU.mult, op1=ALU.add),
                             reads=[psk, kj("Hs"), k_("Et")], writes=[kj("Hs")])
                        S.op("act", lambda: nc.scalar.copy(out=W["Hb"][pr, :dv], in_=W["Hs"][pr, :dv]), reads=[kj("Hs")], writes=[kj("Hb")])

                def reset_state(insts):
                    for dn in ("f", "b"):
                        S.op("pool", lambda: nc.gpsimd.memset(wk[dn]["Hs"][:], 0.0), writes=[f"Hs{dn}{j}" for j in range(2)])
                        S.op("pool", lambda: nc.gpsimd.memset(wk[dn]["Hb"][:], 0.0), writes=[f"Hb{dn}{j}" for j in range(2)])
                    S.op("pool", lambda: nc.gpsimd.memset(Yacc[:], 0.0), writes=[f"Yacc{i}" for i in range(32)])

                TW = xb(10); AL = xb(11); SG = xb(12)
                if DBG['lora']:
                    load_cols([(1536, 384)])
                    proj_taps(0, 128, shift_taps(12), AF.Tanh, lambda s0, n: TW[:, s0:s0 + n], ["TW"])
                    proj_taps(128, 128, shift_taps(13), AF.Identity, lambda s0, n: AL[:, s0:s0 + n], ["AL"])
                    proj_taps(256, 128, shift_taps(14), AF.Sigmoid, lambda s0, n: SG[:, s0:s0 + n], ["SG"])
                lw2 = sb(L, "lw2", [128, 3, 512], BF16)
                K.lw2flat = lw2[:].rearrange("p a n -> p (a n)")
                for i, src in enumerate((cw2_d, ca2_d, cg2_d)):
                    wload(stg, "l1stg", lw2[:, i, :], ["lw2"], src, [128, 512], "sp")
                sq = sb(L, "l1sq", [128, 256], BF16)
                t32 = [sb(L, f"l1t{i}", [128, 256]) for i in range(3)]
                TCH = [(i * 256, 256) for i in range(9)]

                for gi in range(DBG['ng']):
                    rA, kA, vA, kkA, afA, abA = (xb(i) for i in range(6))
                    lwF = xf(3); lwB = xf(4)
                    gA = xb(13)
                    load_cols([(gi * 128, 128), (512 + gi * 128, 128), (1024 + gi * 128, 128)])
                    proj_taps(0, 128, shift_taps(gi), AF.Identity, lambda s0, n: rA[:, s0:s0 + n], ["rA"])
                    proj_taps(128, 128, shift_taps(4 + gi), AF.Identity, lambda s0, n: kA[:, s0:s0 + n], ["kA"])
                    proj_taps(256, 128, shift_taps(8 + gi), AF.Identity, lambda s0, n: vA[:, s0:s0 + n], ["vA"])
                    for (t0, n) in (TCH if DBG['prep'] else []):
                        S.op("dve", lambda: nc.vector.tensor_scalar(out=t32[0][:, :n], in0=kA[:, t0:t0 + n], scalar1=v1[:, 45 + gi:46 + gi], scalar2=None,
                                                                    op0=ALU.mult), reads=["kA", "v1"], writes=["l1t0"])
                        S.op("act", lambda: nc.scalar.activation(out=sq[:, :n], in_=t32[0][:, :n], func=AF.Square), reads=["l1t0"], writes=["l1sq"])
                        S.op("pe", lambda: nc.tensor.matmul(pin[0][:, :n], lhsT=CM("onesA"), rhs=sq[:, :n], start=True, stop=True),
                             reads=["l1sq", "cm_b"], writes=["l1pin0"])
                        S.op("act", lambda: nc.scalar.activation(out=t32[1][:, :n], in_=pin[0][:, :n], func=AF.Sqrt, scale=64.0, bias=eps_t[:, 0:1]),
                             reads=["l1pin0", "eps"], writes=["l1t1"])
                        S.op("dve", lambda: nc.vector.reciprocal(out=t32[1][:, :n], in_=t32[1][:, :n]), reads=["l1t1"], writes=["l1t1"])
                        S.op("dve", lambda: nc.vector.tensor_tensor(out=kkA[:, t0:t0 + n], in0=t32[0][:, :n], in1=t32[1][:, :n], op=ALU.mult),
                             reads=["l1t0", "l1t1"], writes=["kkA"])
                        for d_, (lwX, aX) in enumerate(((lwF, afA), (lwB, abA))):
                            prd = slice(d_ * 64, (d_ + 1) * 64)
                            S.op("pe", lambda: nc.tensor.matmul(pin[1][:, :n], lhsT=lw2[prd, 0, gi * 128:(gi + 1) * 128], rhs=TW[prd, t0:t0 + n],
                                                                start=True, stop=True), reads=["lw2", "TW"], writes=["l1pin1"])
                            S.op("act", lambda: nc.scalar.activation(out=t32[2][:, :n], in_=pin[1][:, :n], func=AF.Sigmoid,
                                                                     bias=v1[:, 49 + d_ * 4 + gi:50 + d_ * 4 + gi]),
                                 reads=["l1pin1", "v1"], writes=["l1t2"])
                            S.op("dve", lambda: nc.vector.tensor_scalar(out=lwX[:, t0:t0 + n], in0=t32[2][:, :n], scalar1=-0.6065306597126334,
                                                                        scalar2=None, op0=ALU.mult), reads=["l1t2"], writes=["lwX"])
                            S.op("pe", lambda: nc.tensor.matmul(pin[1][:, :n], lhsT=lw2[prd, 1, gi * 128:(gi + 1) * 128], rhs=AL[prd, t0:t0 + n],
                                                                start=True, stop=True), reads=["lw2", "AL"], writes=["l1pin1"])
                            S.op("act", lambda: nc.scalar.activation(out=aX[:, t0:t0 + n], in_=pin[1][:, :n], func=AF.Sigmoid,
                                                                     bias=v1[:, 57 + d_ * 4 + gi:58 + d_ * 4 + gi]),
                                 reads=["l1pin1", "v1"], writes=["aX"])
                        S.op("pe", lambda: nc.tensor.matmul(pin[1][:, :n], lhsT=lw2[:, 2, gi * 128:(gi + 1) * 128], rhs=SG[:, t0:t0 + n],
                                                            start=True, stop=True), reads=["lw2", "SG"], writes=["l1pin1"])
                        S.op("act", lambda: nc.scalar.copy(out=gA[:, t0:t0 + n], in_=pin[1][:, :n]), reads=["l1pin1"], writes=["gA"])
                    insts = [(slice(0, 64), 64, 64), (slice(64, 128), 64, 64)]
                    reset_state(insts)
                    for step in range(DBG['nstep']):
                        for dn, order, aX, lwX in (("f", fwd_order, afA, lwF), ("b", bwd_order, abA, lwB)):
                            ch = order[step]; c0 = ch * 64
                            W = wk[dn]
                            csl = slice(c0, c0 + 64)
                            S.op("dve", lambda: nc.vector.tensor_scalar(out=W["al"][:], in0=kkA[:, csl], scalar1=-1.0, scalar2=None, op0=ALU.mult),
                                 reads=["kkA"], writes=[f"al{dn}"])
                            S.op("pool", lambda: nc.gpsimd.tensor_tensor(out=W["be"][:], in0=kkA[:, csl], in1=aX[:, csl], op=ALU.mult),
                                 reads=["kkA", "aX"], writes=[f"be{dn}"])
                            S.op("dve", lambda: nc.vector.tensor_scalar(out=W["kd"][:], in0=aX[:, csl], scalar1=-1.0, scalar2=v1[:, 65 + gi:66 + gi],
                                                                        op0=ALU.add, op1=ALU.mult), reads=["aX", "v1"], writes=[f"kd{dn}"])
                            S.op("dve", lambda: nc.vector.scalar_tensor_tensor(out=W["kd"][:], in0=W["kd"][:], scalar=1.0, in1=kA[:, csl],
                                                                               op0=ALU.add, op1=ALU.mult), reads=[f"kd{dn}", "kA"], writes=[f"kd{dn}"])
                            S.op("pool", lambda: nc.gpsimd.tensor_copy(out=W["rr"][:], in_=rA[:, csl]), reads=["rA"], writes=[f"rr{dn}"])
                            S.op("pool", lambda: nc.gpsimd.tensor_copy(out=W["lw"][:], in_=lwX[:, csl]), reads=["lwX"], writes=[f"lw{dn}"])
                            chunk_step(dn, c0, ch >= 4, insts, vA[:, csl], ["vA"])
                    for qi in range(8 if DBG['rwout'] else 0):
                        t0 = NCTX + qi * 256; n = 256; y0 = qi * 256
                        yk = [f"Yacc{i}" for i in range(y0 // 64, y0 // 64 + 4)]
                        S.op("pe", lambda: nc.tensor.matmul(pin[0][:, :n], lhsT=CM("onesA"), rhs=Yacc[:, y0:y0 + n], start=True, stop=True),
                             reads=yk + ["cm_b"], writes=["l1pin0"])
                        S.op("dve", lambda: nc.vector.tensor_tensor(out=t32[0][:, :n], in0=Yacc[:, y0:y0 + n], in1=pin[0][:, :n], op=ALU.subtract),
                             reads=yk + ["l1pin0"], writes=["l1t0"])
                        S.op("act", lambda: nc.scalar.activation(out=sq[:, :n], in_=t32[0][:, :n], func=AF.Square), reads=["l1t0"], writes=["l1sq"])
                        S.op("pe", lambda: nc.tensor.matmul(pin[0][:, :n], lhsT=CM("onesA"), rhs=sq[:, :n], start=True, stop=True),
                             reads=["l1sq", "cm_b"], writes=["l1pin0"])
                        S.op("act", lambda: nc.scalar.activation(out=t32[1][:, :n], in_=pin[0][:, :n], func=AF.Sqrt, bias=gneps[:, 0:1]),
                             reads=["l1pin0", "gneps"], writes=["l1t1"])
                        S.op("dve", lambda: nc.vector.reciprocal(out=t32[1][:, :n], in_=t32[1][:, :n]), reads=["l1t1"], writes=["l1t1"])
                        S.op("dve", lambda: nc.vector.tensor_tensor(out=t32[0][:, :n], in0=t32[0][:, :n], in1=t32[1][:, :n], op=ALU.mult),
                             reads=["l1t0", "l1t1"], writes=["l1t0"])
                        S.op("act", lambda: nc.scalar.activation(out=t32[0][:, :n], in_=t32[0][:, :n], func=AF.Identity,
                                                                 scale=v1[:, 69 + gi:70 + gi], bias=v1[:, 73 + gi:74 + gi]),
                             reads=["l1t0", "v1"], writes=["l1t0"])
                        S.op("dve", lambda: nc.vector.tensor_tensor(out=t32[1][:, :n], in0=afA[:, t0:t0 + n], in1=abA[:, t0:t0 + n], op=ALU.add),
                             reads=["aX"], writes=["l1t1"])
                        S.op("dve", lambda: nc.vector.tensor_scalar(out=t32[1][:, :n], in0=t32[1][:, :n], scalar1=-2.0, scalar2=v1[:, 65 + gi:66 + gi],
                                                                    op0=ALU.add, op1=ALU.mult), reads=["l1t1", "v1"], writes=["l1t1"])
                        S.op("dve", lambda: nc.vector.scalar_tensor_tensor(out=t32[1][:, :n], in0=t32[1][:, :n], scalar=2.0, in1=kA[:, t0:t0 + n],
                                                                           op0=ALU.add, op1=ALU.mult), reads=["l1t1", "kA"], writes=["l1t1"])
                        S.op("dve", lambda: nc.vector.scalar_tensor_tensor(out=sq[:, :n], in0=t32[1][:, :n], scalar=v1[:, 77 + gi:78 + gi],
                                                                           in1=rA[:, t0:t0 + n], op0=ALU.mult, op1=ALU.mult),
                             reads=["l1t1", "rA", "v1"], writes=["l1sq"])
                        S.op("pe", lambda: nc.tensor.matmul(pin[1][:, :n], lhsT=CM("onesA"), rhs=sq[:, :n], start=True, stop=True),
                             reads=["l1sq", "cm_b"], writes=["l1pin1"])
                        S.op("dve", lambda: nc.vector.scalar_tensor_tensor(out=t32[2][:, :n], in0=pin[1][:, :n], scalar=64.0, in1=vA[:, t0:t0 + n],
                                                                           op0=ALU.mult, op1=ALU.mult), reads=["l1pin1", "vA"], writes=["l1t2"])
                        S.op("pool", lambda: nc.gpsimd.tensor_tensor(out=t32[0][:, :n], in0=t32[0][:, :n], in1=t32[2][:, :n], op=ALU.add),
                             reads=["l1t0", "l1t2"], writes=["l1t0"])
                        S.op("dve", lambda: nc.vector.tensor_tensor(out=Yall[:, gi, y0:y0 + n], in0=t32[0][:, :n], in1=gA[:, t0:t0 + n], op=ALU.mult),
                             reads=["l1t0", "gA"], writes=["Yall"])
                    S.barrier()

                sm = XF[0:16, 7 * T:8 * T]
                smb = sb(L, "smb", [16, 256])
                smg = sb(L, "smg", [16, 256])
                gsm = sb(L, "gsmp", [16, 2])
                sel = sb(L, "sel", [16, 16 * 128])
                S.dma("sp", gsm[:], gsm_d, writes=["gsm"])
                S.dma("sp", sel[:], sel_d, writes=["sel"])
                S.op("act", lambda: nc.scalar.activation(out=gsm[:, 0:1], in_=gsm[:, 0:1], func=AF.Exp), reads=["gsm"], writes=["gsm"])
                S.op("dve", lambda: nc.vector.tensor_scalar(out=gsm[:, 0:1], in0=gsm[:, 0:1], scalar1=-1.0, scalar2=None, op0=ALU.mult),
                     reads=["gsm"], writes=["gsm"])
                load_cols([(INC + 2048, 16)])
                proj_taps(0, 16, [(0, None)], AF.Identity, lambda s0, n: sm[:, s0:s0 + n], ["sm"])
                t32q = xb(12); t32k = xb(13)
                one_t = sb(L, "one_t", [128, 1])
                S.op("pool", lambda: nc.gpsimd.memset(one_t[:], 1.0), writes=["one_t"])

                for hd in range(DBG['gdn']):
                    qA, kA, vA, zA = (xb(i) for i in range(4))
                    bmF = xf(2); bmB = xf(3); gmF = xf(4); gmB = xf(5)
                    load_cols([(INC + hd * 128, 128), (INC + 512 + hd * 128, 128), (INC + 1024 + hd * 128, 128), (INC + 1536 + hd * 128, 128)])

                    def ctaps(fc):
                        return [(0, v1[:, 81 + fc * 5 + 2:81 + fc * 5 + 3])] + [(o, v1[:, 81 + fc * 5 + 2 + o:81 + fc * 5 + 3 + o]) for o in (-2, -1, 1, 2)]
                    proj_taps(0, 128, ctaps(hd), AF.Silu, lambda s0, n: t32q[:, s0:s0 + n], ["t32q"])
                    proj_taps(128, 128, ctaps(4 + hd), AF.Silu, lambda s0, n: t32k[:, s0:s0 + n], ["t32k"])
                    proj_taps(256, 128, ctaps(8 + hd), AF.Silu, lambda s0, n: vA[:, s0:s0 + n], ["vA"])
                    proj_taps(384, 128, [(0, None)], AF.Silu, lambda s0, n: zA[:, s0:s0 + n], ["zA"])
                    for (t0, n) in (TCH if DBG['gprep'] >= 2 else []):
                        for (srcb, dstb, scl, dk_) in ((t32q, qA, 128.0 ** -0.5, "qA"), (t32k, kA, 1.0, "kA")):
                            S.op("act", lambda: nc.scalar.activation(out=sq[:, :n], in_=srcb[:, t0:t0 + n], func=AF.Square), reads=["t32q", "t32k"], writes=["l1sq"])
                            S.op("pe", lambda: nc.tensor.matmul(pin[0][:, :n], lhsT=CM("ones1024"), rhs=sq[:, :n], start=True, stop=True),
                                 reads=["l1sq", "cm_b"], writes=["l1pin0"])
                            S.op("act", lambda: nc.scalar.activation(out=t32[1][:, :n], in_=pin[0][:, :n], func=AF.Sqrt, scale=1024.0, bias=eps_t[:, 0:1]),
                                 reads=["l1pin0", "eps"], writes=["l1t1"])
                            S.op("dve", lambda: nc.vector.reciprocal(out=t32[1][:, :n], in_=t32[1][:, :n]), reads=["l1t1"], writes=["l1t1"])
                            S.op("dve", lambda: nc.vector.scalar_tensor_tensor(out=dstb[:, t0:t0 + n], in0=srcb[:, t0:t0 + n], scalar=float(scl),
                                                                               in1=t32[1][:, :n], op0=ALU.mult, op1=ALU.mult),
                                 reads=["t32q", "t32k", "l1t1"], writes=[dk_])
                        S.op("act", lambda: nc.scalar.activation(out=smb[:, :n], in_=sm[:, t0:t0 + n], func=AF.Sigmoid), reads=["sm"], writes=["smb"])
                        S.op("act", lambda: nc.scalar.activation(out=smg[:, :n], in_=sm[:, t0:t0 + n], func=AF.Exp, bias=gsm[:, 1:2]),
                             reads=["sm", "gsm"], writes=["smg"])
                        S.op("act", lambda: nc.scalar.activation(out=smg[:, :n], in_=smg[:, :n], func=AF.Ln, bias=one_t[0:16, 0:1]),
                             reads=["smg", "one_t"], writes=["smg"])
                        S.op("dve", lambda: nc.vector.tensor_scalar(out=smg[:, :n], in0=smg[:, :n], scalar1=gsm[:, 0:1], scalar2=None, op0=ALU.mult),
                             reads=["smg", "gsm"], writes=["smg"])
                        for (row, dst, srcm, dk_) in (((hd, bmF, smb, "bmF"), (4 + hd, bmB, smb, "bmB"), (8 + hd, gmF, smg, "gmF"), (12 + hd, gmB, smg, "gmB")) if DBG['gprep'] >= 3 else []):
                            S.op("pe", lambda: nc.tensor.matmul(pin[1][:, :n], lhsT=sel[:, row * 128:(row + 1) * 128], rhs=srcm[:, :n],
                                                                start=True, stop=True), reads=["sel", "smb", "smg"], writes=["l1pin1"])
                            e = evac_eng()
                            S.op(e, copy_op(e, dst[:, t0:t0 + n], pin[1][:, :n]), reads=["l1pin1"], writes=[dk_])
                    insts = [(slice(0, 128), 128, 128)]
                    reset_state(insts)
                    for step in range(DBG['nstep']):
                        for dn, order, bmX, gmX in (("f", fwd_order, bmF, gmF), ("b", bwd_order, bmB, gmB)):
                            ch = order[step]; c0 = ch * 64
                            W = wk[dn]
                            csl = slice(c0, c0 + 64)
                            S.op("pool", lambda: nc.gpsimd.tensor_copy(out=W["al"][:], in_=kA[:, csl]), reads=["kA"], writes=[f"al{dn}"])
                            S.op("dve", lambda: nc.vector.tensor_tensor(out=W["kd"][:], in0=kA[:, csl], in1=bmX[:, csl], op=ALU.mult),
                                 reads=["kA", "bmF", "bmB"], writes=[f"kd{dn}"])
                            S.op("pool", lambda: nc.gpsimd.tensor_copy(out=W["lw"][:], in_=gmX[:, csl]), reads=["gmF", "gmB"], writes=[f"lw{dn}"])
                            S.op("act", lambda: nc.scalar.activation(out=W["be"][:], in_=gmX[:, csl], func=AF.Exp), reads=["gmF", "gmB"], writes=[f"be{dn}"])
                            S.op("dve", lambda: nc.vector.scalar_tensor_tensor(out=W["be"][:], in0=W["be"][:], scalar=-1.0, in1=W["kd"][:],
                                                                               op0=ALU.mult, op1=ALU.mult), reads=[f"be{dn}", f"kd{dn}"], writes=[f"be{dn}"])
                            S.op("pool", lambda: nc.gpsimd.tensor_copy(out=W["rr"][:], in_=qA[:, csl]), reads=["qA"], writes=[f"rr{dn}"])
                            chunk_step(dn, c0, ch >= 4, insts, vA[:, csl], ["vA"], sdec=True)
                    for qi in range(8):
                        t0 = NCTX + qi * 256; n = 256; y0 = qi * 256
                        yk = [f"Yacc{i}" for i in range(y0 // 64, y0 // 64 + 4)]
                        S.op("act", lambda: nc.scalar.activation(out=sq[:, :n], in_=Yacc[:, y0:y0 + n], func=AF.Square), reads=yk, writes=["l1sq"])
                        S.op("pe", lambda: nc.tensor.matmul(pin[0][:, :n], lhsT=CM("ones1024"), rhs=sq[:, :n], start=True, stop=True),
                             reads=["l1sq", "cm_b"], writes=["l1pin0"])
                        S.op("act", lambda: nc.scalar.activation(out=t32[1][:, :n], in_=pin[0][:, :n], func=AF.Sqrt, scale=8.0, bias=eps_t[:, 0:1]),
                             reads=["l1pin0", "eps"], writes=["l1t1"])
                        S.op("dve", lambda: nc.vector.reciprocal(out=t32[1][:, :n], in_=t32[1][:, :n]), reads=["l1t1"], writes=["l1t1"])
                        S.op("dve", lambda: nc.vector.scalar_tensor_tensor(out=t32[0][:, :n], in0=Yacc[:, y0:y0 + n], scalar=v1[:, 141:142],
                                                                           in1=t32[1][:, :n], op0=ALU.mult, op1=ALU.mult),
                             reads=yk + ["l1t1", "v1"], writes=["l1t0"])
                        S.op("dve", lambda: nc.vector.tensor_tensor(out=Yall[:, 4 + hd, y0:y0 + n], in0=t32[0][:, :n], in1=zA[:, t0:t0 + n], op=ALU.mult),
                             reads=["l1t0", "zA"], writes=["Yall"])
                    S.barrier()
                LS.close()
                L = L0
                if dbg == "y1":
                    dump_fm(Yall, 8, BF16, 16)
                stg2 = sb(L, "l1stg2", [128, 2048])
                wo = sb(L, "l1wo", [128, 8, D], BF16)
                pin = [ps(L, f"l1pinb{i}", [128, 512]) for i in range(2)]
                cov = cdout_d.rearrange("(kc p) n -> p kc n", p=128)
                for i in range(4):
                    wload(stg2, "l1stg2", wo[:, 2 * i:2 * i + 2, :], ["l1wo"], cov[:, 2 * i:2 * i + 2, :], [128, 2, D], "sp")
                for c in range(8):
                    S.dma("sp" if c % 2 == 0 else "pool", X[:, c, :], xpark_d[:, c * T:(c + 1) * T], writes=[f"X{t}" for t in range(NT)])
                S.barrier()
                it = 0
                for qi in range(4):
                    t0 = NCTX + qi * 512; n = 512; y0 = qi * 512
                    for m in range(8):
                        p_ = pin[it % 2]; pk = f"l1pinb{it%2}"; it += 1
                        for i in range(8):
                            S.op("pe", lambda: nc.tensor.matmul(p_[:, :n], lhsT=wo[:, i, m * 128:(m + 1) * 128], rhs=Yall[:, i, y0:y0 + n],
                                                                start=(i == 0), stop=(i == 7)), reads=["l1wo", "Yall"], writes=[pk])
                        S.op("dve", lambda: nc.vector.scalar_tensor_tensor(out=X[:, m, t0:t0 + n], in0=p_[:, :n], scalar=mod(l, 2, m, 0),
                                                                           in1=X[:, m, t0:t0 + n], op0=ALU.mult, op1=ALU.add),
                             reads=[pk, "modv"] + xkeys(t0, n), writes=xkeys(t0, n))
            S.barrier()

        K.stop = False
        if dbg in ("load", "mods"):
            write_out()
            S.finish()
            return nc, S
        layer0()
        if dbg in ("qa", "oa"):
            S.finish()
            return nc, S
        if dbg != "noffn":
            ffn(0, nlayers == 1)
        if nlayers == 2:
            layer1()
            if dbg not in ("noffn1", "y1"):
                ffn(1, True)
        write_out()
        S.finish()
    return nc, S


def make_in_maps(inputs, cores):
    consts, cnames = _consts()
    f = lambda k: np.asarray(inputs[k], np.float32)
    c_ctx = f("c_ctx")
    shared = {
        "ada_w": np.ascontiguousarray(f("ada_w")),
        "ada_b_fm": np.stack([_fm(f("ada_b")[l]) for l in range(2)], 0),
        "ffn_w_up": np.ascontiguousarray(f("ffn_w_up")),
        "ffn_w_down": np.ascontiguousarray(f("ffn_w_down")),
        "ab_w_in": np.ascontiguousarray(f("ab_w_in")[0]),
        "ab_w_out": np.ascontiguousarray(f("ab_w_out")[0]),
        "b_w_uq": np.ascontiguousarray(f("b_w_uq")[0]),
        "b_w_uk": np.ascontiguousarray(f("b_w_uk")[0]),
        "b_w_uv": np.ascontiguousarray(f("b_w_uv")[0]),
        "cmats": consts["cmats"], "cosA": consts["cosA"], "sinA": consts["sinA"],
        "cosB": consts["cosB"], "sinB": consts["sinB"],
    }
    fc = np.zeros((2, 128, 44, 4), np.float32)
    for l in range(2):
        for j in range(3):
            fc[l, :, :, j] = _fm(f("ffn_conv_w")[l, j])
        fc[l, :, :, 3] = _fm(f("ffn_conv_b")[l])
    shared["ffn_conv_fm"] = fc
    v = np.zeros((128, 16), np.float32)
    v[:, 0] = np.tile(f("a_q_norm")[0], 2)
    v[:, 1] = np.tile(f("a_k_norm")[0], 2)
    v[:, 2:4] = _fm(f("b_cq_norm")[0])
    v[:, 4:6] = _fm(f("b_ckv_norm")[0])
    v[64:96, 6] = f("b_kr_norm")[0]
    v[:64, 7] = f("b_kn_norm")[0]
    v[:64, 8] = f("b_qn_norm")[0]
    v[64:96, 8] = f("b_qr_norm")[0]
    shared["vecs0"] = v
    shared["sink_b"] = np.ascontiguousarray(np.broadcast_to(f("a_sink")[0][None, :], (64, 8)))
    shared["cd_w_in"] = np.ascontiguousarray(f("cd_w_in")[0])
    shared["cd_w_out"] = np.ascontiguousarray(f("cd_w_out")[0])
    shared["c_w2r"] = np.ascontiguousarray(f("c_w2")[0].reshape(128, 512))
    shared["c_a2r"] = np.ascontiguousarray(f("c_a2")[0].reshape(128, 512))
    shared["c_g2"] = np.ascontiguousarray(f("c_g2")[0])
    v1 = np.zeros((128, 160), np.float32)
    v1[:, 0:15] = _fm(f("c_mu_prev")[0]); v1[:, 15:30] = _fm(f("c_mu_next")[0])
    v1[:, 45:49] = _fm(f("c_k_k")[0])
    for d_ in range(2):
        v1[:, 49 + d_ * 4:53 + d_ * 4] = _fm(f("c_w0")[0, d_])
        v1[:, 57 + d_ * 4:61 + d_ * 4] = _fm(f("c_a0")[0, d_])
    v1[:, 65:69] = _fm(f("c_k_a")[0]); v1[:, 69:73] = _fm(f("c_ln_w")[0]); v1[:, 73:77] = _fm(f("c_ln_b")[0])
    v1[:, 77:81] = _fm(f("c_r_k")[0].reshape(-1))
    dcw = f("d_conv_w")[0]
    for fc in range(12):
        for j in range(5):
            v1[:, 81 + fc * 5 + j] = dcw[j, fc * 128:(fc + 1) * 128]
    v1[:, 141] = f("d_o_norm")[0]
    shared["vecs1"] = v1
    gsm = np.zeros((16, 2), np.float32)
    for d_ in range(2):
        gsm[8 + 4 * d_:12 + 4 * d_, 0] = f("d_A_log")[0, d_]
        gsm[8 + 4 * d_:12 + 4 * d_, 1] = f("d_dt_bias")[0, d_]
    shared["gsm"] = gsm
    a_ = np.arange(64)[:, None]; b_ = np.arange(64)[None, :]
    shared["cm2"] = np.concatenate([(a_ < b_), (a_ <= b_), (a_ > b_), (a_ >= b_), (a_ == b_)], 1).astype(np.float32)
    sel = np.zeros((16, 16 * 128), np.float32)
    for q in range(16):
        sel[q, q * 128:(q + 1) * 128] = 1.0
    shared["sel16"] = sel
    maps = []
    for b in cores:
        m = dict(shared)
        m["x"] = np.ascontiguousarray(f("x")[b])
        m["ctx"] = np.ascontiguousarray(f("ctx")[b])
        m["cv"] = np.ascontiguousarray(np.stack([_fm(f("c")[b]), _fm(c_ctx)], -1))
        maps.append(m)
    return maps


def kernel(**inputs):
    nc, S = build_program()
    maps = make_in_maps(inputs, list(range(8)))
    res = run_bass_kernel_spmd(nc, maps, core_ids=list(range(8)))
    return np.stack([r["out"] for r in res.results], 0).astype(np.float32)
```

```python
import numpy as np
from contextlib import ExitStack
import concourse.bass as bass
import concourse.mybir as mybir
from concourse.bass_utils import run_bass_kernel_spmd

F32 = mybir.dt.float32
BF16 = mybir.dt.bfloat16
AF = mybir.ActivationFunctionType
ALU = mybir.AluOpType

D = 1024
T = 2304
NCTX = 256
NT = 18
EPS = 1e-6
CHUNKS = [(0, 256), (256, 512), (768, 512), (1280, 512), (1792, 512)]
DFF = 2816
DBG = {'lora': 1, 'ng': 4, 'prep': 1, 'nstep': 36, 'rwout': 1, 'gdn': 4, 'gprep': 3, 'cs': 5, 'invbf': 0, 'invlv': 5, 'invev': 5, 'invp': 1, 'gseq': 0, 'gpool': 1}


class Sched:
    def __init__(self, nc, stack, n_dma_sems=12):
        self.nc = nc
        self.engs = {"pe": nc.tensor, "dve": nc.vector, "act": nc.scalar, "pool": nc.gpsimd, "sp": nc.sync}
        self.sem = {k: stack.enter_context(nc.semaphore("s_" + k)) for k in self.engs}
        self.cnt = {k: 0 for k in self.engs}
        self.dsem = [stack.enter_context(nc.semaphore(f"dq{i}")) for i in range(n_dma_sems)]
        self.dcnt = [0] * n_dma_sems
        self.dnext = 0
        self.seen = {k: {} for k in self.engs}
        self.lastw = {}
        self.readers = {}
        self.ninst = 0

    def _semobj(self, sk):
        return self.sem[sk] if isinstance(sk, str) else self.dsem[sk]

    def _wait(self, e, sk, val):
        if val <= 0 or self.seen[e].get(sk, 0) >= val:
            return
        self.engs[e].wait_ge(self._semobj(sk), val)
        self.seen[e][sk] = val

    def _deps(self, e, reads, writes):
        deps = {}

        def add(p):
            if p is not None and deps.get(p[0], 0) < p[1]:
                deps[p[0]] = p[1]

        for k in reads:
            add(self.lastw.get(k))
        for k in writes:
            add(self.lastw.get(k))
            for sk, v in self.readers.get(k, {}).items():
                add((sk, v))
        for sk, v in deps.items():
            if sk == "pe" and e == "pe":
                continue
            self._wait(e, sk, v)

    def _commit(self, tok, reads, writes):
        sk, v = tok
        for k in reads:
            self.readers.setdefault(k, {})[sk] = v
        for k in writes:
            self.lastw[k] = tok
            self.readers[k] = {}

    def op(self, e, ins_fn, reads=(), writes=()):
        self._deps(e, reads, writes)
        ins = ins_fn()
        self.cnt[e] += 1
        self.ninst += 1
        ins.then_inc(self.sem[e], 1)
        self._commit((e, self.cnt[e]), reads, writes)
        return ins

    def dma(self, e, out, in_, reads=(), writes=(), **kw):
        i = self.dnext
        self.dnext = (self.dnext + 1) % len(self.dsem)
        self._wait(e, i, self.dcnt[i])
        self._deps(e, reads, writes)
        ins = self.engs[e].dma_start(out=out, in_=in_, **kw)
        self.dcnt[i] += 16
        self.ninst += 1
        ins.then_inc(self.dsem[i], 16)
        self._commit((i, self.dcnt[i]), reads, writes)
        return ins

    def barrier(self):
        for e in self.engs:
            for o in self.engs:
                if o != e:
                    self._wait(e, o, self.cnt[o])
            for i in range(len(self.dsem)):
                self._wait(e, i, self.dcnt[i])
        self.lastw = {}
        self.readers = {}

    def finish(self):
        for o in self.engs:
            if o != "sp":
                self._wait("sp", o, self.cnt[o])
        for i in range(len(self.dsem)):
            self._wait("sp", i, self.dcnt[i])


def _rope_tables():
    theta = 10000.0
    s = np.arange(2048)
    row = (s // 64).astype(np.float64)
    col = (s % 64).astype(np.float64)

    def tab(nd):
        h = nd // 2
        half = h // 2
        inv = theta ** (-np.arange(half, dtype=np.float64) / half)
        cos = np.ones((nd, T)); sin = np.zeros((nd, T))
        for d_ in range(nd):
            b = d_ // h
            i = (d_ % h) % half
            pos = row if b == 0 else col
            ang = (pos.astype(np.float32)[:, None] * inv.astype(np.float32)[None, :])[:, i]
            cos[d_, NCTX:] = np.cos(ang.astype(np.float32))
            sin[d_, NCTX:] = np.sin(ang.astype(np.float32))
        R = np.zeros((nd, nd))
        for d_ in range(nd):
            e = d_ % h
            if e < half:
                R[d_, d_ + half] = -1.0
            else:
                R[d_, d_ - half] = 1.0
        return cos.astype(np.float32), sin.astype(np.float32), R.astype(np.float32)

    cA, sA, RA = tab(64)
    cB, sB, RB = tab(32)
    cosA = np.concatenate([cA, cA], 0); sinA = np.concatenate([sA, sA], 0)
    RA2 = np.zeros((128, 128), np.float32); RA2[:64, :64] = RA; RA2[64:, 64:] = RA
    cosB = np.ones((96, T), np.float32); sinB = np.zeros((96, T), np.float32)
    cosB[64:] = cB; sinB[64:] = sB
    RB96 = np.zeros((96, 96), np.float32); RB96[64:, 64:] = RB
    return cosA, sinA, RA2.T.copy(), cosB, sinB, RB96.T.copy()


def _consts():
    c = {}
    cosA, sinA, RAT, cosB, sinB, RBT = _rope_tables()
    c["cosA"] = cosA; c["sinA"] = sinA; c["cosB"] = cosB; c["sinB"] = sinB
    mats = {}
    mats["ident"] = np.eye(128, dtype=np.float32)
    mats["ones1024"] = np.full((128, 128), 1.0 / 1024, np.float32)
    mats["ones256"] = np.full((128, 128), 1.0 / 256, np.float32)
    o = np.zeros((128, 128), np.float32); o[:64, :64] = 1 / 64; o[64:, 64:] = 1 / 64
    mats["onesA"] = o
    o = np.zeros((128, 128), np.float32); o[:64, :64] = 1 / 64; o[64:96, 64:96] = 1 / 32
    mats["onesB"] = o
    mats["RAT"] = RAT
    r = np.zeros((128, 128), np.float32); r[:96, :96] = RBT
    mats["RBT"] = r
    a = np.arange(128)[:, None]; b = np.arange(128)[None, :]
    mats["maskPrevT"] = np.where(a <= b, 0.0, -30000.0).astype(np.float32)
    mats["maskNextT"] = np.where(b <= a, 0.0, -30000.0).astype(np.float32)
    names = list(mats.keys())
    c["cmats"] = np.stack([mats[n] for n in names], 1).astype(np.float32)
    return c, names


def _fm(v, p=128):
    v = np.asarray(v, np.float32)
    return np.ascontiguousarray(v.reshape(-1, p).T)


class Ctx:
    pass


def build_program(dbg=None, nlayers=2):
    nc = bass.Bass("TRN2", target_bir_lowering=False)
    st = ExitStack()
    K = Ctx()
    with st:
        S = Sched(nc, st)

        def din(name, shape, dt=F32):
            return nc.dram_tensor(name, list(shape), dt, kind="ExternalInput").ap()

        uid = [0]

        def sb(stack, name, shape, dt=F32):
            uid[0] += 1
            return stack.enter_context(nc.sbuf_tensor(f"{name}_s{uid[0]}", list(shape), dt))

        def ps(stack, name, shape, dt=F32):
            uid[0] += 1
            return stack.enter_context(nc.psum_tensor(f"{name}_p{uid[0]}", list(shape), dt))

        consts, cnames = _consts()
        NCM = len(cnames)
        x_d = din("x", [2048, D]); ctx_d = din("ctx", [NCTX, D])
        cv_d = din("cv", [128, 8, 2])
        adaw_d = din("ada_w", [2, D, 6 * D]); adab_d = din("ada_b_fm", [2, 128, 48])
        wup_d = din("ffn_w_up", [2, D, 2 * DFF]); wdn_d = din("ffn_w_down", [2, DFF, D])
        fconv_d = din("ffn_conv_fm", [2, 128, 44, 4])
        abin_d = din("ab_w_in", [D, 1312]); about_d = din("ab_w_out", [D, D])
        wuq_d = din("b_w_uq", [256, 768]); wuk_d = din("b_w_uk", [256, 512]); wuv_d = din("b_w_uv", [256, 512])
        vecs_d = din("vecs0", [128, 16])
        sink_d = din("sink_b", [64, 8])
        cmats_d = din("cmats", [128, NCM, 128])
        cosA_d = din("cosA", [128, T]); sinA_d = din("sinA", [128, T])
        cosB_d = din("cosB", [96, T]); sinB_d = din("sinB", [96, T])
        cdin_d = din("cd_w_in", [D, 3984]); cdout_d = din("cd_w_out", [D, D])
        cw2_d = din("c_w2r", [128, 512]); ca2_d = din("c_a2r", [128, 512]); cg2_d = din("c_g2", [128, 512])
        v1_d = din("vecs1", [128, 160])
        gsm_d = din("gsm", [16, 2])
        cm2_d = din("cm2", [64, 320])
        sel_d = din("sel16", [16, 16 * 128])
        xpark_d = nc.dram_tensor("xpark", [128, 8 * T], F32, kind="Internal").ap()
        out_d = nc.dram_tensor("out", [2048, D], F32, kind="ExternalOutput").ap()
        dbg_d = None
        if dbg is not None:
            dbg_d = nc.dram_tensor("dbg", [T, D], F32, kind="ExternalOutput").ap()

        X = sb(st, "X", [128, 8, T])
        identf = sb(st, "identf", [128, 128])
        cm_b = sb(st, "cm_b", [128, NCM, 128], BF16)
        id4 = sb(st, "id4", [128, 4, 128], BF16)
        modv = sb(st, "modv", [128, 2, 48, 2])
        onep = sb(st, "onep", [128, 2, 2, 8, 2])
        adab = sb(st, "adab", [128, 2, 48])
        cv = sb(st, "cv", [128, 8, 2])
        scv = sb(st, "scv", [128, 8, 2])
        vecs = sb(st, "vecs", [128, 16])
        zeros = sb(st, "zeros", [128, 128])

        def CM(name, bf=True):
            if not bf:
                assert name == "ident"
                return identf[:]
            i = cnames.index(name)
            return cm_b[:, i, :]

        rr = {"cast": 0, "ev": 0}

        def evac_eng():
            rr["ev"] ^= 1
            return "act" if rr["ev"] else "dve"

        def copy_op(e, out, in_):
            if e == "act":
                return lambda: nc.scalar.copy(out=out, in_=in_)
            if e == "dve":
                return lambda: nc.vector.tensor_copy(out=out, in_=in_)
            return lambda: nc.gpsimd.tensor_copy(out=out, in_=in_)

        with ExitStack() as ph:
            cm_f = sb(ph, "cm_f", [128, NCM, 128])
            S.dma("sp", cm_f[:], cmats_d, writes=["cm_f0"])
            S.op("dve", lambda: nc.vector.tensor_copy(out=cm_b[:], in_=cm_f[:]), reads=["cm_f0"], writes=["cm_b"])
            S.op("act", lambda: nc.scalar.copy(out=identf[:], in_=cm_f[:, 0, :]), reads=["cm_f0"], writes=["cm_f"])
            for r in range(4):
                S.op("pool", lambda: nc.gpsimd.tensor_copy(out=id4[:, r, :], in_=cm_f[:, 0, :]), reads=["cm_f0"], writes=["id4"])
            S.barrier()
        S.dma("sp", cv[:], cv_d, writes=["cv"])
        S.dma("sp", adab[:], adab_d.rearrange("l p m -> p l m"), writes=["adab"])
        S.dma("sp", vecs[:], vecs_d, writes=["vecs"])
        S.op("pool", lambda: nc.gpsimd.memset(zeros[:], 0.0), writes=["zeros"])
        S.op("act", lambda: nc.scalar.activation(out=scv[:], in_=cv[:], func=AF.Silu), reads=["cv"], writes=["scv"])

        with ExitStack() as ph:
            xin = [sb(ph, f"xin{i}", [128, D]) for i in range(2)]
            pT = [ps(ph, f"pT{i}", [128, 512]) for i in range(4)]
            for t in range(NT):
                src = ctx_d[t * 128:(t + 1) * 128, :] if t < 2 else x_d[(t - 2) * 128:(t - 1) * 128, :]
                xi = xin[t % 2]
                S.dma("sp" if t % 2 == 0 else "pool", xi[:], src, writes=[f"xin{t%2}"])
                for hh in range(2):
                    pt = pT[(t % 2) * 2 + hh]
                    pk = f"pT{(t%2)*2+hh}"
                    for c4 in range(4):
                        c = hh * 4 + c4
                        S.op("pe", lambda: nc.tensor.transpose(out=pt[:, c4 * 128:(c4 + 1) * 128],
                                                               in_=xi[:, c * 128:(c + 1) * 128],
                                                               identity=CM("ident", False)),
                             reads=[f"xin{t%2}", "cm_f"], writes=[pk])
                    e = evac_eng()
                    S.op(e, copy_op(e, X[:, hh * 4:(hh + 1) * 4, t * 128:(t + 1) * 128],
                                    pt[:].rearrange("p (c n) -> p c n", c=4)),
                         reads=[pk], writes=[f"X{t}"])
        S.barrier()

        with ExitStack() as ph:
            astg = [sb(ph, f"astg{i}", [128, 8, 768]) for i in range(2)]
            pm = ps(ph, "pm", [128, 48, 2])
            adv = adaw_d.rearrange("l (kc p) n -> l p kc n", p=128)
            it = 0
            for l in range(nlayers):
                for g in range(8):
                    a = astg[it % 2]
                    S.dma("sp" if it % 2 == 0 else "pool", a[:], adv[l, :, :, g * 768:(g + 1) * 768],
                          writes=[f"astg{it%2}"])
                    for mm in range(6):
                        m = g * 6 + mm
                        for k in range(8):
                            S.op("pe", lambda: nc.tensor.matmul(pm[:, m, :], lhsT=a[:, k, mm * 128:(mm + 1) * 128],
                                                                rhs=scv[:, k, :], start=(k == 0), stop=(k == 7)),
                                 reads=[f"astg{it%2}", "scv"], writes=["pm"])
                    it += 1
                for w in range(2):
                    S.op("dve", lambda: nc.vector.tensor_tensor(out=modv[:, l, :, w], in0=pm[:, :, w], in1=adab[:, l, :],
                                                                op=ALU.add),
                         reads=["pm", "adab"], writes=["modv"])
                for ji, j in enumerate((1, 4)):
                    S.op("dve", lambda: nc.vector.tensor_scalar_add(out=onep[:, l, ji, :, :],
                                                                    in0=modv[:, l, j * 8:(j + 1) * 8, :], scalar1=1.0),
                         reads=["modv"], writes=["onep"])
        S.barrier()

        def mod(l, j, c, w):
            return modv[:, l, j * 8 + c, w:w + 1]

        def modulate_chunk(ph_bufs, l, jshift, ji_scale, t0, n, w, dst_fn, dst_keys):
            sq, pms, sd, rstd, tmp = ph_bufs
            for c in range(8):
                S.op("act", lambda: nc.scalar.activation(out=sq[c % 2][:, :n], in_=X[:, c, t0:t0 + n], func=AF.Square),
                     reads=[f"X{tt}" for tt in range(t0 // 128, (t0 + n) // 128)], writes=[f"msq{c%2}"])
                S.op("pe", lambda: nc.tensor.matmul(pms[:, :n], lhsT=CM("ones1024"), rhs=sq[c % 2][:, :n],
                                                    start=(c == 0), stop=(c == 7)),
                     reads=[f"msq{c%2}", "cm_b"], writes=["pms"])
            S.op("act", lambda: nc.scalar.activation(out=sd[:, :n], in_=pms[:, :n], func=AF.Sqrt, bias=eps_t[:, 0:1]),
                 reads=["pms", "eps"], writes=["msd"])
            S.op("dve", lambda: nc.vector.reciprocal(out=rstd[:, :n], in_=sd[:, :n]), reads=["msd"], writes=["mrstd"])
            for c in range(8):
                S.op("dve", lambda: nc.vector.scalar_tensor_tensor(out=tmp[c % 2][:, :n], in0=X[:, c, t0:t0 + n],
                                                                   scalar=onep[:, l, ji_scale, c, w:w + 1],
                                                                   in1=rstd[:, :n], op0=ALU.mult, op1=ALU.mult),
                     reads=[f"X{tt}" for tt in range(t0 // 128, (t0 + n) // 128)] + ["mrstd", "onep"],
                     writes=[f"mtmp{c%2}"])
                S.op("act", lambda: nc.scalar.activation(out=dst_fn(c), in_=tmp[c % 2][:, :n], func=AF.Identity,
                                                         bias=mod(l, jshift, c, w), scale=1.0),
                     reads=[f"mtmp{c%2}", "modv"], writes=dst_keys)

        eps_t = sb(st, "eps_t", [128, 1])
        S.op("pool", lambda: nc.gpsimd.memset(eps_t[:], EPS), writes=["eps"])

        def mod_bufs(ph):
            sq = [sb(ph, f"msq{i}", [128, 512], BF16) for i in range(2)]
            pms = ps(ph, "pms", [128, 512])
            sd = sb(ph, "msd", [128, 512])
            rstd = sb(ph, "mrstd", [128, 512])
            tmp = [sb(ph, f"mtmp{i}", [128, 512]) for i in range(2)]
            return (sq, pms, sd, rstd, tmp)

        def wload(stg, stg_key, dst_ap, dst_keys, dram_ap, shape, q):
            view = stg[:, :int(np.prod(shape[1:]))]
            if len(shape) == 3:
                view = view.rearrange("p (a b) -> p a b", a=shape[1])
            view = view[:shape[0]] if shape[0] < 128 else view
            S.dma(q, view, dram_ap, writes=[stg_key])
            rr["cast"] = (rr["cast"] + 1) % 2
            e = ("dve", "pool")[rr["cast"]]
            S.op(e, copy_op(e, dst_ap, view), reads=[stg_key], writes=dst_keys)

        def xkeys(t0, n):
            return [f"X{tt}" for tt in range(t0 // 128, (t0 + n + 127) // 128)]

        def layer0():
            l = 0
            with ExitStack() as L:
                cqn = sb(L, "cqn", [128, 2, T], BF16)
                ckvn = sb(L, "ckvn", [128, 2, T], BF16)
                KR = sb(L, "KR", [96, T], BF16)
                LA = ExitStack()
                QA = sb(LA, "QA", [128, 4, T], BF16)
                KA = sb(LA, "KA", [128, 2, T], BF16)
                VA = sb(LA, "VA", [128, NT, 2, 128], BF16)
                S.op("pool", lambda: nc.gpsimd.memset(VA[:], 1.0), writes=["VA"])
                with ExitStack() as ph:
                    win = sb(ph, "win", [128, 8, 1312], BF16)
                    wkd = sb(ph, "wkd", [128, 8, 256], BF16)
                    wkr = sb(ph, "wkr", [128, 8, 96], BF16)
                    abv = abin_d.rearrange("(kc p) n -> p kc n", p=128)
                    with ExitStack() as phs:
                        stg = [sb(phs, f"stg{i}", [128, 4096]) for i in range(2)]
                        for i, (c0, c1) in enumerate([(0, 512), (512, 1024), (1024, 1312)]):
                            wload(stg[i % 2], f"stg{i%2}", win[:, :, c0:c1], ["win"], abv[:, :, c0:c1], [128, 8, c1 - c0],
                                  "sp" if i % 2 == 0 else "pool")
                        S.barrier()
                    for g in range(2):
                        for hf in range(2):
                            S.op("pool", lambda: nc.gpsimd.tensor_copy(out=wkd[:, :, g * 128 + hf * 64:g * 128 + hf * 64 + 64],
                                                                       in_=win[:, :, 512 + g * 64:512 + g * 64 + 64]),
                                 reads=["win"], writes=["wkd"])
                    S.op("pool", lambda: nc.gpsimd.memset(wkr[:], 0.0), writes=["wkr"])
                    S.op("pool", lambda: nc.gpsimd.tensor_copy(out=wkr[:, :, 64:96], in_=win[:, :, 1280:1312]),
                         reads=["win"], writes=["wkr"])
                    hbuf = sb(ph, "hbuf", [128, 8, 512], BF16)
                    mb = mod_bufs(ph)
                    tabs1 = [sb(ph, f"tab{j}", [128, 512]) for j in range(4)]
                    tabs = [tabs1, tabs1]
                    pin = [ps(ph, f"pin{i}", [128, 512]) for i in range(2)]
                    pms2 = ps(ph, "pms2", [128, 512])
                    prot = ps(ph, "prot", [128, 512])
                    sq2 = [sb(ph, f"sq2{i}", [128, 512], BF16) for i in range(2)]
                    sd2 = sb(ph, "sd2", [128, 512])
                    rs2 = sb(ph, "rs2", [128, 512])
                    qn = sb(ph, "qn", [128, 512], BF16)
                    t1 = sb(ph, "t1", [128, 512])
                    t2 = sb(ph, "t2", [128, 512])
                    cnt = {"pin": 0}

                    def pipeline(mm_list, M, n, ones_name, gain_ap, rope, dst_ap, dst_keys, tb):
                        pi = cnt["pin"] % 2
                        cnt["pin"] += 1
                        p_in = pin[pi]
                        for i, (lh, rh, rk) in enumerate(mm_list):
                            S.op("pe", lambda: nc.tensor.matmul(p_in[:M, :n], lhsT=lh, rhs=rh, start=(i == 0),
                                                                stop=(i == len(mm_list) - 1)),
                                 reads=rk, writes=[f"pin{pi}"])
                        S.op("act", lambda: nc.scalar.activation(out=sq2[pi][:M, :n], in_=p_in[:M, :n], func=AF.Square),
                             reads=[f"pin{pi}"], writes=[f"sq2{pi}"])
                        S.op("pe", lambda: nc.tensor.matmul(pms2[:M, :n], lhsT=CM(ones_name)[:M, :M], rhs=sq2[pi][:M, :n],
                                                            start=True, stop=True),
                             reads=[f"sq2{pi}", "cm_b"], writes=["pms2"])
                        S.op("act", lambda: nc.scalar.activation(out=sd2[:M, :n], in_=pms2[:M, :n], func=AF.Sqrt,
                                                                 bias=eps_t[:M, 0:1]),
                             reads=["pms2", "eps"], writes=["sd2"])
                        S.op("dve", lambda: nc.vector.reciprocal(out=rs2[:M, :n], in_=sd2[:M, :n]), reads=["sd2"], writes=["rs2"])
                        o1 = qn[:M, :n] if rope else dst_ap
                        S.op("dve", lambda: nc.vector.scalar_tensor_tensor(out=o1, in0=p_in[:M, :n], scalar=gain_ap,
                                                                           in1=rs2[:M, :n], op0=ALU.mult, op1=ALU.mult),
                             reads=[f"pin{pi}", "rs2", "vecs"], writes=(["qn"] if rope else dst_keys))
                        if rope:
                            rname, ci, si = rope
                            S.op("pe", lambda: nc.tensor.matmul(prot[:M, :n], lhsT=CM(rname)[:M, :M], rhs=qn[:M, :n],
                                                                start=True, stop=True),
                                 reads=["qn", "cm_b"], writes=["prot"])
                            S.op("dve", lambda: nc.vector.tensor_tensor(out=t1[:M, :n], in0=qn[:M, :n], in1=tb[ci][:M, :n],
                                                                        op=ALU.mult),
                                 reads=["qn", f"tab{ci}"], writes=["t1"])
                            S.op("dve", lambda: nc.vector.tensor_tensor(out=t2[:M, :n], in0=prot[:M, :n], in1=tb[si][:M, :n],
                                                                        op=ALU.mult),
                                 reads=["prot", f"tab{si}"], writes=["t2"])
                            S.op("pool", lambda: nc.gpsimd.tensor_tensor(out=dst_ap, in0=t1[:M, :n], in1=t2[:M, :n], op=ALU.add),
                                 reads=["t1", "t2"], writes=dst_keys)

                    K.pipeline = pipeline
                    for ci_, (t0, n) in enumerate(CHUNKS):
                        w = 1 if ci_ == 0 else 0
                        tb = tabs[ci_ % 2]
                        for j, src in enumerate((cosA_d, sinA_d, cosB_d, sinB_d)):
                            np_ = 128 if j < 2 else 96
                            S.dma("pool", tb[j][:np_, :n], src[:, t0:t0 + n], writes=[f"tab{j}"])
                        modulate_chunk(mb, l, 0, 0, t0, n, w, lambda c: hbuf[:, c, :n], ["hbuf"])
                        ck = [f"tk{tt}" for tt in range(t0 // 128, (t0 + n) // 128)]
                        for i in range(4):
                            pipeline([(win[:, k, i * 128:(i + 1) * 128], hbuf[:, k, :n], ["win", "hbuf"]) for k in range(8)],
                                     128, n, "onesA", vecs[:, 0:1], ("RAT", 0, 1), QA[:, i, t0:t0 + n],
                                     [f"QA{i}.{tt}" for tt in range(t0 // 128, (t0 + n) // 128)], tb)
                        for g in range(2):
                            pipeline([(wkd[:, k, g * 128:(g + 1) * 128], hbuf[:, k, :n], ["wkd", "hbuf"]) for k in range(8)],
                                     128, n, "onesA", vecs[:, 1:2], ("RAT", 0, 1), KA[:, g, t0:t0 + n], ["KA"], tb)
                        for tt in range(n // 128):
                            pi = cnt["pin"] % 2
                            cnt["pin"] += 1
                            for k in range(8):
                                S.op("pe", lambda: nc.tensor.matmul(pin[pi][:, :128], lhsT=hbuf[:, k, tt * 128:(tt + 1) * 128],
                                                                    rhs=win[:, k, 640:768], start=(k == 0), stop=(k == 7)),
                                     reads=["win", "hbuf"], writes=[f"pin{pi}"])
                            e = evac_eng()
                            S.op(e, copy_op(e, VA[:, t0 // 128 + tt, :, 0:64], pin[pi][:, :128].rearrange("p (g d) -> p g d", g=2)),
                                 reads=[f"pin{pi}"], writes=["VA"])
                        for (dst, c0, gcol, dk) in ((cqn, 768, 2, "cqn"), (ckvn, 1024, 4, "ckvn")):
                            for i in range(2):
                                for k in range(8):
                                    S.op("pe", lambda: nc.tensor.matmul(pin[i][:, :n], lhsT=win[:, k, c0 + i * 128:c0 + (i + 1) * 128],
                                                                        rhs=hbuf[:, k, :n], start=(k == 0), stop=(k == 7)),
                                         reads=["win", "hbuf"], writes=[f"pin{i}"])
                                S.op("act", lambda: nc.scalar.activation(out=sq2[i][:, :n], in_=pin[i][:, :n], func=AF.Square),
                                     reads=[f"pin{i}"], writes=[f"sq2{i}"])
                            for i in range(2):
                                S.op("pe", lambda: nc.tensor.matmul(pms2[:, :n], lhsT=CM("ones256"), rhs=sq2[i][:, :n],
                                                                    start=(i == 0), stop=(i == 1)),
                                     reads=[f"sq2{i}", "cm_b"], writes=["pms2"])
                            S.op("act", lambda: nc.scalar.activation(out=sd2[:, :n], in_=pms2[:, :n], func=AF.Sqrt, bias=eps_t[:, 0:1]),
                                 reads=["pms2", "eps"], writes=["sd2"])
                            S.op("dve", lambda: nc.vector.reciprocal(out=rs2[:, :n], in_=sd2[:, :n]), reads=["sd2"], writes=["rs2"])
                            for i in range(2):
                                S.op("dve", lambda: nc.vector.scalar_tensor_tensor(out=dst[:, i, t0:t0 + n], in0=pin[i][:, :n],
                                                                                   scalar=vecs[:, gcol + i:gcol + i + 1], in1=rs2[:, :n],
                                                                                   op0=ALU.mult, op1=ALU.mult),
                                     reads=[f"pin{i}", "rs2", "vecs"], writes=[dk])
                        pipeline([(wkr[:, k, :], hbuf[:, k, :n], ["wkr", "hbuf"]) for k in range(8)],
                                 96, n, "onesB", vecs[:96, 6:7], ("RBT", 2, 3), KR[:96, t0:t0 + n], ["KR"], tb)
                S.barrier()
                if dbg == "qa":
                    dump_fm(QA, 4, BF16)
                    LA.close()
                    return
                with ExitStack() as ph:
                    woA = sb(ph, "woA", [128, 4, D], BF16)
                    aov = about_d.rearrange("(kc p) n -> p kc n", p=128)
                    with ExitStack() as phs:
                        stg = [sb(phs, f"stg{i}", [128, 4096]) for i in range(2)]
                        for i in range(2):
                            wload(stg[i], f"stg{i}", woA[:, 2 * i:2 * i + 2, :], ["woA"], aov[:, 2 * i:2 * i + 2, :], [128, 2, D],
                                  "sp" if i == 0 else "pool")
                        S.barrier()
                    sk_raw = sb(ph, "sk_raw", [64, 8])
                    sk_exp = sb(ph, "sk_exp", [64, 8])
                    SE = sb(ph, "SE", [64, 2, 512])
                    S.dma("sp", sk_raw[:], sink_d, writes=["sk_raw"])
                    S.op("act", lambda: nc.scalar.activation(out=sk_exp[:], in_=sk_raw[:], func=AF.Exp), reads=["sk_raw"], writes=["sk_exp"])
                    for g in range(2):
                        for hb in range(4):
                            hf, j = hb // 2, hb % 2
                            hd = 4 * g + 2 * j + hf
                            S.op("dve", lambda: nc.vector.tensor_scalar(out=SE[:, g, hb * 128:(hb + 1) * 128], in0=zeros[:64, :],
                                                                        scalar1=sk_exp[:, hd:hd + 1], scalar2=None, op0=ALU.add),
                                 reads=["zeros", "sk_exp"], writes=["SE"])
                    pS = [[ps(ph, f"pS{i}{hf}", [128, 512]) for hf in range(2)] for i in range(2)]
                    pO = [ps(ph, f"pO{i}", [128, 512]) for i in range(2)]
                    PT = [sb(ph, f"PT{i}", [128, 512], BF16) for i in range(3)]
                    den = sb(ph, "den", [64, 512])
                    rden = sb(ph, "rden", [64, 512])
                    it = 0
                    nit = 0
                    for qb in range(NT):
                        q0 = qb * 128
                        if qb < 2:
                            kts = [(0, None), (1, None)]
                        else:
                            kts = [(0, None), (1, None)]
                            if qb - 1 >= 2:
                                kts.append((qb - 1, "maskPrevT"))
                            kts.append((qb, None))
                            if qb + 1 < NT:
                                kts.append((qb + 1, "maskNextT"))
                        for g in range(2):
                            po = pO[nit % 2]
                            pok = f"pO{nit%2}"
                            nit += 1
                            for ki, (kt, mk) in enumerate(kts):
                                ptb = PT[it % 3]; ptk = f"PT{it%3}"
                                pss = pS[it % 2]
                                it += 1
                                for hf in range(2):
                                    psb = pss[hf]; psk = f"pS{(it-1)%2}{hf}"
                                    pr = slice(hf * 64, (hf + 1) * 64)
                                    if mk is not None:
                                        S.op("pe", lambda: nc.tensor.matmul(psb[:, :256], lhsT=CM(mk),
                                                                            rhs=id4[:, 0:2, :].rearrange("p r n -> p (r n)"),
                                                                            start=True, stop=False),
                                             reads=["cm_b", "id4"], writes=[psk])
                                    S.op("pe", lambda: nc.tensor.matmul(psb[:, :256].rearrange("p (r n) -> p r n", r=2),
                                                                        lhsT=KA[pr, g, kt * 128:(kt + 1) * 128],
                                                                        rhs=QA[pr, 2 * g:2 * g + 2, q0:q0 + 128],
                                                                        start=(mk is None), stop=True),
                                         reads=["KA", f"QA{2*g}.{qb}", f"QA{2*g+1}.{qb}"], writes=[psk])
                                    S.op("act", lambda: nc.scalar.activation(out=ptb[:, hf * 256:(hf + 1) * 256], in_=psb[:, :256],
                                                                             func=AF.Exp, scale=0.125),
                                         reads=[psk], writes=[ptk])
                                S.op("pe", lambda: nc.tensor.matmul(po[:], lhsT=VA[:, kt, g, :], rhs=ptb[:], start=(ki == 0),
                                                                    stop=(ki == len(kts) - 1)),
                                     reads=["VA", ptk], writes=[pok])
                            S.op("dve", lambda: nc.vector.tensor_tensor(out=den[:], in0=po[64:128, :], in1=SE[:, g, :], op=ALU.add),
                                 reads=[pok, "SE"], writes=["den"])
                            S.op("dve", lambda: nc.vector.reciprocal(out=rden[:], in_=den[:]), reads=["den"], writes=["rden"])
                            for hf in range(2):
                                S.op("dve", lambda: nc.vector.tensor_tensor(
                                    out=QA[hf * 64:(hf + 1) * 64, 2 * g:2 * g + 2, q0:q0 + 128],
                                    in0=po[0:64, hf * 256:(hf + 1) * 256].rearrange("p (r n) -> p r n", r=2),
                                    in1=rden[:, hf * 256:(hf + 1) * 256].rearrange("p (r n) -> p r n", r=2), op=ALU.mult),
                                     reads=[pok, "rden"], writes=[f"QA{2*g}.{qb}", f"QA{2*g+1}.{qb}"])
                    if dbg == "oa":
                        S.barrier()
                        dump_fm(QA, 4, BF16)
                        K.stop = True
                    pd = [ps(ph, f"pd{i}", [128, 512]) for i in range(2)]
                    it = 0
                    for ci_, (t0, n) in enumerate([] if K.stop else CHUNKS):
                        w = 1 if ci_ == 0 else 0
                        for m in range(8):
                            p_ = pd[it % 2]; pk = f"pd{it%2}"; it += 1
                            for i in range(4):
                                S.op("pe", lambda: nc.tensor.matmul(p_[:, :n], lhsT=woA[:, i, m * 128:(m + 1) * 128], rhs=QA[:, i, t0:t0 + n],
                                                                    start=(i == 0), stop=(i == 3)),
                                     reads=["woA"] + [f"QA{i}.{tt}" for tt in range(t0 // 128, (t0 + n) // 128)], writes=[pk])
                            S.op("dve", lambda: nc.vector.scalar_tensor_tensor(out=X[:, m, t0:t0 + n], in0=p_[:, :n], scalar=mod(l, 2, m, w),
                                                                               in1=X[:, m, t0:t0 + n], op0=ALU.mult, op1=ALU.add),
                                 reads=[pk, "modv"] + xkeys(t0, n), writes=xkeys(t0, n))
                S.barrier()
                LA.close()
                if K.stop:
                    return
                with ExitStack() as ph:
                    wuq = sb(ph, "wuq", [128, 2, 768], BF16)
                    wuk = sb(ph, "wuk", [128, 2, 512], BF16)
                    wuv = sb(ph, "wuv", [128, 2, 512], BF16)
                    woB = sb(ph, "woB", [128, 4, D], BF16)
                    with ExitStack() as phs:
                        stg = [sb(phs, f"stg{i}", [128, 4096]) for i in range(2)]
                        wload(stg[0], "stg0", wuq[:], ["wuq"], wuq_d.rearrange("(kc p) n -> p kc n", p=128), [128, 2, 768], "sp")
                        wload(stg[1], "stg1", wuk[:], ["wuk"], wuk_d.rearrange("(kc p) n -> p kc n", p=128), [128, 2, 512], "pool")
                        wload(stg[0], "stg0", wuv[:], ["wuv"], wuv_d.rearrange("(kc p) n -> p kc n", p=128), [128, 2, 512], "sp")
                        aov = about_d.rearrange("(kc p) n -> p kc n", p=128)
                        for i in range(2):
                            wload(stg[(i + 1) % 2], f"stg{(i+1)%2}", woB[:, 2 * i:2 * i + 2, :], ["woB"], aov[:, 4 + 2 * i:4 + 2 * i + 2, :],
                                  [128, 2, D], "pool" if i == 0 else "sp")
                        S.barrier()
                    KB = sb(ph, "KB", [96, 4, T], BF16)
                    VB = sb(ph, "VB", [128, NT, 4, 128], BF16)
                    S.op("pool", lambda: nc.gpsimd.memset(VB[:], 1.0), writes=["VB"])
                    tabs1 = [None, None] + [sb(ph, f"tab{j}", [128, 512]) for j in (2, 3)]
                    tabs = [tabs1, tabs1]
                    pin = [ps(ph, f"pin{i}", [128, 512]) for i in range(2)]
                    pms2 = ps(ph, "pms2", [128, 512])
                    prot = ps(ph, "prot", [128, 512])
                    pS = [ps(ph, f"pS{i}", [128, 512]) for i in range(2)]
                    pO = ps(ph, "pO", [128, 512])
                    pd = ps(ph, "pd", [128, 512])
                    sq2 = [sb(ph, f"sq2{i}", [128, 512], BF16) for i in range(2)]
                    sd2 = sb(ph, "sd2", [128, 512])
                    rs2 = sb(ph, "rs2", [128, 512])
                    qn = sb(ph, "qn", [128, 512], BF16)
                    t1 = sb(ph, "t1", [128, 512])
                    t2 = sb(ph, "t2", [128, 512])
                    QBc = sb(ph, "QBc", [96, 4, 512], BF16)
                    Yc = sb(ph, "Yc", [128, 2, 512], BF16)
                    PT = [sb(ph, f"PT{i}", [128, 512], BF16) for i in range(3)]
                    rden = sb(ph, "rden", [64, 512])
                    cnt = {"pin": 0}

                    def pipeline(mm_list, M, n, ones_name, gain_ap, rope, dst_ap, dst_keys, tb):
                        pi = cnt["pin"] % 2
                        cnt["pin"] += 1
                        p_in = pin[pi]
                        for i, (lh, rh, rk) in enumerate(mm_list):
                            S.op("pe", lambda: nc.tensor.matmul(p_in[:M, :n], lhsT=lh, rhs=rh, start=(i == 0),
                                                                stop=(i == len(mm_list) - 1)),
                                 reads=rk, writes=[f"pin{pi}"])
                        S.op("act", lambda: nc.scalar.activation(out=sq2[pi][:M, :n], in_=p_in[:M, :n], func=AF.Square),
                             reads=[f"pin{pi}"], writes=[f"sq2{pi}"])
                        S.op("pe", lambda: nc.tensor.matmul(pms2[:M, :n], lhsT=CM(ones_name)[:M, :M], rhs=sq2[pi][:M, :n],
                                                            start=True, stop=True),
                             reads=[f"sq2{pi}", "cm_b"], writes=["pms2"])
                        S.op("act", lambda: nc.scalar.activation(out=sd2[:M, :n], in_=pms2[:M, :n], func=AF.Sqrt,
                                                                 bias=eps_t[:M, 0:1]),
                             reads=["pms2", "eps"], writes=["sd2"])
                        S.op("dve", lambda: nc.vector.reciprocal(out=rs2[:M, :n], in_=sd2[:M, :n]), reads=["sd2"], writes=["rs2"])
                        o1 = qn[:M, :n] if rope else dst_ap
                        S.op("dve", lambda: nc.vector.scalar_tensor_tensor(out=o1, in0=p_in[:M, :n], scalar=gain_ap,
                                                                           in1=rs2[:M, :n], op0=ALU.mult, op1=ALU.mult),
                             reads=[f"pin{pi}", "rs2", "vecs"], writes=(["qn"] if rope else dst_keys))
                        if rope:
                            rname, ci, si = rope
                            S.op("pe", lambda: nc.tensor.matmul(prot[:M, :n], lhsT=CM(rname)[:M, :M], rhs=qn[:M, :n],
                                                                start=True, stop=True),
                                 reads=["qn", "cm_b"], writes=["prot"])
                            S.op("dve", lambda: nc.vector.tensor_tensor(out=t1[:M, :n], in0=qn[:M, :n], in1=tb[ci][:M, :n],
                                                                        op=ALU.mult),
                                 reads=["qn", f"tab{ci}"], writes=["t1"])
                            S.op("dve", lambda: nc.vector.tensor_tensor(out=t2[:M, :n], in0=prot[:M, :n], in1=tb[si][:M, :n],
                                                                        op=ALU.mult),
                                 reads=["prot", f"tab{si}"], writes=["t2"])
                            S.op("pool", lambda: nc.gpsimd.tensor_tensor(out=dst_ap, in0=t1[:M, :n], in1=t2[:M, :n], op=ALU.add),
                                 reads=["t1", "t2"], writes=dst_keys)

                    it = 0
                    for p in range(2):
                        for hl in range(4):
                            h = 4 * p + hl
                            for (t0, n) in CHUNKS:
                                pipeline([(wuk[:, k, h * 64:(h + 1) * 64], ckvn[:, k, t0:t0 + n], ["wuk", "ckvn"]) for k in range(2)],
                                         64, n, "onesA", vecs[:64, 7:8], None, KB[0:64, hl, t0:t0 + n], ["KB"], None)
                            S.op("pool", lambda: nc.gpsimd.tensor_copy(out=KB[64:96, hl, :], in_=KR[64:96, :]), reads=["KR"], writes=["KB"])
                        for tt in range(NT):
                            pi = cnt["pin"] % 2
                            cnt["pin"] += 1
                            for k in range(2):
                                S.op("pe", lambda: nc.tensor.matmul(pin[pi][:, :256], lhsT=ckvn[:, k, tt * 128:(tt + 1) * 128],
                                                                    rhs=wuv[:, k, p * 256:(p + 1) * 256], start=(k == 0), stop=(k == 1)),
                                     reads=["wuv", "ckvn"], writes=[f"pin{pi}"])
                            e = evac_eng()
                            S.op(e, copy_op(e, VB[:, tt, :, 0:64], pin[pi][:, :256].rearrange("p (g d) -> p g d", g=4)),
                                 reads=[f"pin{pi}"], writes=["VB"])
                        for ci_, (t0, n) in enumerate(CHUNKS):
                            w = 1 if ci_ == 0 else 0
                            tb = tabs[ci_ % 2]
                            for j, src in ((2, cosB_d), (3, sinB_d)):
                                S.dma("pool", tb[j][:96, :n], src[:, t0:t0 + n], writes=[f"tab{j}"])
                            kts = [0, 1] if ci_ == 0 else list(range(NT))
                            for hl in range(4):
                                h = 4 * p + hl
                                pipeline([(wuq[:, k, h * 96:(h + 1) * 96], cqn[:, k, t0:t0 + n], ["wuq", "cqn"]) for k in range(2)],
                                         96, n, "onesB", vecs[:96, 8:9], ("RBT", 2, 3), QBc[:96, hl, :n], [f"QBc{hl}"], tb)
                            for hl in range(4):
                                for ki, kt in enumerate(kts):
                                    psb = pS[it % 2]; psk = f"pS{it%2}"
                                    ptb = PT[it % 3]; ptk = f"PT{it%3}"
                                    it += 1
                                    S.op("pe", lambda: nc.tensor.matmul(psb[:, :n], lhsT=KB[:96, hl, kt * 128:(kt + 1) * 128],
                                                                        rhs=QBc[:96, hl, :n], start=True, stop=True),
                                         reads=["KB", f"QBc{hl}"], writes=[psk])
                                    S.op("act", lambda: nc.scalar.activation(out=ptb[:, :n], in_=psb[:, :n], func=AF.Exp,
                                                                             scale=float(96 ** -0.5)),
                                         reads=[psk], writes=[ptk])
                                    S.op("pe", lambda: nc.tensor.matmul(pO[:, :n], lhsT=VB[:, kt, hl, :], rhs=ptb[:, :n],
                                                                        start=(ki == 0), stop=(ki == len(kts) - 1)),
                                         reads=["VB", ptk], writes=["pO"])
                                S.op("dve", lambda: nc.vector.reciprocal(out=rden[:, :n], in_=pO[64:128, :n]), reads=["pO"], writes=["rden"])
                                S.op("dve", lambda: nc.vector.tensor_tensor(out=Yc[(hl % 2) * 64:(hl % 2) * 64 + 64, hl // 2, :n],
                                                                            in0=pO[0:64, :n], in1=rden[:, :n], op=ALU.mult),
                                     reads=["pO", "rden"], writes=["Yc"])
                            for m in range(8):
                                for i in range(2):
                                    S.op("pe", lambda: nc.tensor.matmul(pd[:, :n], lhsT=woB[:, 2 * p + i, m * 128:(m + 1) * 128],
                                                                        rhs=Yc[:, i, :n], start=(i == 0), stop=(i == 1)),
                                         reads=["woB", "Yc"], writes=["pd"])
                                S.op("dve", lambda: nc.vector.scalar_tensor_tensor(out=X[:, m, t0:t0 + n], in0=pd[:, :n], scalar=mod(l, 2, m, w),
                                                                                   in1=X[:, m, t0:t0 + n], op0=ALU.mult, op1=ALU.add),
                                     reads=["pd", "modv"] + xkeys(t0, n), writes=xkeys(t0, n))
                S.barrier()

        def ffn(l, last):
            G = 3
            groups = [list(range(j, min(j + G, 22))) for j in range(0, 22, G)]
            tch = [(0, 256, 0, 0)]
            for i in range(8):
                tch.append((256 + i * 256, 256, 0 if i == 0 else 1, 0 if i == 7 else 1))
            if last:
                tch = tch[1:]
            with ExitStack() as ph:
                H2 = sb(ph, "H2", [128, 8, T], BF16)
                fcv = sb(ph, "fcv", [128, 44, 4])
                S.dma("sp", fcv[:], fconv_d[l], writes=["fcv"])
                with ExitStack() as ph2:
                    mb = mod_bufs(ph2)
                    for ci_, (t0, n) in enumerate(CHUNKS):
                        w = 1 if ci_ == 0 else 0
                        if last and ci_ == 0:
                            continue
                        modulate_chunk(mb, l, 3, 1, t0, n, w, lambda c: H2[:, c, t0:t0 + n], ["H2"])
                    S.barrier()
                stg = [sb(ph, f"fstg{i}", [128, 4096]) for i in range(2)]
                wup = [sb(ph, f"wup{i}", [128, 8, 2 * G * 128], BF16) for i in range(2)]
                wdn = [sb(ph, f"wdn{i}", [128, G, D], BF16) for i in range(2)]
                pu = [ps(ph, f"pu{i}", [128, 512]) for i in range(4)]
                pdn = [ps(ph, f"pdn{i}", [128, 512]) for i in range(2)]
                uu = [sb(ph, f"uu{i}", [128, 256]) for i in range(4)]
                sg = sb(ph, "sg", [128, 256])
                actb = [sb(ph, f"actb{i}", [128, 256], BF16) for i in range(2 * G)]
                upv = wup_d[l].rearrange("(kc p) n -> p kc n", p=128)
                dnv = wdn_d[l].rearrange("(kc p) n -> p kc n", p=128)
                si = 0
                iu = 0
                ia = 0
                idn = 0
                for gi, js in enumerate(groups):
                    g_n = len(js)
                    wu = wup[gi % 2]; wd = wdn[gi % 2]
                    j0 = js[0]
                    for part in range(2):
                        wload(stg[si % 2], f"fstg{si%2}", wu[:, :, part * G * 128:part * G * 128 + g_n * 128], [f"wup{gi%2}"],
                              upv[:, :, part * DFF + j0 * 128:part * DFF + (j0 + g_n) * 128], [128, 8, g_n * 128],
                              "sp" if si % 2 == 0 else "pool")
                        si += 1
                    wload(stg[si % 2], f"fstg{si%2}", wd[:, :g_n, :], [f"wdn{gi%2}"], dnv[:, j0:j0 + g_n, :], [128, g_n, D],
                          "sp" if si % 2 == 0 else "pool")
                    si += 1
                    for (s0, n, lo, hi) in tch:
                        w = 1 if s0 == 0 else 0
                        e0 = s0 - lo
                        ne = n + lo + hi
                        abufs = []
                        for jj, j in enumerate(js):
                            us = []
                            for part in range(2):
                                p_ = pu[iu % 4]; pk = f"pu{iu%4}"
                                u_ = uu[iu % 4]; uk = f"uu{iu%4}"
                                iu += 1
                                fc = part * 22 + j
                                for k in range(8):
                                    S.op("pe", lambda: nc.tensor.matmul(p_[:, :ne], lhsT=wu[:, k, (part * G + jj) * 128:(part * G + jj + 1) * 128],
                                                                        rhs=H2[:, k, e0:e0 + ne], start=(k == 0), stop=(k == 7)),
                                         reads=[f"wup{gi%2}", "H2"], writes=[pk])
                                S.op("act", lambda: nc.scalar.activation(out=u_[:, :n], in_=p_[:, lo:lo + n], func=AF.Identity,
                                                                         bias=fcv[:, fc, 3:4], scale=fcv[:, fc, 1:2]),
                                     reads=[pk, "fcv"], writes=[uk])
                                a = 1 if lo == 0 else 0
                                S.op("dve", lambda: nc.vector.scalar_tensor_tensor(out=u_[:, a:n], in0=p_[:, lo - 1 + a:lo + n - 1],
                                                                                   scalar=fcv[:, fc, 0:1], in1=u_[:, a:n],
                                                                                   op0=ALU.mult, op1=ALU.add),
                                     reads=[pk, "fcv", uk], writes=[uk])
                                b = 1 if hi == 0 else 0
                                S.op("dve", lambda: nc.vector.scalar_tensor_tensor(out=u_[:, :n - b], in0=p_[:, lo + 1:lo + 1 + n - b],
                                                                                   scalar=fcv[:, fc, 2:3], in1=u_[:, :n - b],
                                                                                   op0=ALU.mult, op1=ALU.add),
                                     reads=[pk, "fcv", uk], writes=[uk])
                                us.append((u_, uk))
                            (uv, uvk), (ug, ugk) = us
                            S.op("act", lambda: nc.scalar.activation(out=sg[:, :n], in_=ug[:, :n], func=AF.Silu), reads=[ugk], writes=["sg"])
                            ab = actb[ia % (2 * G)]; abk = f"actb{ia%(2*G)}"
                            ia += 1
                            S.op("pool", lambda: nc.gpsimd.tensor_tensor(out=ab[:, :n], in0=sg[:, :n], in1=uv[:, :n], op=ALU.mult),
                                 reads=["sg", uvk], writes=[abk])
                            abufs.append((ab, abk))
                        for m in range(8):
                            p_ = pdn[idn % 2]; pk = f"pdn{idn%2}"; idn += 1
                            for jj in range(g_n):
                                S.op("pe", lambda: nc.tensor.matmul(p_[:, :n], lhsT=wd[:, jj, m * 128:(m + 1) * 128], rhs=abufs[jj][0][:, :n],
                                                                    start=(jj == 0), stop=(jj == g_n - 1)),
                                     reads=[f"wdn{gi%2}", abufs[jj][1]], writes=[pk])
                            S.op("dve", lambda: nc.vector.scalar_tensor_tensor(out=X[:, m, s0:s0 + n], in0=p_[:, :n], scalar=mod(l, 5, m, w),
                                                                               in1=X[:, m, s0:s0 + n], op0=ALU.mult, op1=ALU.add),
                                 reads=[pk, "modv"] + xkeys(s0, n), writes=xkeys(s0, n))
            S.barrier()

        def dump_fm(buf, nchunk, dt, ntile=NT):
            with ExitStack() as ph:
                tin = sb(ph, "d_tin", [128, 128])
                pt_ = ps(ph, "d_pt", [128, 128])
                ob = sb(ph, "d_ob", [128, D])
                for t in range(ntile):
                    for c in range(nchunk):
                        S.op("dve", lambda: nc.vector.tensor_copy(out=tin[:], in_=buf[:, c, t * 128:(t + 1) * 128]), reads=["*"], writes=["d_tin"])
                        S.op("pe", lambda: nc.tensor.transpose(out=pt_[:], in_=tin[:], identity=CM("ident", False)),
                             reads=["d_tin", "cm_f"], writes=["d_pt"])
                        S.op("dve", lambda: nc.vector.tensor_copy(out=ob[:, c * 128:(c + 1) * 128], in_=pt_[:]), reads=["d_pt"], writes=["d_ob"])
                    S.dma("sp", dbg_d[t * 128:(t + 1) * 128, :nchunk * 128], ob[:, :nchunk * 128], reads=["d_ob"])
                S.barrier()

        def write_out():
            with ExitStack() as ph:
                pt_ = [ps(ph, f"o_pt{i}", [128, 512]) for i in range(4)]
                ob = [sb(ph, f"o_ob{i}", [128, D]) for i in range(2)]
                for t in range(2, NT):
                    o_ = ob[t % 2]
                    for hh in range(2):
                        p_ = pt_[(t % 2) * 2 + hh]; pk = f"o_pt{(t%2)*2+hh}"
                        for c4 in range(4):
                            c = hh * 4 + c4
                            S.op("pe", lambda: nc.tensor.transpose(out=p_[:, c4 * 128:(c4 + 1) * 128], in_=X[:, c, t * 128:(t + 1) * 128],
                                                                   identity=CM("ident", False)),
                                 reads=[f"X{t}", "cm_f"], writes=[pk])
                        e = evac_eng()
                        S.op(e, copy_op(e, o_[:, hh * 512:(hh + 1) * 512], p_[:]), reads=[pk], writes=[f"o_ob{t%2}"])
                    S.dma("sp" if t % 2 == 0 else "pool", out_d[(t - 2) * 128:(t - 1) * 128, :], o_[:], reads=[f"o_ob{t%2}"])
                if dbg == "x":
                    for t in range(2):
                        o_ = ob[t % 2]
                        for hh in range(2):
                            p_ = pt_[(t % 2) * 2 + hh]; pk = f"o_pt{(t%2)*2+hh}"
                            for c4 in range(4):
                                c = hh * 4 + c4
                                S.op("pe", lambda: nc.tensor.transpose(out=p_[:, c4 * 128:(c4 + 1) * 128], in_=X[:, c, t * 128:(t + 1) * 128],
                                                                       identity=CM("ident", False)),
                                     reads=[f"X{t}", "cm_f"], writes=[pk])
                            e = evac_eng()
                            S.op(e, copy_op(e, o_[:, hh * 512:(hh + 1) * 512], p_[:]), reads=[pk], writes=[f"o_ob{t%2}"])
                        S.dma("sp", dbg_d[t * 128:(t + 1) * 128, :], o_[:], reads=[f"o_ob{t%2}"])


        def layer1():
            l = 1
            INC = 1920
            NCH = T // 64
            fwd_order = list(range(NCH))
            bwd_order = [3, 2, 1, 0] + list(range(NCH - 1, 3, -1))
            PCH = [(0, 256, 0, 256)] + [(256 + i * 256, 256, 256, T) for i in range(8)]
            with ExitStack() as L:
                L0 = L
                H1 = sb(L, "H1", [128, 8, T], BF16)
                Yall = sb(L, "Yall", [128, 8, 2048], BF16)
                v1 = sb(L, "v1", [128, 160])
                cm2 = sb(L, "cm2", [64, 320])
                ones64 = sb(L, "ones64", [128, 64])
                id64b = sb(L, "id64b", [64, 64], BF16)
                S.dma("sp", v1[:], v1_d, writes=["v1"])
                S.dma("sp", cm2[:], cm2_d, writes=["cm2"])
                S.op("pool", lambda: nc.gpsimd.memset(ones64[:], 1.0), writes=["ones64"])
                S.op("dve", lambda: nc.vector.tensor_copy(out=id64b[:], in_=cm2[:, 256:320]), reads=["cm2"], writes=["id64b"])
                S.op("dve", lambda: nc.vector.tensor_tensor(out=v1[:, 30:45], in0=v1[:, 0:15], in1=v1[:, 15:30], op=ALU.add), reads=["v1"], writes=["v1"])
                S.op("dve", lambda: nc.vector.tensor_scalar(out=v1[:, 30:45], in0=v1[:, 30:45], scalar1=-1.0, scalar2=1.0, op0=ALU.mult, op1=ALU.add),
                     reads=["v1"], writes=["v1"])
                with ExitStack() as ph2:
                    mb = mod_bufs(ph2)
                    for ci_, (t0, n) in enumerate(CHUNKS):
                        w = 1 if ci_ == 0 else 0
                        modulate_chunk(mb, l, 0, 0, t0, n, w, lambda c: H1[:, c, t0:t0 + n], ["H1"])
                    S.barrier()
                for c in range(8):
                    S.dma("sp" if c % 2 == 0 else "pool", xpark_d[:, c * T:(c + 1) * T], X[:, c, :], reads=[f"X{t}" for t in range(NT)])
                S.barrier()
                XB = X[:].rearrange("p c t -> p (c t)").bitcast(BF16)
                XF = X[:].rearrange("p c t -> p (c t)")

                def xb(i):
                    return XB[:, i * T:(i + 1) * T]

                def xf(i):
                    return XF[:, i * T:(i + 1) * T]

                LS = ExitStack()
                stg = sb(LS, "l1stg", [128, 1024])
                wg = sb(LS, "l1wg", [128, 8, 512], BF16)
                gneps = sb(LS, "gneps", [128, 1])
                S.op("pool", lambda: nc.gpsimd.memset(gneps[:], 64e-5), writes=["gneps"])
                L = LS
                pin = [ps(L, f"l1pin{i}", [128, 512]) for i in range(2)]
                pC = [ps(L, f"l1pC{i}", [128, 512]) for i in range(4)]
                pTr = [ps(L, f"l1pT{i}", [64, 1024], BF16) for i in range(2)]
                uw = [sb(L, f"l1u{i}", [128, 260]) for i in range(2)]
                cnt = {"pin": 0}
                cdv = cdin_d.rearrange("(kc p) n -> p kc n", p=128)

                def load_cols(cols_list):
                    off = 0
                    for (c0, ncol) in cols_list:
                        for b0 in range(0, ncol, 128):
                            nb = min(128, ncol - b0)
                            wload(stg, "l1stg", wg[:, :, off:off + nb], ["l1wg"], cdv[:, :, c0 + b0:c0 + b0 + nb], [128, 8, nb], "sp")
                            off += nb

                def proj_taps(woff, M, taps, func, out_fn, out_keys, bias_ap=None, scale_out=None):
                    hw = max(abs(o) for o, _ in taps)
                    for (s0, n, qlo, qhi) in PCH:
                        e0 = max(s0 - hw, qlo); e1 = min(s0 + n + hw, qhi)
                        ne = e1 - e0
                        pi = cnt["pin"] % 2; cnt["pin"] += 1
                        p_ = pin[pi]; u_ = uw[pi]
                        for k in range(8):
                            S.op("pe", lambda: nc.tensor.matmul(p_[:M, :ne], lhsT=wg[:, k, woff:woff + M], rhs=H1[:, k, e0:e1],
                                                                start=(k == 0), stop=(k == 7)),
                                 reads=["l1wg", "H1"], writes=[f"l1pin{pi}"])
                        src = p_
                        if len(taps) > 1:
                            base = s0 - e0
                            o0, c0_ = taps[0]
                            S.op("act", lambda: nc.scalar.activation(out=u_[:M, :n], in_=p_[:M, base:base + n], func=AF.Identity,
                                                                     scale=c0_),
                                 reads=[f"l1pin{pi}", "v1"], writes=[f"l1u{pi}"])
                            for (o, cf) in taps[1:]:
                                i0 = max(0, e0 - s0 - o); i1 = min(n, e1 - s0 - o)
                                S.op("dve", lambda: nc.vector.scalar_tensor_tensor(out=u_[:M, i0:i1], in0=p_[:M, base + i0 + o:base + i1 + o],
                                                                                   scalar=cf, in1=u_[:M, i0:i1], op0=ALU.mult, op1=ALU.add),
                                     reads=[f"l1pin{pi}", f"l1u{pi}", "v1"], writes=[f"l1u{pi}"])
                            src = u_
                            srck = f"l1u{pi}"
                            sl = slice(0, n)
                        else:
                            srck = f"l1pin{pi}"
                            sl = slice(s0 - e0, s0 - e0 + n)
                        kw = {}
                        if bias_ap is not None:
                            kw["bias"] = bias_ap
                        if scale_out is not None:
                            kw["scale"] = scale_out
                        S.op("act", lambda: nc.scalar.activation(out=out_fn(s0, n), in_=src[:M, sl], func=func, **kw),
                             reads=[srck, "v1"], writes=out_keys)

                def shift_taps(fc):
                    return [(0, v1[:, 30 + fc:31 + fc]), (-1, v1[:, fc:fc + 1]), (1, v1[:, 15 + fc:16 + fc])]

                wk = {}
                for dn in ("f", "b"):
                    wk[dn] = dict(
                        al=sb(L, f"al{dn}", [128, 64]), be=sb(L, f"be{dn}", [128, 64]), kd=sb(L, f"kd{dn}", [128, 64]),
                        rr=sb(L, f"rr{dn}", [128, 64]), lw=sb(L, f"lw{dn}", [128, 64]),
                        pfx=sb(L, f"pfx{dn}", [128, 64]), Gi=sb(L, f"Gi{dn}", [128, 64]), Ge=sb(L, f"Ge{dn}", [128, 64]),
                        E=sb(L, f"E{dn}", [128, 4, 64]), Es=sb(L, f"Es{dn}", [128, 3, 64]),
                        negm=sb(L, f"negm{dn}", [128, 1]),
                        BT=sb(L, f"BT{dn}", [128, 64], BF16), KT=sb(L, f"KT{dn}", [128, 64], BF16),
                        Hs=sb(L, f"Hs{dn}", [128, 128]), Hb=sb(L, f"Hb{dn}", [128, 128], BF16), ztmp=sb(L, f"ztmp{dn}", [64, 128]),
                    )
                    for j in range(2):
                        idt = BF16 if DBG['invbf'] else F32
                        wk[dn][f"N{j}"] = [sb(L, f"N{dn}{j}{i}", [64, 64], idt) for i in range(2)]
                        wk[dn][f"NT{j}"] = [sb(L, f"NT{dn}{j}{i}", [64, 64], idt) for i in range(2)]
                        wk[dn][f"IN{j}"] = sb(L, f"IN{dn}{j}", [64, 64], idt)
                        wk[dn][f"PT{j}"] = [sb(L, f"PTi{dn}{j}{i}", [64, 64], idt) for i in range(2)]
                        wk[dn][f"TT{j}"] = sb(L, f"TT{dn}{j}", [64, 64], BF16)
                        wk[dn][f"ARB{j}"] = sb(L, f"ARB{dn}{j}", [64, 64], BF16)
                        wk[dn][f"AKRK{j}"] = sb(L, f"AKRK{dn}{j}", [64, 128], BF16)
                        wk[dn][f"Zb{j}"] = sb(L, f"Zb{dn}{j}", [64, 128], BF16)
                        wk[dn][f"Ub{j}"] = sb(L, f"Ub{dn}{j}", [64, 128], BF16)

                wk2 = {}
                for dn in ("f", "b"):
                    wk2[dn] = []
                    for par in range(2):
                        wk2[dn].append(dict(
                            AR=sb(L, f"AR2{dn}{par}", [128, 2, 64], BF16), BH=sb(L, f"BH2{dn}{par}", [128, 64], BF16), KH=sb(L, f"KH2{dn}{par}", [128, 64], BF16),
                            ART=sb(L, f"ART2{dn}{par}", [128, 2, 64], BF16), BTt=sb(L, f"BTt2{dn}{par}", [64, 128], BF16),
                            KTt=sb(L, f"KTt2{dn}{par}", [64, 128], BF16), VT=sb(L, f"VT2{dn}{par}", [64, 128], BF16), Et=sb(L, f"Et2{dn}{par}", [128, 1])))

                def dm_aps(di, par):
                    if par == 0:
                        v = K.lw2flat[:, di * 648:di * 648 + 648].bitcast(F32)
                        return v[0:64, 0:256], v[0:64, 256:320], v[0:64, 320:322]
                    sqf = sq[:].bitcast(F32)
                    return t32[di][0:64, 0:256], sqf[0:64, di * 64:(di + 1) * 64], t32[2][0:64, di * 2:di * 2 + 2]

                def prefix_gen(dn, pre_ops, vsrc_ap, vkeys, sdec, par=0):
                    W = wk[dn]
                    W2 = wk2[dn][par]
                    k_ = lambda nm: f"{nm}{dn}"
                    k2 = lambda nm: f"{nm}{dn}p{par}"
                    fwd = dn == "f"
                    di = 0 if fwd else 1
                    ptr = pTr[di]; ptk = f"l1pT{di}"
                    for fn in pre_ops:
                        fn()
                        yield
                    S.op("dve", lambda: nc.vector.tensor_tensor_scan(out=W["pfx"][:], data0=ones64[:], data1=W["lw"][:], initial=0.0,
                                                                     op0=ALU.mult, op1=ALU.add),
                         reads=[k_("lw"), "ones64"], writes=[k_("pfx")])
                    yield
                    tot = W["pfx"][:, 63:64]
                    if fwd:
                        S.op("pool", lambda: nc.gpsimd.tensor_copy(out=W["Gi"][:], in_=W["pfx"][:]), reads=[k_("pfx")], writes=[k_("Gi")])
                        yield
                        S.op("dve", lambda: nc.vector.tensor_tensor(out=W["Ge"][:], in0=W["pfx"][:], in1=W["lw"][:], op=ALU.subtract),
                             reads=[k_("pfx"), k_("lw")], writes=[k_("Ge")])
                        yield
                    else:
                        S.op("dve", lambda: nc.vector.tensor_scalar(out=W["Ge"][:], in0=W["pfx"][:], scalar1=-1.0, scalar2=tot,
                                                                    op0=ALU.mult, op1=ALU.add),
                             reads=[k_("pfx")], writes=[k_("Ge")])
                        yield
                        S.op("dve", lambda: nc.vector.tensor_tensor(out=W["Gi"][:], in0=W["Ge"][:], in1=W["lw"][:], op=ALU.add),
                             reads=[k_("Ge"), k_("lw")], writes=[k_("Gi")])
                        yield
                    E = W["E"]; Es = W["Es"]; negm = W["negm"]
                    S.op("act", lambda: nc.scalar.activation(out=E[:, 0, :], in_=W["Ge"][:], func=AF.Exp), reads=[k_("Ge")], writes=[k_("E0")])
                    yield
                    S.op("act", lambda: nc.scalar.activation(out=E[:, 1, :], in_=W["Gi"][:], func=AF.Exp), reads=[k_("Gi")], writes=[k_("E1")])
                    yield
                    S.op("act", lambda: nc.scalar.activation(out=E[:, 3, :], in_=W["Gi"][:], func=AF.Exp, scale=-1.0, bias=tot),
                         reads=[k_("Gi"), k_("pfx")], writes=[k_("E3")])
                    yield
                    S.op("act", lambda: nc.scalar.activation(out=W2["Et"][:], in_=tot, func=AF.Exp), reads=[k_("pfx")], writes=[k2("Et")])
                    yield
                    if sdec:
                        dmA, dm4, gcol = dm_aps(di, par)
                        dk_ = f"dmw{di}p{par}"; gk_ = f"gcol{di}p{par}"
                        msk = (cm2[:, 0:64], cm2[:, 64:128], cm2[:, 128:192]) if fwd else (cm2[:, 128:192], cm2[:, 192:256], cm2[:, 0:64])
                        for ti, srcn in enumerate(("Ge", "Gi")):
                            po_ = di * 256 + ti * 128
                            S.op("pe", lambda: nc.tensor.transpose(out=pin[1][0:64, po_:po_ + 128], in_=W[srcn][:], identity=CM("ident", False)),
                                 reads=[k_(srcn), "cm_f"], writes=["l1pin1"])
                            yield
                            S.op("dve", lambda: nc.vector.tensor_copy(out=gcol[:, ti:ti + 1], in_=pin[1][0:64, po_:po_ + 1]),
                                 reads=["l1pin1"], writes=[gk_])
                            yield
                        rowGe = W["Ge"][0:64, :]; rowGi = W["Gi"][0:64, :]
                        for qd, (row, rk_, cidx) in enumerate(((rowGe, "Ge", 0), (rowGe, "Ge", 1), (rowGi, "Gi", 0), (rowGi, "Gi", 1))):
                            S.op("dve", lambda: nc.vector.tensor_scalar(out=dmA[:, qd * 64:(qd + 1) * 64], in0=row, scalar1=gcol[:, cidx:cidx + 1], scalar2=0.0,
                                                                        op0=ALU.subtract, op1=ALU.min),
                                 reads=[k_(rk_), gk_], writes=[dk_])
                            yield
                        S.op("dve", lambda: nc.vector.tensor_scalar(out=dm4, in0=rowGe, scalar1=-1.0, scalar2=gcol[:, 0:1], op0=ALU.mult, op1=ALU.add),
                             reads=[k_("Ge"), gk_], writes=[dk_])
                        yield
                        S.op("dve", lambda: nc.vector.tensor_scalar(out=dm4, in0=dm4, scalar1=0.0, scalar2=None, op0=ALU.min),
                             reads=[dk_], writes=[dk_])
                        yield
                        S.op("act", lambda: nc.scalar.activation(out=dmA, in_=dmA, func=AF.Exp), reads=[dk_], writes=[dk_])
                        yield
                        S.op("act", lambda: nc.scalar.activation(out=dm4, in_=dm4, func=AF.Exp), reads=[dk_], writes=[dk_])
                        yield
                        for qd, mi in enumerate((0, 0, 1, 1, 2)):
                            dsl = dmA[:, qd * 64:(qd + 1) * 64] if qd < 4 else dm4
                            S.op("pool", lambda: nc.gpsimd.tensor_tensor(out=dsl, in0=dsl, in1=msk[mi], op=ALU.mult),
                                 reads=[dk_, "cm2"], writes=[dk_])
                            yield
                        S.op("dve", lambda: nc.vector.tensor_copy(out=W2["AR"][:, 0, :], in_=W["al"][:]), reads=[k_("al")], writes=[k2("AR")])
                        yield
                        S.op("pool", lambda: nc.gpsimd.tensor_copy(out=W2["AR"][:, 1, :], in_=W["rr"][:]), reads=[k_("rr")], writes=[k2("AR")])
                        yield
                        S.op("pool", lambda: nc.gpsimd.tensor_copy(out=W2["KH"][:], in_=W["kd"][:]), reads=[k_("kd")], writes=[k2("KH")])
                        yield
                    else:
                        mcol = W["Gi"][:, 32:33]
                        S.op("dve", lambda: nc.vector.tensor_scalar(out=negm[:], in0=mcol, scalar1=-1.0, scalar2=None, op0=ALU.mult),
                             reads=[k_("Gi")], writes=[k_("negm")])
                        yield
                        S.op("dve", lambda: nc.vector.tensor_scalar(out=Es[:, 0, :], in0=W["Ge"][:], scalar1=negm[:, 0:1], scalar2=40.0, op0=ALU.add, op1=ALU.min),
                             reads=[k_("Ge"), k_("negm")], writes=[k_("Es0")])
                        yield
                        S.op("dve", lambda: nc.vector.tensor_scalar(out=Es[:, 1, :], in0=W["Gi"][:], scalar1=negm[:, 0:1], scalar2=40.0, op0=ALU.add, op1=ALU.min),
                             reads=[k_("Gi"), k_("negm")], writes=[k_("Es1")])
                        yield
                        S.op("dve", lambda: nc.vector.tensor_scalar(out=Es[:, 2, :], in0=W["Gi"][:], scalar1=-1.0, scalar2=mcol, op0=ALU.mult, op1=ALU.add),
                             reads=[k_("Gi")], writes=[k_("Es2")])
                        yield
                        S.op("dve", lambda: nc.vector.tensor_scalar(out=Es[:, 2, :], in0=Es[:, 2, :], scalar1=40.0, scalar2=None, op0=ALU.min),
                             reads=[k_("Es2")], writes=[k_("Es2")])
                        yield
                        S.op("act", lambda: nc.scalar.activation(out=Es[:, :, :], in_=Es[:, :, :], func=AF.Exp), reads=[k_("Es0"), k_("Es1"), k_("Es2")],
                             writes=[k_("Es0"), k_("Es1"), k_("Es2")])
                        yield
                        S.op("dve", lambda: nc.vector.tensor_tensor(out=W2["AR"][:, 0, :], in0=W["al"][:], in1=Es[:, 0, :], op=ALU.mult),
                             reads=[k_("al"), k_("Es0")], writes=[k2("AR")])
                        yield
                        S.op("pool", lambda: nc.gpsimd.tensor_tensor(out=W2["AR"][:, 1, :], in0=W["rr"][:], in1=Es[:, 1, :], op=ALU.mult),
                             reads=[k_("rr"), k_("Es1")], writes=[k2("AR")])
                        yield
                        S.op("dve", lambda: nc.vector.tensor_tensor(out=W2["BH"][:], in0=W["be"][:], in1=Es[:, 2, :], op=ALU.mult),
                             reads=[k_("be"), k_("Es2")], writes=[k2("BH")])
                        yield
                        S.op("pool", lambda: nc.gpsimd.tensor_tensor(out=W2["KH"][:], in0=W["kd"][:], in1=Es[:, 2, :], op=ALU.mult),
                             reads=[k_("kd"), k_("Es2")], writes=[k2("KH")])
                        yield
                    S.op("dve", lambda: nc.vector.tensor_tensor(out=W2["ART"][:, 0, :], in0=W["al"][:], in1=E[:, 0, :], op=ALU.mult),
                         reads=[k_("al"), k_("E0")], writes=[k2("ART")])
                    yield
                    S.op("pool", lambda: nc.gpsimd.tensor_tensor(out=W2["ART"][:, 1, :], in0=W["rr"][:], in1=E[:, 1, :], op=ALU.mult),
                         reads=[k_("rr"), k_("E1")], writes=[k2("ART")])
                    yield
                    S.op("dve", lambda: nc.vector.tensor_tensor(out=W["BT"][:], in0=W["be"][:], in1=E[:, 3, :], op=ALU.mult),
                         reads=[k_("be"), k_("E3")], writes=[k_("BT")])
                    yield
                    S.op("pool", lambda: nc.gpsimd.tensor_tensor(out=W["KT"][:], in0=W["kd"][:], in1=E[:, 3, :], op=ALU.mult),
                         reads=[k_("kd"), k_("E3")], writes=[k_("KT")])
                    yield
                    for ti, (src, srk, dst, dsk) in enumerate(((W["BT"][:], k_("BT"), W2["BTt"], k2("BTt")), (W["KT"][:], k_("KT"), W2["KTt"], k2("KTt")),
                                                               (vsrc_ap, vkeys, W2["VT"], k2("VT")))):
                        S.op("pe", lambda: nc.tensor.transpose(out=ptr[:, ti * 128:(ti + 1) * 128], in_=src, identity=CM("ident")),
                             reads=([srk] if isinstance(srk, str) else list(srk)) + ["cm_b"], writes=[ptk])
                        yield
                        e = "act" if di == 0 else "dve"
                        S.op(e, copy_op(e, dst[:], ptr[:, ti * 128:(ti + 1) * 128]), reads=[ptk], writes=[dsk])
                        yield

                def head_gen(dn, j, pr, dk, dv, c0, emit, sdec, par=0):
                    W = wk[dn]
                    W2 = wk2[dn][par]
                    k_ = lambda nm: f"{nm}{dn}"
                    k2 = lambda nm: f"{nm}{dn}p{par}"
                    kj = lambda nm: f"{nm}{dn}{j}"
                    fwd = dn == "f"
                    di = 0 if fwd else 1
                    ms_mi = cm2[:, 0:128] if fwd else cm2[:, 128:256]
                    ms_other = cm2[:, 128:192] if fwd else cm2[:, 0:64]
                    I64 = cm2[:, 256:320]
                    pc = pC[di * 2 + j]; pck = f"l1pC{di*2+j}"
                    ev = "dve" if j == 0 else "act"
                    N = W[f"N{j}"]; NTt = W[f"NT{j}"]; PTm = W[f"PT{j}"]; TT = W[f"TT{j}"]
                    ARB = W[f"ARB{j}"]; AKRK = W[f"AKRK{j}"]; Zb = W[f"Zb{j}"]; Ub = W[f"Ub{j}"]
                    ar2 = W2["AR"][pr, :, :].rearrange("p a n -> p (a n)")
                    pa = pc[0:64, :]
                    if sdec:
                        dmA, dm4, _gc = dm_aps(di, par)
                        dk_ = f"dmw{di}p{par}"
                        S.op("pe", lambda: nc.tensor.matmul(pa[:, 0:128], lhsT=W2["KH"][pr, :], rhs=ar2, start=True, stop=True),
                             reads=[k2("KH"), k2("AR")], writes=[pck])
                        yield
                        S.op("pe", lambda: nc.tensor.matmul(pa[:, 256:320], lhsT=W2["AR"][pr, 0, :], rhs=W2["KH"][pr, :], start=True, stop=True),
                             reads=[k2("KH"), k2("AR")], writes=[pck])
                        yield
                        S.op("dve", lambda: nc.vector.scalar_tensor_tensor(out=NTt[0][:], in0=pa[:, 0:64], scalar=-1.0, in1=dmA[:, 0:64], op0=ALU.mult, op1=ALU.mult),
                             reads=[pck, dk_], writes=[kj("NT0")])
                        yield
                        S.op("dve", lambda: nc.vector.tensor_tensor(out=AKRK[:, 0:64], in0=pa[:, 0:64], in1=dmA[:, 64:128], op=ALU.mult),
                             reads=[pck, dk_], writes=[kj("AKRK")])
                        yield
                        S.op("dve", lambda: nc.vector.scalar_tensor_tensor(out=ARB[:], in0=pa[:, 64:128], scalar=-1.0, in1=dmA[:, 128:192], op0=ALU.mult, op1=ALU.mult),
                             reads=[pck, dk_], writes=[kj("ARB")])
                        yield
                        S.op("dve", lambda: nc.vector.tensor_tensor(out=AKRK[:, 64:128], in0=pa[:, 64:128], in1=dmA[:, 192:256], op=ALU.mult),
                             reads=[pck, dk_], writes=[kj("AKRK")])
                        yield
                        S.op("dve", lambda: nc.vector.scalar_tensor_tensor(out=N[0][:], in0=pa[:, 256:320], scalar=-1.0, in1=dm4, op0=ALU.mult, op1=ALU.mult),
                             reads=[pck, dk_], writes=[kj("N0")])
                        yield
                    else:
                        S.op("pe", lambda: nc.tensor.matmul(pa[:, 0:128], lhsT=W2["BH"][pr, :], rhs=ar2, start=True, stop=True),
                             reads=[k2("BH"), k2("AR")], writes=[pck])
                        yield
                        S.op("pe", lambda: nc.tensor.matmul(pa[:, 128:256], lhsT=W2["KH"][pr, :], rhs=ar2, start=True, stop=True),
                             reads=[k2("KH"), k2("AR")], writes=[pck])
                        yield
                        S.op("pe", lambda: nc.tensor.matmul(pa[:, 256:320], lhsT=W2["AR"][pr, 0, :], rhs=W2["BH"][pr, :], start=True, stop=True),
                             reads=[k2("BH"), k2("AR")], writes=[pck])
                        yield
                        S.op("dve", lambda: nc.vector.tensor_tensor(out=NTt[0][:], in0=pa[:, 0:64], in1=ms_mi[:, 0:64], op=ALU.mult),
                             reads=[pck, "cm2"], writes=[kj("NT0")])
                        yield
                        S.op("dve", lambda: nc.vector.tensor_tensor(out=ARB[:], in0=pa[:, 64:128], in1=ms_mi[:, 64:128], op=ALU.mult),
                             reads=[pck, "cm2"], writes=[kj("ARB")])
                        yield
                        S.op("dve", lambda: nc.vector.tensor_tensor(out=AKRK[:], in0=pa[:, 128:256], in1=ms_mi, op=ALU.mult),
                             reads=[pck, "cm2"], writes=[kj("AKRK")])
                        yield
                        S.op("dve", lambda: nc.vector.tensor_tensor(out=N[0][:], in0=pa[:, 256:320], in1=ms_other, op=ALU.mult),
                             reads=[pck, "cm2"], writes=[kj("N0")])
                        yield
                    S.op("pool", lambda: nc.gpsimd.tensor_tensor(out=PTm[0][:], in0=NTt[0][:], in1=I64, op=ALU.add),
                         reads=[kj("NT0"), "cm2"], writes=[kj("PT0")])
                    yield
                    for lv in range(1, 6):
                        a_, b_ = (lv - 1) % 2, lv % 2
                        S.op("pe", lambda: nc.tensor.matmul(pa[:, 320:384], lhsT=NTt[a_][:], rhs=N[a_][:], start=True, stop=True),
                             reads=[kj(f"NT{a_}"), kj(f"N{a_}")], writes=[pck])
                        yield
                        if lv < 5:
                            S.op("pe", lambda: nc.tensor.matmul(pa[:, 384:448], lhsT=N[a_][:], rhs=NTt[a_][:], start=True, stop=True),
                                 reads=[kj(f"NT{a_}"), kj(f"N{a_}")], writes=[pck])
                            yield
                        S.op(ev, copy_op(ev, N[b_][:], pa[:, 320:384]), reads=[pck], writes=[kj(f"N{b_}")])
                        yield
                        if lv < 5:
                            S.op(ev, copy_op(ev, NTt[b_][:], pa[:, 384:448]), reads=[pck], writes=[kj(f"NT{b_}")])
                            yield
                        S.op("pe", lambda: nc.tensor.matmul(pa[:, 448:512], lhsT=I64, rhs=PTm[a_][:], start=True, stop=False),
                             reads=["cm2", kj(f"PT{a_}")], writes=[pck])
                        yield
                        S.op("pe", lambda: nc.tensor.matmul(pa[:, 448:512], lhsT=N[b_][:], rhs=PTm[a_][:], start=False, stop=True),
                             reads=[kj(f"N{b_}"), kj(f"PT{a_}")], writes=[pck])
                        yield
                        if lv < 5:
                            S.op(ev, copy_op(ev, PTm[b_][:], pa[:, 448:512]), reads=[pck], writes=[kj(f"PT{b_}")])
                        else:
                            S.op(ev, copy_op(ev, TT[:], pa[:, 448:512]), reads=[pck], writes=[kj("TT")])
                        yield
                    vt = W2["VT"][:, j * dv:(j + 1) * dv] if dv == 64 else W2["VT"][:, :]
                    cs = slice(j * 64, (j + 1) * 64) if dk == 64 else slice(0, 128)
                    split = (pr.start == 64)
                    psy = pin[0]; psyk = "l1pin0"
                    yo = di * 256
                    if split:
                        S.op("pe", lambda: nc.tensor.matmul(psy[0:64, yo:yo + dv], lhsT=W2["ART"][pr, 0, :], rhs=W["Hb"][pr, :dv], start=True, stop=True),
                             reads=[k2("ART"), kj("Hb")], writes=[psyk])
                        yield
                        S.op("pe", lambda: nc.tensor.matmul(pc[0:64, 0:dv], lhsT=AKRK[:, 0:64], rhs=vt, start=True, stop=True),
                             reads=[kj("AKRK"), k2("VT")], writes=[pck])
                        yield
                        S.op("act", lambda: nc.scalar.copy(out=W["ztmp"][:, :dv], in_=psy[0:64, yo:yo + dv]), reads=[psyk], writes=[k_("ztmp")])
                        yield
                        S.op("dve", lambda: nc.vector.tensor_tensor(out=Zb[:, :dv], in0=pc[0:64, 0:dv], in1=W["ztmp"][:, :dv], op=ALU.add),
                             reads=[pck, k_("ztmp")], writes=[kj("Zb")])
                        yield
                    else:
                        S.op("pe", lambda: nc.tensor.matmul(pc[0:64, 0:dv], lhsT=W2["ART"][pr, 0, :], rhs=W["Hb"][pr, :dv], start=True, stop=False),
                             reads=[k2("ART"), kj("Hb")], writes=[pck])
                        yield
                        S.op("pe", lambda: nc.tensor.matmul(pc[0:64, 0:dv], lhsT=AKRK[:, 0:64], rhs=vt, start=False, stop=True),
                             reads=[kj("AKRK"), k2("VT")], writes=[pck])
                        yield
                        S.op(ev, copy_op(ev, Zb[:, :dv], pc[0:64, 0:dv]), reads=[pck], writes=[kj("Zb")])
                        yield
                    S.op("pe", lambda: nc.tensor.matmul(pc[0:64, 128:128 + dv], lhsT=TT[:], rhs=Zb[:, :dv], start=True, stop=True),
                         reads=[kj("TT"), kj("Zb")], writes=[pck])
                    yield
                    S.op(ev, copy_op(ev, Ub[:, :dv], pc[0:64, 128:128 + dv]), reads=[pck], writes=[kj("Ub")])
                    yield
                    if emit:
                        t_lat = c0 - NCTX
                        yk_ = f"Yacc{t_lat//64}.{pr.start}"
                        if split:
                            S.op("pe", lambda: nc.tensor.matmul(psy[pr, yo + 64:yo + 128], lhsT=W["Hb"][pr, :dv], rhs=W2["ART"][pr, 1, :], start=True, stop=True),
                                 reads=[k2("ART"), kj("Hb")], writes=[psyk])
                            yield
                            S.op("dve", lambda: nc.vector.tensor_tensor(out=K.yacc[pr, t_lat:t_lat + 64], in0=psy[pr, yo + 64:yo + 128],
                                                                        in1=K.yacc[pr, t_lat:t_lat + 64], op=ALU.add),
                                 reads=[psyk, yk_], writes=[yk_])
                            yield
                        else:
                            S.op("pe", lambda: nc.tensor.matmul(pc[pr, 256:320], lhsT=W["Hb"][pr, :dv], rhs=W2["ART"][pr, 1, :], start=True, stop=False),
                                 reads=[k2("ART"), kj("Hb")], writes=[pck])
                            yield
                        S.op("pe", lambda: nc.tensor.matmul(pc[pr, 256:320], lhsT=Ub[:, :dv], rhs=ARB[:], start=split, stop=False),
                             reads=[kj("Ub"), kj("ARB")], writes=[pck])
                        yield
                        S.op("pe", lambda: nc.tensor.matmul(pc[pr, 256:320], lhsT=vt, rhs=AKRK[:, 64:128], start=False, stop=True),
                             reads=[kj("AKRK"), k2("VT")], writes=[pck])
                        yield
                        S.op("dve", lambda: nc.vector.tensor_tensor(out=K.yacc[pr, t_lat:t_lat + 64], in0=pc[pr, 256:320],
                                                                    in1=K.yacc[pr, t_lat:t_lat + 64], op=ALU.add),
                             reads=[pck, yk_], writes=[yk_])
                        yield
                    S.op("pe", lambda: nc.tensor.matmul(pc[pr, 320:320 + dv], lhsT=W2["BTt"][:, cs], rhs=Ub[:, :dv], start=True, stop=False),
                         reads=[k2("BTt"), kj("Ub")], writes=[pck])
                    yield
                    S.op("pe", lambda: nc.tensor.matmul(pc[pr, 320:320 + dv], lhsT=W2["KTt"][:, cs], rhs=vt, start=False, stop=True),
                         reads=[k2("KTt"), k2("VT")], writes=[pck])
                    yield
                    S.op("dve", lambda: nc.vector.scalar_tensor_tensor(out=W["Hs"][pr, :dv], in0=W["Hs"][pr, :dv], scalar=W2["Et"][pr, 0:1],
                                                                       in1=pc[pr, 320:320 + dv], op0=ALU.mult, op1=ALU.add),
                         reads=[pck, kj("Hs"), k2("Et")], writes=[kj("Hs")])
                    yield
                    S.op("act", lambda: nc.scalar.copy(out=W["Hb"][pr, :dv], in_=W["Hs"][pr, :dv]), reads=[kj("Hs")], writes=[kj("Hb")])
                    yield

                def round_robin(gens):
                    gens = list(gens)
                    while gens:
                        for g in list(gens):
                            try:
                                next(g)
                            except StopIteration:
                                gens.remove(g)

                def reset_state(insts):
                    for dn in ("f", "b"):
                        S.op("pool", lambda: nc.gpsimd.memset(wk[dn]["Hs"][:], 0.0), writes=[f"Hs{dn}{j}" for j in range(2)])
                        S.op("pool", lambda: nc.gpsimd.memset(wk[dn]["Hb"][:], 0.0), writes=[f"Hb{dn}{j}" for j in range(2)])
                    S.op("pool", lambda: nc.gpsimd.memset(K.yacc, 0.0), writes=[f"Yacc{i}.{p}" for i in range(32) for p in (0, 64)])

                TW = xb(10); AL = xb(11); SG = xb(12)
                if DBG['lora']:
                    load_cols([(1536, 384)])
                    proj_taps(0, 128, shift_taps(12), AF.Tanh, lambda s0, n: TW[:, s0:s0 + n], ["TW"])
                    proj_taps(128, 128, shift_taps(13), AF.Identity, lambda s0, n: AL[:, s0:s0 + n], ["AL"])
                    proj_taps(256, 128, shift_taps(14), AF.Sigmoid, lambda s0, n: SG[:, s0:s0 + n], ["SG"])
                lw2 = sb(L, "lw2", [128, 3, 512], BF16)
                K.lw2flat = lw2[:].rearrange("p a n -> p (a n)")
                for i, src in enumerate((cw2_d, ca2_d, cg2_d)):
                    wload(stg, "l1stg", lw2[:, i, :], ["lw2"], src, [128, 512], "sp")
                sq = sb(L, "l1sq", [128, 256], BF16)
                t32 = [sb(L, f"l1t{i}", [128, 256]) for i in range(3)]
                TCH = [(i * 256, 256) for i in range(9)]

                for gi in range(DBG['ng']):
                    rA, kA, vA, kkA, afA, abA = (xb(i) for i in range(6))
                    lwF = xf(3); lwB = xf(4)
                    gA = xb(13)
                    load_cols([(gi * 128, 128), (512 + gi * 128, 128), (1024 + gi * 128, 128)])
                    proj_taps(0, 128, shift_taps(gi), AF.Identity, lambda s0, n: rA[:, s0:s0 + n], ["rA"])
                    proj_taps(128, 128, shift_taps(4 + gi), AF.Identity, lambda s0, n: kA[:, s0:s0 + n], ["kA"])
                    proj_taps(256, 128, shift_taps(8 + gi), AF.Identity, lambda s0, n: vA[:, s0:s0 + n], ["vA"])
                    for (t0, n) in (TCH if DBG['prep'] else []):
                        S.op("dve", lambda: nc.vector.tensor_scalar(out=t32[0][:, :n], in0=kA[:, t0:t0 + n], scalar1=v1[:, 45 + gi:46 + gi], scalar2=None,
                                                                    op0=ALU.mult), reads=["kA", "v1"], writes=["l1t0"])
                        S.op("act", lambda: nc.scalar.activation(out=sq[:, :n], in_=t32[0][:, :n], func=AF.Square), reads=["l1t0"], writes=["l1sq"])
                        S.op("pe", lambda: nc.tensor.matmul(pin[0][:, :n], lhsT=CM("onesA"), rhs=sq[:, :n], start=True, stop=True),
                             reads=["l1sq", "cm_b"], writes=["l1pin0"])
                        S.op("act", lambda: nc.scalar.activation(out=t32[1][:, :n], in_=pin[0][:, :n], func=AF.Sqrt, scale=64.0, bias=eps_t[:, 0:1]),
                             reads=["l1pin0", "eps"], writes=["l1t1"])
                        S.op("dve", lambda: nc.vector.reciprocal(out=t32[1][:, :n], in_=t32[1][:, :n]), reads=["l1t1"], writes=["l1t1"])
                        S.op("dve", lambda: nc.vector.tensor_tensor(out=kkA[:, t0:t0 + n], in0=t32[0][:, :n], in1=t32[1][:, :n], op=ALU.mult),
                             reads=["l1t0", "l1t1"], writes=["kkA"])
                        for d_, (lwX, aX) in enumerate(((lwF, afA), (lwB, abA))):
                            prd = slice(d_ * 64, (d_ + 1) * 64)
                            S.op("pe", lambda: nc.tensor.matmul(pin[1][:, :n], lhsT=lw2[prd, 0, gi * 128:(gi + 1) * 128], rhs=TW[prd, t0:t0 + n],
                                                                start=True, stop=True), reads=["lw2", "TW"], writes=["l1pin1"])
                            S.op("act", lambda: nc.scalar.activation(out=t32[2][:, :n], in_=pin[1][:, :n], func=AF.Sigmoid,
                                                                     bias=v1[:, 49 + d_ * 4 + gi:50 + d_ * 4 + gi]),
                                 reads=["l1pin1", "v1"], writes=["l1t2"])
                            S.op("dve", lambda: nc.vector.tensor_scalar(out=lwX[:, t0:t0 + n], in0=t32[2][:, :n], scalar1=-0.6065306597126334,
                                                                        scalar2=None, op0=ALU.mult), reads=["l1t2"], writes=["lwX"])
                            S.op("pe", lambda: nc.tensor.matmul(pin[1][:, :n], lhsT=lw2[prd, 1, gi * 128:(gi + 1) * 128], rhs=AL[prd, t0:t0 + n],
                                                                start=True, stop=True), reads=["lw2", "AL"], writes=["l1pin1"])
                            S.op("act", lambda: nc.scalar.activation(out=aX[:, t0:t0 + n], in_=pin[1][:, :n], func=AF.Sigmoid,
                                                                     bias=v1[:, 57 + d_ * 4 + gi:58 + d_ * 4 + gi]),
                                 reads=["l1pin1", "v1"], writes=["aX"])
                        S.op("pe", lambda: nc.tensor.matmul(pin[1][:, :n], lhsT=lw2[:, 2, gi * 128:(gi + 1) * 128], rhs=SG[:, t0:t0 + n],
                                                            start=True, stop=True), reads=["lw2", "SG"], writes=["l1pin1"])
                        S.op("act", lambda: nc.scalar.copy(out=gA[:, t0:t0 + n], in_=pin[1][:, :n]), reads=["l1pin1"], writes=["gA"])
                    insts = [(slice(0, 64), 64, 64), (slice(64, 128), 64, 64)]
                    K.yacc = Yall[:, gi, :]
                    reset_state(insts)
                    def rw_pre(dn, aX, lwX, csl, gi_):
                        W = wk[dn]
                        return [
                            lambda: S.op("dve", lambda: nc.vector.tensor_scalar(out=W["al"][:], in0=kkA[:, csl], scalar1=-1.0, scalar2=None, op0=ALU.mult),
                                         reads=["kkA"], writes=[f"al{dn}"]),
                            lambda: S.op("pool", lambda: nc.gpsimd.tensor_tensor(out=W["be"][:], in0=kkA[:, csl], in1=aX[:, csl], op=ALU.mult),
                                         reads=["kkA", "aX"], writes=[f"be{dn}"]),
                            lambda: S.op("dve", lambda: nc.vector.tensor_scalar(out=W["kd"][:], in0=aX[:, csl], scalar1=-1.0, scalar2=v1[:, 65 + gi_:66 + gi_],
                                                                                op0=ALU.add, op1=ALU.mult), reads=["aX", "v1"], writes=[f"kd{dn}"]),
                            lambda: S.op("dve", lambda: nc.vector.scalar_tensor_tensor(out=W["kd"][:], in0=W["kd"][:], scalar=1.0, in1=kA[:, csl],
                                                                                       op0=ALU.add, op1=ALU.mult), reads=[f"kd{dn}", "kA"], writes=[f"kd{dn}"]),
                            lambda: S.op("pool", lambda: nc.gpsimd.tensor_copy(out=W["rr"][:], in_=rA[:, csl]), reads=["rA"], writes=[f"rr{dn}"]),
                            lambda: S.op("pool", lambda: nc.gpsimd.tensor_copy(out=W["lw"][:], in_=lwX[:, csl]), reads=["lwX"], writes=[f"lw{dn}"]),
                        ]
                    prev_h = []
                    for step in range(DBG['nstep'] + 1):
                        pg = []; hg = []
                        if step < DBG['nstep']:
                            for dn, order, aX, lwX in (("f", fwd_order, afA, lwF), ("b", bwd_order, abA, lwB)):
                                ch = order[step]; c0 = ch * 64
                                csl = slice(c0, c0 + 64)
                                pg.append(prefix_gen(dn, rw_pre(dn, aX, lwX, csl, gi), vA[:, csl], ["vA"], False, step % 2))
                                for j, (pr, dk, dv) in enumerate(insts):
                                    hg.append(head_gen(dn, j, pr, dk, dv, c0, ch >= 4, False, step % 2))
                        round_robin(prev_h + pg)
                        prev_h = hg
                    for qi in range(8 if DBG['rwout'] else 0):
                        t0 = NCTX + qi * 256; n = 256; y0 = qi * 256
                        yk = [f"Yacc{i}.{p}" for i in range(y0 // 64, y0 // 64 + 4) for p in (0, 64)]
                        S.op("pe", lambda: nc.tensor.matmul(pin[0][:, :n], lhsT=CM("onesA"), rhs=K.yacc[:, y0:y0 + n], start=True, stop=True),
                             reads=yk + ["cm_b"], writes=["l1pin0"])
                        S.op("dve", lambda: nc.vector.tensor_tensor(out=t32[0][:, :n], in0=K.yacc[:, y0:y0 + n], in1=pin[0][:, :n], op=ALU.subtract),
                             reads=yk + ["l1pin0"], writes=["l1t0"])
                        S.op("act", lambda: nc.scalar.activation(out=sq[:, :n], in_=t32[0][:, :n], func=AF.Square), reads=["l1t0"], writes=["l1sq"])
                        S.op("pe", lambda: nc.tensor.matmul(pin[0][:, :n], lhsT=CM("onesA"), rhs=sq[:, :n], start=True, stop=True),
                             reads=["l1sq", "cm_b"], writes=["l1pin0"])
                        S.op("act", lambda: nc.scalar.activation(out=t32[1][:, :n], in_=pin[0][:, :n], func=AF.Sqrt, bias=gneps[:, 0:1]),
                             reads=["l1pin0", "gneps"], writes=["l1t1"])
                        S.op("dve", lambda: nc.vector.reciprocal(out=t32[1][:, :n], in_=t32[1][:, :n]), reads=["l1t1"], writes=["l1t1"])
                        S.op("dve", lambda: nc.vector.tensor_tensor(out=t32[0][:, :n], in0=t32[0][:, :n], in1=t32[1][:, :n], op=ALU.mult),
                             reads=["l1t0", "l1t1"], writes=["l1t0"])
                        S.op("act", lambda: nc.scalar.activation(out=t32[0][:, :n], in_=t32[0][:, :n], func=AF.Identity,
                                                                 scale=v1[:, 69 + gi:70 + gi], bias=v1[:, 73 + gi:74 + gi]),
                             reads=["l1t0", "v1"], writes=["l1t0"])
                        S.op("dve", lambda: nc.vector.tensor_tensor(out=t32[1][:, :n], in0=afA[:, t0:t0 + n], in1=abA[:, t0:t0 + n], op=ALU.add),
                             reads=["aX"], writes=["l1t1"])
                        S.op("dve", lambda: nc.vector.tensor_scalar(out=t32[1][:, :n], in0=t32[1][:, :n], scalar1=-2.0, scalar2=v1[:, 65 + gi:66 + gi],
                                                                    op0=ALU.add, op1=ALU.mult), reads=["l1t1", "v1"], writes=["l1t1"])
                        S.op("dve", lambda: nc.vector.scalar_tensor_tensor(out=t32[1][:, :n], in0=t32[1][:, :n], scalar=2.0, in1=kA[:, t0:t0 + n],
                                                                           op0=ALU.add, op1=ALU.mult), reads=["l1t1", "kA"], writes=["l1t1"])
                        S.op("dve", lambda: nc.vector.scalar_tensor_tensor(out=sq[:, :n], in0=t32[1][:, :n], scalar=v1[:, 77 + gi:78 + gi],
                                                                           in1=rA[:, t0:t0 + n], op0=ALU.mult, op1=ALU.mult),
                             reads=["l1t1", "rA", "v1"], writes=["l1sq"])
                        S.op("pe", lambda: nc.tensor.matmul(pin[1][:, :n], lhsT=CM("onesA"), rhs=sq[:, :n], start=True, stop=True),
                             reads=["l1sq", "cm_b"], writes=["l1pin1"])
                        S.op("dve", lambda: nc.vector.scalar_tensor_tensor(out=t32[2][:, :n], in0=pin[1][:, :n], scalar=64.0, in1=vA[:, t0:t0 + n],
                                                                           op0=ALU.mult, op1=ALU.mult), reads=["l1pin1", "vA"], writes=["l1t2"])
                        S.op("pool", lambda: nc.gpsimd.tensor_tensor(out=t32[0][:, :n], in0=t32[0][:, :n], in1=t32[2][:, :n], op=ALU.add),
                             reads=["l1t0", "l1t2"], writes=["l1t0"])
                        S.op("dve", lambda: nc.vector.tensor_tensor(out=Yall[:, gi, y0:y0 + n], in0=t32[0][:, :n], in1=gA[:, t0:t0 + n], op=ALU.mult),
                             reads=["l1t0", "gA"], writes=["Yall"])
                    S.barrier()

                sm = XF[0:16, 7 * T:8 * T]
                smb = sb(L, "smb", [16, 256])
                smg = sb(L, "smg", [16, 256])
                gsm = sb(L, "gsmp", [16, 2])
                sel = sb(L, "sel", [16, 16 * 128])
                S.dma("sp", gsm[:], gsm_d, writes=["gsm"])
                S.dma("sp", sel[:], sel_d, writes=["sel"])
                S.op("act", lambda: nc.scalar.activation(out=gsm[:, 0:1], in_=gsm[:, 0:1], func=AF.Exp), reads=["gsm"], writes=["gsm"])
                S.op("dve", lambda: nc.vector.tensor_scalar(out=gsm[:, 0:1], in0=gsm[:, 0:1], scalar1=-1.0, scalar2=None, op0=ALU.mult),
                     reads=["gsm"], writes=["gsm"])
                load_cols([(INC + 2048, 16)])
                proj_taps(0, 16, [(0, None)], AF.Identity, lambda s0, n: sm[:, s0:s0 + n], ["sm"])
                t32q = xb(12); t32k = xb(13)
                one_t = sb(L, "one_t", [128, 1])
                S.op("pool", lambda: nc.gpsimd.memset(one_t[:], 1.0), writes=["one_t"])

                for hd in range(DBG['gdn']):
                    qA, kA, vA, zA = (xb(i) for i in range(4))
                    bmF = xf(2); bmB = xf(3); gmF = xf(4); gmB = xf(5)
                    load_cols([(INC + hd * 128, 128), (INC + 512 + hd * 128, 128), (INC + 1024 + hd * 128, 128), (INC + 1536 + hd * 128, 128)])

                    def ctaps(fc):
                        return [(0, v1[:, 81 + fc * 5 + 2:81 + fc * 5 + 3])] + [(o, v1[:, 81 + fc * 5 + 2 + o:81 + fc * 5 + 3 + o]) for o in (-2, -1, 1, 2)]
                    proj_taps(0, 128, ctaps(hd), AF.Silu, lambda s0, n: t32q[:, s0:s0 + n], ["t32q"])
                    proj_taps(128, 128, ctaps(4 + hd), AF.Silu, lambda s0, n: t32k[:, s0:s0 + n], ["t32k"])
                    proj_taps(256, 128, ctaps(8 + hd), AF.Silu, lambda s0, n: vA[:, s0:s0 + n], ["vA"])
                    proj_taps(384, 128, [(0, None)], AF.Silu, lambda s0, n: zA[:, s0:s0 + n], ["zA"])
                    for (t0, n) in (TCH if DBG['gprep'] >= 2 else []):
                        for (srcb, dstb, scl, dk_) in ((t32q, qA, 128.0 ** -0.5, "qA"), (t32k, kA, 1.0, "kA")):
                            S.op("act", lambda: nc.scalar.activation(out=sq[:, :n], in_=srcb[:, t0:t0 + n], func=AF.Square), reads=["t32q", "t32k"], writes=["l1sq"])
                            S.op("pe", lambda: nc.tensor.matmul(pin[0][:, :n], lhsT=CM("ones1024"), rhs=sq[:, :n], start=True, stop=True),
                                 reads=["l1sq", "cm_b"], writes=["l1pin0"])
                            S.op("act", lambda: nc.scalar.activation(out=t32[1][:, :n], in_=pin[0][:, :n], func=AF.Sqrt, scale=1024.0, bias=eps_t[:, 0:1]),
                                 reads=["l1pin0", "eps"], writes=["l1t1"])
                            S.op("dve", lambda: nc.vector.reciprocal(out=t32[1][:, :n], in_=t32[1][:, :n]), reads=["l1t1"], writes=["l1t1"])
                            S.op("dve", lambda: nc.vector.scalar_tensor_tensor(out=dstb[:, t0:t0 + n], in0=srcb[:, t0:t0 + n], scalar=float(scl),
                                                                               in1=t32[1][:, :n], op0=ALU.mult, op1=ALU.mult),
                                 reads=["t32q", "t32k", "l1t1"], writes=[dk_])
                        S.op("act", lambda: nc.scalar.activation(out=smb[:, :n], in_=sm[:, t0:t0 + n], func=AF.Sigmoid), reads=["sm"], writes=["smb"])
                        S.op("act", lambda: nc.scalar.activation(out=smg[:, :n], in_=sm[:, t0:t0 + n], func=AF.Exp, bias=gsm[:, 1:2]),
                             reads=["sm", "gsm"], writes=["smg"])
                        S.op("act", lambda: nc.scalar.activation(out=smg[:, :n], in_=smg[:, :n], func=AF.Ln, bias=one_t[0:16, 0:1]),
                             reads=["smg", "one_t"], writes=["smg"])
                        S.op("dve", lambda: nc.vector.tensor_scalar(out=smg[:, :n], in0=smg[:, :n], scalar1=gsm[:, 0:1], scalar2=None, op0=ALU.mult),
                             reads=["smg", "gsm"], writes=["smg"])
                        for (row, dst, srcm, dk_) in (((hd, bmF, smb, "bmF"), (4 + hd, bmB, smb, "bmB"), (8 + hd, gmF, smg, "gmF"), (12 + hd, gmB, smg, "gmB")) if DBG['gprep'] >= 3 else []):
                            S.op("pe", lambda: nc.tensor.matmul(pin[1][:, :n], lhsT=sel[:, row * 128:(row + 1) * 128], rhs=srcm[:, :n],
                                                                start=True, stop=True), reads=["sel", "smb", "smg"], writes=["l1pin1"])
                            e = evac_eng()
                            S.op(e, copy_op(e, dst[:, t0:t0 + n], pin[1][:, :n]), reads=["l1pin1"], writes=[dk_])
                    insts = [(slice(0, 128), 128, 128)]
                    K.yacc = Yall[:, 4 + hd, :]
                    reset_state(insts)
                    def gd_pre(dn, bmX, gmX, csl):
                        W = wk[dn]
                        return [
                            lambda: S.op("pool", lambda: nc.gpsimd.tensor_copy(out=W["al"][:], in_=kA[:, csl]), reads=["kA"], writes=[f"al{dn}"]),
                            lambda: S.op("dve", lambda: nc.vector.tensor_tensor(out=W["kd"][:], in0=kA[:, csl], in1=bmX[:, csl], op=ALU.mult),
                                         reads=["kA", "bmF", "bmB"], writes=[f"kd{dn}"]),
                            lambda: S.op("pool", lambda: nc.gpsimd.tensor_copy(out=W["lw"][:], in_=gmX[:, csl]), reads=["gmF", "gmB"], writes=[f"lw{dn}"]),
                            lambda: S.op("act", lambda: nc.scalar.activation(out=W["be"][:], in_=gmX[:, csl], func=AF.Exp), reads=["gmF", "gmB"], writes=[f"be{dn}"]),
                            lambda: S.op("dve", lambda: nc.vector.scalar_tensor_tensor(out=W["be"][:], in0=W["be"][:], scalar=-1.0, in1=W["kd"][:],
                                                                                       op0=ALU.mult, op1=ALU.mult), reads=[f"be{dn}", f"kd{dn}"], writes=[f"be{dn}"]),
                            lambda: S.op("pool", lambda: nc.gpsimd.tensor_copy(out=W["rr"][:], in_=qA[:, csl]), reads=["qA"], writes=[f"rr{dn}"]),
                        ]
                    prev_h = []
                    for step in range(DBG['nstep'] + 1):
                        pg = []; hg = []
                        for dn, order, bmX, gmX in ((("f", fwd_order, bmF, gmF), ("b", bwd_order, bmB, gmB)) if step < DBG['nstep'] else ()):
                            ch = order[step]; c0 = ch * 64
                            csl = slice(c0, c0 + 64)
                            pg.append(prefix_gen(dn, gd_pre(dn, bmX, gmX, csl), vA[:, csl], ["vA"], True, step % 2))
                            for j, (pr, dk, dv) in enumerate(insts):
                                hg.append(head_gen(dn, j, pr, dk, dv, c0, ch >= 4, True, step % 2))
                        round_robin(prev_h + pg)
                        prev_h = hg
                    for qi in range(8):
                        t0 = NCTX + qi * 256; n = 256; y0 = qi * 256
                        yk = [f"Yacc{i}.{p}" for i in range(y0 // 64, y0 // 64 + 4) for p in (0, 64)]
                        S.op("act", lambda: nc.scalar.activation(out=sq[:, :n], in_=K.yacc[:, y0:y0 + n], func=AF.Square), reads=yk, writes=["l1sq"])
                        S.op("pe", lambda: nc.tensor.matmul(pin[0][:, :n], lhsT=CM("ones1024"), rhs=sq[:, :n], start=True, stop=True),
                             reads=["l1sq", "cm_b"], writes=["l1pin0"])
                        S.op("act", lambda: nc.scalar.activation(out=t32[1][:, :n], in_=pin[0][:, :n], func=AF.Sqrt, scale=8.0, bias=eps_t[:, 0:1]),
                             reads=["l1pin0", "eps"], writes=["l1t1"])
                        S.op("dve", lambda: nc.vector.reciprocal(out=t32[1][:, :n], in_=t32[1][:, :n]), reads=["l1t1"], writes=["l1t1"])
                        S.op("dve", lambda: nc.vector.scalar_tensor_tensor(out=t32[0][:, :n], in0=K.yacc[:, y0:y0 + n], scalar=v1[:, 141:142],
                                                                           in1=t32[1][:, :n], op0=ALU.mult, op1=ALU.mult),
                             reads=yk + ["l1t1", "v1"], writes=["l1t0"])
                        S.op("dve", lambda: nc.vector.tensor_tensor(out=Yall[:, 4 + hd, y0:y0 + n], in0=t32[0][:, :n], in1=zA[:, t0:t0 + n], op=ALU.mult),
                             reads=["l1t0", "zA"], writes=["Yall"])
                    S.barrier()
                LS.close()
                L = L0
                if dbg == "y1":
                    dump_fm(Yall, 8, BF16, 16)
                stg2 = sb(L, "l1stg2", [128, 2048])
                wo = sb(L, "l1wo", [128, 8, D], BF16)
                pin = [ps(L, f"l1pinb{i}", [128, 512]) for i in range(2)]
                cov = cdout_d.rearrange("(kc p) n -> p kc n", p=128)
                for i in range(4):
                    wload(stg2, "l1stg2", wo[:, 2 * i:2 * i + 2, :], ["l1wo"], cov[:, 2 * i:2 * i + 2, :], [128, 2, D], "sp")
                for c in range(8):
                    S.dma("sp" if c % 2 == 0 else "pool", X[:, c, :], xpark_d[:, c * T:(c + 1) * T], writes=[f"X{t}" for t in range(NT)])
                S.barrier()
                it = 0
                for qi in range(4):
                    t0 = NCTX + qi * 512; n = 512; y0 = qi * 512
                    for m in range(8):
                        p_ = pin[it % 2]; pk = f"l1pinb{it%2}"; it += 1
                        for i in range(8):
                            S.op("pe", lambda: nc.tensor.matmul(p_[:, :n], lhsT=wo[:, i, m * 128:(m + 1) * 128], rhs=Yall[:, i, y0:y0 + n],
                                                                start=(i == 0), stop=(i == 7)), reads=["l1wo", "Yall"], writes=[pk])
                        S.op("dve", lambda: nc.vector.scalar_tensor_tensor(out=X[:, m, t0:t0 + n], in0=p_[:, :n], scalar=mod(l, 2, m, 0),
                                                                           in1=X[:, m, t0:t0 + n], op0=ALU.mult, op1=ALU.add),
                             reads=[pk, "modv"] + xkeys(t0, n), writes=xkeys(t0, n))
            S.barrier()

        K.stop = False
        if dbg in ("load", "mods"):
            write_out()
            S.finish()
            return nc, S
        layer0()
        if dbg in ("qa", "oa"):
            S.finish()
            return nc, S
        if dbg != "noffn":
            ffn(0, nlayers == 1)
        if nlayers == 2:
            layer1()
            if dbg not in ("noffn1", "y1"):
                ffn(1, True)
        write_out()
        S.finish()
    return nc, S


def make_in_maps(inputs, cores):
    consts, cnames = _consts()
    f = lambda k: np.asarray(inputs[k], np.float32)
    c_ctx = f("c_ctx")
    shared = {
        "ada_w": np.ascontiguousarray(f("ada_w")),
        "ada_b_fm": np.stack([_fm(f("ada_b")[l]) for l in range(2)], 0),
        "ffn_w_up": np.ascontiguousarray(f("ffn_w_up")),
        "ffn_w_down": np.ascontiguousarray(f("ffn_w_down")),
        "ab_w_in": np.ascontiguousarray(f("ab_w_in")[0]),
        "ab_w_out": np.ascontiguousarray(f("ab_w_out")[0]),
        "b_w_uq": np.ascontiguousarray(f("b_w_uq")[0]),
        "b_w_uk": np.ascontiguousarray(f("b_w_uk")[0]),
        "b_w_uv": np.ascontiguousarray(f("b_w_uv")[0]),
        "cmats": consts["cmats"], "cosA": consts["cosA"], "sinA": consts["sinA"],
        "cosB": consts["cosB"], "sinB": consts["sinB"],
    }
    fc = np.zeros((2, 128, 44, 4), np.float32)
    for l in range(2):
        for j in range(3):
            fc[l, :, :, j] = _fm(f("ffn_conv_w")[l, j])
        fc[l, :, :, 3] = _fm(f("ffn_conv_b")[l])
    shared["ffn_conv_fm"] = fc
    v = np.zeros((128, 16), np.float32)
    v[:, 0] = np.tile(f("a_q_norm")[0], 2)
    v[:, 1] = np.tile(f("a_k_norm")[0], 2)
    v[:, 2:4] = _fm(f("b_cq_norm")[0])
    v[:, 4:6] = _fm(f("b_ckv_norm")[0])
    v[64:96, 6] = f("b_kr_norm")[0]
    v[:64, 7] = f("b_kn_norm")[0]
    v[:64, 8] = f("b_qn_norm")[0]
    v[64:96, 8] = f("b_qr_norm")[0]
    shared["vecs0"] = v
    shared["sink_b"] = np.ascontiguousarray(np.broadcast_to(f("a_sink")[0][None, :], (64, 8)))
    shared["cd_w_in"] = np.ascontiguousarray(f("cd_w_in")[0])
    shared["cd_w_out"] = np.ascontiguousarray(f("cd_w_out")[0])
    shared["c_w2r"] = np.ascontiguousarray(f("c_w2")[0].reshape(128, 512))
    shared["c_a2r"] = np.ascontiguousarray(f("c_a2")[0].reshape(128, 512))
    shared["c_g2"] = np.ascontiguousarray(f("c_g2")[0])
    v1 = np.zeros((128, 160), np.float32)
    v1[:, 0:15] = _fm(f("c_mu_prev")[0]); v1[:, 15:30] = _fm(f("c_mu_next")[0])
    v1[:, 45:49] = _fm(f("c_k_k")[0])
    for d_ in range(2):
        v1[:, 49 + d_ * 4:53 + d_ * 4] = _fm(f("c_w0")[0, d_])
        v1[:, 57 + d_ * 4:61 + d_ * 4] = _fm(f("c_a0")[0, d_])
    v1[:, 65:69] = _fm(f("c_k_a")[0]); v1[:, 69:73] = _fm(f("c_ln_w")[0]); v1[:, 73:77] = _fm(f("c_ln_b")[0])
    v1[:, 77:81] = _fm(f("c_r_k")[0].reshape(-1))
    dcw = f("d_conv_w")[0]
    for fc in range(12):
        for j in range(5):
            v1[:, 81 + fc * 5 + j] = dcw[j, fc * 128:(fc + 1) * 128]
    v1[:, 141] = f("d_o_norm")[0]
    shared["vecs1"] = v1
    gsm = np.zeros((16, 2), np.float32)
    for d_ in range(2):
        gsm[8 + 4 * d_:12 + 4 * d_, 0] = f("d_A_log")[0, d_]
        gsm[8 + 4 * d_:12 + 4 * d_, 1] = f("d_dt_bias")[0, d_]
    shared["gsm"] = gsm
    a_ = np.arange(64)[:, None]; b_ = np.arange(64)[None, :]
    shared["cm2"] = np.concatenate([(a_ < b_), (a_ <= b_), (a_ > b_), (a_ >= b_), (a_ == b_)], 1).astype(np.float32)
    sel = np.zeros((16, 16 * 128), np.float32)
    for q in range(16):
        sel[q, q * 128:(q + 1) * 128] = 1.0
    shared["sel16"] = sel
    maps = []
    for b in cores:
        m = dict(shared)
        m["x"] = np.ascontiguousarray(f("x")[b])
        m["ctx"] = np.ascontiguousarray(f("ctx")[b])
        m["cv"] = np.ascontiguousarray(np.stack([_fm(f("c")[b]), _fm(c_ctx)], -1))
        maps.append(m)
    return maps


def kernel(**inputs):
    nc, S = build_program()
    maps = make_in_maps(inputs, list(range(8)))
    res = run_bass_kernel_spmd(nc, maps, core_ids=list(range(8)))
    return np.stack([r["out"] for r in res.results], 0).astype(np.float32)
```

```python
import numpy as np
from contextlib import ExitStack
import concourse.bass as bass
import concourse.mybir as mybir
from concourse.bass_utils import run_bass_kernel_spmd

F32 = mybir.dt.float32
BF16 = mybir.dt.bfloat16
AF = mybir.ActivationFunctionType
ALU = mybir.AluOpType

D = 1024
T = 2304
NCTX = 256
NT = 18
EPS = 1e-6
CHUNKS = [(0, 256), (256, 512), (768, 512), (1280, 512), (1792, 512)]
DFF = 2816
DBG = {'lora': 1, 'ng': 4, 'prep': 1, 'nstep': 36, 'rwout': 1, 'gdn': 4, 'gprep': 3, 'cs': 5, 'invbf': 0, 'invlv': 5, 'invev': 5, 'invp': 1, 'gseq': 0, 'gpool': 1}


class Sched:
    def __init__(self, nc, stack, n_dma_sems=12):
        self.nc = nc
        self.engs = {"pe": nc.tensor, "dve": nc.vector, "act": nc.scalar, "pool": nc.gpsimd, "sp": nc.sync}
        self.sem = {k: stack.enter_context(nc.semaphore("s_" + k)) for k in self.engs}
        self.cnt = {k: 0 for k in self.engs}
        self.dsem = [stack.enter_context(nc.semaphore(f"dq{i}")) for i in range(n_dma_sems)]
        self.dcnt = [0] * n_dma_sems
        self.dnext = 0
        self.seen = {k: {} for k in self.engs}
        self.lastw = {}
        self.readers = {}
        self.ninst = 0

    def _semobj(self, sk):
        return self.sem[sk] if isinstance(sk, str) else self.dsem[sk]

    def _wait(self, e, sk, val):
        if val <= 0 or self.seen[e].get(sk, 0) >= val:
            return
        self.engs[e].wait_ge(self._semobj(sk), val)
        self.seen[e][sk] = val

    def _deps(self, e, reads, writes):
        deps = {}

        def add(p):
            if p is not None and deps.get(p[0], 0) < p[1]:
                deps[p[0]] = p[1]

        for k in reads:
            add(self.lastw.get(k))
        for k in writes:
            add(self.lastw.get(k))
            for sk, v in self.readers.get(k, {}).items():
                add((sk, v))
        for sk, v in deps.items():
            if sk == "pe" and e == "pe":
                continue
            self._wait(e, sk, v)

    def _commit(self, tok, reads, writes):
        sk, v = tok
        for k in reads:
            self.readers.setdefault(k, {})[sk] = v
        for k in writes:
            self.lastw[k] = tok
            self.readers[k] = {}

    def op(self, e, ins_fn, reads=(), writes=()):
        self._deps(e, reads, writes)
        ins = ins_fn()
        self.cnt[e] += 1
        self.ninst += 1
        ins.then_inc(self.sem[e], 1)
        self._commit((e, self.cnt[e]), reads, writes)
        return ins

    def dma(self, e, out, in_, reads=(), writes=(), **kw):
        i = self.dnext
        self.dnext = (self.dnext + 1) % len(self.dsem)
        self._wait(e, i, self.dcnt[i])
        self._deps(e, reads, writes)
        ins = self.engs[e].dma_start(out=out, in_=in_, **kw)
        self.dcnt[i] += 16
        self.ninst += 1
        ins.then_inc(self.dsem[i], 16)
        self._commit((i, self.dcnt[i]), reads, writes)
        return ins

    def barrier(self):
        for e in self.engs:
            for o in self.engs:
                if o != e:
                    self._wait(e, o, self.cnt[o])
            for i in range(len(self.dsem)):
                self._wait(e, i, self.dcnt[i])
        self.lastw = {}
        self.readers = {}

    def finish(self):
        for o in self.engs:
            if o != "sp":
                self._wait("sp", o, self.cnt[o])
        for i in range(len(self.dsem)):
            self._wait("sp", i, self.dcnt[i])


def _rope_tables():
    theta = 10000.0
    s = np.arange(2048)
    row = (s // 64).astype(np.float64)
    col = (s % 64).astype(np.float64)

    def tab(nd):
        h = nd // 2
        half = h // 2
        inv = theta ** (-np.arange(half, dtype=np.float64) / half)
        cos = np.ones((nd, T)); sin = np.zeros((nd, T))
        for d_ in range(nd):
            b = d_ // h
            i = (d_ % h) % half
            pos = row if b == 0 else col
            ang = (pos.astype(np.float32)[:, None] * inv.astype(np.float32)[None, :])[:, i]
            cos[d_, NCTX:] = np.cos(ang.astype(np.float32))
            sin[d_, NCTX:] = np.sin(ang.astype(np.float32))
        R = np.zeros((nd, nd))
        for d_ in range(nd):
            e = d_ % h
            if e < half:
                R[d_, d_ + half] = -1.0
            else:
                R[d_, d_ - half] = 1.0
        return cos.astype(np.float32), sin.astype(np.float32), R.astype(np.float32)

    cA, sA, RA = tab(64)
    cB, sB, RB = tab(32)
    cosA = np.concatenate([cA, cA], 0); sinA = np.concatenate([sA, sA], 0)
    RA2 = np.zeros((128, 128), np.float32); RA2[:64, :64] = RA; RA2[64:, 64:] = RA
    cosB = np.ones((96, T), np.float32); sinB = np.zeros((96, T), np.float32)
    cosB[64:] = cB; sinB[64:] = sB
    RB96 = np.zeros((96, 96), np.float32); RB96[64:, 64:] = RB
    return cosA, sinA, RA2.T.copy(), cosB, sinB, RB96.T.copy()


def _consts():
    c = {}
    cosA, sinA, RAT, cosB, sinB, RBT = _rope_tables()
    c["cosA"] = cosA; c["sinA"] = sinA; c["cosB"] = cosB; c["sinB"] = sinB
    mats = {}
    mats["ident"] = np.eye(128, dtype=np.float32)
    mats["ones1024"] = np.full((128, 128), 1.0 / 1024, np.float32)
    mats["ones256"] = np.full((128, 128), 1.0 / 256, np.float32)
    o = np.zeros((128, 128), np.float32); o[:64, :64] = 1 / 64; o[64:, 64:] = 1 / 64
    mats["onesA"] = o
    o = np.zeros((128, 128), np.float32); o[:64, :64] = 1 / 64; o[64:96, 64:96] = 1 / 32
    mats["onesB"] = o
    mats["RAT"] = RAT
    r = np.zeros((128, 128), np.float32); r[:96, :96] = RBT
    mats["RBT"] = r
    a = np.arange(128)[:, None]; b = np.arange(128)[None, :]
    mats["maskPrevT"] = np.where(a <= b, 0.0, -30000.0).astype(np.float32)
    mats["maskNextT"] = np.where(b <= a, 0.0, -30000.0).astype(np.float32)
    names = list(mats.keys())
    c["cmats"] = np.stack([mats[n] for n in names], 1).astype(np.float32)
    return c, names


def _fm(v, p=128):
    v = np.asarray(v, np.float32)
    return np.ascontiguousarray(v.reshape(-1, p).T)


class Ctx:
    pass


def build_program(dbg=None, nlayers=2):
    nc = bass.Bass("TRN2", target_bir_lowering=False)
    st = ExitStack()
    K = Ctx()
    with st:
        S = Sched(nc, st)

        def din(name, shape, dt=F32):
            return nc.dram_tensor(name, list(shape), dt, kind="ExternalInput").ap()

        uid = [0]

        def sb(stack, name, shape, dt=F32):
            uid[0] += 1
            return stack.enter_context(nc.sbuf_tensor(f"{name}_s{uid[0]}", list(shape), dt))

        def ps(stack, name, shape, dt=F32):
            uid[0] += 1
            return stack.enter_context(nc.psum_tensor(f"{name}_p{uid[0]}", list(shape), dt))

        consts, cnames = _consts()
        NCM = len(cnames)
        x_d = din("x", [2048, D]); ctx_d = din("ctx", [NCTX, D])
        cv_d = din("cv", [128, 8, 2])
        adaw_d = din("ada_w", [2, D, 6 * D]); adab_d = din("ada_b_fm", [2, 128, 48])
        wup_d = din("ffn_w_up", [2, D, 2 * DFF]); wdn_d = din("ffn_w_down", [2, DFF, D])
        fconv_d = din("ffn_conv_fm", [2, 128, 44, 4])
        abin_d = din("ab_w_in", [D, 1312]); about_d = din("ab_w_out", [D, D])
        wuq_d = din("b_w_uq", [256, 768]); wuk_d = din("b_w_uk", [256, 512]); wuv_d = din("b_w_uv", [256, 512])
        vecs_d = din("vecs0", [128, 16])
        sink_d = din("sink_b", [64, 8])
        cmats_d = din("cmats", [128, NCM, 128])
        cosA_d = din("cosA", [128, T]); sinA_d = din("sinA", [128, T])
        cosB_d = din("cosB", [96, T]); sinB_d = din("sinB", [96, T])
        cdin_d = din("cd_w_in", [D, 3984]); cdout_d = din("cd_w_out", [D, D])
        cw2_d = din("c_w2r", [128, 512]); ca2_d = din("c_a2r", [128, 512]); cg2_d = din("c_g2", [128, 512])
        v1_d = din("vecs1", [128, 160])
        gsm_d = din("gsm", [16, 2])
        cm2_d = din("cm2", [64, 320])
        sel_d = din("sel16", [16, 16 * 128])
        xpark_d = nc.dram_tensor("xpark", [128, 8 * T], F32, kind="Internal").ap()
        out_d = nc.dram_tensor("out", [2048, D], F32, kind="ExternalOutput").ap()
        dbg_d = None
        if dbg is not None:
            dbg_d = nc.dram_tensor("dbg", [T, D], F32, kind="ExternalOutput").ap()

        X = sb(st, "X", [128, 8, T])
        identf = sb(st, "identf", [128, 128])
        cm_b = sb(st, "cm_b", [128, NCM, 128], BF16)
        id4 = sb(st, "id4", [128, 4, 128], BF16)
        modv = sb(st, "modv", [128, 2, 48, 2])
        onep = sb(st, "onep", [128, 2, 2, 8, 2])
        adab = sb(st, "adab", [128, 2, 48])
        cv = sb(st, "cv", [128, 8, 2])
        scv = sb(st, "scv", [128, 8, 2])
        vecs = sb(st, "vecs", [128, 16])
        zeros = sb(st, "zeros", [128, 128])

        def CM(name, bf=True):
            if not bf:
                assert name == "ident"
                return identf[:]
            i = cnames.index(name)
            return cm_b[:, i, :]

        rr = {"cast": 0, "ev": 0}

        def evac_eng():
            rr["ev"] ^= 1
            return "act" if rr["ev"] else "dve"

        def copy_op(e, out, in_):
            if e == "act":
                return lambda: nc.scalar.copy(out=out, in_=in_)
            if e == "dve":
                return lambda: nc.vector.tensor_copy(out=out, in_=in_)
            return lambda: nc.gpsimd.tensor_copy(out=out, in_=in_)

        with ExitStack() as ph:
            cm_f = sb(ph, "cm_f", [128, NCM, 128])
            S.dma("sp", cm_f[:], cmats_d, writes=["cm_f0"])
            S.op("dve", lambda: nc.vector.tensor_copy(out=cm_b[:], in_=cm_f[:]), reads=["cm_f0"], writes=["cm_b"])
            S.op("act", lambda: nc.scalar.copy(out=identf[:], in_=cm_f[:, 0, :]), reads=["cm_f0"], writes=["cm_f"])
            for r in range(4):
                S.op("pool", lambda: nc.gpsimd.tensor_copy(out=id4[:, r, :], in_=cm_f[:, 0, :]), reads=["cm_f0"], writes=["id4"])
            S.barrier()
        S.dma("sp", cv[:], cv_d, writes=["cv"])
        S.dma("sp", adab[:], adab_d.rearrange("l p m -> p l m"), writes=["adab"])
        S.dma("sp", vecs[:], vecs_d, writes=["vecs"])
        S.op("pool", lambda: nc.gpsimd.memset(zeros[:], 0.0), writes=["zeros"])
        S.op("act", lambda: nc.scalar.activation(out=scv[:], in_=cv[:], func=AF.Silu), reads=["cv"], writes=["scv"])

        with ExitStack() as ph:
            xin = [sb(ph, f"xin{i}", [128, D]) for i in range(2)]
            pT = [ps(ph, f"pT{i}", [128, 512]) for i in range(4)]
            for t in range(NT):
                src = ctx_d[t * 128:(t + 1) * 128, :] if t < 2 else x_d[(t - 2) * 128:(t - 1) * 128, :]
                xi = xin[t % 2]
                S.dma("sp" if t % 2 == 0 else "pool", xi[:], src, writes=[f"xin{t%2}"])
                for hh in range(2):
                    pt = pT[(t % 2) * 2 + hh]
                    pk = f"pT{(t%2)*2+hh}"
                    for c4 in range(4):
                        c = hh * 4 + c4
                        S.op("pe", lambda: nc.tensor.transpose(out=pt[:, c4 * 128:(c4 + 1) * 128],
                                                               in_=xi[:, c * 128:(c + 1) * 128],
                                                               identity=CM("ident", False)),
                             reads=[f"xin{t%2}", "cm_f"], writes=[pk])
                    e = evac_eng()
                    S.op(e, copy_op(e, X[:, hh * 4:(hh + 1) * 4, t * 128:(t + 1) * 128],
                                    pt[:].rearrange("p (c n) -> p c n", c=4)),
                         reads=[pk], writes=[f"X{t}"])
        S.barrier()

        with ExitStack() as ph:
            astg = [sb(ph, f"astg{i}", [128, 8, 768]) for i in range(2)]
            pm = ps(ph, "pm", [128, 48, 2])
            adv = adaw_d.rearrange("l (kc p) n -> l p kc n", p=128)
            it = 0
            for l in range(nlayers):
                for g in range(8):
                    a = astg[it % 2]
                    S.dma("sp" if it % 2 == 0 else "pool", a[:], adv[l, :, :, g * 768:(g + 1) * 768],
                          writes=[f"astg{it%2}"])
                    for mm in range(6):
                        m = g * 6 + mm
                        for k in range(8):
                            S.op("pe", lambda: nc.tensor.matmul(pm[:, m, :], lhsT=a[:, k, mm * 128:(mm + 1) * 128],
                                                                rhs=scv[:, k, :], start=(k == 0), stop=(k == 7)),
                                 reads=[f"astg{it%2}", "scv"], writes=["pm"])
                    it += 1
                for w in range(2):
                    S.op("dve", lambda: nc.vector.tensor_tensor(out=modv[:, l, :, w], in0=pm[:, :, w], in1=adab[:, l, :],
                                                                op=ALU.add),
                         reads=["pm", "adab"], writes=["modv"])
                for ji, j in enumerate((1, 4)):
                    S.op("dve", lambda: nc.vector.tensor_scalar_add(out=onep[:, l, ji, :, :],
                                                                    in0=modv[:, l, j * 8:(j + 1) * 8, :], scalar1=1.0),
                         reads=["modv"], writes=["onep"])
        S.barrier()

        def mod(l, j, c, w):
            return modv[:, l, j * 8 + c, w:w + 1]

        def modulate_chunk(ph_bufs, l, jshift, ji_scale, t0, n, w, dst_fn, dst_keys):
            sq, pms, sd, rstd, tmp = ph_bufs
            for c in range(8):
                S.op("act", lambda: nc.scalar.activation(out=sq[c % 2][:, :n], in_=X[:, c, t0:t0 + n], func=AF.Square),
                     reads=[f"X{tt}" for tt in range(t0 // 128, (t0 + n) // 128)], writes=[f"msq{c%2}"])
                S.op("pe", lambda: nc.tensor.matmul(pms[:, :n], lhsT=CM("ones1024"), rhs=sq[c % 2][:, :n],
                                                    start=(c == 0), stop=(c == 7)),
                     reads=[f"msq{c%2}", "cm_b"], writes=["pms"])
            S.op("act", lambda: nc.scalar.activation(out=sd[:, :n], in_=pms[:, :n], func=AF.Sqrt, bias=eps_t[:, 0:1]),
                 reads=["pms", "eps"], writes=["msd"])
            S.op("dve", lambda: nc.vector.reciprocal(out=rstd[:, :n], in_=sd[:, :n]), reads=["msd"], writes=["mrstd"])
            for c in range(8):
                S.op("dve", lambda: nc.vector.scalar_tensor_tensor(out=tmp[c % 2][:, :n], in0=X[:, c, t0:t0 + n],
                                                                   scalar=onep[:, l, ji_scale, c, w:w + 1],
                                                                   in1=rstd[:, :n], op0=ALU.mult, op1=ALU.mult),
                     reads=[f"X{tt}" for tt in range(t0 // 128, (t0 + n) // 128)] + ["mrstd", "onep"],
                     writes=[f"mtmp{c%2}"])
                S.op("act", lambda: nc.scalar.activation(out=dst_fn(c), in_=tmp[c % 2][:, :n], func=AF.Identity,
                                                         bias=mod(l, jshift, c, w), scale=1.0),
                     reads=[f"mtmp{c%2}", "modv"], writes=dst_keys)

        eps_t = sb(st, "eps_t", [128, 1])
        S.op("pool", lambda: nc.gpsimd.memset(eps_t[:], EPS), writes=["eps"])

        def mod_bufs(ph):
            sq = [sb(ph, f"msq{i}", [128, 512], BF16) for i in range(2)]
            pms = ps(ph, "pms", [128, 512])
            sd = sb(ph, "msd", [128, 512])
            rstd = sb(ph, "mrstd", [128, 512])
            tmp = [sb(ph, f"mtmp{i}", [128, 512]) for i in range(2)]
            return (sq, pms, sd, rstd, tmp)

        def wload(stg, stg_key, dst_ap, dst_keys, dram_ap, shape, q):
            view = stg[:, :int(np.prod(shape[1:]))]
            if len(shape) == 3:
                view = view.rearrange("p (a b) -> p a b", a=shape[1])
            view = view[:shape[0]] if shape[0] < 128 else view
            S.dma(q, view, dram_ap, writes=[stg_key])
            rr["cast"] = (rr["cast"] + 1) % 2
            e = ("dve", "pool")[rr["cast"]]
            S.op(e, copy_op(e, dst_ap, view), reads=[stg_key], writes=dst_keys)

        def xkeys(t0, n):
            return [f"X{tt}" for tt in range(t0 // 128, (t0 + n + 127) // 128)]

        def layer0():
            l = 0
            with ExitStack() as L:
                cqn = sb(L, "cqn", [128, 2, T], BF16)
                ckvn = sb(L, "ckvn", [128, 2, T], BF16)
                KR = sb(L, "KR", [96, T], BF16)
                LA = ExitStack()
                QA = sb(LA, "QA", [128, 4, T], BF16)
                KA = sb(LA, "KA", [128, 2, T], BF16)
                VA = sb(LA, "VA", [128, NT, 2, 128], BF16)
                S.op("pool", lambda: nc.gpsimd.memset(VA[:], 1.0), writes=["VA"])
                with ExitStack() as ph:
                    win = sb(ph, "win", [128, 8, 1312], BF16)
                    wkd = sb(ph, "wkd", [128, 8, 256], BF16)
                    wkr = sb(ph, "wkr", [128, 8, 96], BF16)
                    abv = abin_d.rearrange("(kc p) n -> p kc n", p=128)
                    with ExitStack() as phs:
                        stg = [sb(phs, f"stg{i}", [128, 4096]) for i in range(2)]
                        for i, (c0, c1) in enumerate([(0, 512), (512, 1024), (1024, 1312)]):
                            wload(stg[i % 2], f"stg{i%2}", win[:, :, c0:c1], ["win"], abv[:, :, c0:c1], [128, 8, c1 - c0],
                                  "sp" if i % 2 == 0 else "pool")
                        S.barrier()
                    for g in range(2):
                        for hf in range(2):
                            S.op("pool", lambda: nc.gpsimd.tensor_copy(out=wkd[:, :, g * 128 + hf * 64:g * 128 + hf * 64 + 64],
                                                                       in_=win[:, :, 512 + g * 64:512 + g * 64 + 64]),
                                 reads=["win"], writes=["wkd"])
                    S.op("pool", lambda: nc.gpsimd.memset(wkr[:], 0.0), writes=["wkr"])
                    S.op("pool", lambda: nc.gpsimd.tensor_copy(out=wkr[:, :, 64:96], in_=win[:, :, 1280:1312]),
                         reads=["win"], writes=["wkr"])
                    hbuf = sb(ph, "hbuf", [128, 8, 512], BF16)
                    mb = mod_bufs(ph)
                    tabs1 = [sb(ph, f"tab{j}", [128, 512]) for j in range(4)]
                    tabs = [tabs1, tabs1]
                    pin = [ps(ph, f"pin{i}", [128, 512]) for i in range(2)]
                    pms2 = ps(ph, "pms2", [128, 512])
                    prot = ps(ph, "prot", [128, 512])
                    sq2 = [sb(ph, f"sq2{i}", [128, 512], BF16) for i in range(2)]
                    sd2 = sb(ph, "sd2", [128, 512])
                    rs2 = sb(ph, "rs2", [128, 512])
                    qn = sb(ph, "qn", [128, 512], BF16)
                    t1 = sb(ph, "t1", [128, 512])
                    t2 = sb(ph, "t2", [128, 512])
                    cnt = {"pin": 0}

                    def pipeline(mm_list, M, n, ones_name, gain_ap, rope, dst_ap, dst_keys, tb):
                        pi = cnt["pin"] % 2
                        cnt["pin"] += 1
                        p_in = pin[pi]
                        for i, (lh, rh, rk) in enumerate(mm_list):
                            S.op("pe", lambda: nc.tensor.matmul(p_in[:M, :n], lhsT=lh, rhs=rh, start=(i == 0),
                                                                stop=(i == len(mm_list) - 1)),
                                 reads=rk, writes=[f"pin{pi}"])
                        S.op("act", lambda: nc.scalar.activation(out=sq2[pi][:M, :n], in_=p_in[:M, :n], func=AF.Square),
                             reads=[f"pin{pi}"], writes=[f"sq2{pi}"])
                        S.op("pe", lambda: nc.tensor.matmul(pms2[:M, :n], lhsT=CM(ones_name)[:M, :M], rhs=sq2[pi][:M, :n],
                                                            start=True, stop=True),
                             reads=[f"sq2{pi}", "cm_b"], writes=["pms2"])
                        S.op("act", lambda: nc.scalar.activation(out=sd2[:M, :n], in_=pms2[:M, :n], func=AF.Sqrt,
                                                                 bias=eps_t[:M, 0:1]),
                             reads=["pms2", "eps"], writes=["sd2"])
                        S.op("dve", lambda: nc.vector.reciprocal(out=rs2[:M, :n], in_=sd2[:M, :n]), reads=["sd2"], writes=["rs2"])
                        o1 = qn[:M, :n] if rope else dst_ap
                        S.op("dve", lambda: nc.vector.scalar_tensor_tensor(out=o1, in0=p_in[:M, :n], scalar=gain_ap,
                                                                           in1=rs2[:M, :n], op0=ALU.mult, op1=ALU.mult),
                             reads=[f"pin{pi}", "rs2", "vecs"], writes=(["qn"] if rope else dst_keys))
                        if rope:
                            rname, ci, si = rope
                            S.op("pe", lambda: nc.tensor.matmul(prot[:M, :n], lhsT=CM(rname)[:M, :M], rhs=qn[:M, :n],
                                                                start=True, stop=True),
                                 reads=["qn", "cm_b"], writes=["prot"])
                            S.op("dve", lambda: nc.vector.tensor_tensor(out=t1[:M, :n], in0=qn[:M, :n], in1=tb[ci][:M, :n],
                                                                        op=ALU.mult),
                                 reads=["qn", f"tab{ci}"], writes=["t1"])
                            S.op("dve", lambda: nc.vector.tensor_tensor(out=t2[:M, :n], in0=prot[:M, :n], in1=tb[si][:M, :n],
                                                                        op=ALU.mult),
                                 reads=["prot", f"tab{si}"], writes=["t2"])
                            S.op("pool", lambda: nc.gpsimd.tensor_tensor(out=dst_ap, in0=t1[:M, :n], in1=t2[:M, :n], op=ALU.add),
                                 reads=["t1", "t2"], writes=dst_keys)

                    K.pipeline = pipeline
                    for ci_, (t0, n) in enumerate(CHUNKS):
                        w = 1 if ci_ == 0 else 0
                        tb = tabs[ci_ % 2]
                        for j, src in enumerate((cosA_d, sinA_d, cosB_d, sinB_d)):
                            np_ = 128 if j < 2 else 96
                            S.dma("pool", tb[j][:np_, :n], src[:, t0:t0 + n], writes=[f"tab{j}"])
                        modulate_chunk(mb, l, 0, 0, t0, n, w, lambda c: hbuf[:, c, :n], ["hbuf"])
                        ck = [f"tk{tt}" for tt in range(t0 // 128, (t0 + n) // 128)]
                        for i in range(4):
                            pipeline([(win[:, k, i * 128:(i + 1) * 128], hbuf[:, k, :n], ["win", "hbuf"]) for k in range(8)],
                                     128, n, "onesA", vecs[:, 0:1], ("RAT", 0, 1), QA[:, i, t0:t0 + n],
                                     [f"QA{i}.{tt}" for tt in range(t0 // 128, (t0 + n) // 128)], tb)
                        for g in range(2):
                            pipeline([(wkd[:, k, g * 128:(g + 1) * 128], hbuf[:, k, :n], ["wkd", "hbuf"]) for k in range(8)],
                                     128, n, "onesA", vecs[:, 1:2], ("RAT", 0, 1), KA[:, g, t0:t0 + n], ["KA"], tb)
                        for tt in range(n // 128):
                            pi = cnt["pin"] % 2
                            cnt["pin"] += 1
                            for k in range(8):
                                S.op("pe", lambda: nc.tensor.matmul(pin[pi][:, :128], lhsT=hbuf[:, k, tt * 128:(tt + 1) * 128],
                                                                    rhs=win[:, k, 640:768], start=(k == 0), stop=(k == 7)),
                                     reads=["win", "hbuf"], writes=[f"pin{pi}"])
                            e = evac_eng()
                            S.op(e, copy_op(e, VA[:, t0 // 128 + tt, :, 0:64], pin[pi][:, :128].rearrange("p (g d) -> p g d", g=2)),
                                 reads=[f"pin{pi}"], writes=["VA"])
                        for (dst, c0, gcol, dk) in ((cqn, 768, 2, "cqn"), (ckvn, 1024, 4, "ckvn")):
                            for i in range(2):
                                for k in range(8):
                                    S.op("pe", lambda: nc.tensor.matmul(pin[i][:, :n], lhsT=win[:, k, c0 + i * 128:c0 + (i + 1) * 128],
                                                                        rhs=hbuf[:, k, :n], start=(k == 0), stop=(k == 7)),
                                         reads=["win", "hbuf"], writes=[f"pin{i}"])
                                S.op("act", lambda: nc.scalar.activation(out=sq2[i][:, :n], in_=pin[i][:, :n], func=AF.Square),
                                     reads=[f"pin{i}"], writes=[f"sq2{i}"])
                            for i in range(2):
                                S.op("pe", lambda: nc.tensor.matmul(pms2[:, :n], lhsT=CM("ones256"), rhs=sq2[i][:, :n],
                                                                    start=(i == 0), stop=(i == 1)),
                                     reads=[f"sq2{i}", "cm_b"], writes=["pms2"])
                            S.op("act", lambda: nc.scalar.activation(out=sd2[:, :n], in_=pms2[:, :n], func=AF.Sqrt, bias=eps_t[:, 0:1]),
                                 reads=["pms2", "eps"], writes=["sd2"])
                            S.op("dve", lambda: nc.vector.reciprocal(out=rs2[:, :n], in_=sd2[:, :n]), reads=["sd2"], writes=["rs2"])
                            for i in range(2):
                                S.op("dve", lambda: nc.vector.scalar_tensor_tensor(out=dst[:, i, t0:t0 + n], in0=pin[i][:, :n],
                                                                                   scalar=vecs[:, gcol + i:gcol + i + 1], in1=rs2[:, :n],
                                                                                   op0=ALU.mult, op1=ALU.mult),
                                     reads=[f"pin{i}", "rs2", "vecs"], writes=[dk])
                        pipeline([(wkr[:, k, :], hbuf[:, k, :n], ["wkr", "hbuf"]) for k in range(8)],
                                 96, n, "onesB", vecs[:96, 6:7], ("RBT", 2, 3), KR[:96, t0:t0 + n], ["KR"], tb)
                S.barrier()
                if dbg == "qa":
                    dump_fm(QA, 4, BF16)
                    LA.close()
                    return
                with ExitStack() as ph:
                    woA = sb(ph, "woA", [128, 4, D], BF16)
                    aov = about_d.rearrange("(kc p) n -> p kc n", p=128)
                    with ExitStack() as phs:
                        stg = [sb(phs, f"stg{i}", [128, 4096]) for i in range(2)]
                        for i in range(2):
                            wload(stg[i], f"stg{i}", woA[:, 2 * i:2 * i + 2, :], ["woA"], aov[:, 2 * i:2 * i + 2, :], [128, 2, D],
                                  "sp" if i == 0 else "pool")
                        S.barrier()
                    sk_raw = sb(ph, "sk_raw", [64, 8])
                    sk_exp = sb(ph, "sk_exp", [64, 8])
                    SE = sb(ph, "SE", [64, 2, 512])
                    S.dma("sp", sk_raw[:], sink_d, writes=["sk_raw"])
                    S.op("act", lambda: nc.scalar.activation(out=sk_exp[:], in_=sk_raw[:], func=AF.Exp), reads=["sk_raw"], writes=["sk_exp"])
                    for g in range(2):
                        for hb in range(4):
                            hf, j = hb // 2, hb % 2
                            hd = 4 * g + 2 * j + hf
                            S.op("dve", lambda: nc.vector.tensor_scalar(out=SE[:, g, hb * 128:(hb + 1) * 128], in0=zeros[:64, :],
                                                                        scalar1=sk_exp[:, hd:hd + 1], scalar2=None, op0=ALU.add),
                                 reads=["zeros", "sk_exp"], writes=["SE"])
                    pS = [[ps(ph, f"pS{i}{hf}", [128, 512]) for hf in range(2)] for i in range(2)]
                    pO = [ps(ph, f"pO{i}", [128, 512]) for i in range(2)]
                    PT = [sb(ph, f"PT{i}", [128, 512], BF16) for i in range(3)]
                    den = sb(ph, "den", [64, 512])
                    rden = sb(ph, "rden", [64, 512])
                    it = 0
                    nit = 0
                    for qb in range(NT):
                        q0 = qb * 128
                        if qb < 2:
                            kts = [(0, None), (1, None)]
                        else:
                            kts = [(0, None), (1, None)]
                            if qb - 1 >= 2:
                                kts.append((qb - 1, "maskPrevT"))
                            kts.append((qb, None))
                            if qb + 1 < NT:
                                kts.append((qb + 1, "maskNextT"))
                        for g in range(2):
                            po = pO[nit % 2]
                            pok = f"pO{nit%2}"
                            nit += 1
                            for ki, (kt, mk) in enumerate(kts):
                                ptb = PT[it % 3]; ptk = f"PT{it%3}"
                                pss = pS[it % 2]
                                it += 1
                                for hf in range(2):
                                    psb = pss[hf]; psk = f"pS{(it-1)%2}{hf}"
                                    pr = slice(hf * 64, (hf + 1) * 64)
                                    if mk is not None:
                                        S.op("pe", lambda: nc.tensor.matmul(psb[:, :256], lhsT=CM(mk),
                                                                            rhs=id4[:, 0:2, :].rearrange("p r n -> p (r n)"),
                                                                            start=True, stop=False),
                                             reads=["cm_b", "id4"], writes=[psk])
                                    S.op("pe", lambda: nc.tensor.matmul(psb[:, :256].rearrange("p (r n) -> p r n", r=2),
                                                                        lhsT=KA[pr, g, kt * 128:(kt + 1) * 128],
                                                                        rhs=QA[pr, 2 * g:2 * g + 2, q0:q0 + 128],
                                                                        start=(mk is None), stop=True),
                                         reads=["KA", f"QA{2*g}.{qb}", f"QA{2*g+1}.{qb}"], writes=[psk])
                                    S.op("act", lambda: nc.scalar.activation(out=ptb[:, hf * 256:(hf + 1) * 256], in_=psb[:, :256],
                                                                             func=AF.Exp, scale=0.125),
                                         reads=[psk], writes=[ptk])
                                S.op("pe", lambda: nc.tensor.matmul(po[:], lhsT=VA[:, kt, g, :], rhs=ptb[:], start=(ki == 0),
                                                                    stop=(ki == len(kts) - 1)),
                                     reads=["VA", ptk], writes=[pok])
                            S.op("dve", lambda: nc.vector.tensor_tensor(out=den[:], in0=po[64:128, :], in1=SE[:, g, :], op=ALU.add),
                                 reads=[pok, "SE"], writes=["den"])
                            S.op("dve", lambda: nc.vector.reciprocal(out=rden[:], in_=den[:]), reads=["den"], writes=["rden"])
                            for hf in range(2):
                                S.op("dve", lambda: nc.vector.tensor_tensor(
                                    out=QA[hf * 64:(hf + 1) * 64, 2 * g:2 * g + 2, q0:q0 + 128],
                                    in0=po[0:64, hf * 256:(hf + 1) * 256].rearrange("p (r n) -> p r n", r=2),
                                    in1=rden[:, hf * 256:(hf + 1) * 256].rearrange("p (r n) -> p r n", r=2), op=ALU.mult),
                                     reads=[pok, "rden"], writes=[f"QA{2*g}.{qb}", f"QA{2*g+1}.{qb}"])
                    if dbg == "oa":
                        S.barrier()
                        dump_fm(QA, 4, BF16)
                        K.stop = True
                    pd = [ps(ph, f"pd{i}", [128, 512]) for i in range(2)]
                    it = 0
                    for ci_, (t0, n) in enumerate([] if K.stop else CHUNKS):
                        w = 1 if ci_ == 0 else 0
                        for m in range(8):
                            p_ = pd[it % 2]; pk = f"pd{it%2}"; it += 1
                            for i in range(4):
                                S.op("pe", lambda: nc.tensor.matmul(p_[:, :n], lhsT=woA[:, i, m * 128:(m + 1) * 128], rhs=QA[:, i, t0:t0 + n],
                                                                    start=(i == 0), stop=(i == 3)),
                                     reads=["woA"] + [f"QA{i}.{tt}" for tt in range(t0 // 128, (t0 + n) // 128)], writes=[pk])
                            S.op("dve", lambda: nc.vector.scalar_tensor_tensor(out=X[:, m, t0:t0 + n], in0=p_[:, :n], scalar=mod(l, 2, m, w),
                                                                               in1=X[:, m, t0:t0 + n], op0=ALU.mult, op1=ALU.add),
                                 reads=[pk, "modv"] + xkeys(t0, n), writes=xkeys(t0, n))
                S.barrier()
                LA.close()
                if K.stop:
                    return
                with ExitStack() as ph:
                    wuq = sb(ph, "wuq", [128, 2, 768], BF16)
                    wuk = sb(ph, "wuk", [128, 2, 512], BF16)
                    wuv = sb(ph, "wuv", [128, 2, 512], BF16)
                    woB = sb(ph, "woB", [128, 4, D], BF16)
                    with ExitStack() as phs:
                        stg = [sb(phs, f"stg{i}", [128, 4096]) for i in range(2)]
                        wload(stg[0], "stg0", wuq[:], ["wuq"], wuq_d.rearrange("(kc p) n -> p kc n", p=128), [128, 2, 768], "sp")
                        wload(stg[1], "stg1", wuk[:], ["wuk"], wuk_d.rearrange("(kc p) n -> p kc n", p=128), [128, 2, 512], "pool")
                        wload(stg[0], "stg0", wuv[:], ["wuv"], wuv_d.rearrange("(kc p) n -> p kc n", p=128), [128, 2, 512], "sp")
                        aov = about_d.rearrange("(kc p) n -> p kc n", p=128)
                        for i in range(2):
                            wload(stg[(i + 1) % 2], f"stg{(i+1)%2}", woB[:, 2 * i:2 * i + 2, :], ["woB"], aov[:, 4 + 2 * i:4 + 2 * i + 2, :],
                                  [128, 2, D], "pool" if i == 0 else "sp")
                        S.barrier()
                    KB = sb(ph, "KB", [96, 4, T], BF16)
                    VB = sb(ph, "VB", [128, NT, 4, 128], BF16)
                    S.op("pool", lambda: nc.gpsimd.memset(VB[:], 1.0), writes=["VB"])
                    tabs1 = [None, None] + [sb(ph, f"tab{j}", [128, 512]) for j in (2, 3)]
                    tabs = [tabs1, tabs1]
                    pin = [ps(ph, f"pin{i}", [128, 512]) for i in range(2)]
                    pms2 = ps(ph, "pms2", [128, 512])
                    prot = ps(ph, "prot", [128, 512])
                    pS = [ps(ph, f"pS{i}", [128, 512]) for i in range(2)]
                    pO = ps(ph, "pO", [128, 512])
                    pd = ps(ph, "pd", [128, 512])
                    sq2 = [sb(ph, f"sq2{i}", [128, 512], BF16) for i in range(2)]
                    sd2 = sb(ph, "sd2", [128, 512])
                    rs2 = sb(ph, "rs2", [128, 512])
                    qn = sb(ph, "qn", [128, 512], BF16)
                    t1 = sb(ph, "t1", [128, 512])
                    t2 = sb(ph, "t2", [128, 512])
                    QBc = sb(ph, "QBc", [96, 4, 512], BF16)
                    Yc = sb(ph, "Yc", [128, 2, 512], BF16)
                    PT = [sb(ph, f"PT{i}", [128, 512], BF16) for i in range(3)]
                    rden = sb(ph, "rden", [64, 512])
                    cnt = {"pin": 0}

                    def pipeline(mm_list, M, n, ones_name, gain_ap, rope, dst_ap, dst_keys, tb):
                        pi = cnt["pin"] % 2
                        cnt["pin"] += 1
                        p_in = pin[pi]
                        for i, (lh, rh, rk) in enumerate(mm_list):
                            S.op("pe", lambda: nc.tensor.matmul(p_in[:M, :n], lhsT=lh, rhs=rh, start=(i == 0),
                                                                stop=(i == len(mm_list) - 1)),
                                 reads=rk, writes=[f"pin{pi}"])
                        S.op("act", lambda: nc.scalar.activation(out=sq2[pi][:M, :n], in_=p_in[:M, :n], func=AF.Square),
                             reads=[f"pin{pi}"], writes=[f"sq2{pi}"])
                        S.op("pe", lambda: nc.tensor.matmul(pms2[:M, :n], lhsT=CM(ones_name)[:M, :M], rhs=sq2[pi][:M, :n],
                                                            start=True, stop=True),
                             reads=[f"sq2{pi}", "cm_b"], writes=["pms2"])
                        S.op("act", lambda: nc.scalar.activation(out=sd2[:M, :n], in_=pms2[:M, :n], func=AF.Sqrt,
                                                                 bias=eps_t[:M, 0:1]),
                             reads=["pms2", "eps"], writes=["sd2"])
                        S.op("dve", lambda: nc.vector.reciprocal(out=rs2[:M, :n], in_=sd2[:M, :n]), reads=["sd2"], writes=["rs2"])
                        o1 = qn[:M, :n] if rope else dst_ap
                        S.op("dve", lambda: nc.vector.scalar_tensor_tensor(out=o1, in0=p_in[:M, :n], scalar=gain_ap,
                                                                           in1=rs2[:M, :n], op0=ALU.mult, op1=ALU.mult),
                             reads=[f"pin{pi}", "rs2", "vecs"], writes=(["qn"] if rope else dst_keys))
                        if rope:
                            rname, ci, si = rope
                            S.op("pe", lambda: nc.tensor.matmul(prot[:M, :n], lhsT=CM(rname)[:M, :M], rhs=qn[:M, :n],
                                                                start=True, stop=True),
                                 reads=["qn", "cm_b"], writes=["prot"])
                            S.op("dve", lambda: nc.vector.tensor_tensor(out=t1[:M, :n], in0=qn[:M, :n], in1=tb[ci][:M, :n],
                                                                        op=ALU.mult),
                                 reads=["qn", f"tab{ci}"], writes=["t1"])
                            S.op("dve", lambda: nc.vector.tensor_tensor(out=t2[:M, :n], in0=prot[:M, :n], in1=tb[si][:M, :n],
                                                                        op=ALU.mult),
                                 reads=["prot", f"tab{si}"], writes=["t2"])
                            S.op("pool", lambda: nc.gpsimd.tensor_tensor(out=dst_ap, in0=t1[:M, :n], in1=t2[:M, :n], op=ALU.add),
                                 reads=["t1", "t2"], writes=dst_keys)

                    it = 0
                    for p in range(2):
                        for hl in range(4):
                            h = 4 * p + hl
                            for (t0, n) in CHUNKS:
                                pipeline([(wuk[:, k, h * 64:(h + 1) * 64], ckvn[:, k, t0:t0 + n], ["wuk", "ckvn"]) for k in range(2)],
                                         64, n, "onesA", vecs[:64, 7:8], None, KB[0:64, hl, t0:t0 + n], ["KB"], None)
                            S.op("pool", lambda: nc.gpsimd.tensor_copy(out=KB[64:96, hl, :], in_=KR[64:96, :]), reads=["KR"], writes=["KB"])
                        for tt in range(NT):
                            pi = cnt["pin"] % 2
                            cnt["pin"] += 1
                            for k in range(2):
                                S.op("pe", lambda: nc.tensor.matmul(pin[pi][:, :256], lhsT=ckvn[:, k, tt * 128:(tt + 1) * 128],
                                                                    rhs=wuv[:, k, p * 256:(p + 1) * 256], start=(k == 0), stop=(k == 1)),
                                     reads=["wuv", "ckvn"], writes=[f"pin{pi}"])
                            e = evac_eng()
                            S.op(e, copy_op(e, VB[:, tt, :, 0:64], pin[pi][:, :256].rearrange("p (g d) -> p g d", g=4)),
                                 reads=[f"pin{pi}"], writes=["VB"])
                        for ci_, (t0, n) in enumerate(CHUNKS):
                            w = 1 if ci_ == 0 else 0
                            tb = tabs[ci_ % 2]
                            for j, src in ((2, cosB_d), (3, sinB_d)):
                                S.dma("pool", tb[j][:96, :n], src[:, t0:t0 + n], writes=[f"tab{j}"])
                            kts = [0, 1] if ci_ == 0 else list(range(NT))
                            for hl in range(4):
                                h = 4 * p + hl
                                pipeline([(wuq[:, k, h * 96:(h + 1) * 96], cqn[:, k, t0:t0 + n], ["wuq", "cqn"]) for k in range(2)],
                                         96, n, "onesB", vecs[:96, 8:9], ("RBT", 2, 3), QBc[:96, hl, :n], [f"QBc{hl}"], tb)
                            for hl in range(4):
                                for ki, kt in enumerate(kts):
                                    psb = pS[it % 2]; psk = f"pS{it%2}"
                                    ptb = PT[it % 3]; ptk = f"PT{it%3}"
                                    it += 1
                                    S.op("pe", lambda: nc.tensor.matmul(psb[:, :n], lhsT=KB[:96, hl, kt * 128:(kt + 1) * 128],
                                                                        rhs=QBc[:96, hl, :n], start=True, stop=True),
                                         reads=["KB", f"QBc{hl}"], writes=[psk])
                                    S.op("act", lambda: nc.scalar.activation(out=ptb[:, :n], in_=psb[:, :n], func=AF.Exp,
                                                                             scale=float(96 ** -0.5)),
                                         reads=[psk], writes=[ptk])
                                    S.op("pe", lambda: nc.tensor.matmul(pO[:, :n], lhsT=VB[:, kt, hl, :], rhs=ptb[:, :n],
                                                                        start=(ki == 0), stop=(ki == len(kts) - 1)),
                                         reads=["VB", ptk], writes=["pO"])
                                S.op("dve", lambda: nc.vector.reciprocal(out=rden[:, :n], in_=pO[64:128, :n]), reads=["pO"], writes=["rden"])
                                S.op("dve", lambda: nc.vector.tensor_tensor(out=Yc[(hl % 2) * 64:(hl % 2) * 64 + 64, hl // 2, :n],
                                                                            in0=pO[0:64, :n], in1=rden[:, :n], op=ALU.mult),
                                     reads=["pO", "rden"], writes=["Yc"])
                            for m in range(8):
                                for i in range(2):
                                    S.op("pe", lambda: nc.tensor.matmul(pd[:, :n], lhsT=woB[:, 2 * p + i, m * 128:(m + 1) * 128],
                                                                        rhs=Yc[:, i, :n], start=(i == 0), stop=(i == 1)),
                                         reads=["woB", "Yc"], writes=["pd"])
                                S.op("dve", lambda: nc.vector.scalar_tensor_tensor(out=X[:, m, t0:t0 + n], in0=pd[:, :n], scalar=mod(l, 2, m, w),
                                                                                   in1=X[:, m, t0:t0 + n], op0=ALU.mult, op1=ALU.add),
                                     reads=["pd", "modv"] + xkeys(t0, n), writes=xkeys(t0, n))
                S.barrier()

        def ffn(l, last):
            G = 4
            groups = [list(range(j, min(j + G, 22))) for j in range(0, 22, G)]
            tch = [(0, 256, 0, 0)]
            for i in range(8):
                tch.append((256 + i * 256, 256, 0 if i == 0 else 1, 0 if i == 7 else 1))
            if last:
                tch = tch[1:]
            with ExitStack() as ph:
                H2 = sb(ph, "H2", [128, 8, T], BF16)
                fcv = sb(ph, "fcv", [128, 44, 4])
                S.dma("sp", fcv[:], fconv_d[l], writes=["fcv"])
                with ExitStack() as ph2:
                    mb = mod_bufs(ph2)
                    for ci_, (t0, n) in enumerate(CHUNKS):
                        w = 1 if ci_ == 0 else 0
                        if last and ci_ == 0:
                            continue
                        modulate_chunk(mb, l, 3, 1, t0, n, w, lambda c: H2[:, c, t0:t0 + n], ["H2"])
                    S.barrier()
                stg1 = sb(ph, "fstg0", [128, 4096])
                stg = [stg1, stg1]
                wup = [sb(ph, f"wup{i}", [128, 8, 2 * G * 128], BF16) for i in range(2)]
                wdn = [sb(ph, f"wdn{i}", [128, G, D], BF16) for i in range(2)]
                pu = [ps(ph, f"pu{i}", [128, 512]) for i in range(4)]
                pdn = [ps(ph, f"pdn{i}", [128, 512]) for i in range(2)]
                uu = [sb(ph, f"uu{i}", [128, 256]) for i in range(4)]
                sg = sb(ph, "sg", [128, 256])
                actb = [sb(ph, f"actb{i}", [128, 256], BF16) for i in range(2 * G)]
                upv = wup_d[l].rearrange("(kc p) n -> p kc n", p=128)
                dnv = wdn_d[l].rearrange("(kc p) n -> p kc n", p=128)
                si = 0
                iu = 0
                ia = 0
                idn = 0
                for gi, js in enumerate(groups):
                    g_n = len(js)
                    wu = wup[gi % 2]; wd = wdn[gi % 2]
                    j0 = js[0]
                    for part in range(2):
                        wload(stg[si % 2], "fstg0", wu[:, :, part * G * 128:part * G * 128 + g_n * 128], [f"wup{gi%2}"],
                              upv[:, :, part * DFF + j0 * 128:part * DFF + (j0 + g_n) * 128], [128, 8, g_n * 128],
                              "sp" if si % 2 == 0 else "pool")
                        si += 1
                    wload(stg[si % 2], "fstg0", wd[:, :g_n, :], [f"wdn{gi%2}"], dnv[:, j0:j0 + g_n, :], [128, g_n, D],
                          "sp" if si % 2 == 0 else "pool")
                    si += 1
                    for (s0, n, lo, hi) in tch:
                        w = 1 if s0 == 0 else 0
                        e0 = s0 - lo
                        ne = n + lo + hi
                        abufs = []
                        for jj, j in enumerate(js):
                            us = []
                            for part in range(2):
                                p_ = pu[iu % 4]; pk = f"pu{iu%4}"
                                u_ = uu[iu % 4]; uk = f"uu{iu%4}"
                                iu += 1
                                fc = part * 22 + j
                                for k in range(8):
                                    S.op("pe", lambda: nc.tensor.matmul(p_[:, :ne], lhsT=wu[:, k, (part * G + jj) * 128:(part * G + jj + 1) * 128],
                                                                        rhs=H2[:, k, e0:e0 + ne], start=(k == 0), stop=(k == 7)),
                                         reads=[f"wup{gi%2}", "H2"], writes=[pk])
                                S.op("act", lambda: nc.scalar.activation(out=u_[:, :n], in_=p_[:, lo:lo + n], func=AF.Identity,
                                                                         bias=fcv[:, fc, 3:4], scale=fcv[:, fc, 1:2]),
                                     reads=[pk, "fcv"], writes=[uk])
                                a = 1 if lo == 0 else 0
                                S.op("dve", lambda: nc.vector.scalar_tensor_tensor(out=u_[:, a:n], in0=p_[:, lo - 1 + a:lo + n - 1],
                                                                                   scalar=fcv[:, fc, 0:1], in1=u_[:, a:n],
                                                                                   op0=ALU.mult, op1=ALU.add),
                                     reads=[pk, "fcv", uk], writes=[uk])
                                b = 1 if hi == 0 else 0
                                S.op("dve", lambda: nc.vector.scalar_tensor_tensor(out=u_[:, :n - b], in0=p_[:, lo + 1:lo + 1 + n - b],
                                                                                   scalar=fcv[:, fc, 2:3], in1=u_[:, :n - b],
                                                                                   op0=ALU.mult, op1=ALU.add),
                                     reads=[pk, "fcv", uk], writes=[uk])
                                us.append((u_, uk))
                            (uv, uvk), (ug, ugk) = us
                            S.op("act", lambda: nc.scalar.activation(out=sg[:, :n], in_=ug[:, :n], func=AF.Silu), reads=[ugk], writes=["sg"])
                            ab = actb[ia % (2 * G)]; abk = f"actb{ia%(2*G)}"
                            ia += 1
                            S.op("pool", lambda: nc.gpsimd.tensor_tensor(out=ab[:, :n], in0=sg[:, :n], in1=uv[:, :n], op=ALU.mult),
                                 reads=["sg", uvk], writes=[abk])
                            abufs.append((ab, abk))
                        for m in range(8):
                            p_ = pdn[idn % 2]; pk = f"pdn{idn%2}"; idn += 1
                            for jj in range(g_n):
                                S.op("pe", lambda: nc.tensor.matmul(p_[:, :n], lhsT=wd[:, jj, m * 128:(m + 1) * 128], rhs=abufs[jj][0][:, :n],
                                                                    start=(jj == 0), stop=(jj == g_n - 1)),
                                     reads=[f"wdn{gi%2}", abufs[jj][1]], writes=[pk])
                            S.op("dve", lambda: nc.vector.scalar_tensor_tensor(out=X[:, m, s0:s0 + n], in0=p_[:, :n], scalar=mod(l, 5, m, w),
                                                                               in1=X[:, m, s0:s0 + n], op0=ALU.mult, op1=ALU.add),
                                 reads=[pk, "modv"] + xkeys(s0, n), writes=xkeys(s0, n))
            S.barrier()

        def dump_fm(buf, nchunk, dt, ntile=NT):
            with ExitStack() as ph:
                tin = sb(ph, "d_tin", [128, 128])
                pt_ = ps(ph, "d_pt", [128, 128])
                ob = sb(ph, "d_ob", [128, D])
                for t in range(ntile):
                    for c in range(nchunk):
                        S.op("dve", lambda: nc.vector.tensor_copy(out=tin[:], in_=buf[:, c, t * 128:(t + 1) * 128]), reads=["*"], writes=["d_tin"])
                        S.op("pe", lambda: nc.tensor.transpose(out=pt_[:], in_=tin[:], identity=CM("ident", False)),
                             reads=["d_tin", "cm_f"], writes=["d_pt"])
                        S.op("dve", lambda: nc.vector.tensor_copy(out=ob[:, c * 128:(c + 1) * 128], in_=pt_[:]), reads=["d_pt"], writes=["d_ob"])
                    S.dma("sp", dbg_d[t * 128:(t + 1) * 128, :nchunk * 128], ob[:, :nchunk * 128], reads=["d_ob"])
                S.barrier()

        def write_out():
            with ExitStack() as ph:
                pt_ = [ps(ph, f"o_pt{i}", [128, 512]) for i in range(4)]
                ob = [sb(ph, f"o_ob{i}", [128, D]) for i in range(2)]
                for t in range(2, NT):
                    o_ = ob[t % 2]
                    for hh in range(2):
                        p_ = pt_[(t % 2) * 2 + hh]; pk = f"o_pt{(t%2)*2+hh}"
                        for c4 in range(4):
                            c = hh * 4 + c4
                            S.op("pe", lambda: nc.tensor.transpose(out=p_[:, c4 * 128:(c4 + 1) * 128], in_=X[:, c, t * 128:(t + 1) * 128],
                                                                   identity=CM("ident", False)),
                                 reads=[f"X{t}", "cm_f"], writes=[pk])
                        e = evac_eng()
                        S.op(e, copy_op(e, o_[:, hh * 512:(hh + 1) * 512], p_[:]), reads=[pk], writes=[f"o_ob{t%2}"])
                    S.dma("sp" if t % 2 == 0 else "pool", out_d[(t - 2) * 128:(t - 1) * 128, :], o_[:], reads=[f"o_ob{t%2}"])
                if dbg == "x":
                    for t in range(2):
                        o_ = ob[t % 2]
                        for hh in range(2):
                            p_ = pt_[(t % 2) * 2 + hh]; pk = f"o_pt{(t%2)*2+hh}"
                            for c4 in range(4):
                                c = hh * 4 + c4
                                S.op("pe", lambda: nc.tensor.transpose(out=p_[:, c4 * 128:(c4 + 1) * 128], in_=X[:, c, t * 128:(t + 1) * 128],
                                                                       identity=CM("ident", False)),
                                     reads=[f"X{t}", "cm_f"], writes=[pk])
                            e = evac_eng()
                            S.op(e, copy_op(e, o_[:, hh * 512:(hh + 1) * 512], p_[:]), reads=[pk], writes=[f"o_ob{t%2}"])
                        S.dma("sp", dbg_d[t * 128:(t + 1) * 128, :], o_[:], reads=[f"o_ob{t%2}"])


        def layer1():
            l = 1
            INC = 1920
            NCH = T // 64
            fwd_order = list(range(NCH))
            bwd_order = [3, 2, 1, 0] + list(range(NCH - 1, 3, -1))
            PCH = [(0, 256, 0, 256)] + [(256 + i * 256, 256, 256, T) for i in range(8)]
            with ExitStack() as L:
                L0 = L
                H1 = sb(L, "H1", [128, 8, T], BF16)
                Yall = sb(L, "Yall", [128, 8, 2048], BF16)
                v1 = sb(L, "v1", [128, 160])
                cm2 = sb(L, "cm2", [64, 320])
                ones64 = sb(L, "ones64", [128, 64])
                id64b = sb(L, "id64b", [64, 64], BF16)
                S.dma("sp", v1[:], v1_d, writes=["v1"])
                S.dma("sp", cm2[:], cm2_d, writes=["cm2"])
                S.op("pool", lambda: nc.gpsimd.memset(ones64[:], 1.0), writes=["ones64"])
                S.op("dve", lambda: nc.vector.tensor_copy(out=id64b[:], in_=cm2[:, 256:320]), reads=["cm2"], writes=["id64b"])
                S.op("dve", lambda: nc.vector.tensor_tensor(out=v1[:, 30:45], in0=v1[:, 0:15], in1=v1[:, 15:30], op=ALU.add), reads=["v1"], writes=["v1"])
                S.op("dve", lambda: nc.vector.tensor_scalar(out=v1[:, 30:45], in0=v1[:, 30:45], scalar1=-1.0, scalar2=1.0, op0=ALU.mult, op1=ALU.add),
                     reads=["v1"], writes=["v1"])
                with ExitStack() as ph2:
                    mb = mod_bufs(ph2)
                    for ci_, (t0, n) in enumerate(CHUNKS):
                        w = 1 if ci_ == 0 else 0
                        modulate_chunk(mb, l, 0, 0, t0, n, w, lambda c: H1[:, c, t0:t0 + n], ["H1"])
                    S.barrier()
                for c in range(8):
                    S.dma("sp" if c % 2 == 0 else "pool", xpark_d[:, c * T:(c + 1) * T], X[:, c, :], reads=[f"X{t}" for t in range(NT)])
                S.barrier()
                XB = X[:].rearrange("p c t -> p (c t)").bitcast(BF16)
                XF = X[:].rearrange("p c t -> p (c t)")

                def xb(i):
                    return XB[:, i * T:(i + 1) * T]

                def xf(i):
                    return XF[:, i * T:(i + 1) * T]

                LS = ExitStack()
                stg = sb(LS, "l1stg", [128, 1024])
                wg = sb(LS, "l1wg", [128, 8, 512], BF16)
                gneps = sb(LS, "gneps", [128, 1])
                S.op("pool", lambda: nc.gpsimd.memset(gneps[:], 64e-5), writes=["gneps"])
                L = LS
                pin = [ps(L, f"l1pin{i}", [128, 512]) for i in range(2)]
                pC = [ps(L, f"l1pC{i}", [128, 512]) for i in range(4)]
                pTr = [ps(L, f"l1pT{i}", [64, 1024], BF16) for i in range(2)]
                uw = [sb(L, f"l1u{i}", [128, 260]) for i in range(2)]
                cnt = {"pin": 0}
                cdv = cdin_d.rearrange("(kc p) n -> p kc n", p=128)

                def load_cols(cols_list):
                    off = 0
                    for (c0, ncol) in cols_list:
                        for b0 in range(0, ncol, 128):
                            nb = min(128, ncol - b0)
                            wload(stg, "l1stg", wg[:, :, off:off + nb], ["l1wg"], cdv[:, :, c0 + b0:c0 + b0 + nb], [128, 8, nb], "sp")
                            off += nb

                def proj_taps(woff, M, taps, func, out_fn, out_keys, bias_ap=None, scale_out=None):
                    hw = max(abs(o) for o, _ in taps)
                    for (s0, n, qlo, qhi) in PCH:
                        e0 = max(s0 - hw, qlo); e1 = min(s0 + n + hw, qhi)
                        ne = e1 - e0
                        pi = cnt["pin"] % 2; cnt["pin"] += 1
                        p_ = pin[pi]; u_ = uw[pi]
                        for k in range(8):
                            S.op("pe", lambda: nc.tensor.matmul(p_[:M, :ne], lhsT=wg[:, k, woff:woff + M], rhs=H1[:, k, e0:e1],
                                                                start=(k == 0), stop=(k == 7)),
                                 reads=["l1wg", "H1"], writes=[f"l1pin{pi}"])
                        src = p_
                        if len(taps) > 1:
                            base = s0 - e0
                            o0, c0_ = taps[0]
                            S.op("act", lambda: nc.scalar.activation(out=u_[:M, :n], in_=p_[:M, base:base + n], func=AF.Identity,
                                                                     scale=c0_),
                                 reads=[f"l1pin{pi}", "v1"], writes=[f"l1u{pi}"])
                            for (o, cf) in taps[1:]:
                                i0 = max(0, e0 - s0 - o); i1 = min(n, e1 - s0 - o)
                                S.op("dve", lambda: nc.vector.scalar_tensor_tensor(out=u_[:M, i0:i1], in0=p_[:M, base + i0 + o:base + i1 + o],
                                                                                   scalar=cf, in1=u_[:M, i0:i1], op0=ALU.mult, op1=ALU.add),
                                     reads=[f"l1pin{pi}", f"l1u{pi}", "v1"], writes=[f"l1u{pi}"])
                            src = u_
                            srck = f"l1u{pi}"
                            sl = slice(0, n)
                        else:
                            srck = f"l1pin{pi}"
                            sl = slice(s0 - e0, s0 - e0 + n)
                        kw = {}
                        if bias_ap is not None:
                            kw["bias"] = bias_ap
                        if scale_out is not None:
                            kw["scale"] = scale_out
                        S.op("act", lambda: nc.scalar.activation(out=out_fn(s0, n), in_=src[:M, sl], func=func, **kw),
                             reads=[srck, "v1"], writes=out_keys)

                def shift_taps(fc):
                    return [(0, v1[:, 30 + fc:31 + fc]), (-1, v1[:, fc:fc + 1]), (1, v1[:, 15 + fc:16 + fc])]

                wk = {}
                for dn in ("f", "b"):
                    wk[dn] = dict(
                        al=sb(L, f"al{dn}", [128, 64]), be=sb(L, f"be{dn}", [128, 64]), kd=sb(L, f"kd{dn}", [128, 64]),
                        rr=sb(L, f"rr{dn}", [128, 64]), lw=sb(L, f"lw{dn}", [128, 64]),
                        pfx=sb(L, f"pfx{dn}", [128, 64]), Gi=sb(L, f"Gi{dn}", [128, 64]), Ge=sb(L, f"Ge{dn}", [128, 64]),
                        E=sb(L, f"E{dn}", [128, 4, 64]), Es=sb(L, f"Es{dn}", [128, 3, 64]),
                        negm=sb(L, f"negm{dn}", [128, 1]),
                        BT=sb(L, f"BT{dn}", [128, 64], BF16), KT=sb(L, f"KT{dn}", [128, 64], BF16),
                        Hs=sb(L, f"Hs{dn}", [128, 128]), Hb=sb(L, f"Hb{dn}", [128, 128], BF16), ztmp=sb(L, f"ztmp{dn}", [64, 128]),
                    )
                    for j in range(2):
                        idt = BF16 if DBG['invbf'] else F32
                        wk[dn][f"N{j}"] = [sb(L, f"N{dn}{j}{i}", [64, 64], idt) for i in range(2)]
                        wk[dn][f"NT{j}"] = [sb(L, f"NT{dn}{j}{i}", [64, 64], idt) for i in range(2)]
                        wk[dn][f"IN{j}"] = sb(L, f"IN{dn}{j}", [64, 64], idt)
                        wk[dn][f"PT{j}"] = [sb(L, f"PTi{dn}{j}{i}", [64, 64], idt) for i in range(2)]
                        wk[dn][f"TT{j}"] = sb(L, f"TT{dn}{j}", [64, 64], BF16)
                        wk[dn][f"ARB{j}"] = sb(L, f"ARB{dn}{j}", [64, 64], BF16)
                        wk[dn][f"AKRK{j}"] = sb(L, f"AKRK{dn}{j}", [64, 128], BF16)
                        wk[dn][f"Zb{j}"] = sb(L, f"Zb{dn}{j}", [64, 128], BF16)
                        wk[dn][f"Ub{j}"] = sb(L, f"Ub{dn}{j}", [64, 128], BF16)

                wk2 = {}
                for dn in ("f", "b"):
                    wk2[dn] = []
                    for par in range(2):
                        wk2[dn].append(dict(
                            AR=sb(L, f"AR2{dn}{par}", [128, 2, 64], BF16), BH=sb(L, f"BH2{dn}{par}", [128, 64], BF16), KH=sb(L, f"KH2{dn}{par}", [128, 64], BF16),
                            ART=sb(L, f"ART2{dn}{par}", [128, 2, 64], BF16), BTt=sb(L, f"BTt2{dn}{par}", [64, 128], BF16),
                            KTt=sb(L, f"KTt2{dn}{par}", [64, 128], BF16), VT=sb(L, f"VT2{dn}{par}", [64, 128], BF16), Et=sb(L, f"Et2{dn}{par}", [128, 1])))

                def dm_aps(di, par):
                    if par == 0:
                        v = K.lw2flat[:, di * 648:di * 648 + 648].bitcast(F32)
                        return v[0:64, 0:256], v[0:64, 256:320], v[0:64, 320:322]
                    sqf = sq[:].bitcast(F32)
                    return t32[di][0:64, 0:256], sqf[0:64, di * 64:(di + 1) * 64], t32[2][0:64, di * 2:di * 2 + 2]

                def prefix_gen(dn, pre_ops, vsrc_ap, vkeys, sdec, par=0):
                    W = wk[dn]
                    W2 = wk2[dn][par]
                    k_ = lambda nm: f"{nm}{dn}"
                    k2 = lambda nm: f"{nm}{dn}p{par}"
                    fwd = dn == "f"
                    di = 0 if fwd else 1
                    ptr = pTr[di]; ptk = f"l1pT{di}"
                    for fn in pre_ops:
                        fn()
                        yield
                    S.op("dve", lambda: nc.vector.tensor_tensor_scan(out=W["pfx"][:], data0=ones64[:], data1=W["lw"][:], initial=0.0,
                                                                     op0=ALU.mult, op1=ALU.add),
                         reads=[k_("lw"), "ones64"], writes=[k_("pfx")])
                    yield
                    tot = W["pfx"][:, 63:64]
                    if fwd:
                        S.op("pool", lambda: nc.gpsimd.tensor_copy(out=W["Gi"][:], in_=W["pfx"][:]), reads=[k_("pfx")], writes=[k_("Gi")])
                        yield
                        S.op("dve", lambda: nc.vector.tensor_tensor(out=W["Ge"][:], in0=W["pfx"][:], in1=W["lw"][:], op=ALU.subtract),
                             reads=[k_("pfx"), k_("lw")], writes=[k_("Ge")])
                        yield
                    else:
                        S.op("dve", lambda: nc.vector.tensor_scalar(out=W["Ge"][:], in0=W["pfx"][:], scalar1=-1.0, scalar2=tot,
                                                                    op0=ALU.mult, op1=ALU.add),
                             reads=[k_("pfx")], writes=[k_("Ge")])
                        yield
                        S.op("dve", lambda: nc.vector.tensor_tensor(out=W["Gi"][:], in0=W["Ge"][:], in1=W["lw"][:], op=ALU.add),
                             reads=[k_("Ge"), k_("lw")], writes=[k_("Gi")])
                        yield
                    E = W["E"]; Es = W["Es"]; negm = W["negm"]
                    S.op("act", lambda: nc.scalar.activation(out=E[:, 0, :], in_=W["Ge"][:], func=AF.Exp), reads=[k_("Ge")], writes=[k_("E0")])
                    yield
                    S.op("act", lambda: nc.scalar.activation(out=E[:, 1, :], in_=W["Gi"][:], func=AF.Exp), reads=[k_("Gi")], writes=[k_("E1")])
                    yield
                    S.op("act", lambda: nc.scalar.activation(out=E[:, 3, :], in_=W["Gi"][:], func=AF.Exp, scale=-1.0, bias=tot),
                         reads=[k_("Gi"), k_("pfx")], writes=[k_("E3")])
                    yield
                    S.op("act", lambda: nc.scalar.activation(out=W2["Et"][:], in_=tot, func=AF.Exp), reads=[k_("pfx")], writes=[k2("Et")])
                    yield
                    if sdec:
                        dmA, dm4, gcol = dm_aps(di, par)
                        dk_ = f"dmw{di}p{par}"; gk_ = f"gcol{di}p{par}"
                        msk = (cm2[:, 0:64], cm2[:, 64:128], cm2[:, 128:192]) if fwd else (cm2[:, 128:192], cm2[:, 192:256], cm2[:, 0:64])
                        for ti, srcn in enumerate(("Ge", "Gi")):
                            po_ = di * 256 + ti * 128
                            S.op("pe", lambda: nc.tensor.transpose(out=pin[1][0:64, po_:po_ + 128], in_=W[srcn][:], identity=CM("ident", False)),
                                 reads=[k_(srcn), "cm_f"], writes=["l1pin1"])
                            yield
                            S.op("dve", lambda: nc.vector.tensor_copy(out=gcol[:, ti:ti + 1], in_=pin[1][0:64, po_:po_ + 1]),
                                 reads=["l1pin1"], writes=[gk_])
                            yield
                        rowGe = W["Ge"][0:64, :]; rowGi = W["Gi"][0:64, :]
                        for qd, (row, rk_, cidx) in enumerate(((rowGe, "Ge", 0), (rowGe, "Ge", 1), (rowGi, "Gi", 0), (rowGi, "Gi", 1))):
                            S.op("dve", lambda: nc.vector.tensor_scalar(out=dmA[:, qd * 64:(qd + 1) * 64], in0=row, scalar1=gcol[:, cidx:cidx + 1], scalar2=0.0,
                                                                        op0=ALU.subtract, op1=ALU.min),
                                 reads=[k_(rk_), gk_], writes=[dk_])
                            yield
                        S.op("dve", lambda: nc.vector.tensor_scalar(out=dm4, in0=rowGe, scalar1=-1.0, scalar2=gcol[:, 0:1], op0=ALU.mult, op1=ALU.add),
                             reads=[k_("Ge"), gk_], writes=[dk_])
                        yield
                        S.op("dve", lambda: nc.vector.tensor_scalar(out=dm4, in0=dm4, scalar1=0.0, scalar2=None, op0=ALU.min),
                             reads=[dk_], writes=[dk_])
                        yield
                        S.op("act", lambda: nc.scalar.activation(out=dmA, in_=dmA, func=AF.Exp), reads=[dk_], writes=[dk_])
                        yield
                        S.op("act", lambda: nc.scalar.activation(out=dm4, in_=dm4, func=AF.Exp), reads=[dk_], writes=[dk_])
                        yield
                        for qd, mi in enumerate((0, 0, 1, 1, 2)):
                            dsl = dmA[:, qd * 64:(qd + 1) * 64] if qd < 4 else dm4
                            S.op("pool", lambda: nc.gpsimd.tensor_tensor(out=dsl, in0=dsl, in1=msk[mi], op=ALU.mult),
                                 reads=[dk_, "cm2"], writes=[dk_])
                            yield
                        S.op("dve", lambda: nc.vector.tensor_copy(out=W2["AR"][:, 0, :], in_=W["al"][:]), reads=[k_("al")], writes=[k2("AR")])
                        yield
                        S.op("pool", lambda: nc.gpsimd.tensor_copy(out=W2["AR"][:, 1, :], in_=W["rr"][:]), reads=[k_("rr")], writes=[k2("AR")])
                        yield
                        S.op("pool", lambda: nc.gpsimd.tensor_copy(out=W2["KH"][:], in_=W["kd"][:]), reads=[k_("kd")], writes=[k2("KH")])
                        yield
                    else:
                        mcol = W["Gi"][:, 32:33]
                        S.op("dve", lambda: nc.vector.tensor_scalar(out=negm[:], in0=mcol, scalar1=-1.0, scalar2=None, op0=ALU.mult),
                             reads=[k_("Gi")], writes=[k_("negm")])
                        yield
                        S.op("dve", lambda: nc.vector.tensor_scalar(out=Es[:, 0, :], in0=W["Ge"][:], scalar1=negm[:, 0:1], scalar2=40.0, op0=ALU.add, op1=ALU.min),
                             reads=[k_("Ge"), k_("negm")], writes=[k_("Es0")])
                        yield
                        S.op("dve", lambda: nc.vector.tensor_scalar(out=Es[:, 1, :], in0=W["Gi"][:], scalar1=negm[:, 0:1], scalar2=40.0, op0=ALU.add, op1=ALU.min),
                             reads=[k_("Gi"), k_("negm")], writes=[k_("Es1")])
                        yield
                        S.op("dve", lambda: nc.vector.tensor_scalar(out=Es[:, 2, :], in0=W["Gi"][:], scalar1=-1.0, scalar2=mcol, op0=ALU.mult, op1=ALU.add),
                             reads=[k_("Gi")], writes=[k_("Es2")])
                        yield
                        S.op("dve", lambda: nc.vector.tensor_scalar(out=Es[:, 2, :], in0=Es[:, 2, :], scalar1=40.0, scalar2=None, op0=ALU.min),
                             reads=[k_("Es2")], writes=[k_("Es2")])
                        yield
                        S.op("act", lambda: nc.scalar.activation(out=Es[:, :, :], in_=Es[:, :, :], func=AF.Exp), reads=[k_("Es0"), k_("Es1"), k_("Es2")],
                             writes=[k_("Es0"), k_("Es1"), k_("Es2")])
                        yield
                        S.op("dve", lambda: nc.vector.tensor_tensor(out=W2["AR"][:, 0, :], in0=W["al"][:], in1=Es[:, 0, :], op=ALU.mult),
                             reads=[k_("al"), k_("Es0")], writes=[k2("AR")])
                        yield
                        S.op("pool", lambda: nc.gpsimd.tensor_tensor(out=W2["AR"][:, 1, :], in0=W["rr"][:], in1=Es[:, 1, :], op=ALU.mult),
                             reads=[k_("rr"), k_("Es1")], writes=[k2("AR")])
                        yield
                        S.op("dve", lambda: nc.vector.tensor_tensor(out=W2["BH"][:], in0=W["be"][:], in1=Es[:, 2, :], op=ALU.mult),
                             reads=[k_("be"), k_("Es2")], writes=[k2("BH")])
                        yield
                        S.op("pool", lambda: nc.gpsimd.tensor_tensor(out=W2["KH"][:], in0=W["kd"][:], in1=Es[:, 2, :], op=ALU.mult),
                             reads=[k_("kd"), k_("Es2")], writes=[k2("KH")])
                        yield
                    S.op("dve", lambda: nc.vector.tensor_tensor(out=W2["ART"][:, 0, :], in0=W["al"][:], in1=E[:, 0, :], op=ALU.mult),
                         reads=[k_("al"), k_("E0")], writes=[k2("ART")])
                    yield
                    S.op("pool", lambda: nc.gpsimd.tensor_tensor(out=W2["ART"][:, 1, :], in0=W["rr"][:], in1=E[:, 1, :], op=ALU.mult),
                         reads=[k_("rr"), k_("E1")], writes=[k2("ART")])
                    yield
                    S.op("dve", lambda: nc.vector.tensor_tensor(out=W["BT"][:], in0=W["be"][:], in1=E[:, 3, :], op=ALU.mult),
                         reads=[k_("be"), k_("E3")], writes=[k_("BT")])
                    yield
                    S.op("pool", lambda: nc.gpsimd.tensor_tensor(out=W["KT"][:], in0=W["kd"][:], in1=E[:, 3, :], op=ALU.mult),
                         reads=[k_("kd"), k_("E3")], writes=[k_("KT")])
                    yield
                    for ti, (src, srk, dst, dsk) in enumerate(((W["BT"][:], k_("BT"), W2["BTt"], k2("BTt")), (W["KT"][:], k_("KT"), W2["KTt"], k2("KTt")),
                                                               (vsrc_ap, vkeys, W2["VT"], k2("VT")))):
                        S.op("pe", lambda: nc.tensor.transpose(out=ptr[:, ti * 128:(ti + 1) * 128], in_=src, identity=CM("ident")),
                             reads=([srk] if isinstance(srk, str) else list(srk)) + ["cm_b"], writes=[ptk])
                        yield
                        e = "act" if di == 0 else "dve"
                        S.op(e, copy_op(e, dst[:], ptr[:, ti * 128:(ti + 1) * 128]), reads=[ptk], writes=[dsk])
                        yield

                def head_gen(dn, j, pr, dk, dv, c0, emit, sdec, par=0):
                    W = wk[dn]
                    W2 = wk2[dn][par]
                    k_ = lambda nm: f"{nm}{dn}"
                    k2 = lambda nm: f"{nm}{dn}p{par}"
                    kj = lambda nm: f"{nm}{dn}{j}"
                    fwd = dn == "f"
                    di = 0 if fwd else 1
                    ms_mi = cm2[:, 0:128] if fwd else cm2[:, 128:256]
                    ms_other = cm2[:, 128:192] if fwd else cm2[:, 0:64]
                    I64 = cm2[:, 256:320]
                    pc = pC[di * 2 + j]; pck = f"l1pC{di*2+j}"
                    ev = "dve" if j == 0 else "act"
                    N = W[f"N{j}"]; NTt = W[f"NT{j}"]; PTm = W[f"PT{j}"]; TT = W[f"TT{j}"]
                    ARB = W[f"ARB{j}"]; AKRK = W[f"AKRK{j}"]; Zb = W[f"Zb{j}"]; Ub = W[f"Ub{j}"]
                    ar2 = W2["AR"][pr, :, :].rearrange("p a n -> p (a n)")
                    pa = pc[0:64, :]
                    if sdec:
                        dmA, dm4, _gc = dm_aps(di, par)
                        dk_ = f"dmw{di}p{par}"
                        S.op("pe", lambda: nc.tensor.matmul(pa[:, 0:128], lhsT=W2["KH"][pr, :], rhs=ar2, start=True, stop=True),
                             reads=[k2("KH"), k2("AR")], writes=[pck])
                        yield
                        S.op("pe", lambda: nc.tensor.matmul(pa[:, 256:320], lhsT=W2["AR"][pr, 0, :], rhs=W2["KH"][pr, :], start=True, stop=True),
                             reads=[k2("KH"), k2("AR")], writes=[pck])
                        yield
                        S.op("dve", lambda: nc.vector.scalar_tensor_tensor(out=NTt[0][:], in0=pa[:, 0:64], scalar=-1.0, in1=dmA[:, 0:64], op0=ALU.mult, op1=ALU.mult),
                             reads=[pck, dk_], writes=[kj("NT0")])
                        yield
                        S.op("dve", lambda: nc.vector.tensor_tensor(out=AKRK[:, 0:64], in0=pa[:, 0:64], in1=dmA[:, 64:128], op=ALU.mult),
                             reads=[pck, dk_], writes=[kj("AKRK")])
                        yield
                        S.op("dve", lambda: nc.vector.scalar_tensor_tensor(out=ARB[:], in0=pa[:, 64:128], scalar=-1.0, in1=dmA[:, 128:192], op0=ALU.mult, op1=ALU.mult),
                             reads=[pck, dk_], writes=[kj("ARB")])
                        yield
                        S.op("dve", lambda: nc.vector.tensor_tensor(out=AKRK[:, 64:128], in0=pa[:, 64:128], in1=dmA[:, 192:256], op=ALU.mult),
                             reads=[pck, dk_], writes=[kj("AKRK")])
                        yield
                        S.op("dve", lambda: nc.vector.scalar_tensor_tensor(out=N[0][:], in0=pa[:, 256:320], scalar=-1.0, in1=dm4, op0=ALU.mult, op1=ALU.mult),
                             reads=[pck, dk_], writes=[kj("N0")])
                        yield
                    else:
                        S.op("pe", lambda: nc.tensor.matmul(pa[:, 0:128], lhsT=W2["BH"][pr, :], rhs=ar2, start=True, stop=True),
                             reads=[k2("BH"), k2("AR")], writes=[pck])
                        yield
                        S.op("pe", lambda: nc.tensor.matmul(pa[:, 128:256], lhsT=W2["KH"][pr, :], rhs=ar2, start=True, stop=True),
                             reads=[k2("KH"), k2("AR")], writes=[pck])
                        yield
                        S.op("pe", lambda: nc.tensor.matmul(pa[:, 256:320], lhsT=W2["AR"][pr, 0, :], rhs=W2["BH"][pr, :], start=True, stop=True),
                             reads=[k2("BH"), k2("AR")], writes=[pck])
                        yield
                        S.op("dve", lambda: nc.vector.tensor_tensor(out=NTt[0][:], in0=pa[:, 0:64], in1=ms_mi[:, 0:64], op=ALU.mult),
                             reads=[pck, "cm2"], writes=[kj("NT0")])
                        yield
                        S.op("dve", lambda: nc.vector.tensor_tensor(out=ARB[:], in0=pa[:, 64:128], in1=ms_mi[:, 64:128], op=ALU.mult),
                             reads=[pck, "cm2"], writes=[kj("ARB")])
                        yield
                        S.op("dve", lambda: nc.vector.tensor_tensor(out=AKRK[:], in0=pa[:, 128:256], in1=ms_mi, op=ALU.mult),
                             reads=[pck, "cm2"], writes=[kj("AKRK")])
                        yield
                        S.op("dve", lambda: nc.vector.tensor_tensor(out=N[0][:], in0=pa[:, 256:320], in1=ms_other, op=ALU.mult),
                             reads=[pck, "cm2"], writes=[kj("N0")])
                        yield
                    S.op("pool", lambda: nc.gpsimd.tensor_tensor(out=PTm[0][:], in0=NTt[0][:], in1=I64, op=ALU.add),
                         reads=[kj("NT0"), "cm2"], writes=[kj("PT0")])
                    yield
                    for lv in range(1, 6):
                        a_, b_ = (lv - 1) % 2, lv % 2
                        S.op("pe", lambda: nc.tensor.matmul(pa[:, 320:384], lhsT=NTt[a_][:], rhs=N[a_][:], start=True, stop=True),
                             reads=[kj(f"NT{a_}"), kj(f"N{a_}")], writes=[pck])
                        yield
                        if lv < 5:
                            S.op("pe", lambda: nc.tensor.matmul(pa[:, 384:448], lhsT=N[a_][:], rhs=NTt[a_][:], start=True, stop=True),
                                 reads=[kj(f"NT{a_}"), kj(f"N{a_}")], writes=[pck])
                            yield
                        S.op(ev, copy_op(ev, N[b_][:], pa[:, 320:384]), reads=[pck], writes=[kj(f"N{b_}")])
                        yield
                        if lv < 5:
                            S.op(ev, copy_op(ev, NTt[b_][:], pa[:, 384:448]), reads=[pck], writes=[kj(f"NT{b_}")])
                            yield
                        S.op("pe", lambda: nc.tensor.matmul(pa[:, 448:512], lhsT=I64, rhs=PTm[a_][:], start=True, stop=False),
                             reads=["cm2", kj(f"PT{a_}")], writes=[pck])
                        yield
                        S.op("pe", lambda: nc.tensor.matmul(pa[:, 448:512], lhsT=N[b_][:], rhs=PTm[a_][:], start=False, stop=True),
                             reads=[kj(f"N{b_}"), kj(f"PT{a_}")], writes=[pck])
                        yield
                        if lv < 5:
                            S.op(ev, copy_op(ev, PTm[b_][:], pa[:, 448:512]), reads=[pck], writes=[kj(f"PT{b_}")])
                        else:
                            S.op(ev, copy_op(ev, TT[:], pa[:, 448:512]), reads=[pck], writes=[kj("TT")])
                        yield
                    vt = W2["VT"][:, j * dv:(j + 1) * dv] if dv == 64 else W2["VT"][:, :]
                    cs = slice(j * 64, (j + 1) * 64) if dk == 64 else slice(0, 128)
                    split = (pr.start == 64)
                    psy = pin[0]; psyk = "l1pin0"
                    yo = di * 256
                    if split:
                        S.op("pe", lambda: nc.tensor.matmul(psy[0:64, yo:yo + dv], lhsT=W2["ART"][pr, 0, :], rhs=W["Hb"][pr, :dv], start=True, stop=True),
                             reads=[k2("ART"), kj("Hb")], writes=[psyk])
                        yield
                        S.op("pe", lambda: nc.tensor.matmul(pc[0:64, 0:dv], lhsT=AKRK[:, 0:64], rhs=vt, start=True, stop=True),
                             reads=[kj("AKRK"), k2("VT")], writes=[pck])
                        yield
                        S.op("act", lambda: nc.scalar.copy(out=W["ztmp"][:, :dv], in_=psy[0:64, yo:yo + dv]), reads=[psyk], writes=[k_("ztmp")])
                        yield
                        S.op("dve", lambda: nc.vector.tensor_tensor(out=Zb[:, :dv], in0=pc[0:64, 0:dv], in1=W["ztmp"][:, :dv], op=ALU.add),
                             reads=[pck, k_("ztmp")], writes=[kj("Zb")])
                        yield
                    else:
                        S.op("pe", lambda: nc.tensor.matmul(pc[0:64, 0:dv], lhsT=W2["ART"][pr, 0, :], rhs=W["Hb"][pr, :dv], start=True, stop=False),
                             reads=[k2("ART"), kj("Hb")], writes=[pck])
                        yield
                        S.op("pe", lambda: nc.tensor.matmul(pc[0:64, 0:dv], lhsT=AKRK[:, 0:64], rhs=vt, start=False, stop=True),
                             reads=[kj("AKRK"), k2("VT")], writes=[pck])
                        yield
                        S.op(ev, copy_op(ev, Zb[:, :dv], pc[0:64, 0:dv]), reads=[pck], writes=[kj("Zb")])
                        yield
                    S.op("pe", lambda: nc.tensor.matmul(pc[0:64, 128:128 + dv], lhsT=TT[:], rhs=Zb[:, :dv], start=True, stop=True),
                         reads=[kj("TT"), kj("Zb")], writes=[pck])
                    yield
                    S.op(ev, copy_op(ev, Ub[:, :dv], pc[0:64, 128:128 + dv]), reads=[pck], writes=[kj("Ub")])
                    yield
                    if emit:
                        t_lat = c0 - NCTX
                        yk_ = f"Yacc{t_lat//64}.{pr.start}"
                        if split:
                            S.op("pe", lambda: nc.tensor.matmul(psy[pr, yo + 64:yo + 128], lhsT=W["Hb"][pr, :dv], rhs=W2["ART"][pr, 1, :], start=True, stop=True),
                                 reads=[k2("ART"), kj("Hb")], writes=[psyk])
                            yield
                            S.op("dve", lambda: nc.vector.tensor_tensor(out=K.yacc[pr, t_lat:t_lat + 64], in0=psy[pr, yo + 64:yo + 128],
                                                                        in1=K.yacc[pr, t_lat:t_lat + 64], op=ALU.add),
                                 reads=[psyk, yk_], writes=[yk_])
                            yield
                        else:
                            S.op("pe", lambda: nc.tensor.matmul(pc[pr, 256:320], lhsT=W["Hb"][pr, :dv], rhs=W2["ART"][pr, 1, :], start=True, stop=False),
                                 reads=[k2("ART"), kj("Hb")], writes=[pck])
                            yield
                        S.op("pe", lambda: nc.tensor.matmul(pc[pr, 256:320], lhsT=Ub[:, :dv], rhs=ARB[:], start=split, stop=False),
                             reads=[kj("Ub"), kj("ARB")], writes=[pck])
                        yield
                        S.op("pe", lambda: nc.tensor.matmul(pc[pr, 256:320], lhsT=vt, rhs=AKRK[:, 64:128], start=False, stop=True),
                             reads=[kj("AKRK"), k2("VT")], writes=[pck])
                        yield
                        S.op("dve", lambda: nc.vector.tensor_tensor(out=K.yacc[pr, t_lat:t_lat + 64], in0=pc[pr, 256:320],
                                                                    in1=K.yacc[pr, t_lat:t_lat + 64], op=ALU.add),
                             reads=[pck, yk_], writes=[yk_])
                        yield
                    S.op("pe", lambda: nc.tensor.matmul(pc[pr, 320:320 + dv], lhsT=W2["BTt"][:, cs], rhs=Ub[:, :dv], start=True, stop=False),
                         reads=[k2("BTt"), kj("Ub")], writes=[pck])
                    yield
                    S.op("pe", lambda: nc.tensor.matmul(pc[pr, 320:320 + dv], lhsT=W2["KTt"][:, cs], rhs=vt, start=False, stop=True),
                         reads=[k2("KTt"), k2("VT")], writes=[pck])
                    yield
                    S.op("dve", lambda: nc.vector.scalar_tensor_tensor(out=W["Hs"][pr, :dv], in0=W["Hs"][pr, :dv], scalar=W2["Et"][pr, 0:1],
                                                                       in1=pc[pr, 320:320 + dv], op0=ALU.mult, op1=ALU.add),
                         reads=[pck, kj("Hs"), k2("Et")], writes=[kj("Hs")])
                    yield
                    S.op("act", lambda: nc.scalar.copy(out=W["Hb"][pr, :dv], in_=W["Hs"][pr, :dv]), reads=[kj("Hs")], writes=[kj("Hb")])
                    yield

                def round_robin(gens):
                    gens = list(gens)
                    while gens:
                        for g in list(gens):
                            try:
                                next(g)
                            except StopIteration:
                                gens.remove(g)

                def reset_state(insts):
                    for dn in ("f", "b"):
                        S.op("pool", lambda: nc.gpsimd.memset(wk[dn]["Hs"][:], 0.0), writes=[f"Hs{dn}{j}" for j in range(2)])
                        S.op("pool", lambda: nc.gpsimd.memset(wk[dn]["Hb"][:], 0.0), writes=[f"Hb{dn}{j}" for j in range(2)])
                    S.op("pool", lambda: nc.gpsimd.memset(K.yacc, 0.0), writes=[f"Yacc{i}.{p}" for i in range(32) for p in (0, 64)])

                TW = xb(10); AL = xb(11); SG = xb(12)
                if DBG['lora']:
                    load_cols([(1536, 384)])
                    proj_taps(0, 128, shift_taps(12), AF.Tanh, lambda s0, n: TW[:, s0:s0 + n], ["TW"])
                    proj_taps(128, 128, shift_taps(13), AF.Identity, lambda s0, n: AL[:, s0:s0 + n], ["AL"])
                    proj_taps(256, 128, shift_taps(14), AF.Sigmoid, lambda s0, n: SG[:, s0:s0 + n], ["SG"])
                lw2 = sb(L, "lw2", [128, 3, 512], BF16)
                K.lw2flat = lw2[:].rearrange("p a n -> p (a n)")
                for i, src in enumerate((cw2_d, ca2_d, cg2_d)):
                    wload(stg, "l1stg", lw2[:, i, :], ["lw2"], src, [128, 512], "sp")
                sq = sb(L, "l1sq", [128, 256], BF16)
                t32 = [sb(L, f"l1t{i}", [128, 256]) for i in range(3)]
                TCH = [(i * 256, 256) for i in range(9)]

                for gi in range(DBG['ng']):
                    rA, kA, vA, kkA, afA, abA = (xb(i) for i in range(6))
                    lwF = xf(3); lwB = xf(4)
                    gA = xb(13)
                    load_cols([(gi * 128, 128), (512 + gi * 128, 128), (1024 + gi * 128, 128)])
                    proj_taps(0, 128, shift_taps(gi), AF.Identity, lambda s0, n: rA[:, s0:s0 + n], ["rA"])
                    proj_taps(128, 128, shift_taps(4 + gi), AF.Identity, lambda s0, n: kA[:, s0:s0 + n], ["kA"])
                    proj_taps(256, 128, shift_taps(8 + gi), AF.Identity, lambda s0, n: vA[:, s0:s0 + n], ["vA"])
                    for (t0, n) in (TCH if DBG['prep'] else []):
                        S.op("dve", lambda: nc.vector.tensor_scalar(out=t32[0][:, :n], in0=kA[:, t0:t0 + n], scalar1=v1[:, 45 + gi:46 + gi], scalar2=None,
                                                                    op0=ALU.mult), reads=["kA", "v1"], writes=["l1t0"])
                        S.op("act", lambda: nc.scalar.activation(out=sq[:, :n], in_=t32[0][:, :n], func=AF.Square), reads=["l1t0"], writes=["l1sq"])
                        S.op("pe", lambda: nc.tensor.matmul(pin[0][:, :n], lhsT=CM("onesA"), rhs=sq[:, :n], start=True, stop=True),
                             reads=["l1sq", "cm_b"], writes=["l1pin0"])
                        S.op("act", lambda: nc.scalar.activation(out=t32[1][:, :n], in_=pin[0][:, :n], func=AF.Sqrt, scale=64.0, bias=eps_t[:, 0:1]),
                             reads=["l1pin0", "eps"], writes=["l1t1"])
                        S.op("dve", lambda: nc.vector.reciprocal(out=t32[1][:, :n], in_=t32[1][:, :n]), reads=["l1t1"], writes=["l1t1"])
                        S.op("dve", lambda: nc.vector.tensor_tensor(out=kkA[:, t0:t0 + n], in0=t32[0][:, :n], in1=t32[1][:, :n], op=ALU.mult),
                             reads=["l1t0", "l1t1"], writes=["kkA"])
                        for d_, (lwX, aX) in enumerate(((lwF, afA), (lwB, abA))):
                            prd = slice(d_ * 64, (d_ + 1) * 64)
                            S.op("pe", lambda: nc.tensor.matmul(pin[1][:, :n], lhsT=lw2[prd, 0, gi * 128:(gi + 1) * 128], rhs=TW[prd, t0:t0 + n],
                                                                start=True, stop=True), reads=["lw2", "TW"], writes=["l1pin1"])
                            S.op("act", lambda: nc.scalar.activation(out=t32[2][:, :n], in_=pin[1][:, :n], func=AF.Sigmoid,
                                                                     bias=v1[:, 49 + d_ * 4 + gi:50 + d_ * 4 + gi]),
                                 reads=["l1pin1", "v1"], writes=["l1t2"])
                            S.op("dve", lambda: nc.vector.tensor_scalar(out=lwX[:, t0:t0 + n], in0=t32[2][:, :n], scalar1=-0.6065306597126334,
                                                                        scalar2=None, op0=ALU.mult), reads=["l1t2"], writes=["lwX"])
                            S.op("pe", lambda: nc.tensor.matmul(pin[1][:, :n], lhsT=lw2[prd, 1, gi * 128:(gi + 1) * 128], rhs=AL[prd, t0:t0 + n],
                                                                start=True, stop=True), reads=["lw2", "AL"], writes=["l1pin1"])
                            S.op("act", lambda: nc.scalar.activation(out=aX[:, t0:t0 + n], in_=pin[1][:, :n], func=AF.Sigmoid,
                                                                     bias=v1[:, 57 + d_ * 4 + gi:58 + d_ * 4 + gi]),
                                 reads=["l1pin1", "v1"], writes=["aX"])
                        S.op("pe", lambda: nc.tensor.matmul(pin[1][:, :n], lhsT=lw2[:, 2, gi * 128:(gi + 1) * 128], rhs=SG[:, t0:t0 + n],
                                                            start=True, stop=True), reads=["lw2", "SG"], writes=["l1pin1"])
                        S.op("act", lambda: nc.scalar.copy(out=gA[:, t0:t0 + n], in_=pin[1][:, :n]), reads=["l1pin1"], writes=["gA"])
                    insts = [(slice(0, 64), 64, 64), (slice(64, 128), 64, 64)]
                    K.yacc = Yall[:, gi, :]
                    reset_state(insts)
                    def rw_pre(dn, aX, lwX, csl, gi_):
                        W = wk[dn]
                        return [
                            lambda: S.op("dve", lambda: nc.vector.tensor_scalar(out=W["al"][:], in0=kkA[:, csl], scalar1=-1.0, scalar2=None, op0=ALU.mult),
                                         reads=["kkA"], writes=[f"al{dn}"]),
                            lambda: S.op("pool", lambda: nc.gpsimd.tensor_tensor(out=W["be"][:], in0=kkA[:, csl], in1=aX[:, csl], op=ALU.mult),
                                         reads=["kkA", "aX"], writes=[f"be{dn}"]),
                            lambda: S.op("dve", lambda: nc.vector.tensor_scalar(out=W["kd"][:], in0=aX[:, csl], scalar1=-1.0, scalar2=v1[:, 65 + gi_:66 + gi_],
                                                                                op0=ALU.add, op1=ALU.mult), reads=["aX", "v1"], writes=[f"kd{dn}"]),
                            lambda: S.op("dve", lambda: nc.vector.scalar_tensor_tensor(out=W["kd"][:], in0=W["kd"][:], scalar=1.0, in1=kA[:, csl],
                                                                                       op0=ALU.add, op1=ALU.mult), reads=[f"kd{dn}", "kA"], writes=[f"kd{dn}"]),
                            lambda: S.op("pool", lambda: nc.gpsimd.tensor_copy(out=W["rr"][:], in_=rA[:, csl]), reads=["rA"], writes=[f"rr{dn}"]),
                            lambda: S.op("pool", lambda: nc.gpsimd.tensor_copy(out=W["lw"][:], in_=lwX[:, csl]), reads=["lwX"], writes=[f"lw{dn}"]),
                        ]
                    prev_h = []
                    for step in range(DBG['nstep'] + 1):
                        pg = []; hg = []
                        if step < DBG['nstep']:
                            for dn, order, aX, lwX in (("f", fwd_order, afA, lwF), ("b", bwd_order, abA, lwB)):
                                ch = order[step]; c0 = ch * 64
                                csl = slice(c0, c0 + 64)
                                pg.append(prefix_gen(dn, rw_pre(dn, aX, lwX, csl, gi), vA[:, csl], ["vA"], False, step % 2))
                                for j, (pr, dk, dv) in enumerate(insts):
                                    hg.append(head_gen(dn, j, pr, dk, dv, c0, ch >= 4, False, step % 2))
                        round_robin(prev_h + pg)
                        prev_h = hg
                    for qi in range(8 if DBG['rwout'] else 0):
                        t0 = NCTX + qi * 256; n = 256; y0 = qi * 256
                        yk = [f"Yacc{i}.{p}" for i in range(y0 // 64, y0 // 64 + 4) for p in (0, 64)]
                        S.op("pe", lambda: nc.tensor.matmul(pin[0][:, :n], lhsT=CM("onesA"), rhs=K.yacc[:, y0:y0 + n], start=True, stop=True),
                             reads=yk + ["cm_b"], writes=["l1pin0"])
                        S.op("dve", lambda: nc.vector.tensor_tensor(out=t32[0][:, :n], in0=K.yacc[:, y0:y0 + n], in1=pin[0][:, :n], op=ALU.subtract),
                             reads=yk + ["l1pin0"], writes=["l1t0"])
                        S.op("act", lambda: nc.scalar.activation(out=sq[:, :n], in_=t32[0][:, :n], func=AF.Square), reads=["l1t0"], writes=["l1sq"])
                        S.op("pe", lambda: nc.tensor.matmul(pin[0][:, :n], lhsT=CM("onesA"), rhs=sq[:, :n], start=True, stop=True),
                             reads=["l1sq", "cm_b"], writes=["l1pin0"])
                        S.op("act", lambda: nc.scalar.activation(out=t32[1][:, :n], in_=pin[0][:, :n], func=AF.Sqrt, bias=gneps[:, 0:1]),
                             reads=["l1pin0", "gneps"], writes=["l1t1"])
                        S.op("dve", lambda: nc.vector.reciprocal(out=t32[1][:, :n], in_=t32[1][:, :n]), reads=["l1t1"], writes=["l1t1"])
                        S.op("dve", lambda: nc.vector.tensor_tensor(out=t32[0][:, :n], in0=t32[0][:, :n], in1=t32[1][:, :n], op=ALU.mult),
                             reads=["l1t0", "l1t1"], writes=["l1t0"])
                        S.op("act", lambda: nc.scalar.activation(out=t32[0][:, :n], in_=t32[0][:, :n], func=AF.Identity,
                                                                 scale=v1[:, 69 + gi:70 + gi], bias=v1[:, 73 + gi:74 + gi]),
                             reads=["l1t0", "v1"], writes=["l1t0"])
                        S.op("dve", lambda: nc.vector.tensor_tensor(out=t32[1][:, :n], in0=afA[:, t0:t0 + n], in1=abA[:, t0:t0 + n], op=ALU.add),
                             reads=["aX"], writes=["l1t1"])
                        S.op("dve", lambda: nc.vector.tensor_scalar(out=t32[1][:, :n], in0=t32[1][:, :n], scalar1=-2.0, scalar2=v1[:, 65 + gi:66 + gi],
                                                                    op0=ALU.add, op1=ALU.mult), reads=["l1t1", "v1"], writes=["l1t1"])
                        S.op("dve", lambda: nc.vector.scalar_tensor_tensor(out=t32[1][:, :n], in0=t32[1][:, :n], scalar=2.0, in1=kA[:, t0:t0 + n],
                                                                           op0=ALU.add, op1=ALU.mult), reads=["l1t1", "kA"], writes=["l1t1"])
                        S.op("dve", lambda: nc.vector.scalar_tensor_tensor(out=sq[:, :n], in0=t32[1][:, :n], scalar=v1[:, 77 + gi:78 + gi],
                                                                           in1=rA[:, t0:t0 + n], op0=ALU.mult, op1=ALU.mult),
                             reads=["l1t1", "rA", "v1"], writes=["l1sq"])
                        S.op("pe", lambda: nc.tensor.matmul(pin[1][:, :n], lhsT=CM("onesA"), rhs=sq[:, :n], start=True, stop=True),
                             reads=["l1sq", "cm_b"], writes=["l1pin1"])
                        S.op("dve", lambda: nc.vector.scalar_tensor_tensor(out=t32[2][:, :n], in0=pin[1][:, :n], scalar=64.0, in1=vA[:, t0:t0 + n],
                                                                           op0=ALU.mult, op1=ALU.mult), reads=["l1pin1", "vA"], writes=["l1t2"])
                        S.op("pool", lambda: nc.gpsimd.tensor_tensor(out=t32[0][:, :n], in0=t32[0][:, :n], in1=t32[2][:, :n], op=ALU.add),
                             reads=["l1t0", "l1t2"], writes=["l1t0"])
                        S.op("dve", lambda: nc.vector.tensor_tensor(out=Yall[:, gi, y0:y0 + n], in0=t32[0][:, :n], in1=gA[:, t0:t0 + n], op=ALU.mult),
                             reads=["l1t0", "gA"], writes=["Yall"])
                    S.barrier()

                sm = XF[0:16, 7 * T:8 * T]
                smb = sb(L, "smb", [16, 256])
                smg = sb(L, "smg", [16, 256])
                gsm = sb(L, "gsmp", [16, 2])
                sel = sb(L, "sel", [16, 16 * 128])
                S.dma("sp", gsm[:], gsm_d, writes=["gsm"])
                S.dma("sp", sel[:], sel_d, writes=["sel"])
                S.op("act", lambda: nc.scalar.activation(out=gsm[:, 0:1], in_=gsm[:, 0:1], func=AF.Exp), reads=["gsm"], writes=["gsm"])
                S.op("dve", lambda: nc.vector.tensor_scalar(out=gsm[:, 0:1], in0=gsm[:, 0:1], scalar1=-1.0, scalar2=None, op0=ALU.mult),
                     reads=["gsm"], writes=["gsm"])
                load_cols([(INC + 2048, 16)])
                proj_taps(0, 16, [(0, None)], AF.Identity, lambda s0, n: sm[:, s0:s0 + n], ["sm"])
                t32q = xb(12); t32k = xb(13)
                one_t = sb(L, "one_t", [128, 1])
                S.op("pool", lambda: nc.gpsimd.memset(one_t[:], 1.0), writes=["one_t"])

                for hd in range(DBG['gdn']):
                    qA, kA, vA, zA = (xb(i) for i in range(4))
                    bmF = xf(2); bmB = xf(3); gmF = xf(4); gmB = xf(5)
                    load_cols([(INC + hd * 128, 128), (INC + 512 + hd * 128, 128), (INC + 1024 + hd * 128, 128), (INC + 1536 + hd * 128, 128)])

                    def ctaps(fc):
                        return [(0, v1[:, 81 + fc * 5 + 2:81 + fc * 5 + 3])] + [(o, v1[:, 81 + fc * 5 + 2 + o:81 + fc * 5 + 3 + o]) for o in (-2, -1, 1, 2)]
                    proj_taps(0, 128, ctaps(hd), AF.Silu, lambda s0, n: t32q[:, s0:s0 + n], ["t32q"])
                    proj_taps(128, 128, ctaps(4 + hd), AF.Silu, lambda s0, n: t32k[:, s0:s0 + n], ["t32k"])
                    proj_taps(256, 128, ctaps(8 + hd), AF.Silu, lambda s0, n: vA[:, s0:s0 + n], ["vA"])
                    proj_taps(384, 128, [(0, None)], AF.Silu, lambda s0, n: zA[:, s0:s0 + n], ["zA"])
                    for (t0, n) in (TCH if DBG['gprep'] >= 2 else []):
                        for (srcb, dstb, scl, dk_) in ((t32q, qA, 128.0 ** -0.5, "qA"), (t32k, kA, 1.0, "kA")):
                            S.op("act", lambda: nc.scalar.activation(out=sq[:, :n], in_=srcb[:, t0:t0 + n], func=AF.Square), reads=["t32q", "t32k"], writes=["l1sq"])
                            S.op("pe", lambda: nc.tensor.matmul(pin[0][:, :n], lhsT=CM("ones1024"), rhs=sq[:, :n], start=True, stop=True),
                                 reads=["l1sq", "cm_b"], writes=["l1pin0"])
                            S.op("act", lambda: nc.scalar.activation(out=t32[1][:, :n], in_=pin[0][:, :n], func=AF.Sqrt, scale=1024.0, bias=eps_t[:, 0:1]),
                                 reads=["l1pin0", "eps"], writes=["l1t1"])
                            S.op("dve", lambda: nc.vector.reciprocal(out=t32[1][:, :n], in_=t32[1][:, :n]), reads=["l1t1"], writes=["l1t1"])
                            S.op("dve", lambda: nc.vector.scalar_tensor_tensor(out=dstb[:, t0:t0 + n], in0=srcb[:, t0:t0 + n], scalar=float(scl),
                                                                               in1=t32[1][:, :n], op0=ALU.mult, op1=ALU.mult),
                                 reads=["t32q", "t32k", "l1t1"], writes=[dk_])
                        S.op("act", lambda: nc.scalar.activation(out=smb[:, :n], in_=sm[:, t0:t0 + n], func=AF.Sigmoid), reads=["sm"], writes=["smb"])
                        S.op("act", lambda: nc.scalar.activation(out=smg[:, :n], in_=sm[:, t0:t0 + n], func=AF.Exp, bias=gsm[:, 1:2]),
                             reads=["sm", "gsm"], writes=["smg"])
                        S.op("act", lambda: nc.scalar.activation(out=smg[:, :n], in_=smg[:, :n], func=AF.Ln, bias=one_t[0:16, 0:1]),
                             reads=["smg", "one_t"], writes=["smg"])
                        S.op("dve", lambda: nc.vector.tensor_scalar(out=smg[:, :n], in0=smg[:, :n], scalar1=gsm[:, 0:1], scalar2=None, op0=ALU.mult),
                             reads=["smg", "gsm"], writes=["smg"])
                        for (row, dst, srcm, dk_) in (((hd, bmF, smb, "bmF"), (4 + hd, bmB, smb, "bmB"), (8 + hd, gmF, smg, "gmF"), (12 + hd, gmB, smg, "gmB")) if DBG['gprep'] >= 3 else []):
                            S.op("pe", lambda: nc.tensor.matmul(pin[1][:, :n], lhsT=sel[:, row * 128:(row + 1) * 128], rhs=srcm[:, :n],
                                                                start=True, stop=True), reads=["sel", "smb", "smg"], writes=["l1pin1"])
                            e = evac_eng()
                            S.op(e, copy_op(e, dst[:, t0:t0 + n], pin[1][:, :n]), reads=["l1pin1"], writes=[dk_])
                    insts = [(slice(0, 128), 128, 128)]
                    K.yacc = Yall[:, 4 + hd, :]
                    reset_state(insts)
                    def gd_pre(dn, bmX, gmX, csl):
                        W = wk[dn]
                        return [
                            lambda: S.op("pool", lambda: nc.gpsimd.tensor_copy(out=W["al"][:], in_=kA[:, csl]), reads=["kA"], writes=[f"al{dn}"]),
                            lambda: S.op("dve", lambda: nc.vector.tensor_tensor(out=W["kd"][:], in0=kA[:, csl], in1=bmX[:, csl], op=ALU.mult),
                                         reads=["kA", "bmF", "bmB"], writes=[f"kd{dn}"]),
                            lambda: S.op("pool", lambda: nc.gpsimd.tensor_copy(out=W["lw"][:], in_=gmX[:, csl]), reads=["gmF", "gmB"], writes=[f"lw{dn}"]),
                            lambda: S.op("act", lambda: nc.scalar.activation(out=W["be"][:], in_=gmX[:, csl], func=AF.Exp), reads=["gmF", "gmB"], writes=[f"be{dn}"]),
                            lambda: S.op("dve", lambda: nc.vector.scalar_tensor_tensor(out=W["be"][:], in0=W["be"][:], scalar=-1.0, in1=W["kd"][:],
                                                                                       op0=ALU.mult, op1=ALU.mult), reads=[f"be{dn}", f"kd{dn}"], writes=[f"be{dn}"]),
                            lambda: S.op("pool", lambda: nc.gpsimd.tensor_copy(out=W["rr"][:], in_=qA[:, csl]), reads=["qA"], writes=[f"rr{dn}"]),
                        ]
                    prev_h = []
                    for step in range(DBG['nstep'] + 1):
                        pg = []; hg = []
                        for dn, order, bmX, gmX in ((("f", fwd_order, bmF, gmF), ("b", bwd_order, bmB, gmB)) if step < DBG['nstep'] else ()):
                            ch = order[step]; c0 = ch * 64
                            csl = slice(c0, c0 + 64)
                            pg.append(prefix_gen(dn, gd_pre(dn, bmX, gmX, csl), vA[:, csl], ["vA"], True, step % 2))
                            for j, (pr, dk, dv) in enumerate(insts):
                                hg.append(head_gen(dn, j, pr, dk, dv, c0, ch >= 4, True, step % 2))
                        round_robin(prev_h + pg)
                        prev_h = hg
                    for qi in range(8):
                        t0 = NCTX + qi * 256; n = 256; y0 = qi * 256
                        yk = [f"Yacc{i}.{p}" for i in range(y0 // 64, y0 // 64 + 4) for p in (0, 64)]
                        S.op("act", lambda: nc.scalar.activation(out=sq[:, :n], in_=K.yacc[:, y0:y0 + n], func=AF.Square), reads=yk, writes=["l1sq"])
                        S.op("pe", lambda: nc.tensor.matmul(pin[0][:, :n], lhsT=CM("ones1024"), rhs=sq[:, :n], start=True, stop=True),
                             reads=["l1sq", "cm_b"], writes=["l1pin0"])
                        S.op("act", lambda: nc.scalar.activation(out=t32[1][:, :n], in_=pin[0][:, :n], func=AF.Sqrt, scale=8.0, bias=eps_t[:, 0:1]),
                             reads=["l1pin0", "eps"], writes=["l1t1"])
                        S.op("dve", lambda: nc.vector.reciprocal(out=t32[1][:, :n], in_=t32[1][:, :n]), reads=["l1t1"], writes=["l1t1"])
                        S.op("dve", lambda: nc.vector.scalar_tensor_tensor(out=t32[0][:, :n], in0=K.yacc[:, y0:y0 + n], scalar=v1[:, 141:142],
                                                                           in1=t32[1][:, :n], op0=ALU.mult, op1=ALU.mult),
                             reads=yk + ["l1t1", "v1"], writes=["l1t0"])
                        S.op("dve", lambda: nc.vector.tensor_tensor(out=Yall[:, 4 + hd, y0:y0 + n], in0=t32[0][:, :n], in1=zA[:, t0:t0 + n], op=ALU.mult),
                             reads=["l1t0", "zA"], writes=["Yall"])
                    S.barrier()
                LS.close()
                L = L0
                if dbg == "y1":
                    dump_fm(Yall, 8, BF16, 16)
                stg2 = sb(L, "l1stg2", [128, 2048])
                wo = sb(L, "l1wo", [128, 8, D], BF16)
                pin = [ps(L, f"l1pinb{i}", [128, 512]) for i in range(2)]
                cov = cdout_d.rearrange("(kc p) n -> p kc n", p=128)
                for i in range(4):
                    wload(stg2, "l1stg2", wo[:, 2 * i:2 * i + 2, :], ["l1wo"], cov[:, 2 * i:2 * i + 2, :], [128, 2, D], "sp")
                for c in range(8):
                    S.dma("sp" if c % 2 == 0 else "pool", X[:, c, :], xpark_d[:, c * T:(c + 1) * T], writes=[f"X{t}" for t in range(NT)])
                S.barrier()
                it = 0
                for qi in range(4):
                    t0 = NCTX + qi * 512; n = 512; y0 = qi * 512
                    for m in range(8):
                        p_ = pin[it % 2]; pk = f"l1pinb{it%2}"; it += 1
                        for i in range(8):
                            S.op("pe", lambda: nc.tensor.matmul(p_[:, :n], lhsT=wo[:, i, m * 128:(m + 1) * 128], rhs=Yall[:, i, y0:y0 + n],
                                                                start=(i == 0), stop=(i == 7)), reads=["l1wo", "Yall"], writes=[pk])
                        S.op("dve", lambda: nc.vector.scalar_tensor_tensor(out=X[:, m, t0:t0 + n], in0=p_[:, :n], scalar=mod(l, 2, m, 0),
                                                                           in1=X[:, m, t0:t0 + n], op0=ALU.mult, op1=ALU.add),
                             reads=[pk, "modv"] + xkeys(t0, n), writes=xkeys(t0, n))
            S.barrier()

        K.stop = False
        if dbg in ("load", "mods"):
            write_out()
            S.finish()
            return nc, S
        layer0()
        if dbg in ("qa", "oa"):
            S.finish()
            return nc, S
        if dbg != "noffn":
            ffn(0, nlayers == 1)
        if nlayers == 2:
            layer1()
            if dbg not in ("noffn1", "y1"):
                ffn(1, True)
        write_out()
        S.finish()
    return nc, S


def make_in_maps(inputs, cores):
    consts, cnames = _consts()
    f = lambda k: np.asarray(inputs[k], np.float32)
    c_ctx = f("c_ctx")
    shared = {
        "ada_w": np.ascontiguousarray(f("ada_w")),
        "ada_b_fm": np.stack([_fm(f("ada_b")[l]) for l in range(2)], 0),
        "ffn_w_up": np.ascontiguousarray(f("ffn_w_up")),
        "ffn_w_down": np.ascontiguousarray(f("ffn_w_down")),
        "ab_w_in": np.ascontiguousarray(f("ab_w_in")[0]),
        "ab_w_out": np.ascontiguousarray(f("ab_w_out")[0]),
        "b_w_uq": np.ascontiguousarray(f("b_w_uq")[0]),
        "b_w_uk": np.ascontiguousarray(f("b_w_uk")[0]),
        "b_w_uv": np.ascontiguousarray(f("b_w_uv")[0]),
        "cmats": consts["cmats"], "cosA": consts["cosA"], "sinA": consts["sinA"],
        "cosB": consts["cosB"], "sinB": consts["sinB"],
    }
    fc = np.zeros((2, 128, 44, 4), np.float32)
    for l in range(2):
        for j in range(3):
            fc[l, :, :, j] = _fm(f("ffn_conv_w")[l, j])
        fc[l, :, :, 3] = _fm(f("ffn_conv_b")[l])
    shared["ffn_conv_fm"] = fc
    v = np.zeros((128, 16), np.float32)
    v[:, 0] = np.tile(f("a_q_norm")[0], 2)
    v[:, 1] = np.tile(f("a_k_norm")[0], 2)
    v[:, 2:4] = _fm(f("b_cq_norm")[0])
    v[:, 4:6] = _fm(f("b_ckv_norm")[0])
    v[64:96, 6] = f("b_kr_norm")[0]
    v[:64, 7] = f("b_kn_norm")[0]
    v[:64, 8] = f("b_qn_norm")[0]
    v[64:96, 8] = f("b_qr_norm")[0]
    shared["vecs0"] = v
    shared["sink_b"] = np.ascontiguousarray(np.broadcast_to(f("a_sink")[0][None, :], (64, 8)))
    shared["cd_w_in"] = np.ascontiguousarray(f("cd_w_in")[0])
    shared["cd_w_out"] = np.ascontiguousarray(f("cd_w_out")[0])
    shared["c_w2r"] = np.ascontiguousarray(f("c_w2")[0].reshape(128, 512))
    shared["c_a2r"] = np.ascontiguousarray(f("c_a2")[0].reshape(128, 512))
    shared["c_g2"] = np.ascontiguousarray(f("c_g2")[0])
    v1 = np.zeros((128, 160), np.float32)
    v1[:, 0:15] = _fm(f("c_mu_prev")[0]); v1[:, 15:30] = _fm(f("c_mu_next")[0])
    v1[:, 45:49] = _fm(f("c_k_k")[0])
    for d_ in range(2):
        v1[:, 49 + d_ * 4:53 + d_ * 4] = _fm(f("c_w0")[0, d_])
        v1[:, 57 + d_ * 4:61 + d_ * 4] = _fm(f("c_a0")[0, d_])
    v1[:, 65:69] = _fm(f("c_k_a")[0]); v1[:, 69:73] = _fm(f("c_ln_w")[0]); v1[:, 73:77] = _fm(f("c_ln_b")[0])
    v1[:, 77:81] = _fm(f("c_r_k")[0].reshape(-1))
    dcw = f("d_conv_w")[0]
    for fc in range(12):
        for j in range(5):
            v1[:, 81 + fc * 5 + j] = dcw[j, fc * 128:(fc + 1) * 128]
    v1[:, 141] = f("d_o_norm")[0]
    shared["vecs1"] = v1
    gsm = np.zeros((16, 2), np.float32)
    for d_ in range(2):
        gsm[8 + 4 * d_:12 + 4 * d_, 0] = f("d_A_log")[0, d_]
        gsm[8 + 4 * d_:12 + 4 * d_, 1] = f("d_dt_bias")[0, d_]
    shared["gsm"] = gsm
    a_ = np.arange(64)[:, None]; b_ = np.arange(64)[None, :]
    shared["cm2"] = np.concatenate([(a_ < b_), (a_ <= b_), (a_ > b_), (a_ >= b_), (a_ == b_)], 1).astype(np.float32)
    sel = np.zeros((16, 16 * 128), np.float32)
    for q in range(16):
        sel[q, q * 128:(q + 1) * 128] = 1.0
    shared["sel16"] = sel
    maps = []
    for b in cores:
        m = dict(shared)
        m["x"] = np.ascontiguousarray(f("x")[b])
        m["ctx"] = np.ascontiguousarray(f("ctx")[b])
        m["cv"] = np.ascontiguousarray(np.stack([_fm(f("c")[b]), _fm(c_ctx)], -1))
        maps.append(m)
    return maps


def kernel(**inputs):
    nc, S = build_program()
    maps = make_in_maps(inputs, list(range(8)))
    res = run_bass_kernel_spmd(nc, maps, core_ids=list(range(8)))
    return np.stack([r["out"] for r in res.results], 0).astype(np.float32)
```

```python
import numpy as np
from contextlib import ExitStack
import concourse.bass as bass
import concourse.mybir as mybir
from concourse.bass_utils import run_bass_kernel_spmd

F32 = mybir.dt.float32
BF16 = mybir.dt.bfloat16
AF = mybir.ActivationFunctionType
ALU = mybir.AluOpType

D = 1024
T = 2304
NCTX = 256
NT = 18
EPS = 1e-6
CHUNKS = [(0, 256), (256, 512), (768, 512), (1280, 512), (1792, 512)]
DFF = 2816
DBG = {'lora': 1, 'ng': 4, 'prep': 1, 'nstep': 36, 'rwout': 1, 'gdn': 4, 'gprep': 3, 'cs': 5, 'invbf': 0, 'invlv': 5, 'invev': 5, 'invp': 1, 'gseq': 0, 'gpool': 1}


class Sched:
    def __init__(self, nc, stack, n_dma_sems=12):
        self.nc = nc
        self.engs = {"pe": nc.tensor, "dve": nc.vector, "act": nc.scalar, "pool": nc.gpsimd, "sp": nc.sync}
        self.sem = {k: stack.enter_context(nc.semaphore("s_" + k)) for k in self.engs}
        self.cnt = {k: 0 for k in self.engs}
        self.dsem = [stack.enter_context(nc.semaphore(f"dq{i}")) for i in range(n_dma_sems)]
        self.dcnt = [0] * n_dma_sems
        self.dnext = 0
        self.seen = {k: {} for k in self.engs}
        self.lastw = {}
        self.readers = {}
        self.ninst = 0

    def _semobj(self, sk):
        return self.sem[sk] if isinstance(sk, str) else self.dsem[sk]

    def _wait(self, e, sk, val):
        if val <= 0 or self.seen[e].get(sk, 0) >= val:
            return
        self.engs[e].wait_ge(self._semobj(sk), val)
        self.seen[e][sk] = val

    def _deps(self, e, reads, writes):
        deps = {}

        def add(p):
            if p is not None and deps.get(p[0], 0) < p[1]:
                deps[p[0]] = p[1]

        for k in reads:
            add(self.lastw.get(k))
        for k in writes:
            add(self.lastw.get(k))
            for sk, v in self.readers.get(k, {}).items():
                add((sk, v))
        for sk, v in deps.items():
            if sk == "pe" and e == "pe":
                continue
            self._wait(e, sk, v)

    def _commit(self, tok, reads, writes):
        sk, v = tok
        for k in reads:
            self.readers.setdefault(k, {})[sk] = v
        for k in writes:
            self.lastw[k] = tok
            self.readers[k] = {}

    def op(self, e, ins_fn, reads=(), writes=()):
        self._deps(e, reads, writes)
        ins = ins_fn()
        self.cnt[e] += 1
        self.ninst += 1
        ins.then_inc(self.sem[e], 1)
        self._commit((e, self.cnt[e]), reads, writes)
        return ins

    def dma(self, e, out, in_, reads=(), writes=(), **kw):
        i = self.dnext
        self.dnext = (self.dnext + 1) % len(self.dsem)
        self._wait(e, i, self.dcnt[i])
        self._deps(e, reads, writes)
        ins = self.engs[e].dma_start(out=out, in_=in_, **kw)
        self.dcnt[i] += 16
        self.ninst += 1
        ins.then_inc(self.dsem[i], 16)
        self._commit((i, self.dcnt[i]), reads, writes)
        return ins

    def barrier(self):
        for e in self.engs:
            for o in self.engs:
                if o != e:
                    self._wait(e, o, self.cnt[o])
            for i in range(len(self.dsem)):
                self._wait(e, i, self.dcnt[i])
        self.lastw = {}
        self.readers = {}

    def finish(self):
        for o in self.engs:
            if o != "sp":
                self._wait("sp", o, self.cnt[o])
        for i in range(len(self.dsem)):
            self._wait("sp", i, self.dcnt[i])


def _rope_tables():
    theta = 10000.0
    s = np.arange(2048)
    row = (s // 64).astype(np.float64)
    col = (s % 64).astype(np.float64)

    def tab(nd):
        h = nd // 2
        half = h // 2
        inv = theta ** (-np.arange(half, dtype=np.float64) / half)
        cos = np.ones((nd, T)); sin = np.zeros((nd, T))
        for d_ in range(nd):
            b = d_ // h
            i = (d_ % h) % half
            pos = row if b == 0 else col
            ang = (pos.astype(np.float32)[:, None] * inv.astype(np.float32)[None, :])[:, i]
            cos[d_, NCTX:] = np.cos(ang.astype(np.float32))
            sin[d_, NCTX:] = np.sin(ang.astype(np.float32))
        R = np.zeros((nd, nd))
        for d_ in range(nd):
            e = d_ % h
            if e < half:
                R[d_, d_ + half] = -1.0
            else:
                R[d_, d_ - half] = 1.0
        return cos.astype(np.float32), sin.astype(np.float32), R.astype(np.float32)

    cA, sA, RA = tab(64)
    cB, sB, RB = tab(32)
    cosA = np.concatenate([cA, cA], 0); sinA = np.concatenate([sA, sA], 0)
    RA2 = np.zeros((128, 128), np.float32); RA2[:64, :64] = RA; RA2[64:, 64:] = RA
    cosB = np.ones((96, T), np.float32); sinB = np.zeros((96, T), np.float32)
    cosB[64:] = cB; sinB[64:] = sB
    RB96 = np.zeros((96, 96), np.float32); RB96[64:, 64:] = RB
    return cosA, sinA, RA2.T.copy(), cosB, sinB, RB96.T.copy()


def _consts():
    c = {}
    cosA, sinA, RAT, cosB, sinB, RBT = _rope_tables()
    c["cosA"] = cosA; c["sinA"] = sinA; c["cosB"] = cosB; c["sinB"] = sinB
    mats = {}
    mats["ident"] = np.eye(128, dtype=np.float32)
    mats["ones1024"] = np.full((128, 128), 1.0 / 1024, np.float32)
    mats["ones256"] = np.full((128, 128), 1.0 / 256, np.float32)
    o = np.zeros((128, 128), np.float32); o[:64, :64] = 1 / 64; o[64:, 64:] = 1 / 64
    mats["onesA"] = o
    o = np.zeros((128, 128), np.float32); o[:64, :64] = 1 / 64; o[64:96, 64:96] = 1 / 32
    mats["onesB"] = o
    mats["RAT"] = RAT
    r = np.zeros((128, 128), np.float32); r[:96, :96] = RBT
    mats["RBT"] = r
    a = np.arange(128)[:, None]; b = np.arange(128)[None, :]
    mats["maskPrevT"] = np.where(a <= b, 0.0, -30000.0).astype(np.float32)
    mats["maskNextT"] = np.where(b <= a, 0.0, -30000.0).astype(np.float32)
    names = list(mats.keys())
    c["cmats"] = np.stack([mats[n] for n in names], 1).astype(np.float32)
    return c, names


def _fm(v, p=128):
    v = np.asarray(v, np.float32)
    return np.ascontiguousarray(v.reshape(-1, p).T)


class Ctx:
    pass


def build_program(dbg=None, nlayers=2):
    nc = bass.Bass("TRN2", target_bir_lowering=False)
    st = ExitStack()
    K = Ctx()
    with st:
        S = Sched(nc, st)

        def din(name, shape, dt=F32):
            return nc.dram_tensor(name, list(shape), dt, kind="ExternalInput").ap()

        uid = [0]

        def sb(stack, name, shape, dt=F32):
            uid[0] += 1
            return stack.enter_context(nc.sbuf_tensor(f"{name}_s{uid[0]}", list(shape), dt))

        def ps(stack, name, shape, dt=F32):
            uid[0] += 1
            return stack.enter_context(nc.psum_tensor(f"{name}_p{uid[0]}", list(shape), dt))

        consts, cnames = _consts()
        NCM = len(cnames)
        x_d = din("x", [2048, D]); ctx_d = din("ctx", [NCTX, D])
        cv_d = din("cv", [128, 8, 2])
        adaw_d = din("ada_w", [2, D, 6 * D]); adab_d = din("ada_b_fm", [2, 128, 48])
        wup_d = din("ffn_w_up", [2, D, 2 * DFF]); wdn_d = din("ffn_w_down", [2, DFF, D])
        fconv_d = din("ffn_conv_fm", [2, 128, 44, 4])
        abin_d = din("ab_w_in", [D, 1312]); about_d = din("ab_w_out", [D, D])
        wuq_d = din("b_w_uq", [256, 768]); wuk_d = din("b_w_uk", [256, 512]); wuv_d = din("b_w_uv", [256, 512])
        vecs_d = din("vecs0", [128, 16])
        sink_d = din("sink_b", [64, 8])
        cmats_d = din("cmats", [128, NCM, 128])
        cosA_d = din("cosA", [128, T]); sinA_d = din("sinA", [128, T])
        cosB_d = din("cosB", [96, T]); sinB_d = din("sinB", [96, T])
        cdin_d = din("cd_w_in", [D, 3984]); cdout_d = din("cd_w_out", [D, D])
        cw2_d = din("c_w2r", [128, 512]); ca2_d = din("c_a2r", [128, 512]); cg2_d = din("c_g2", [128, 512])
        v1_d = din("vecs1", [128, 160])
        gsm_d = din("gsm", [16, 2])
        cm2_d = din("cm2", [64, 320])
        sel_d = din("sel16", [16, 16 * 128])
        xpark_d = nc.dram_tensor("xpark", [128, 8 * T], F32, kind="Internal").ap()
        out_d = nc.dram_tensor("out", [2048, D], F32, kind="ExternalOutput").ap()
        dbg_d = None
        if dbg is not None:
            dbg_d = nc.dram_tensor("dbg", [T, D], F32, kind="ExternalOutput").ap()

        X = sb(st, "X", [128, 8, T])
        identf = sb(st, "identf", [128, 128])
        cm_b = sb(st, "cm_b", [128, NCM, 128], BF16)
        id4 = sb(st, "id4", [128, 4, 128], BF16)
        modv = sb(st, "modv", [128, 2, 48, 2])
        onep = sb(st, "onep", [128, 2, 2, 8, 2])
        adab = sb(st, "adab", [128, 2, 48])
        cv = sb(st, "cv", [128, 8, 2])
        scv = sb(st, "scv", [128, 8, 2])
        vecs = sb(st, "vecs", [128, 16])
        zeros = sb(st, "zeros", [128, 128])

        def CM(name, bf=True):
            if not bf:
                assert name == "ident"
                return identf[:]
            i = cnames.index(name)
            return cm_b[:, i, :]

        rr = {"cast": 0, "ev": 0}

        def evac_eng():
            rr["ev"] ^= 1
            return "act" if rr["ev"] else "dve"

        def copy_op(e, out, in_):
            if e == "act":
                return lambda: nc.scalar.copy(out=out, in_=in_)
            if e == "dve":
                return lambda: nc.vector.tensor_copy(out=out, in_=in_)
            return lambda: nc.gpsimd.tensor_copy(out=out, in_=in_)

        with ExitStack() as ph:
            cm_f = sb(ph, "cm_f", [128, NCM, 128])
            S.dma("sp", cm_f[:], cmats_d, writes=["cm_f0"])
            S.op("dve", lambda: nc.vector.tensor_copy(out=cm_b[:], in_=cm_f[:]), reads=["cm_f0"], writes=["cm_b"])
            S.op("act", lambda: nc.scalar.copy(out=identf[:], in_=cm_f[:, 0, :]), reads=["cm_f0"], writes=["cm_f"])
            for r in range(4):
                S.op("pool", lambda: nc.gpsimd.tensor_copy(out=id4[:, r, :], in_=cm_f[:, 0, :]), reads=["cm_f0"], writes=["id4"])
            S.barrier()
        S.dma("sp", cv[:], cv_d, writes=["cv"])
        S.dma("sp", adab[:], adab_d.rearrange("l p m -> p l m"), writes=["adab"])
        S.dma("sp", vecs[:], vecs_d, writes=["vecs"])
        S.op("pool", lambda: nc.gpsimd.memset(zeros[:], 0.0), writes=["zeros"])
        S.op("act", lambda: nc.scalar.activation(out=scv[:], in_=cv[:], func=AF.Silu), reads=["cv"], writes=["scv"])

        with ExitStack() as ph:
            xin = [sb(ph, f"xin{i}", [128, D]) for i in range(2)]
            pT = [ps(ph, f"pT{i}", [128, 512]) for i in range(4)]
            for t in range(NT):
                src = ctx_d[t * 128:(t + 1) * 128, :] if t < 2 else x_d[(t - 2) * 128:(t - 1) * 128, :]
                xi = xin[t % 2]
                S.dma("sp" if t % 2 == 0 else "pool", xi[:], src, writes=[f"xin{t%2}"])
                for hh in range(2):
                    pt = pT[(t % 2) * 2 + hh]
                    pk = f"pT{(t%2)*2+hh}"
                    for c4 in range(4):
                        c = hh * 4 + c4
                        S.op("pe", lambda: nc.tensor.transpose(out=pt[:, c4 * 128:(c4 + 1) * 128],
                                                               in_=xi[:, c * 128:(c + 1) * 128],
                                                               identity=CM("ident", False)),
                             reads=[f"xin{t%2}", "cm_f"], writes=[pk])
                    e = evac_eng()
                    S.op(e, copy_op(e, X[:, hh * 4:(hh + 1) * 4, t * 128:(t + 1) * 128],
                                    pt[:].rearrange("p (c n) -> p c n", c=4)),
                         reads=[pk], writes=[f"X{t}"])
        S.barrier()

        with ExitStack() as ph:
            astg = [sb(ph, f"astg{i}", [128, 8, 768]) for i in range(2)]
            pm = ps(ph, "pm", [128, 48, 2])
            adv = adaw_d.rearrange("l (kc p) n -> l p kc n", p=128)
            it = 0
            for l in range(nlayers):
                for g in range(8):
                    a = astg[it % 2]
                    S.dma("sp" if it % 2 == 0 else "pool", a[:], adv[l, :, :, g * 768:(g + 1) * 768],
                          writes=[f"astg{it%2}"])
                    for mm in range(6):
                        m = g * 6 + mm
                        for k in range(8):
                            S.op("pe", lambda: nc.tensor.matmul(pm[:, m, :], lhsT=a[:, k, mm * 128:(mm + 1) * 128],
                                                                rhs=scv[:, k, :], start=(k == 0), stop=(k == 7)),
                                 reads=[f"astg{it%2}", "scv"], writes=["pm"])
                    it += 1
                for w in range(2):
                    S.op("dve", lambda: nc.vector.tensor_tensor(out=modv[:, l, :, w], in0=pm[:, :, w], in1=adab[:, l, :],
                                                                op=ALU.add),
                         reads=["pm", "adab"], writes=["modv"])
                for ji, j in enumerate((1, 4)):
                    S.op("dve", lambda: nc.vector.tensor_scalar_add(out=onep[:, l, ji, :, :],
                                                                    in0=modv[:, l, j * 8:(j + 1) * 8, :], scalar1=1.0),
                         reads=["modv"], writes=["onep"])
        S.barrier()

        def mod(l, j, c, w):
            return modv[:, l, j * 8 + c, w:w + 1]

        def modulate_chunk(ph_bufs, l, jshift, ji_scale, t0, n, w, dst_fn, dst_keys):
            sq, pms, sd, rstd, tmp = ph_bufs
            for c in range(8):
                S.op("act", lambda: nc.scalar.activation(out=sq[c % 2][:, :n], in_=X[:, c, t0:t0 + n], func=AF.Square),
                     reads=[f"X{tt}" for tt in range(t0 // 128, (t0 + n) // 128)], writes=[f"msq{c%2}"])
                S.op("pe", lambda: nc.tensor.matmul(pms[:, :n], lhsT=CM("ones1024"), rhs=sq[c % 2][:, :n],
                                                    start=(c == 0), stop=(c == 7)),
                     reads=[f"msq{c%2}", "cm_b"], writes=["pms"])
            S.op("act", lambda: nc.scalar.activation(out=sd[:, :n], in_=pms[:, :n], func=AF.Sqrt, bias=eps_t[:, 0:1]),
                 reads=["pms", "eps"], writes=["msd"])
            S.op("dve", lambda: nc.vector.reciprocal(out=rstd[:, :n], in_=sd[:, :n]), reads=["msd"], writes=["mrstd"])
            for c in range(8):
                S.op("dve", lambda: nc.vector.scalar_tensor_tensor(out=tmp[c % 2][:, :n], in0=X[:, c, t0:t0 + n],
                                                                   scalar=onep[:, l, ji_scale, c, w:w + 1],
                                                                   in1=rstd[:, :n], op0=ALU.mult, op1=ALU.mult),
                     reads=[f"X{tt}" for tt in range(t0 // 128, (t0 + n) // 128)] + ["mrstd", "onep"],
                     writes=[f"mtmp{c%2}"])
                S.op("act", lambda: nc.scalar.activation(out=dst_fn(c), in_=tmp[c % 2][:, :n], func=AF.Identity,
                                                         bias=mod(l, jshift, c, w), scale=1.0),
                     reads=[f"mtmp{c%2}", "modv"], writes=dst_keys)

        eps_t = sb(st, "eps_t", [128, 1])
        S.op("pool", lambda: nc.gpsimd.memset(eps_t[:], EPS), writes=["eps"])

        def mod_bufs(ph):
            sq = [sb(ph, f"msq{i}", [128, 512], BF16) for i in range(2)]
            pms = ps(ph, "pms", [128, 512])
            sd = sb(ph, "msd", [128, 512])
            rstd = sb(ph, "mrstd", [128, 512])
            tmp = [sb(ph, f"mtmp{i}", [128, 512]) for i in range(2)]
            return (sq, pms, sd, rstd, tmp)

        def wload(stg, stg_key, dst_ap, dst_keys, dram_ap, shape, q):
            view = stg[:, :int(np.prod(shape[1:]))]
            if len(shape) == 3:
                view = view.rearrange("p (a b) -> p a b", a=shape[1])
            view = view[:shape[0]] if shape[0] < 128 else view
            S.dma(q, view, dram_ap, writes=[stg_key])
            rr["cast"] = (rr["cast"] + 1) % 2
            e = ("dve", "pool")[rr["cast"]]
            S.op(e, copy_op(e, dst_ap, view), reads=[stg_key], writes=dst_keys)

        def xkeys(t0, n):
            return [f"X{tt}" for tt in range(t0 // 128, (t0 + n + 127) // 128)]

        def layer0():
            l = 0
            with ExitStack() as L:
                cqn = sb(L, "cqn", [128, 2, T], BF16)
                ckvn = sb(L, "ckvn", [128, 2, T], BF16)
                KR = sb(L, "KR", [96, T], BF16)
                LA = ExitStack()
                QA = sb(LA, "QA", [128, 4, T], BF16)
                KA = sb(LA, "KA", [128, 2, T], BF16)
                VA = sb(LA, "VA", [128, NT, 2, 128], BF16)
                S.op("pool", lambda: nc.gpsimd.memset(VA[:], 1.0), writes=["VA"])
                with ExitStack() as ph:
                    win = sb(ph, "win", [128, 8, 1312], BF16)
                    wkd = sb(ph, "wkd", [128, 8, 256], BF16)
                    wkr = sb(ph, "wkr", [128, 8, 96], BF16)
                    abv = abin_d.rearrange("(kc p) n -> p kc n", p=128)
                    with ExitStack() as phs:
                        stg = [sb(phs, f"stg{i}", [128, 4096]) for i in range(2)]
                        for i, (c0, c1) in enumerate([(0, 512), (512, 1024), (1024, 1312)]):
                            wload(stg[i % 2], f"stg{i%2}", win[:, :, c0:c1], ["win"], abv[:, :, c0:c1], [128, 8, c1 - c0],
                                  "sp" if i % 2 == 0 else "pool")
                        S.barrier()
                    for g in range(2):
                        for hf in range(2):
                            S.op("pool", lambda: nc.gpsimd.tensor_copy(out=wkd[:, :, g * 128 + hf * 64:g * 128 + hf * 64 + 64],
                                                                       in_=win[:, :, 512 + g * 64:512 + g * 64 + 64]),
                                 reads=["win"], writes=["wkd"])
                    S.op("pool", lambda: nc.gpsimd.memset(wkr[:], 0.0), writes=["wkr"])
                    S.op("pool", lambda: nc.gpsimd.tensor_copy(out=wkr[:, :, 64:96], in_=win[:, :, 1280:1312]),
                         reads=["win"], writes=["wkr"])
                    hbuf = sb(ph, "hbuf", [128, 8, 512], BF16)
                    mb = mod_bufs(ph)
                    tabs1 = [sb(ph, f"tab{j}", [128, 512]) for j in range(4)]
                    tabs = [tabs1, tabs1]
                    pin = [ps(ph, f"pin{i}", [128, 512]) for i in range(2)]
                    pms2 = ps(ph, "pms2", [128, 512])
                    prot = ps(ph, "prot", [128, 512])
                    sq2 = [sb(ph, f"sq2{i}", [128, 512], BF16) for i in range(2)]
                    sd2 = sb(ph, "sd2", [128, 512])
                    rs2 = sb(ph, "rs2", [128, 512])
                    qn = sb(ph, "qn", [128, 512], BF16)
                    t1 = sb(ph, "t1", [128, 512])
                    t2 = sb(ph, "t2", [128, 512])
                    cnt = {"pin": 0}

                    def pipeline(mm_list, M, n, ones_name, gain_ap, rope, dst_ap, dst_keys, tb):
                        pi = cnt["pin"] % 2
                        cnt["pin"] += 1
                        p_in = pin[pi]
                        for i, (lh, rh, rk) in enumerate(mm_list):
                            S.op("pe", lambda: nc.tensor.matmul(p_in[:M, :n], lhsT=lh, rhs=rh, start=(i == 0),
                                                                stop=(i == len(mm_list) - 1)),
                                 reads=rk, writes=[f"pin{pi}"])
                        S.op("act", lambda: nc.scalar.activation(out=sq2[pi][:M, :n], in_=p_in[:M, :n], func=AF.Square),
                             reads=[f"pin{pi}"], writes=[f"sq2{pi}"])
                        S.op("pe", lambda: nc.tensor.matmul(pms2[:M, :n], lhsT=CM(ones_name)[:M, :M], rhs=sq2[pi][:M, :n],
                                                            start=True, stop=True),
                             reads=[f"sq2{pi}", "cm_b"], writes=["pms2"])
                        S.op("act", lambda: nc.scalar.activation(out=sd2[:M, :n], in_=pms2[:M, :n], func=AF.Sqrt,
                                                                 bias=eps_t[:M, 0:1]),
                             reads=["pms2", "eps"], writes=["sd2"])
                        S.op("dve", lambda: nc.vector.reciprocal(out=rs2[:M, :n], in_=sd2[:M, :n]), reads=["sd2"], writes=["rs2"])
                        o1 = qn[:M, :n] if rope else dst_ap
                        S.op("dve", lambda: nc.vector.scalar_tensor_tensor(out=o1, in0=p_in[:M, :n], scalar=gain_ap,
                                                                           in1=rs2[:M, :n], op0=ALU.mult, op1=ALU.mult),
                             reads=[f"pin{pi}", "rs2", "vecs"], writes=(["qn"] if rope else dst_keys))
                        if rope:
                            rname, ci, si = rope
                            S.op("pe", lambda: nc.tensor.matmul(prot[:M, :n], lhsT=CM(rname)[:M, :M], rhs=qn[:M, :n],
                                                                start=True, stop=True),
                                 reads=["qn", "cm_b"], writes=["prot"])
                            S.op("dve", lambda: nc.vector.tensor_tensor(out=t1[:M, :n], in0=qn[:M, :n], in1=tb[ci][:M, :n],
                                                                        op=ALU.mult),
                                 reads=["qn", f"tab{ci}"], writes=["t1"])
                            S.op("dve", lambda: nc.vector.tensor_tensor(out=t2[:M, :n], in0=prot[:M, :n], in1=tb[si][:M, :n],
                                                                        op=ALU.mult),
                                 reads=["prot", f"tab{si}"], writes=["t2"])
                            S.op("pool", lambda: nc.gpsimd.tensor_tensor(out=dst_ap, in0=t1[:M, :n], in1=t2[:M, :n], op=ALU.add),
                                 reads=["t1", "t2"], writes=dst_keys)

                    K.pipeline = pipeline
                    for ci_, (t0, n) in enumerate(CHUNKS):
                        w = 1 if ci_ == 0 else 0
                        tb = tabs[ci_ % 2]
                        for j, src in enumerate((cosA_d, sinA_d, cosB_d, sinB_d)):
                            np_ = 128 if j < 2 else 96
                            S.dma("pool", tb[j][:np_, :n], src[:, t0:t0 + n], writes=[f"tab{j}"])
                        modulate_chunk(mb, l, 0, 0, t0, n, w, lambda c: hbuf[:, c, :n], ["hbuf"])
                        ck = [f"tk{tt}" for tt in range(t0 // 128, (t0 + n) // 128)]
                        for i in range(4):
                            pipeline([(win[:, k, i * 128:(i + 1) * 128], hbuf[:, k, :n], ["win", "hbuf"]) for k in range(8)],
                                     128, n, "onesA", vecs[:, 0:1], ("RAT", 0, 1), QA[:, i, t0:t0 + n],
                                     [f"QA{i}.{tt}" for tt in range(t0 // 128, (t0 + n) // 128)], tb)
                        for g in range(2):
                            pipeline([(wkd[:, k, g * 128:(g + 1) * 128], hbuf[:, k, :n], ["wkd", "hbuf"]) for k in range(8)],
                                     128, n, "onesA", vecs[:, 1:2], ("RAT", 0, 1), KA[:, g, t0:t0 + n], ["KA"], tb)
                        for tt in range(n // 128):
                            pi = cnt["pin"] % 2
                            cnt["pin"] += 1
                            for k in range(8):
                                S.op("pe", lambda: nc.tensor.matmul(pin[pi][:, :128], lhsT=hbuf[:, k, tt * 128:(tt + 1) * 128],
                                                                    rhs=win[:, k, 640:768], start=(k == 0), stop=(k == 7)),
                                     reads=["win", "hbuf"], writes=[f"pin{pi}"])
                            e = evac_eng()
                            S.op(e, copy_op(e, VA[:, t0 // 128 + tt, :, 0:64], pin[pi][:, :128].rearrange("p (g d) -> p g d", g=2)),
                                 reads=[f"pin{pi}"], writes=["VA"])
                        for (dst, c0, gcol, dk) in ((cqn, 768, 2, "cqn"), (ckvn, 1024, 4, "ckvn")):
                            for i in range(2):
                                for k in range(8):
                                    S.op("pe", lambda: nc.tensor.matmul(pin[i][:, :n], lhsT=win[:, k, c0 + i * 128:c0 + (i + 1) * 128],
                                                                        rhs=hbuf[:, k, :n], start=(k == 0), stop=(k == 7)),
                                         reads=["win", "hbuf"], writes=[f"pin{i}"])
                                S.op("act", lambda: nc.scalar.activation(out=sq2[i][:, :n], in_=pin[i][:, :n], func=AF.Square),
                                     reads=[f"pin{i}"], writes=[f"sq2{i}"])
                            for i in range(2):
                                S.op("pe", lambda: nc.tensor.matmul(pms2[:, :n], lhsT=CM("ones256"), rhs=sq2[i][:, :n],
                                                                    start=(i == 0), stop=(i == 1)),
                                     reads=[f"sq2{i}", "cm_b"], writes=["pms2"])
                            S.op("act", lambda: nc.scalar.activation(out=sd2[:, :n], in_=pms2[:, :n], func=AF.Sqrt, bias=eps_t[:, 0:1]),
                                 reads=["pms2", "eps"], writes=["sd2"])
                            S.op("dve", lambda: nc.vector.reciprocal(out=rs2[:, :n], in_=sd2[:, :n]), reads=["sd2"], writes=["rs2"])
                            for i in range(2):
                                S.op("dve", lambda: nc.vector.scalar_tensor_tensor(out=dst[:, i, t0:t0 + n], in0=pin[i][:, :n],
                                                                                   scalar=vecs[:, gcol + i:gcol + i + 1], in1=rs2[:, :n],
                                                                                   op0=ALU.mult, op1=ALU.mult),
                                     reads=[f"pin{i}", "rs2", "vecs"], writes=[dk])
                        pipeline([(wkr[:, k, :], hbuf[:, k, :n], ["wkr", "hbuf"]) for k in range(8)],
                                 96, n, "onesB", vecs[:96, 6:7], ("RBT", 2, 3), KR[:96, t0:t0 + n], ["KR"], tb)
                S.barrier()
                if dbg == "qa":
                    dump_fm(QA, 4, BF16)
                    LA.close()
                    return
                with ExitStack() as ph:
                    woA = sb(ph, "woA", [128, 4, D], BF16)
                    aov = about_d.rearrange("(kc p) n -> p kc n", p=128)
                    with ExitStack() as phs:
                        stg = [sb(phs, f"stg{i}", [128, 4096]) for i in range(2)]
                        for i in range(2):
                            wload(stg[i], f"stg{i}", woA[:, 2 * i:2 * i + 2, :], ["woA"], aov[:, 2 * i:2 * i + 2, :], [128, 2, D],
                                  "sp" if i == 0 else "pool")
                        S.barrier()
                    sk_raw = sb(ph, "sk_raw", [64, 8])
                    sk_exp = sb(ph, "sk_exp", [64, 8])
                    SE = sb(ph, "SE", [64, 2, 512])
                    S.dma("sp", sk_raw[:], sink_d, writes=["sk_raw"])
                    S.op("act", lambda: nc.scalar.activation(out=sk_exp[:], in_=sk_raw[:], func=AF.Exp), reads=["sk_raw"], writes=["sk_exp"])
                    for g in range(2):
                        for hb in range(4):
                            hf, j = hb // 2, hb % 2
                            hd = 4 * g + 2 * j + hf
                            S.op("dve", lambda: nc.vector.tensor_scalar(out=SE[:, g, hb * 128:(hb + 1) * 128], in0=zeros[:64, :],
                                                                        scalar1=sk_exp[:, hd:hd + 1], scalar2=None, op0=ALU.add),
                                 reads=["zeros", "sk_exp"], writes=["SE"])
                    pS = [[ps(ph, f"pS{i}{hf}", [128, 512]) for hf in range(2)] for i in range(2)]
                    pO = [ps(ph, f"pO{i}", [128, 512]) for i in range(2)]
                    PT = [sb(ph, f"PT{i}", [128, 512], BF16) for i in range(3)]
                    den = sb(ph, "den", [64, 512])
                    rden = sb(ph, "rden", [64, 512])
                    it = 0
                    nit = 0
                    for qb in range(NT):
                        q0 = qb * 128
                        if qb < 2:
                            kts = [(0, None), (1, None)]
                        else:
                            kts = [(0, None), (1, None)]
                            if qb - 1 >= 2:
                                kts.append((qb - 1, "maskPrevT"))
                            kts.append((qb, None))
                            if qb + 1 < NT:
                                kts.append((qb + 1, "maskNextT"))
                        for g in range(2):
                            po = pO[nit % 2]
                            pok = f"pO{nit%2}"
                            nit += 1
                            for ki, (kt, mk) in enumerate(kts):
                                ptb = PT[it % 3]; ptk = f"PT{it%3}"
                                pss = pS[it % 2]
                                it += 1
                                for hf in range(2):
                                    psb = pss[hf]; psk = f"pS{(it-1)%2}{hf}"
                                    pr = slice(hf * 64, (hf + 1) * 64)
                                    if mk is not None:
                                        S.op("pe", lambda: nc.tensor.matmul(psb[:, :256], lhsT=CM(mk),
                                                                            rhs=id4[:, 0:2, :].rearrange("p r n -> p (r n)"),
                                                                            start=True, stop=False),
                                             reads=["cm_b", "id4"], writes=[psk])
                                    S.op("pe", lambda: nc.tensor.matmul(psb[:, :256].rearrange("p (r n) -> p r n", r=2),
                                                                        lhsT=KA[pr, g, kt * 128:(kt + 1) * 128],
                                                                        rhs=QA[pr, 2 * g:2 * g + 2, q0:q0 + 128],
                                                                        start=(mk is None), stop=True),
                                         reads=["KA", f"QA{2*g}.{qb}", f"QA{2*g+1}.{qb}"], writes=[psk])
                                    S.op("act", lambda: nc.scalar.activation(out=ptb[:, hf * 256:(hf + 1) * 256], in_=psb[:, :256],
                                                                             func=AF.Exp, scale=0.125),
                                         reads=[psk], writes=[ptk])
                                S.op("pe", lambda: nc.tensor.matmul(po[:], lhsT=VA[:, kt, g, :], rhs=ptb[:], start=(ki == 0),
                                                                    stop=(ki == len(kts) - 1)),
                                     reads=["VA", ptk], writes=[pok])
                            S.op("dve", lambda: nc.vector.tensor_tensor(out=den[:], in0=po[64:128, :], in1=SE[:, g, :], op=ALU.add),
                                 reads=[pok, "SE"], writes=["den"])
                            S.op("dve", lambda: nc.vector.reciprocal(out=rden[:], in_=den[:]), reads=["den"], writes=["rden"])
                            for hf in range(2):
                                S.op("dve", lambda: nc.vector.tensor_tensor(
                                    out=QA[hf * 64:(hf + 1) * 64, 2 * g:2 * g + 2, q0:q0 + 128],
                                    in0=po[0:64, hf * 256:(hf + 1) * 256].rearrange("p (r n) -> p r n", r=2),
                                    in1=rden[:, hf * 256:(hf + 1) * 256].rearrange("p (r n) -> p r n", r=2), op=ALU.mult),
                                     reads=[pok, "rden"], writes=[f"QA{2*g}.{qb}", f"QA{2*g+1}.{qb}"])
                    if dbg == "oa":
                        S.barrier()
                        dump_fm(QA, 4, BF16)
                        K.stop = True
                    pd = [ps(ph, f"pd{i}", [128, 512]) for i in range(2)]
                    it = 0
                    for ci_, (t0, n) in enumerate([] if K.stop else CHUNKS):
                        w = 1 if ci_ == 0 else 0
                        for m in range(8):
                            p_ = pd[it % 2]; pk = f"pd{it%2}"; it += 1
                            for i in range(4):
                                S.op("pe", lambda: nc.tensor.matmul(p_[:, :n], lhsT=woA[:, i, m * 128:(m + 1) * 128], rhs=QA[:, i, t0:t0 + n],
                                                                    start=(i == 0), stop=(i == 3)),
                                     reads=["woA"] + [f"QA{i}.{tt}" for tt in range(t0 // 128, (t0 + n) // 128)], writes=[pk])
                            S.op("dve", lambda: nc.vector.scalar_tensor_tensor(out=X[:, m, t0:t0 + n], in0=p_[:, :n], scalar=mod(l, 2, m, w),
                                                                               in1=X[:, m, t0:t0 + n], op0=ALU.mult, op1=ALU.add),
                                 reads=[pk, "modv"] + xkeys(t0, n), writes=xkeys(t0, n))
                S.barrier()
                LA.close()
                if K.stop:
                    return
                with ExitStack() as ph:
                    wuq = sb(ph, "wuq", [128, 2, 768], BF16)
                    wuk = sb(ph, "wuk", [128, 2, 512], BF16)
                    wuv = sb(ph, "wuv", [128, 2, 512], BF16)
                    woB = sb(ph, "woB", [128, 4, D], BF16)
                    with ExitStack() as phs:
                        stg = [sb(phs, f"stg{i}", [128, 4096]) for i in range(2)]
                        wload(stg[0], "stg0", wuq[:], ["wuq"], wuq_d.rearrange("(kc p) n -> p kc n", p=128), [128, 2, 768], "sp")
                        wload(stg[1], "stg1", wuk[:], ["wuk"], wuk_d.rearrange("(kc p) n -> p kc n", p=128), [128, 2, 512], "pool")
                        wload(stg[0], "stg0", wuv[:], ["wuv"], wuv_d.rearrange("(kc p) n -> p kc n", p=128), [128, 2, 512], "sp")
                        aov = about_d.rearrange("(kc p) n -> p kc n", p=128)
                        for i in range(2):
                            wload(stg[(i + 1) % 2], f"stg{(i+1)%2}", woB[:, 2 * i:2 * i + 2, :], ["woB"], aov[:, 4 + 2 * i:4 + 2 * i + 2, :],
                                  [128, 2, D], "pool" if i == 0 else "sp")
                        S.barrier()
                    KB = sb(ph, "KB", [96, 4, T], BF16)
                    VB = sb(ph, "VB", [128, NT, 4, 128], BF16)
                    S.op("pool", lambda: nc.gpsimd.memset(VB[:], 1.0), writes=["VB"])
                    tabs1 = [None, None] + [sb(ph, f"tab{j}", [128, 512]) for j in (2, 3)]
                    tabs = [tabs1, tabs1]
                    pin = [ps(ph, f"pin{i}", [128, 512]) for i in range(2)]
                    pms2 = ps(ph, "pms2", [128, 512])
                    prot = ps(ph, "prot", [128, 512])
                    pS = [ps(ph, f"pS{i}", [128, 512]) for i in range(2)]
                    pO = ps(ph, "pO", [128, 512])
                    pd = ps(ph, "pd", [128, 512])
                    sq2 = [sb(ph, f"sq2{i}", [128, 512], BF16) for i in range(2)]
                    sd2 = sb(ph, "sd2", [128, 512])
                    rs2 = sb(ph, "rs2", [128, 512])
                    qn = sb(ph, "qn", [128, 512], BF16)
                    t1 = sb(ph, "t1", [128, 512])
                    t2 = sb(ph, "t2", [128, 512])
                    QBc = sb(ph, "QBc", [96, 4, 512], BF16)
                    Yc = sb(ph, "Yc", [128, 2, 512], BF16)
                    PT = [sb(ph, f"PT{i}", [128, 512], BF16) for i in range(3)]
                    rden = sb(ph, "rden", [64, 512])
                    cnt = {"pin": 0}

                    def pipeline(mm_list, M, n, ones_name, gain_ap, rope, dst_ap, dst_keys, tb):
                        pi = cnt["pin"] % 2
                        cnt["pin"] += 1
                        p_in = pin[pi]
                        for i, (lh, rh, rk) in enumerate(mm_list):
                            S.op("pe", lambda: nc.tensor.matmul(p_in[:M, :n], lhsT=lh, rhs=rh, start=(i == 0),
                                                                stop=(i == len(mm_list) - 1)),
                                 reads=rk, writes=[f"pin{pi}"])
                        S.op("act", lambda: nc.scalar.activation(out=sq2[pi][:M, :n], in_=p_in[:M, :n], func=AF.Square),
                             reads=[f"pin{pi}"], writes=[f"sq2{pi}"])
                        S.op("pe", lambda: nc.tensor.matmul(pms2[:M, :n], lhsT=CM(ones_name)[:M, :M], rhs=sq2[pi][:M, :n],
                                                            start=True, stop=True),
                             reads=[f"sq2{pi}", "cm_b"], writes=["pms2"])
                        S.op("act", lambda: nc.scalar.activation(out=sd2[:M, :n], in_=pms2[:M, :n], func=AF.Sqrt,
                                                                 bias=eps_t[:M, 0:1]),
                             reads=["pms2", "eps"], writes=["sd2"])
                        S.op("dve", lambda: nc.vector.reciprocal(out=rs2[:M, :n], in_=sd2[:M, :n]), reads=["sd2"], writes=["rs2"])
                        o1 = qn[:M, :n] if rope else dst_ap
                        S.op("dve", lambda: nc.vector.scalar_tensor_tensor(out=o1, in0=p_in[:M, :n], scalar=gain_ap,
                                                                           in1=rs2[:M, :n], op0=ALU.mult, op1=ALU.mult),
                             reads=[f"pin{pi}", "rs2", "vecs"], writes=(["qn"] if rope else dst_keys))
                        if rope:
                            rname, ci, si = rope
                            S.op("pe", lambda: nc.tensor.matmul(prot[:M, :n], lhsT=CM(rname)[:M, :M], rhs=qn[:M, :n],
                                                                start=True, stop=True),
                                 reads=["qn", "cm_b"], writes=["prot"])
                            S.op("dve", lambda: nc.vector.tensor_tensor(out=t1[:M, :n], in0=qn[:M, :n], in1=tb[ci][:M, :n],
                                                                        op=ALU.mult),
                                 reads=["qn", f"tab{ci}"], writes=["t1"])
                            S.op("dve", lambda: nc.vector.tensor_tensor(out=t2[:M, :n], in0=prot[:M, :n], in1=tb[si][:M, :n],
                                                                        op=ALU.mult),
                                 reads=["prot", f"tab{si}"], writes=["t2"])
                            S.op("pool", lambda: nc.gpsimd.tensor_tensor(out=dst_ap, in0=t1[:M, :n], in1=t2[:M, :n], op=ALU.add),
                                 reads=["t1", "t2"], writes=dst_keys)

                    it = 0
                    for p in range(2):
                        for hl in range(4):
                            h = 4 * p + hl
                            for (t0, n) in CHUNKS:
                                pipeline([(wuk[:, k, h * 64:(h + 1) * 64], ckvn[:, k, t0:t0 + n], ["wuk", "ckvn"]) for k in range(2)],
                                         64, n, "onesA", vecs[:64, 7:8], None, KB[0:64, hl, t0:t0 + n], ["KB"], None)
                            S.op("pool", lambda: nc.gpsimd.tensor_copy(out=KB[64:96, hl, :], in_=KR[64:96, :]), reads=["KR"], writes=["KB"])
                        for tt in range(NT):
                            pi = cnt["pin"] % 2
                            cnt["pin"] += 1
                            for k in range(2):
                                S.op("pe", lambda: nc.tensor.matmul(pin[pi][:, :256], lhsT=ckvn[:, k, tt * 128:(tt + 1) * 128],
                                                                    rhs=wuv[:, k, p * 256:(p + 1) * 256], start=(k == 0), stop=(k == 1)),
                                     reads=["wuv", "ckvn"], writes=[f"pin{pi}"])
                            e = evac_eng()
                            S.op(e, copy_op(e, VB[:, tt, :, 0:64], pin[pi][:, :256].rearrange("p (g d) -> p g d", g=4)),
                                 reads=[f"pin{pi}"], writes=["VB"])
                        for ci_, (t0, n) in enumerate(CHUNKS):
                            w = 1 if ci_ == 0 else 0
                            tb = tabs[ci_ % 2]
                            for j, src in ((2, cosB_d), (3, sinB_d)):
                                S.dma("pool", tb[j][:96, :n], src[:, t0:t0 + n], writes=[f"tab{j}"])
                            kts = [0, 1] if ci_ == 0 else list(range(NT))
                            for hl in range(4):
                                h = 4 * p + hl
                                pipeline([(wuq[:, k, h * 96:(h + 1) * 96], cqn[:, k, t0:t0 + n], ["wuq", "cqn"]) for k in range(2)],
                                         96, n, "onesB", vecs[:96, 8:9], ("RBT", 2, 3), QBc[:96, hl, :n], [f"QBc{hl}"], tb)
                            for hl in range(4):
                                for ki, kt in enumerate(kts):
                                    psb = pS[it % 2]; psk = f"pS{it%2}"
                                    ptb = PT[it % 3]; ptk = f"PT{it%3}"
                                    it += 1
                                    S.op("pe", lambda: nc.tensor.matmul(psb[:, :n], lhsT=KB[:96, hl, kt * 128:(kt + 1) * 128],
                                                                        rhs=QBc[:96, hl, :n], start=True, stop=True),
                                         reads=["KB", f"QBc{hl}"], writes=[psk])
                                    S.op("act", lambda: nc.scalar.activation(out=ptb[:, :n], in_=psb[:, :n], func=AF.Exp,
                                                                             scale=float(96 ** -0.5)),
                                         reads=[psk], writes=[ptk])
                                    S.op("pe", lambda: nc.tensor.matmul(pO[:, :n], lhsT=VB[:, kt, hl, :], rhs=ptb[:, :n],
                                                                        start=(ki == 0), stop=(ki == len(kts) - 1)),
                                         reads=["VB", ptk], writes=["pO"])
                                S.op("dve", lambda: nc.vector.reciprocal(out=rden[:, :n], in_=pO[64:128, :n]), reads=["pO"], writes=["rden"])
                                S.op("dve", lambda: nc.vector.tensor_tensor(out=Yc[(hl % 2) * 64:(hl % 2) * 64 + 64, hl // 2, :n],
                                                                            in0=pO[0:64, :n], in1=rden[:, :n], op=ALU.mult),
                                     reads=["pO", "rden"], writes=["Yc"])
                            for m in range(8):
                                for i in range(2):
                                    S.op("pe", lambda: nc.tensor.matmul(pd[:, :n], lhsT=woB[:, 2 * p + i, m * 128:(m + 1) * 128],
                                                                        rhs=Yc[:, i, :n], start=(i == 0), stop=(i == 1)),
                                         reads=["woB", "Yc"], writes=["pd"])
                                S.op("dve", lambda: nc.vector.scalar_tensor_tensor(out=X[:, m, t0:t0 + n], in0=pd[:, :n], scalar=mod(l, 2, m, w),
                                                                                   in1=X[:, m, t0:t0 + n], op0=ALU.mult, op1=ALU.add),
                                     reads=["pd", "modv"] + xkeys(t0, n), writes=xkeys(t0, n))
                S.barrier()

        def ffn(l, last):
            G = 4
            groups = [list(range(j, min(j + G, 22))) for j in range(0, 22, G)]
            tch = [(0, 256, 0, 0)]
            for i in range(8):
                tch.append((256 + i * 256, 256, 0 if i == 0 else 1, 0 if i == 7 else 1))
            if last:
                tch = tch[1:]
            with ExitStack() as ph:
                H2 = sb(ph, "H2", [128, 8, T], BF16)
                fcv = sb(ph, "fcv", [128, 44, 4])
                S.dma("sp", fcv[:], fconv_d[l], writes=["fcv"])
                with ExitStack() as ph2:
                    mb = mod_bufs(ph2)
                    for ci_, (t0, n) in enumerate(CHUNKS):
                        w = 1 if ci_ == 0 else 0
                        if last and ci_ == 0:
                            continue
                        modulate_chunk(mb, l, 3, 1, t0, n, w, lambda c: H2[:, c, t0:t0 + n], ["H2"])
                    S.barrier()
                stg1 = sb(ph, "fstg0", [128, 4096])
                stg = [stg1, stg1]
                wup = [sb(ph, f"wup{i}", [128, 8, 2 * G * 128], BF16) for i in range(2)]
                wdn = [sb(ph, f"wdn{i}", [128, G, D], BF16) for i in range(2)]
                pu = [ps(ph, f"pu{i}", [128, 512]) for i in range(4)]
                pdn = [ps(ph, f"pdn{i}", [128, 512]) for i in range(2)]
                uu = [sb(ph, f"uu{i}", [128, 256]) for i in range(4)]
                sg = sb(ph, "sg", [128, 256])
                actb = [sb(ph, f"actb{i}", [128, 256], BF16) for i in range(2 * G)]
                upv = wup_d[l].rearrange("(kc p) n -> p kc n", p=128)
                dnv = wdn_d[l].rearrange("(kc p) n -> p kc n", p=128)
                si = 0
                iu = 0
                ia = 0
                idn = 0
                for gi, js in enumerate(groups):
                    g_n = len(js)
                    wu = wup[gi % 2]; wd = wdn[gi % 2]
                    j0 = js[0]
                    for part in range(2):
                        wload(stg[si % 2], "fstg0", wu[:, :, part * G * 128:part * G * 128 + g_n * 128], [f"wup{gi%2}"],
                              upv[:, :, part * DFF + j0 * 128:part * DFF + (j0 + g_n) * 128], [128, 8, g_n * 128],
                              "sp" if si % 2 == 0 else "pool")
                        si += 1
                    wload(stg[si % 2], "fstg0", wd[:, :g_n, :], [f"wdn{gi%2}"], dnv[:, j0:j0 + g_n, :], [128, g_n, D],
                          "sp" if si % 2 == 0 else "pool")
                    si += 1
                    for (s0, n, lo, hi) in tch:
                        w = 1 if s0 == 0 else 0
                        e0 = s0 - lo
                        ne = n + lo + hi
                        abufs = []
                        for jj, j in enumerate(js):
                            us = []
                            for part in range(2):
                                p_ = pu[iu % 4]; pk = f"pu{iu%4}"
                                u_ = uu[iu % 4]; uk = f"uu{iu%4}"
                                iu += 1
                                fc = part * 22 + j
                                for k in range(8):
                                    S.op("pe", lambda: nc.tensor.matmul(p_[:, :ne], lhsT=wu[:, k, (part * G + jj) * 128:(part * G + jj + 1) * 128],
                                                                        rhs=H2[:, k, e0:e0 + ne], start=(k == 0), stop=(k == 7)),
                                         reads=[f"wup{gi%2}", "H2"], writes=[pk])
                                S.op("act", lambda: nc.scalar.activation(out=u_[:, :n], in_=p_[:, lo:lo + n], func=AF.Identity,
                                                                         bias=fcv[:, fc, 3:4], scale=fcv[:, fc, 1:2]),
                                     reads=[pk, "fcv"], writes=[uk])
                                a = 1 if lo == 0 else 0
                                S.op("dve", lambda: nc.vector.scalar_tensor_tensor(out=u_[:, a:n], in0=p_[:, lo - 1 + a:lo + n - 1],
                                                                                   scalar=fcv[:, fc, 0:1], in1=u_[:, a:n],
                                                                                   op0=ALU.mult, op1=ALU.add),
                                     reads=[pk, "fcv", uk], writes=[uk])
                                b = 1 if hi == 0 else 0
                                S.op("dve", lambda: nc.vector.scalar_tensor_tensor(out=u_[:, :n - b], in0=p_[:, lo + 1:lo + 1 + n - b],
                                                                                   scalar=fcv[:, fc, 2:3], in1=u_[:, :n - b],
                                                                                   op0=ALU.mult, op1=ALU.add),
                                     reads=[pk, "fcv", uk], writes=[uk])
                                us.append((u_, uk))
                            (uv, uvk), (ug, ugk) = us
                            S.op("act", lambda: nc.scalar.activation(out=sg[:, :n], in_=ug[:, :n], func=AF.Silu), reads=[ugk], writes=["sg"])
                            ab = actb[ia % (2 * G)]; abk = f"actb{ia%(2*G)}"
                            ia += 1
                            S.op("pool", lambda: nc.gpsimd.tensor_tensor(out=ab[:, :n], in0=sg[:, :n], in1=uv[:, :n], op=ALU.mult),
                                 reads=["sg", uvk], writes=[abk])
                            abufs.append((ab, abk))
                        for m in range(8):
                            p_ = pdn[idn % 2]; pk = f"pdn{idn%2}"; idn += 1
                            for jj in range(g_n):
                                S.op("pe", lambda: nc.tensor.matmul(p_[:, :n], lhsT=wd[:, jj, m * 128:(m + 1) * 128], rhs=abufs[jj][0][:, :n],
                                                                    start=(jj == 0), stop=(jj == g_n - 1)),
                                     reads=[f"wdn{gi%2}", abufs[jj][1]], writes=[pk])
                            S.op("dve", lambda: nc.vector.scalar_tensor_tensor(out=X[:, m, s0:s0 + n], in0=p_[:, :n], scalar=mod(l, 5, m, w),
                                                                               in1=X[:, m, s0:s0 + n], op0=ALU.mult, op1=ALU.add),
                                 reads=[pk, "modv"] + xkeys(s0, n), writes=xkeys(s0, n))
            S.barrier()

        def dump_fm(buf, nchunk, dt, ntile=NT):
            with ExitStack() as ph:
                tin = sb(ph, "d_tin", [128, 128])
                pt_ = ps(ph, "d_pt", [128, 128])
                ob = sb(ph, "d_ob", [128, D])
                for t in range(ntile):
                    for c in range(nchunk):
                        S.op("dve", lambda: nc.vector.tensor_copy(out=tin[:], in_=buf[:, c, t * 128:(t + 1) * 128]), reads=["*"], writes=["d_tin"])
                        S.op("pe", lambda: nc.tensor.transpose(out=pt_[:], in_=tin[:], identity=CM("ident", False)),
                             reads=["d_tin", "cm_f"], writes=["d_pt"])
                        S.op("dve", lambda: nc.vector.tensor_copy(out=ob[:, c * 128:(c + 1) * 128], in_=pt_[:]), reads=["d_pt"], writes=["d_ob"])
                    S.dma("sp", dbg_d[t * 128:(t + 1) * 128, :nchunk * 128], ob[:, :nchunk * 128], reads=["d_ob"])
                S.barrier()

        def write_out():
            with ExitStack() as ph:
                pt_ = [ps(ph, f"o_pt{i}", [128, 512]) for i in range(4)]
                ob = [sb(ph, f"o_ob{i}", [128, D]) for i in range(2)]
                for t in range(2, NT):
                    o_ = ob[t % 2]
                    for hh in range(2):
                        p_ = pt_[(t % 2) * 2 + hh]; pk = f"o_pt{(t%2)*2+hh}"
                        for c4 in range(4):
                            c = hh * 4 + c4
                            S.op("pe", lambda: nc.tensor.transpose(out=p_[:, c4 * 128:(c4 + 1) * 128], in_=X[:, c, t * 128:(t + 1) * 128],
                                                                   identity=CM("ident", False)),
                                 reads=[f"X{t}", "cm_f"], writes=[pk])
                        e = evac_eng()
                        S.op(e, copy_op(e, o_[:, hh * 512:(hh + 1) * 512], p_[:]), reads=[pk], writes=[f"o_ob{t%2}"])
                    S.dma("sp" if t % 2 == 0 else "pool", out_d[(t - 2) * 128:(t - 1) * 128, :], o_[:], reads=[f"o_ob{t%2}"])
                if dbg == "x":
                    for t in range(2):
                        o_ = ob[t % 2]
                        for hh in range(2):
                            p_ = pt_[(t % 2) * 2 + hh]; pk = f"o_pt{(t%2)*2+hh}"
                            for c4 in range(4):
                                c = hh * 4 + c4
                                S.op("pe", lambda: nc.tensor.transpose(out=p_[:, c4 * 128:(c4 + 1) * 128], in_=X[:, c, t * 128:(t + 1) * 128],
                                                                       identity=CM("ident", False)),
                                     reads=[f"X{t}", "cm_f"], writes=[pk])
                            e = evac_eng()
                            S.op(e, copy_op(e, o_[:, hh * 512:(hh + 1) * 512], p_[:]), reads=[pk], writes=[f"o_ob{t%2}"])
                        S.dma("sp", dbg_d[t * 128:(t + 1) * 128, :], o_[:], reads=[f"o_ob{t%2}"])


        def layer1():
            l = 1
            INC = 1920
            NCH = T // 64
            fwd_order = list(range(NCH))
            bwd_order = [3, 2, 1, 0] + list(range(NCH - 1, 3, -1))
            PCH = [(0, 256, 0, 256)] + [(256 + i * 256, 256, 256, T) for i in range(8)]
            with ExitStack() as L:
                L0 = L
                H1 = sb(L, "H1", [128, 8, T], BF16)
                Yall = sb(L, "Yall", [128, 8, 2048], BF16)
                v1 = sb(L, "v1", [128, 160])
                cm2 = sb(L, "cm2", [64, 320])
                ones64 = sb(L, "ones64", [128, 64])
                id64b = sb(L, "id64b", [64, 64], BF16)
                S.dma("sp", v1[:], v1_d, writes=["v1"])
                S.dma("sp", cm2[:], cm2_d, writes=["cm2"])
                S.op("pool", lambda: nc.gpsimd.memset(ones64[:], 1.0), writes=["ones64"])
                S.op("dve", lambda: nc.vector.tensor_copy(out=id64b[:], in_=cm2[:, 256:320]), reads=["cm2"], writes=["id64b"])
                S.op("dve", lambda: nc.vector.tensor_tensor(out=v1[:, 30:45], in0=v1[:, 0:15], in1=v1[:, 15:30], op=ALU.add), reads=["v1"], writes=["v1"])
                S.op("dve", lambda: nc.vector.tensor_scalar(out=v1[:, 30:45], in0=v1[:, 30:45], scalar1=-1.0, scalar2=1.0, op0=ALU.mult, op1=ALU.add),
                     reads=["v1"], writes=["v1"])
                with ExitStack() as ph2:
                    mb = mod_bufs(ph2)
                    for ci_, (t0, n) in enumerate(CHUNKS):
                        w = 1 if ci_ == 0 else 0
                        modulate_chunk(mb, l, 0, 0, t0, n, w, lambda c: H1[:, c, t0:t0 + n], ["H1"])
                    S.barrier()
                for c in range(8):
                    S.dma("sp" if c % 2 == 0 else "pool", xpark_d[:, c * T:(c + 1) * T], X[:, c, :], reads=[f"X{t}" for t in range(NT)])
                S.barrier()
                XB = X[:].rearrange("p c t -> p (c t)").bitcast(BF16)
                XF = X[:].rearrange("p c t -> p (c t)")

                def xb(i):
                    return XB[:, i * T:(i + 1) * T]

                def xf(i):
                    return XF[:, i * T:(i + 1) * T]

                LS = ExitStack()
                stg = sb(LS, "l1stg", [128, 1024])
                wg = sb(LS, "l1wg", [128, 8, 512], BF16)
                gneps = sb(LS, "gneps", [128, 1])
                S.op("pool", lambda: nc.gpsimd.memset(gneps[:], 64e-5), writes=["gneps"])
                L = LS
                pin = [ps(L, f"l1pin{i}", [128, 512]) for i in range(2)]
                pC = [ps(L, f"l1pC{i}", [128, 512]) for i in range(4)]
                pTr = [ps(L, f"l1pT{i}", [64, 1024], BF16) for i in range(2)]
                uw = [sb(L, f"l1u{i}", [128, 260]) for i in range(2)]
                cnt = {"pin": 0}
                cdv = cdin_d.rearrange("(kc p) n -> p kc n", p=128)

                def load_cols(cols_list):
                    off = 0
                    for (c0, ncol) in cols_list:
                        for b0 in range(0, ncol, 128):
                            nb = min(128, ncol - b0)
                            wload(stg, "l1stg", wg[:, :, off:off + nb], ["l1wg"], cdv[:, :, c0 + b0:c0 + b0 + nb], [128, 8, nb], "sp")
                            off += nb

                def proj_taps(woff, M, taps, func, out_fn, out_keys, bias_ap=None, scale_out=None):
                    hw = max(abs(o) for o, _ in taps)
                    for (s0, n, qlo, qhi) in PCH:
                        e0 = max(s0 - hw, qlo); e1 = min(s0 + n + hw, qhi)
                        ne = e1 - e0
                        pi = cnt["pin"] % 2; cnt["pin"] += 1
                        p_ = pin[pi]; u_ = uw[pi]
                        for k in range(8):
                            S.op("pe", lambda: nc.tensor.matmul(p_[:M, :ne], lhsT=wg[:, k, woff:woff + M], rhs=H1[:, k, e0:e1],
                                                                start=(k == 0), stop=(k == 7)),
                                 reads=["l1wg", "H1"], writes=[f"l1pin{pi}"])
                        src = p_
                        if len(taps) > 1:
                            base = s0 - e0
                            o0, c0_ = taps[0]
                            S.op("act", lambda: nc.scalar.activation(out=u_[:M, :n], in_=p_[:M, base:base + n], func=AF.Identity,
                                                                     scale=c0_),
                                 reads=[f"l1pin{pi}", "v1"], writes=[f"l1u{pi}"])
                            for (o, cf) in taps[1:]:
                                i0 = max(0, e0 - s0 - o); i1 = min(n, e1 - s0 - o)
                                S.op("dve", lambda: nc.vector.scalar_tensor_tensor(out=u_[:M, i0:i1], in0=p_[:M, base + i0 + o:base + i1 + o],
                                                                                   scalar=cf, in1=u_[:M, i0:i1], op0=ALU.mult, op1=ALU.add),
                                     reads=[f"l1pin{pi}", f"l1u{pi}", "v1"], writes=[f"l1u{pi}"])
                            src = u_
                            srck = f"l1u{pi}"
                            sl = slice(0, n)
                        else:
                            srck = f"l1pin{pi}"
                            sl = slice(s0 - e0, s0 - e0 + n)
                        kw = {}
                        if bias_ap is not None:
                            kw["bias"] = bias_ap
                        if scale_out is not None:
                            kw["scale"] = scale_out
                        S.op("act", lambda: nc.scalar.activation(out=out_fn(s0, n), in_=src[:M, sl], func=func, **kw),
                             reads=[srck, "v1"], writes=out_keys)

                def shift_taps(fc):
                    return [(0, v1[:, 30 + fc:31 + fc]), (-1, v1[:, fc:fc + 1]), (1, v1[:, 15 + fc:16 + fc])]

                wk = {}
                for dn in ("f", "b"):
                    wk[dn] = dict(
                        al=sb(L, f"al{dn}", [128, 64]), be=sb(L, f"be{dn}", [128, 64]), kd=sb(L, f"kd{dn}", [128, 64]),
                        rr=sb(L, f"rr{dn}", [128, 64]), lw=sb(L, f"lw{dn}", [128, 64]),
                        pfx=sb(L, f"pfx{dn}", [128, 64]), Gi=sb(L, f"Gi{dn}", [128, 64]), Ge=sb(L, f"Ge{dn}", [128, 64]),
                        E=sb(L, f"E{dn}", [128, 4, 64]), Es=sb(L, f"Es{dn}", [128, 3, 64]),
                        negm=sb(L, f"negm{dn}", [128, 1]),
                        BT=sb(L, f"BT{dn}", [128, 64], BF16), KT=sb(L, f"KT{dn}", [128, 64], BF16),
                        Hs=sb(L, f"Hs{dn}", [128, 128]), Hb=sb(L, f"Hb{dn}", [128, 128], BF16), ztmp=sb(L, f"ztmp{dn}", [64, 128]),
                    )
                    for j in range(2):
                        idt = BF16 if DBG['invbf'] else F32
                        wk[dn][f"N{j}"] = [sb(L, f"N{dn}{j}{i}", [64, 64], idt) for i in range(2)]
                        wk[dn][f"NT{j}"] = [sb(L, f"NT{dn}{j}{i}", [64, 64], idt) for i in range(2)]
                        wk[dn][f"IN{j}"] = sb(L, f"IN{dn}{j}", [64, 64], idt)
                        wk[dn][f"PT{j}"] = [sb(L, f"PTi{dn}{j}{i}", [64, 64], idt) for i in range(2)]
                        wk[dn][f"TT{j}"] = sb(L, f"TT{dn}{j}", [64, 64], BF16)
                        wk[dn][f"ARB{j}"] = sb(L, f"ARB{dn}{j}", [64, 64], BF16)
                        wk[dn][f"AKRK{j}"] = sb(L, f"AKRK{dn}{j}", [64, 128], BF16)
                        wk[dn][f"Zb{j}"] = sb(L, f"Zb{dn}{j}", [64, 128], BF16)
                        wk[dn][f"Ub{j}"] = sb(L, f"Ub{dn}{j}", [64, 128], BF16)

                wk2 = {}
                for dn in ("f", "b"):
                    wk2[dn] = []
                    for par in range(2):
                        wk2[dn].append(dict(
                            AR=sb(L, f"AR2{dn}{par}", [128, 2, 64], BF16), BH=sb(L, f"BH2{dn}{par}", [128, 64], BF16), KH=sb(L, f"KH2{dn}{par}", [128, 64], BF16),
                            ART=sb(L, f"ART2{dn}{par}", [128, 2, 64], BF16), BTt=sb(L, f"BTt2{dn}{par}", [64, 128], BF16),
                            KTt=sb(L, f"KTt2{dn}{par}", [64, 128], BF16), VT=sb(L, f"VT2{dn}{par}", [64, 128], BF16), Et=sb(L, f"Et2{dn}{par}", [128, 1])))

                def dm_aps(di, par):
                    if par == 0:
                        v = K.lw2flat[:, di * 648:di * 648 + 648].bitcast(F32)
                        return v[0:64, 0:256], v[0:64, 256:320], v[0:64, 320:322]
                    sqf = sq[:].bitcast(F32)
                    return t32[di][0:64, 0:256], sqf[0:64, di * 64:(di + 1) * 64], t32[2][0:64, di * 2:di * 2 + 2]

                def prefix_gen(dn, pre_ops, vsrc_ap, vkeys, sdec, par=0):
                    W = wk[dn]
                    W2 = wk2[dn][par]
                    k_ = lambda nm: f"{nm}{dn}"
                    k2 = lambda nm: f"{nm}{dn}p{par}"
                    fwd = dn == "f"
                    di = 0 if fwd else 1
                    ptr = pTr[di]; ptk = f"l1pT{di}"
                    for fn in pre_ops:
                        fn()
                        yield
                    S.op("dve", lambda: nc.vector.tensor_tensor_scan(out=W["pfx"][:], data0=ones64[:], data1=W["lw"][:], initial=0.0,
                                                                     op0=ALU.mult, op1=ALU.add),
                         reads=[k_("lw"), "ones64"], writes=[k_("pfx")])
                    yield
                    tot = W["pfx"][:, 63:64]
                    if fwd:
                        S.op("pool", lambda: nc.gpsimd.tensor_copy(out=W["Gi"][:], in_=W["pfx"][:]), reads=[k_("pfx")], writes=[k_("Gi")])
                        yield
                        S.op("dve", lambda: nc.vector.tensor_tensor(out=W["Ge"][:], in0=W["pfx"][:], in1=W["lw"][:], op=ALU.subtract),
                             reads=[k_("pfx"), k_("lw")], writes=[k_("Ge")])
                        yield
                    else:
                        S.op("dve", lambda: nc.vector.tensor_scalar(out=W["Ge"][:], in0=W["pfx"][:], scalar1=-1.0, scalar2=tot,
                                                                    op0=ALU.mult, op1=ALU.add),
                             reads=[k_("pfx")], writes=[k_("Ge")])
                        yield
                        S.op("dve", lambda: nc.vector.tensor_tensor(out=W["Gi"][:], in0=W["Ge"][:], in1=W["lw"][:], op=ALU.add),
                             reads=[k_("Ge"), k_("lw")], writes=[k_("Gi")])
                        yield
                    E = W["E"]; Es = W["Es"]; negm = W["negm"]
                    S.op("act", lambda: nc.scalar.activation(out=E[:, 0, :], in_=W["Ge"][:], func=AF.Exp), reads=[k_("Ge")], writes=[k_("E0")])
                    yield
                    S.op("act", lambda: nc.scalar.activation(out=E[:, 1, :], in_=W["Gi"][:], func=AF.Exp), reads=[k_("Gi")], writes=[k_("E1")])
                    yield
                    S.op("act", lambda: nc.scalar.activation(out=E[:, 3, :], in_=W["Gi"][:], func=AF.Exp, scale=-1.0, bias=tot),
                         reads=[k_("Gi"), k_("pfx")], writes=[k_("E3")])
                    yield
                    S.op("act", lambda: nc.scalar.activation(out=W2["Et"][:], in_=tot, func=AF.Exp), reads=[k_("pfx")], writes=[k2("Et")])
                    yield
                    if sdec:
                        dmA, dm4, gcol = dm_aps(di, par)
                        dk_ = f"dmw{di}p{par}"; gk_ = f"gcol{di}p{par}"
                        msk = (cm2[:, 0:64], cm2[:, 64:128], cm2[:, 128:192]) if fwd else (cm2[:, 128:192], cm2[:, 192:256], cm2[:, 0:64])
                        for ti, srcn in enumerate(("Ge", "Gi")):
                            po_ = di * 256 + ti * 128
                            S.op("pe", lambda: nc.tensor.transpose(out=pin[1][0:64, po_:po_ + 128], in_=W[srcn][:], identity=CM("ident", False)),
                                 reads=[k_(srcn), "cm_f"], writes=["l1pin1"])
                            yield
                            S.op("dve", lambda: nc.vector.tensor_copy(out=gcol[:, ti:ti + 1], in_=pin[1][0:64, po_:po_ + 1]),
                                 reads=["l1pin1"], writes=[gk_])
                            yield
                        rowGe = W["Ge"][0:64, :]; rowGi = W["Gi"][0:64, :]
                        for qd, (row, rk_, cidx) in enumerate(((rowGe, "Ge", 0), (rowGe, "Ge", 1), (rowGi, "Gi", 0), (rowGi, "Gi", 1))):
                            S.op("dve", lambda: nc.vector.tensor_scalar(out=dmA[:, qd * 64:(qd + 1) * 64], in0=row, scalar1=gcol[:, cidx:cidx + 1], scalar2=0.0,
                                                                        op0=ALU.subtract, op1=ALU.min),
                                 reads=[k_(rk_), gk_], writes=[dk_])
                            yield
                        S.op("dve", lambda: nc.vector.tensor_scalar(out=dm4, in0=rowGe, scalar1=-1.0, scalar2=gcol[:, 0:1], op0=ALU.mult, op1=ALU.add),
                             reads=[k_("Ge"), gk_], writes=[dk_])
                        yield
                        S.op("dve", lambda: nc.vector.tensor_scalar(out=dm4, in0=dm4, scalar1=0.0, scalar2=None, op0=ALU.min),
                             reads=[dk_], writes=[dk_])
                        yield
                        S.op("act", lambda: nc.scalar.activation(out=dmA, in_=dmA, func=AF.Exp), reads=[dk_], writes=[dk_])
                        yield
                        S.op("act", lambda: nc.scalar.activation(out=dm4, in_=dm4, func=AF.Exp), reads=[dk_], writes=[dk_])
                        yield
                        for qd, mi in enumerate((0, 0, 1, 1, 2)):
                            dsl = dmA[:, qd * 64:(qd + 1) * 64] if qd < 4 else dm4
                            S.op("pool", lambda: nc.gpsimd.tensor_tensor(out=dsl, in0=dsl, in1=msk[mi], op=ALU.mult),
                                 reads=[dk_, "cm2"], writes=[dk_])
                            yield
                        S.op("dve", lambda: nc.vector.tensor_copy(out=W2["AR"][:, 0, :], in_=W["al"][:]), reads=[k_("al")], writes=[k2("AR")])
                        yield
                        S.op("pool", lambda: nc.gpsimd.tensor_copy(out=W2["AR"][:, 1, :], in_=W["rr"][:]), reads=[k_("rr")], writes=[k2("AR")])
                        yield
                        S.op("pool", lambda: nc.gpsimd.tensor_copy(out=W2["KH"][:], in_=W["kd"][:]), reads=[k_("kd")], writes=[k2("KH")])
                        yield
                    else:
                        mcol = W["Gi"][:, 32:33]
                        S.op("dve", lambda: nc.vector.tensor_scalar(out=negm[:], in0=mcol, scalar1=-1.0, scalar2=None, op0=ALU.mult),
                             reads=[k_("Gi")], writes=[k_("negm")])
                        yield
                        S.op("dve", lambda: nc.vector.tensor_scalar(out=Es[:, 0, :], in0=W["Ge"][:], scalar1=negm[:, 0:1], scalar2=40.0, op0=ALU.add, op1=ALU.min),
                             reads=[k_("Ge"), k_("negm")], writes=[k_("Es0")])
                        yield
                        S.op("dve", lambda: nc.vector.tensor_scalar(out=Es[:, 1, :], in0=W["Gi"][:], scalar1=negm[:, 0:1], scalar2=40.0, op0=ALU.add, op1=ALU.min),
                             reads=[k_("Gi"), k_("negm")], writes=[k_("Es1")])
                        yield
                        S.op("dve", lambda: nc.vector.tensor_scalar(out=Es[:, 2, :], in0=W["Gi"][:], scalar1=-1.0, scalar2=mcol, op0=ALU.mult, op1=ALU.add),
                             reads=[k_("Gi")], writes=[k_("Es2")])
                        yield
                        S.op("dve", lambda: nc.vector.tensor_scalar(out=Es[:, 2, :], in0=Es[:, 2, :], scalar1=40.0, scalar2=None, op0=ALU.min),
                             reads=[k_("Es2")], writes=[k_("Es2")])
                        yield
                        S.op("act", lambda: nc.scalar.activation(out=Es[:, :, :], in_=Es[:, :, :], func=AF.Exp), reads=[k_("Es0"), k_("Es1"), k_("Es2")],
                             writes=[k_("Es0"), k_("Es1"), k_("Es2")])
                        yield
                        S.op("dve", lambda: nc.vector.tensor_tensor(out=W2["AR"][:, 0, :], in0=W["al"][:], in1=Es[:, 0, :], op=ALU.mult),
                             reads=[k_("al"), k_("Es0")], writes=[k2("AR")])
                        yield
                        S.op("pool", lambda: nc.gpsimd.tensor_tensor(out=W2["AR"][:, 1, :], in0=W["rr"][:], in1=Es[:, 1, :], op=ALU.mult),
                             reads=[k_("rr"), k_("Es1")], writes=[k2("AR")])
                        yield
                        S.op("dve", lambda: nc.vector.tensor_tensor(out=W2["BH"][:], in0=W["be"][:], in1=Es[:, 2, :], op=ALU.mult),
                             reads=[k_("be"), k_("Es2")], writes=[k2("BH")])
                        yield
                        S.op("pool", lambda: nc.gpsimd.tensor_tensor(out=W2["KH"][:], in0=W["kd"][:], in1=Es[:, 2, :], op=ALU.mult),
                             reads=[k_("kd"), k_("Es2")], writes=[k2("KH")])
                        yield
                    S.op("dve", lambda: nc.vector.tensor_tensor(out=W2["ART"][:, 0, :], in0=W["al"][:], in1=E[:, 0, :], op=ALU.mult),
                         reads=[k_("al"), k_("E0")], writes=[k2("ART")])
                    yield
                    S.op("pool", lambda: nc.gpsimd.tensor_tensor(out=W2["ART"][:, 1, :], in0=W["rr"][:], in1=E[:, 1, :], op=ALU.mult),
                         reads=[k_("rr"), k_("E1")], writes=[k2("ART")])
                    yield
                    S.op("dve", lambda: nc.vector.tensor_tensor(out=W["BT"][:], in0=W["be"][:], in1=E[:, 3, :], op=ALU.mult),
                         reads=[k_("be"), k_("E3")], writes=[k_("BT")])
                    yield
                    S.op("pool", lambda: nc.gpsimd.tensor_tensor(out=W["KT"][:], in0=W["kd"][:], in1=E[:, 3, :], op=ALU.mult),
                         reads=[k_("kd"), k_("E3")], writes=[k_("KT")])
                    yield
                    for ti, (src, srk, dst, dsk) in enumerate(((W["BT"][:], k_("BT"), W2["BTt"], k2("BTt")), (W["KT"][:], k_("KT"), W2["KTt"], k2("KTt")),
                                                               (vsrc_ap, vkeys, W2["VT"], k2("VT")))):
                        S.op("pe", lambda: nc.tensor.transpose(out=ptr[:, ti * 128:(ti + 1) * 128], in_=src, identity=CM("ident")),
                             reads=([srk] if isinstance(srk, str) else list(srk)) + ["cm_b"], writes=[ptk])
                        yield
                        e = "act" if di == 0 else "dve"
                        S.op(e, copy_op(e, dst[:], ptr[:, ti * 128:(ti + 1) * 128]), reads=[ptk], writes=[dsk])
                        yield

                def head_gen(dn, j, pr, dk, dv, c0, emit, sdec, par=0):
                    W = wk[dn]
                    W2 = wk2[dn][par]
                    k_ = lambda nm: f"{nm}{dn}"
                    k2 = lambda nm: f"{nm}{dn}p{par}"
                    kj = lambda nm: f"{nm}{dn}{j}"
                    fwd = dn == "f"
                    di = 0 if fwd else 1
                    ms_mi = cm2[:, 0:128] if fwd else cm2[:, 128:256]
                    ms_other = cm2[:, 128:192] if fwd else cm2[:, 0:64]
                    I64 = cm2[:, 256:320]
                    pc = pC[di * 2 + j]; pck = f"l1pC{di*2+j}"
                    ev = ("dve" if (j == 0 or di == 0) else "act") if not sdec else ("dve" if di == 0 else "act")
                    N = W[f"N{j}"]; NTt = W[f"NT{j}"]; PTm = W[f"PT{j}"]; TT = W[f"TT{j}"]
                    ARB = W[f"ARB{j}"]; AKRK = W[f"AKRK{j}"]; Zb = W[f"Zb{j}"]; Ub = W[f"Ub{j}"]
                    ar2 = W2["AR"][pr, :, :].rearrange("p a n -> p (a n)")
                    pa = pc[0:64, :]
                    if sdec:
                        dmA, dm4, _gc = dm_aps(di, par)
                        dk_ = f"dmw{di}p{par}"
                        S.op("pe", lambda: nc.tensor.matmul(pa[:, 0:128], lhsT=W2["KH"][pr, :], rhs=ar2, start=True, stop=True),
                             reads=[k2("KH"), k2("AR")], writes=[pck])
                        yield
                        S.op("pe", lambda: nc.tensor.matmul(pa[:, 256:320], lhsT=W2["AR"][pr, 0, :], rhs=W2["KH"][pr, :], start=True, stop=True),
                             reads=[k2("KH"), k2("AR")], writes=[pck])
                        yield
                        S.op("dve", lambda: nc.vector.scalar_tensor_tensor(out=NTt[0][:], in0=pa[:, 0:64], scalar=-1.0, in1=dmA[:, 0:64], op0=ALU.mult, op1=ALU.mult),
                             reads=[pck, dk_], writes=[kj("NT0")])
                        yield
                        S.op("dve", lambda: nc.vector.tensor_tensor(out=AKRK[:, 0:64], in0=pa[:, 0:64], in1=dmA[:, 64:128], op=ALU.mult),
                             reads=[pck, dk_], writes=[kj("AKRK")])
                        yield
                        S.op("dve", lambda: nc.vector.scalar_tensor_tensor(out=ARB[:], in0=pa[:, 64:128], scalar=-1.0, in1=dmA[:, 128:192], op0=ALU.mult, op1=ALU.mult),
                             reads=[pck, dk_], writes=[kj("ARB")])
                        yield
                        S.op("dve", lambda: nc.vector.tensor_tensor(out=AKRK[:, 64:128], in0=pa[:, 64:128], in1=dmA[:, 192:256], op=ALU.mult),
                             reads=[pck, dk_], writes=[kj("AKRK")])
                        yield
                        S.op("dve", lambda: nc.vector.scalar_tensor_tensor(out=N[0][:], in0=pa[:, 256:320], scalar=-1.0, in1=dm4, op0=ALU.mult, op1=ALU.mult),
                             reads=[pck, dk_], writes=[kj("N0")])
                        yield
                    else:
                        S.op("pe", lambda: nc.tensor.matmul(pa[:, 0:128], lhsT=W2["BH"][pr, :], rhs=ar2, start=True, stop=True),
                             reads=[k2("BH"), k2("AR")], writes=[pck])
                        yield
                        S.op("pe", lambda: nc.tensor.matmul(pa[:, 128:256], lhsT=W2["KH"][pr, :], rhs=ar2, start=True, stop=True),
                             reads=[k2("KH"), k2("AR")], writes=[pck])
                        yield
                        S.op("pe", lambda: nc.tensor.matmul(pa[:, 256:320], lhsT=W2["AR"][pr, 0, :], rhs=W2["BH"][pr, :], start=True, stop=True),
                             reads=[k2("BH"), k2("AR")], writes=[pck])
                        yield
                        S.op("dve", lambda: nc.vector.tensor_tensor(out=NTt[0][:], in0=pa[:, 0:64], in1=ms_mi[:, 0:64], op=ALU.mult),
                             reads=[pck, "cm2"], writes=[kj("NT0")])
                        yield
                        S.op("dve", lambda: nc.vector.tensor_tensor(out=ARB[:], in0=pa[:, 64:128], in1=ms_mi[:, 64:128], op=ALU.mult),
                             reads=[pck, "cm2"], writes=[kj("ARB")])
                        yield
                        S.op("dve", lambda: nc.vector.tensor_tensor(out=AKRK[:], in0=pa[:, 128:256], in1=ms_mi, op=ALU.mult),
                             reads=[pck, "cm2"], writes=[kj("AKRK")])
                        yield
                        S.op("dve", lambda: nc.vector.tensor_tensor(out=N[0][:], in0=pa[:, 256:320], in1=ms_other, op=ALU.mult),
                             reads=[pck, "cm2"], writes=[kj("N0")])
                        yield
                    S.op("pool", lambda: nc.gpsimd.tensor_tensor(out=PTm[0][:], in0=NTt[0][:], in1=I64, op=ALU.add),
                         reads=[kj("NT0"), "cm2"], writes=[kj("PT0")])
                    yield
                    for lv in range(1, 6):
                        a_, b_ = (lv - 1) % 2, lv % 2
                        S.op("pe", lambda: nc.tensor.matmul(pa[:, 320:384], lhsT=NTt[a_][:], rhs=N[a_][:], start=True, stop=True),
                             reads=[kj(f"NT{a_}"), kj(f"N{a_}")], writes=[pck])
                        yield
                        if lv < 5:
                            S.op("pe", lambda: nc.tensor.matmul(pa[:, 384:448], lhsT=N[a_][:], rhs=NTt[a_][:], start=True, stop=True),
                                 reads=[kj(f"NT{a_}"), kj(f"N{a_}")], writes=[pck])
                            yield
                        S.op(ev, copy_op(ev, N[b_][:], pa[:, 320:384]), reads=[pck], writes=[kj(f"N{b_}")])
                        yield
                        if lv < 5:
                            S.op(ev, copy_op(ev, NTt[b_][:], pa[:, 384:448]), reads=[pck], writes=[kj(f"NT{b_}")])
                            yield
                        if ev == "dve":
                            S.op("pe", lambda: nc.tensor.matmul(pa[:, 448:512], lhsT=N[b_][:], rhs=PTm[a_][:], start=True, stop=True),
                                 reads=[kj(f"N{b_}"), kj(f"PT{a_}")], writes=[pck])
                            yield
                            dst_ = PTm[b_][:] if lv < 5 else TT[:]
                            S.op("dve", lambda: nc.vector.tensor_tensor(out=dst_, in0=pa[:, 448:512], in1=PTm[a_][:], op=ALU.add),
                                 reads=[pck, kj(f"PT{a_}")], writes=[kj(f"PT{b_}") if lv < 5 else kj("TT")])
                            yield
                        else:
                            S.op("pe", lambda: nc.tensor.matmul(pa[:, 448:512], lhsT=I64, rhs=PTm[a_][:], start=True, stop=False),
                                 reads=["cm2", kj(f"PT{a_}")], writes=[pck])
                            yield
                            S.op("pe", lambda: nc.tensor.matmul(pa[:, 448:512], lhsT=N[b_][:], rhs=PTm[a_][:], start=False, stop=True),
                                 reads=[kj(f"N{b_}"), kj(f"PT{a_}")], writes=[pck])
                            yield
                            if lv < 5:
                                S.op(ev, copy_op(ev, PTm[b_][:], pa[:, 448:512]), reads=[pck], writes=[kj(f"PT{b_}")])
                            else:
                                S.op(ev, copy_op(ev, TT[:], pa[:, 448:512]), reads=[pck], writes=[kj("TT")])
                            yield
                    vt = W2["VT"][:, j * dv:(j + 1) * dv] if dv == 64 else W2["VT"][:, :]
                    cs = slice(j * 64, (j + 1) * 64) if dk == 64 else slice(0, 128)
                    split = (pr.start == 64)
                    psy = pin[0]; psyk = "l1pin0"
                    yo = di * 256
                    if split:
                        S.op("pe", lambda: nc.tensor.matmul(psy[0:64, yo:yo + dv], lhsT=W2["ART"][pr, 0, :], rhs=W["Hb"][pr, :dv], start=True, stop=True),
                             reads=[k2("ART"), kj("Hb")], writes=[psyk])
                        yield
                        S.op("pe", lambda: nc.tensor.matmul(pc[0:64, 0:dv], lhsT=AKRK[:, 0:64], rhs=vt, start=True, stop=True),
                             reads=[kj("AKRK"), k2("VT")], writes=[pck])
                        yield
                        S.op("act", lambda: nc.scalar.copy(out=W["ztmp"][:, :dv], in_=psy[0:64, yo:yo + dv]), reads=[psyk], writes=[k_("ztmp")])
                        yield
                        S.op("dve", lambda: nc.vector.tensor_tensor(out=Zb[:, :dv], in0=pc[0:64, 0:dv], in1=W["ztmp"][:, :dv], op=ALU.add),
                             reads=[pck, k_("ztmp")], writes=[kj("Zb")])
                        yield
                    else:
                        S.op("pe", lambda: nc.tensor.matmul(pc[0:64, 0:dv], lhsT=W2["ART"][pr, 0, :], rhs=W["Hb"][pr, :dv], start=True, stop=False),
                             reads=[k2("ART"), kj("Hb")], writes=[pck])
                        yield
                        S.op("pe", lambda: nc.tensor.matmul(pc[0:64, 0:dv], lhsT=AKRK[:, 0:64], rhs=vt, start=False, stop=True),
                             reads=[kj("AKRK"), k2("VT")], writes=[pck])
                        yield
                        S.op(ev, copy_op(ev, Zb[:, :dv], pc[0:64, 0:dv]), reads=[pck], writes=[kj("Zb")])
                        yield
                    S.op("pe", lambda: nc.tensor.matmul(pc[0:64, 128:128 + dv], lhsT=TT[:], rhs=Zb[:, :dv], start=True, stop=True),
                         reads=[kj("TT"), kj("Zb")], writes=[pck])
                    yield
                    S.op(ev, copy_op(ev, Ub[:, :dv], pc[0:64, 128:128 + dv]), reads=[pck], writes=[kj("Ub")])
                    yield
                    if emit:
                        t_lat = c0 - NCTX
                        yk_ = f"Yacc{t_lat//64}.{pr.start}"
                        if split:
                            S.op("pe", lambda: nc.tensor.matmul(psy[pr, yo + 64:yo + 128], lhsT=W["Hb"][pr, :dv], rhs=W2["ART"][pr, 1, :], start=True, stop=True),
                                 reads=[k2("ART"), kj("Hb")], writes=[psyk])
                            yield
                            S.op("dve", lambda: nc.vector.tensor_tensor(out=K.yacc[pr, t_lat:t_lat + 64], in0=psy[pr, yo + 64:yo + 128],
                                                                        in1=K.yacc[pr, t_lat:t_lat + 64], op=ALU.add),
                                 reads=[psyk, yk_], writes=[yk_])
                            yield
                        else:
                            S.op("pe", lambda: nc.tensor.matmul(pc[pr, 256:320], lhsT=W["Hb"][pr, :dv], rhs=W2["ART"][pr, 1, :], start=True, stop=False),
                                 reads=[k2("ART"), kj("Hb")], writes=[pck])
                            yield
                        S.op("pe", lambda: nc.tensor.matmul(pc[pr, 256:320], lhsT=Ub[:, :dv], rhs=ARB[:], start=split, stop=False),
                             reads=[kj("Ub"), kj("ARB")], writes=[pck])
                        yield
                        S.op("pe", lambda: nc.tensor.matmul(pc[pr, 256:320], lhsT=vt, rhs=AKRK[:, 64:128], start=False, stop=True),
                             reads=[kj("AKRK"), k2("VT")], writes=[pck])
                        yield
                        S.op("dve", lambda: nc.vector.tensor_tensor(out=K.yacc[pr, t_lat:t_lat + 64], in0=pc[pr, 256:320],
                                                                    in1=K.yacc[pr, t_lat:t_lat + 64], op=ALU.add),
                             reads=[pck, yk_], writes=[yk_])
                        yield
                    S.op("pe", lambda: nc.tensor.matmul(pc[pr, 320:320 + dv], lhsT=W2["BTt"][:, cs], rhs=Ub[:, :dv], start=True, stop=False),
                         reads=[k2("BTt"), kj("Ub")], writes=[pck])
                    yield
                    S.op("pe", lambda: nc.tensor.matmul(pc[pr, 320:320 + dv], lhsT=W2["KTt"][:, cs], rhs=vt, start=False, stop=True),
                         reads=[k2("KTt"), k2("VT")], writes=[pck])
                    yield
                    S.op("dve", lambda: nc.vector.scalar_tensor_tensor(out=W["Hs"][pr, :dv], in0=W["Hs"][pr, :dv], scalar=W2["Et"][pr, 0:1],
                                                                       in1=pc[pr, 320:320 + dv], op0=ALU.mult, op1=ALU.add),
                         reads=[pck, kj("Hs"), k2("Et")], writes=[kj("Hs")])
                    yield
                    S.op("act", lambda: nc.scalar.copy(out=W["Hb"][pr, :dv], in_=W["Hs"][pr, :dv]), reads=[kj("Hs")], writes=[kj("Hb")])
                    yield

                def round_robin(gens):
                    gens = list(gens)
                    while gens:
                        for g in list(gens):
                            try:
                                next(g)
                            except StopIteration:
                                gens.remove(g)

                def reset_state(insts):
                    for dn in ("f", "b"):
                        S.op("pool", lambda: nc.gpsimd.memset(wk[dn]["Hs"][:], 0.0), writes=[f"Hs{dn}{j}" for j in range(2)])
                        S.op("pool", lambda: nc.gpsimd.memset(wk[dn]["Hb"][:], 0.0), writes=[f"Hb{dn}{j}" for j in range(2)])
                    S.op("pool", lambda: nc.gpsimd.memset(K.yacc, 0.0), writes=[f"Yacc{i}.{p}" for i in range(32) for p in (0, 64)])

                TW = xb(10); AL = xb(11); SG = xb(12)
                if DBG['lora']:
                    load_cols([(1536, 384)])
                    proj_taps(0, 128, shift_taps(12), AF.Tanh, lambda s0, n: TW[:, s0:s0 + n], ["TW"])
                    proj_taps(128, 128, shift_taps(13), AF.Identity, lambda s0, n: AL[:, s0:s0 + n], ["AL"])
                    proj_taps(256, 128, shift_taps(14), AF.Sigmoid, lambda s0, n: SG[:, s0:s0 + n], ["SG"])
                lw2 = sb(L, "lw2", [128, 3, 512], BF16)
                K.lw2flat = lw2[:].rearrange("p a n -> p (a n)")
                for i, src in enumerate((cw2_d, ca2_d, cg2_d)):
                    wload(stg, "l1stg", lw2[:, i, :], ["lw2"], src, [128, 512], "sp")
                sq = sb(L, "l1sq", [128, 256], BF16)
                t32 = [sb(L, f"l1t{i}", [128, 256]) for i in range(3)]
                TCH = [(i * 256, 256) for i in range(9)]

                for gi in range(DBG['ng']):
                    rA, kA, vA, kkA, afA, abA = (xb(i) for i in range(6))
                    lwF = xf(3); lwB = xf(4)
                    gA = xb(13)
                    load_cols([(gi * 128, 128), (512 + gi * 128, 128), (1024 + gi * 128, 128)])
                    proj_taps(0, 128, shift_taps(gi), AF.Identity, lambda s0, n: rA[:, s0:s0 + n], ["rA"])
                    proj_taps(128, 128, shift_taps(4 + gi), AF.Identity, lambda s0, n: kA[:, s0:s0 + n], ["kA"])
                    proj_taps(256, 128, shift_taps(8 + gi), AF.Identity, lambda s0, n: vA[:, s0:s0 + n], ["vA"])
                    for (t0, n) in (TCH if DBG['prep'] else []):
                        S.op("dve", lambda: nc.vector.tensor_scalar(out=t32[0][:, :n], in0=kA[:, t0:t0 + n], scalar1=v1[:, 45 + gi:46 + gi], scalar2=None,
                                                                    op0=ALU.mult), reads=["kA", "v1"], writes=["l1t0"])
                        S.op("act", lambda: nc.scalar.activation(out=sq[:, :n], in_=t32[0][:, :n], func=AF.Square), reads=["l1t0"], writes=["l1sq"])
                        S.op("pe", lambda: nc.tensor.matmul(pin[0][:, :n], lhsT=CM("onesA"), rhs=sq[:, :n], start=True, stop=True),
                             reads=["l1sq", "cm_b"], writes=["l1pin0"])
                        S.op("act", lambda: nc.scalar.activation(out=t32[1][:, :n], in_=pin[0][:, :n], func=AF.Sqrt, scale=64.0, bias=eps_t[:, 0:1]),
                             reads=["l1pin0", "eps"], writes=["l1t1"])
                        S.op("dve", lambda: nc.vector.reciprocal(out=t32[1][:, :n], in_=t32[1][:, :n]), reads=["l1t1"], writes=["l1t1"])
                        S.op("dve", lambda: nc.vector.tensor_tensor(out=kkA[:, t0:t0 + n], in0=t32[0][:, :n], in1=t32[1][:, :n], op=ALU.mult),
                             reads=["l1t0", "l1t1"], writes=["kkA"])
                        for d_, (lwX, aX) in enumerate(((lwF, afA), (lwB, abA))):
                            prd = slice(d_ * 64, (d_ + 1) * 64)
                            S.op("pe", lambda: nc.tensor.matmul(pin[1][:, :n], lhsT=lw2[prd, 0, gi * 128:(gi + 1) * 128], rhs=TW[prd, t0:t0 + n],
                                                                start=True, stop=True), reads=["lw2", "TW"], writes=["l1pin1"])
                            S.op("act", lambda: nc.scalar.activation(out=t32[2][:, :n], in_=pin[1][:, :n], func=AF.Sigmoid,
                                                                     bias=v1[:, 49 + d_ * 4 + gi:50 + d_ * 4 + gi]),
                                 reads=["l1pin1", "v1"], writes=["l1t2"])
                            S.op("dve", lambda: nc.vector.tensor_scalar(out=lwX[:, t0:t0 + n], in0=t32[2][:, :n], scalar1=-0.6065306597126334,
                                                                        scalar2=None, op0=ALU.mult), reads=["l1t2"], writes=["lwX"])
                            S.op("pe", lambda: nc.tensor.matmul(pin[1][:, :n], lhsT=lw2[prd, 1, gi * 128:(gi + 1) * 128], rhs=AL[prd, t0:t0 + n],
                                                                start=True, stop=True), reads=["lw2", "AL"], writes=["l1pin1"])
                            S.op("act", lambda: nc.scalar.activation(out=aX[:, t0:t0 + n], in_=pin[1][:, :n], func=AF.Sigmoid,
                                                                     bias=v1[:, 57 + d_ * 4 + gi:58 + d_ * 4 + gi]),
                                 reads=["l1pin1", "v1"], writes=["aX"])
                        S.op("pe", lambda: nc.tensor.matmul(pin[1][:, :n], lhsT=lw2[:, 2, gi * 128:(gi + 1) * 128], rhs=SG[:, t0:t0 + n],
                                                            start=True, stop=True), reads=["lw2", "SG"], writes=["l1pin1"])
                        S.op("act", lambda: nc.scalar.copy(out=gA[:, t0:t0 + n], in_=pin[1][:, :n]), reads=["l1pin1"], writes=["gA"])
                    insts = [(slice(0, 64), 64, 64), (slice(64, 128), 64, 64)]
                    K.yacc = Yall[:, gi, :]
                    reset_state(insts)
                    def rw_pre(dn, aX, lwX, csl, gi_):
                        W = wk[dn]
                        return [
                            lambda: S.op("dve", lambda: nc.vector.tensor_scalar(out=W["al"][:], in0=kkA[:, csl], scalar1=-1.0, scalar2=None, op0=ALU.mult),
                                         reads=["kkA"], writes=[f"al{dn}"]),
                            lambda: S.op("pool", lambda: nc.gpsimd.tensor_tensor(out=W["be"][:], in0=kkA[:, csl], in1=aX[:, csl], op=ALU.mult),
                                         reads=["kkA", "aX"], writes=[f"be{dn}"]),
                            lambda: S.op("dve", lambda: nc.vector.tensor_scalar(out=W["kd"][:], in0=aX[:, csl], scalar1=-1.0, scalar2=v1[:, 65 + gi_:66 + gi_],
                                                                                op0=ALU.add, op1=ALU.mult), reads=["aX", "v1"], writes=[f"kd{dn}"]),
                            lambda: S.op("dve", lambda: nc.vector.scalar_tensor_tensor(out=W["kd"][:], in0=W["kd"][:], scalar=1.0, in1=kA[:, csl],
                                                                                       op0=ALU.add, op1=ALU.mult), reads=[f"kd{dn}", "kA"], writes=[f"kd{dn}"]),
                            lambda: S.op("pool", lambda: nc.gpsimd.tensor_copy(out=W["rr"][:], in_=rA[:, csl]), reads=["rA"], writes=[f"rr{dn}"]),
                            lambda: S.op("pool", lambda: nc.gpsimd.tensor_copy(out=W["lw"][:], in_=lwX[:, csl]), reads=["lwX"], writes=[f"lw{dn}"]),
                        ]
                    prev_h = []
                    for step in range(DBG['nstep'] + 1):
                        pg = []; hg = []
                        if step < DBG['nstep']:
                            for dn, order, aX, lwX in (("f", fwd_order, afA, lwF), ("b", bwd_order, abA, lwB)):
                                ch = order[step]; c0 = ch * 64
                                csl = slice(c0, c0 + 64)
                                pg.append(prefix_gen(dn, rw_pre(dn, aX, lwX, csl, gi), vA[:, csl], ["vA"], False, step % 2))
                                for j, (pr, dk, dv) in enumerate(insts):
                                    hg.append(head_gen(dn, j, pr, dk, dv, c0, ch >= 4, False, step % 2))
                        round_robin(prev_h + pg)
                        prev_h = hg
                    for qi in range(8 if DBG['rwout'] else 0):
                        t0 = NCTX + qi * 256; n = 256; y0 = qi * 256
                        yk = [f"Yacc{i}.{p}" for i in range(y0 // 64, y0 // 64 + 4) for p in (0, 64)]
                        S.op("pe", lambda: nc.tensor.matmul(pin[0][:, :n], lhsT=CM("onesA"), rhs=K.yacc[:, y0:y0 + n], start=True, stop=True),
                             reads=yk + ["cm_b"], writes=["l1pin0"])
                        S.op("dve", lambda: nc.vector.tensor_tensor(out=t32[0][:, :n], in0=K.yacc[:, y0:y0 + n], in1=pin[0][:, :n], op=ALU.subtract),
                             reads=yk + ["l1pin0"], writes=["l1t0"])
                        S.op("act", lambda: nc.scalar.activation(out=sq[:, :n], in_=t32[0][:, :n], func=AF.Square), reads=["l1t0"], writes=["l1sq"])
                        S.op("pe", lambda: nc.tensor.matmul(pin[0][:, :n], lhsT=CM("onesA"), rhs=sq[:, :n], start=True, stop=True),
                             reads=["l1sq", "cm_b"], writes=["l1pin0"])
                        S.op("act", lambda: nc.scalar.activation(out=t32[1][:, :n], in_=pin[0][:, :n], func=AF.Sqrt, bias=gneps[:, 0:1]),
                             reads=["l1pin0", "gneps"], writes=["l1t1"])
                        S.op("dve", lambda: nc.vector.reciprocal(out=t32[1][:, :n], in_=t32[1][:, :n]), reads=["l1t1"], writes=["l1t1"])
                        S.op("dve", lambda: nc.vector.tensor_tensor(out=t32[0][:, :n], in0=t32[0][:, :n], in1=t32[1][:, :n], op=ALU.mult),
                             reads=["l1t0", "l1t1"], writes=["l1t0"])
                        S.op("act", lambda: nc.scalar.activation(out=t32[0][:, :n], in_=t32[0][:, :n], func=AF.Identity,
                                                                 scale=v1[:, 69 + gi:70 + gi], bias=v1[:, 73 + gi:74 + gi]),
                             reads=["l1t0", "v1"], writes=["l1t0"])
                        S.op("dve", lambda: nc.vector.tensor_tensor(out=t32[1][:, :n], in0=afA[:, t0:t0 + n], in1=abA[:, t0:t0 + n], op=ALU.add),
                             reads=["aX"], writes=["l1t1"])
                        S.op("dve", lambda: nc.vector.tensor_scalar(out=t32[1][:, :n], in0=t32[1][:, :n], scalar1=-2.0, scalar2=v1[:, 65 + gi:66 + gi],
                                                                    op0=ALU.add, op1=ALU.mult), reads=["l1t1", "v1"], writes=["l1t1"])
                        S.op("dve", lambda: nc.vector.scalar_tensor_tensor(out=t32[1][:, :n], in0=t32[1][:, :n], scalar=2.0, in1=kA[:, t0:t0 + n],
                                                                           op0=ALU.add, op1=ALU.mult), reads=["l1t1", "kA"], writes=["l1t1"])
                        S.op("dve", lambda: nc.vector.scalar_tensor_tensor(out=sq[:, :n], in0=t32[1][:, :n], scalar=v1[:, 77 + gi:78 + gi],
                                                                           in1=rA[:, t0:t0 + n], op0=ALU.mult, op1=ALU.mult),
                             reads=["l1t1", "rA", "v1"], writes=["l1sq"])
                        S.op("pe", lambda: nc.tensor.matmul(pin[1][:, :n], lhsT=CM("onesA"), rhs=sq[:, :n], start=True, stop=True),
                             reads=["l1sq", "cm_b"], writes=["l1pin1"])
                        S.op("dve", lambda: nc.vector.scalar_tensor_tensor(out=t32[2][:, :n], in0=pin[1][:, :n], scalar=64.0, in1=vA[:, t0:t0 + n],
                                                                           op0=ALU.mult, op1=ALU.mult), reads=["l1pin1", "vA"], writes=["l1t2"])
                        S.op("pool", lambda: nc.gpsimd.tensor_tensor(out=t32[0][:, :n], in0=t32[0][:, :n], in1=t32[2][:, :n], op=ALU.add),
                             reads=["l1t0", "l1t2"], writes=["l1t0"])
                        S.op("dve", lambda: nc.vector.tensor_tensor(out=Yall[:, gi, y0:y0 + n], in0=t32[0][:, :n], in1=gA[:, t0:t0 + n], op=ALU.mult),
                             reads=["l1t0", "gA"], writes=["Yall"])
                    S.barrier()

                sm = XF[0:16, 7 * T:8 * T]
                smb = sb(L, "smb", [16, 256])
                smg = sb(L, "smg", [16, 256])
                gsm = sb(L, "gsmp", [16, 2])
                sel = sb(L, "sel", [16, 16 * 128])
                S.dma("sp", gsm[:], gsm_d, writes=["gsm"])
                S.dma("sp", sel[:], sel_d, writes=["sel"])
                S.op("act", lambda: nc.scalar.activation(out=gsm[:, 0:1], in_=gsm[:, 0:1], func=AF.Exp), reads=["gsm"], writes=["gsm"])
                S.op("dve", lambda: nc.vector.tensor_scalar(out=gsm[:, 0:1], in0=gsm[:, 0:1], scalar1=-1.0, scalar2=None, op0=ALU.mult),
                     reads=["gsm"], writes=["gsm"])
                load_cols([(INC + 2048, 16)])
                proj_taps(0, 16, [(0, None)], AF.Identity, lambda s0, n: sm[:, s0:s0 + n], ["sm"])
                t32q = xb(12); t32k = xb(13)
                one_t = sb(L, "one_t", [128, 1])
                S.op("pool", lambda: nc.gpsimd.memset(one_t[:], 1.0), writes=["one_t"])

                for hd in range(DBG['gdn']):
                    qA, kA, vA, zA = (xb(i) for i in range(4))
                    bmF = xf(2); bmB = xf(3); gmF = xf(4); gmB = xf(5)
                    load_cols([(INC + hd * 128, 128), (INC + 512 + hd * 128, 128), (INC + 1024 + hd * 128, 128), (INC + 1536 + hd * 128, 128)])

                    def ctaps(fc):
                        return [(0, v1[:, 81 + fc * 5 + 2:81 + fc * 5 + 3])] + [(o, v1[:, 81 + fc * 5 + 2 + o:81 + fc * 5 + 3 + o]) for o in (-2, -1, 1, 2)]
                    proj_taps(0, 128, ctaps(hd), AF.Silu, lambda s0, n: t32q[:, s0:s0 + n], ["t32q"])
                    proj_taps(128, 128, ctaps(4 + hd), AF.Silu, lambda s0, n: t32k[:, s0:s0 + n], ["t32k"])
                    proj_taps(256, 128, ctaps(8 + hd), AF.Silu, lambda s0, n: vA[:, s0:s0 + n], ["vA"])
                    proj_taps(384, 128, [(0, None)], AF.Silu, lambda s0, n: zA[:, s0:s0 + n], ["zA"])
                    for (t0, n) in (TCH if DBG['gprep'] >= 2 else []):
                        for (srcb, dstb, scl, dk_) in ((t32q, qA, 128.0 ** -0.5, "qA"), (t32k, kA, 1.0, "kA")):
                            S.op("act", lambda: nc.scalar.activation(out=sq[:, :n], in_=srcb[:, t0:t0 + n], func=AF.Square), reads=["t32q", "t32k"], writes=["l1sq"])
                            S.op("pe", lambda: nc.tensor.matmul(pin[0][:, :n], lhsT=CM("ones1024"), rhs=sq[:, :n], start=True, stop=True),
                                 reads=["l1sq", "cm_b"], writes=["l1pin0"])
                            S.op("act", lambda: nc.scalar.activation(out=t32[1][:, :n], in_=pin[0][:, :n], func=AF.Sqrt, scale=1024.0, bias=eps_t[:, 0:1]),
                                 reads=["l1pin0", "eps"], writes=["l1t1"])
                            S.op("dve", lambda: nc.vector.reciprocal(out=t32[1][:, :n], in_=t32[1][:, :n]), reads=["l1t1"], writes=["l1t1"])
                            S.op("dve", lambda: nc.vector.scalar_tensor_tensor(out=dstb[:, t0:t0 + n], in0=srcb[:, t0:t0 + n], scalar=float(scl),
                                                                               in1=t32[1][:, :n], op0=ALU.mult, op1=ALU.mult),
                                 reads=["t32q", "t32k", "l1t1"], writes=[dk_])
                        S.op("act", lambda: nc.scalar.activation(out=smb[:, :n], in_=sm[:, t0:t0 + n], func=AF.Sigmoid), reads=["sm"], writes=["smb"])
                        S.op("act", lambda: nc.scalar.activation(out=smg[:, :n], in_=sm[:, t0:t0 + n], func=AF.Exp, bias=gsm[:, 1:2]),
                             reads=["sm", "gsm"], writes=["smg"])
                        S.op("act", lambda: nc.scalar.activation(out=smg[:, :n], in_=smg[:, :n], func=AF.Ln, bias=one_t[0:16, 0:1]),
                             reads=["smg", "one_t"], writes=["smg"])
                        S.op("dve", lambda: nc.vector.tensor_scalar(out=smg[:, :n], in0=smg[:, :n], scalar1=gsm[:, 0:1], scalar2=None, op0=ALU.mult),
                             reads=["smg", "gsm"], writes=["smg"])
                        for (row, dst, srcm, dk_) in (((hd, bmF, smb, "bmF"), (4 + hd, bmB, smb, "bmB"), (8 + hd, gmF, smg, "gmF"), (12 + hd, gmB, smg, "gmB")) if DBG['gprep'] >= 3 else []):
                            S.op("pe", lambda: nc.tensor.matmul(pin[1][:, :n], lhsT=sel[:, row * 128:(row + 1) * 128], rhs=srcm[:, :n],
                                                                start=True, stop=True), reads=["sel", "smb", "smg"], writes=["l1pin1"])
                            e = evac_eng()
                            S.op(e, copy_op(e, dst[:, t0:t0 + n], pin[1][:, :n]), reads=["l1pin1"], writes=[dk_])
                    insts = [(slice(0, 128), 128, 128)]
                    K.yacc = Yall[:, 4 + hd, :]
                    reset_state(insts)
                    def gd_pre(dn, bmX, gmX, csl):
                        W = wk[dn]
                        return [
                            lambda: S.op("pool", lambda: nc.gpsimd.tensor_copy(out=W["al"][:], in_=kA[:, csl]), reads=["kA"], writes=[f"al{dn}"]),
                            lambda: S.op("dve", lambda: nc.vector.tensor_tensor(out=W["kd"][:], in0=kA[:, csl], in1=bmX[:, csl], op=ALU.mult),
                                         reads=["kA", "bmF", "bmB"], writes=[f"kd{dn}"]),
                            lambda: S.op("pool", lambda: nc.gpsimd.tensor_copy(out=W["lw"][:], in_=gmX[:, csl]), reads=["gmF", "gmB"], writes=[f"lw{dn}"]),
                            lambda: S.op("act", lambda: nc.scalar.activation(out=W["be"][:], in_=gmX[:, csl], func=AF.Exp), reads=["gmF", "gmB"], writes=[f"be{dn}"]),
                            lambda: S.op("dve", lambda: nc.vector.scalar_tensor_tensor(out=W["be"][:], in0=W["be"][:], scalar=-1.0, in1=W["kd"][:],
                                                                                       op0=ALU.mult, op1=ALU.mult), reads=[f"be{dn}", f"kd{dn}"], writes=[f"be{dn}"]),
                            lambda: S.op("pool", lambda: nc.gpsimd.tensor_copy(out=W["rr"][:], in_=qA[:, csl]), reads=["qA"], writes=[f"rr{dn}"]),
                        ]
                    prev_h = []
                    for step in range(DBG['nstep'] + 1):
                        pg = []; hg = []
                        for dn, order, bmX, gmX in ((("f", fwd_order, bmF, gmF), ("b", bwd_order, bmB, gmB)) if step < DBG['nstep'] else ()):
                            ch = order[step]; c0 = ch * 64
                            csl = slice(c0, c0 + 64)
                            pg.append(prefix_gen(dn, gd_pre(dn, bmX, gmX, csl), vA[:, csl], ["vA"], True, step % 2))
                            for j, (pr, dk, dv) in enumerate(insts):
                                hg.append(head_gen(dn, j, pr, dk, dv, c0, ch >= 4, True, step % 2))
                        round_robin(prev_h + pg)
                        prev_h = hg
                    for qi in range(8):
                        t0 = NCTX + qi * 256; n = 256; y0 = qi * 256
                        yk = [f"Yacc{i}.{p}" for i in range(y0 // 64, y0 // 64 + 4) for p in (0, 64)]
                        S.op("act", lambda: nc.scalar.activation(out=sq[:, :n], in_=K.yacc[:, y0:y0 + n], func=AF.Square), reads=yk, writes=["l1sq"])
                        S.op("pe", lambda: nc.tensor.matmul(pin[0][:, :n], lhsT=CM("ones1024"), rhs=sq[:, :n], start=True, stop=True),
                             reads=["l1sq", "cm_b"], writes=["l1pin0"])
                        S.op("act", lambda: nc.scalar.activation(out=t32[1][:, :n], in_=pin[0][:, :n], func=AF.Sqrt, scale=8.0, bias=eps_t[:, 0:1]),
                             reads=["l1pin0", "eps"], writes=["l1t1"])
                        S.op("dve", lambda: nc.vector.reciprocal(out=t32[1][:, :n], in_=t32[1][:, :n]), reads=["l1t1"], writes=["l1t1"])
                        S.op("dve", lambda: nc.vector.scalar_tensor_tensor(out=t32[0][:, :n], in0=K.yacc[:, y0:y0 + n], scalar=v1[:, 141:142],
                                                                           in1=t32[1][:, :n], op0=ALU.mult, op1=ALU.mult),
                             reads=yk + ["l1t1", "v1"], writes=["l1t0"])
                        S.op("dve", lambda: nc.vector.tensor_tensor(out=Yall[:, 4 + hd, y0:y0 + n], in0=t32[0][:, :n], in1=zA[:, t0:t0 + n], op=ALU.mult),
                             reads=["l1t0", "zA"], writes=["Yall"])
                    S.barrier()
                LS.close()
                L = L0
                if dbg == "y1":
                    dump_fm(Yall, 8, BF16, 16)
                stg2 = sb(L, "l1stg2", [128, 2048])
                wo = sb(L, "l1wo", [128, 8, D], BF16)
                pin = [ps(L, f"l1pinb{i}", [128, 512]) for i in range(2)]
                cov = cdout_d.rearrange("(kc p) n -> p kc n", p=128)
                for i in range(4):
                    wload(stg2, "l1stg2", wo[:, 2 * i:2 * i + 2, :], ["l1wo"], cov[:, 2 * i:2 * i + 2, :], [128, 2, D], "sp")
                for c in range(8):
                    S.dma("sp" if c % 2 == 0 else "pool", X[:, c, :], xpark_d[:, c * T:(c + 1) * T], writes=[f"X{t}" for t in range(NT)])
                S.barrier()
                it = 0
                for qi in range(4):
                    t0 = NCTX + qi * 512; n = 512; y0 = qi * 512
                    for m in range(8):
                        p_ = pin[it % 2]; pk = f"l1pinb{it%2}"; it += 1
                        for i in range(8):
                            S.op("pe", lambda: nc.tensor.matmul(p_[:, :n], lhsT=wo[:, i, m * 128:(m + 1) * 128], rhs=Yall[:, i, y0:y0 + n],
                                                                start=(i == 0), stop=(i == 7)), reads=["l1wo", "Yall"], writes=[pk])
                        S.op("dve", lambda: nc.vector.scalar_tensor_tensor(out=X[:, m, t0:t0 + n], in0=p_[:, :n], scalar=mod(l, 2, m, 0),
                                                                           in1=X[:, m, t0:t0 + n], op0=ALU.mult, op1=ALU.add),
                             reads=[pk, "modv"] + xkeys(t0, n), writes=xkeys(t0, n))
            S.barrier()

        K.stop = False
        if dbg in ("load", "mods"):
            write_out()
            S.finish()
            return nc, S
        layer0()
        if dbg in ("qa", "oa"):
            S.finish()
            return nc, S
        if dbg != "noffn":
            ffn(0, nlayers == 1)
        if nlayers == 2:
            layer1()
            if dbg not in ("noffn1", "y1"):
                ffn(1, True)
        write_out()
        S.finish()
    return nc, S


def make_in_maps(inputs, cores):
    consts, cnames = _consts()
    f = lambda k: np.asarray(inputs[k], np.float32)
    c_ctx = f("c_ctx")
    shared = {
        "ada_w": np.ascontiguousarray(f("ada_w")),
        "ada_b_fm": np.stack([_fm(f("ada_b")[l]) for l in range(2)], 0),
        "ffn_w_up": np.ascontiguousarray(f("ffn_w_up")),
        "ffn_w_down": np.ascontiguousarray(f("ffn_w_down")),
        "ab_w_in": np.ascontiguousarray(f("ab_w_in")[0]),
        "ab_w_out": np.ascontiguousarray(f("ab_w_out")[0]),
        "b_w_uq": np.ascontiguousarray(f("b_w_uq")[0]),
        "b_w_uk": np.ascontiguousarray(f("b_w_uk")[0]),
        "b_w_uv": np.ascontiguousarray(f("b_w_uv")[0]),
        "cmats": consts["cmats"], "cosA": consts["cosA"], "sinA": consts["sinA"],
        "cosB": consts["cosB"], "sinB": consts["sinB"],
    }
    fc = np.zeros((2, 128, 44, 4), np.float32)
    for l in range(2):
        for j in range(3):
            fc[l, :, :, j] = _fm(f("ffn_conv_w")[l, j])
        fc[l, :, :, 3] = _fm(f("ffn_conv_b")[l])
    shared["ffn_conv_fm"] = fc
    v = np.zeros((128, 16), np.float32)
    v[:, 0] = np.tile(f("a_q_norm")[0], 2)
    v[:, 1] = np.tile(f("a_k_norm")[0], 2)
    v[:, 2:4] = _fm(f("b_cq_norm")[0])
    v[:, 4:6] = _fm(f("b_ckv_norm")[0])
    v[64:96, 6] = f("b_kr_norm")[0]
    v[:64, 7] = f("b_kn_norm")[0]
    v[:64, 8] = f("b_qn_norm")[0]
    v[64:96, 8] = f("b_qr_norm")[0]
    shared["vecs0"] = v
    shared["sink_b"] = np.ascontiguousarray(np.broadcast_to(f("a_sink")[0][None, :], (64, 8)))
    shared["cd_w_in"] = np.ascontiguousarray(f("cd_w_in")[0])
    shared["cd_w_out"] = np.ascontiguousarray(f("cd_w_out")[0])
    shared["c_w2r"] = np.ascontiguousarray(f("c_w2")[0].reshape(128, 512))
    shared["c_a2r"] = np.ascontiguousarray(f("c_a2")[0].reshape(128, 512))
    shared["c_g2"] = np.ascontiguousarray(f("c_g2")[0])
    v1 = np.zeros((128, 160), np.float32)
    v1[:, 0:15] = _fm(f("c_mu_prev")[0]); v1[:, 15:30] = _fm(f("c_mu_next")[0])
    v1[:, 45:49] = _fm(f("c_k_k")[0])
    for d_ in range(2):
        v1[:, 49 + d_ * 4:53 + d_ * 4] = _fm(f("c_w0")[0, d_])
        v1[:, 57 + d_ * 4:61 + d_ * 4] = _fm(f("c_a0")[0, d_])
    v1[:, 65:69] = _fm(f("c_k_a")[0]); v1[:, 69:73] = _fm(f("c_ln_w")[0]); v1[:, 73:77] = _fm(f("c_ln_b")[0])
    v1[:, 77:81] = _fm(f("c_r_k")[0].reshape(-1))
    dcw = f("d_conv_w")[0]
    for fc in range(12):
        for j in range(5):
            v1[:, 81 + fc * 5 + j] = dcw[j, fc * 128:(fc + 1) * 128]
    v1[:, 141] = f("d_o_norm")[0]
    shared["vecs1"] = v1
    gsm = np.zeros((16, 2), np.float32)
    for d_ in range(2):
        gsm[8 + 4 * d_:12 + 4 * d_, 0] = f("d_A_log")[0, d_]
        gsm[8 + 4 * d_:12 + 4 * d_, 1] = f("d_dt_bias")[0, d_]
    shared["gsm"] = gsm
    a_ = np.arange(64)[:, None]; b_ = np.arange(64)[None, :]
    shared["cm2"] = np.concatenate([(a_ < b_), (a_ <= b_), (a_ > b_), (a_ >= b_), (a_ == b_)], 1).astype(np.float32)
    sel = np.zeros((16, 16 * 128), np.float32)
    for q in range(16):
        sel[q, q * 128:(q + 1) * 128] = 1.0
    shared["sel16"] = sel
    maps = []
    for b in cores:
        m = dict(shared)
        m["x"] = np.ascontiguousarray(f("x")[b])
        m["ctx"] = np.ascontiguousarray(f("ctx")[b])
        m["cv"] = np.ascontiguousarray(np.stack([_fm(f("c")[b]), _fm(c_ctx)], -1))
        maps.append(m)
    return maps


def kernel(**inputs):
    nc, S = build_program()
    maps = make_in_maps(inputs, list(range(8)))
    res = run_bass_kernel_spmd(nc, maps, core_ids=list(range(8)))
    return np.stack([r["out"] for r in res.results], 0).astype(np.float32)
```

```python
import numpy as np
from contextlib import ExitStack
import concourse.bass as bass
import concourse.mybir as mybir
from concourse.bass_utils import run_bass_kernel_spmd

F32 = mybir.dt.float32
BF16 = mybir.dt.bfloat16
AF = mybir.ActivationFunctionType
ALU = mybir.AluOpType

D = 1024
T = 2304
NCTX = 256
NT = 18
EPS = 1e-6
CHUNKS = [(0, 256), (256, 512), (768, 512), (1280, 512), (1792, 512)]
DFF = 2816
DBG = {'lora': 1, 'ng': 4, 'prep': 1, 'nstep': 36, 'rwout': 1, 'gdn': 4, 'gprep': 3, 'cs': 5, 'invbf': 0, 'invlv': 5, 'invev': 5, 'invp': 1, 'gseq': 0, 'gpool': 1}


class Sched:
    def __init__(self, nc, stack, n_dma_sems=12):
        self.nc = nc
        self.engs = {"pe": nc.tensor, "dve": nc.vector, "act": nc.scalar, "pool": nc.gpsimd, "sp": nc.sync}
        self.sem = {k: stack.enter_context(nc.semaphore("s_" + k)) for k in self.engs}
        self.cnt = {k: 0 for k in self.engs}
        self.dsem = [stack.enter_context(nc.semaphore(f"dq{i}")) for i in range(n_dma_sems)]
        self.dcnt = [0] * n_dma_sems
        self.dnext = 0
        self.seen = {k: {} for k in self.engs}
        self.lastw = {}
        self.readers = {}
        self.ninst = 0

    def _semobj(self, sk):
        return self.sem[sk] if isinstance(sk, str) else self.dsem[sk]

    def _wait(self, e, sk, val):
        if val <= 0 or self.seen[e].get(sk, 0) >= val:
            return
        self.engs[e].wait_ge(self._semobj(sk), val)
        self.seen[e][sk] = val

    def _deps(self, e, reads, writes):
        deps = {}

        def add(p):
            if p is not None and deps.get(p[0], 0) < p[1]:
                deps[p[0]] = p[1]

        for k in reads:
            add(self.lastw.get(k))
        for k in writes:
            add(self.lastw.get(k))
            for sk, v in self.readers.get(k, {}).items():
                add((sk, v))
        for sk, v in deps.items():
            if sk == "pe" and e == "pe":
                continue
            self._wait(e, sk, v)

    def _commit(self, tok, reads, writes):
        sk, v = tok
        for k in reads:
            self.readers.setdefault(k, {})[sk] = v
        for k in writes:
            self.lastw[k] = tok
            self.readers[k] = {}

    def op(self, e, ins_fn, reads=(), writes=()):
        self._deps(e, reads, writes)
        ins = ins_fn()
        self.cnt[e] += 1
        self.ninst += 1
        ins.then_inc(self.sem[e], 1)
        self._commit((e, self.cnt[e]), reads, writes)
        return ins

    def dma(self, e, out, in_, reads=(), writes=(), **kw):
        i = self.dnext
        self.dnext = (self.dnext + 1) % len(self.dsem)
        self._wait(e, i, self.dcnt[i])
        self._deps(e, reads, writes)
        ins = self.engs[e].dma_start(out=out, in_=in_, **kw)
        self.dcnt[i] += 16
        self.ninst += 1
        ins.then_inc(self.dsem[i], 16)
        self._commit((i, self.dcnt[i]), reads, writes)
        return ins

    def barrier(self):
        for e in self.engs:
            for o in self.engs:
                if o != e:
                    self._wait(e, o, self.cnt[o])
            for i in range(len(self.dsem)):
                self._wait(e, i, self.dcnt[i])
        self.lastw = {}
        self.readers = {}

    def finish(self):
        for o in self.engs:
            if o != "sp":
                self._wait("sp", o, self.cnt[o])
        for i in range(len(self.dsem)):
            self._wait("sp", i, self.dcnt[i])


def _rope_tables():
    theta = 10000.0
    s = np.arange(2048)
    row = (s // 64).astype(np.float64)
    col = (s % 64).astype(np.float64)

    def tab(nd):
        h = nd // 2
        half = h // 2
        inv = theta ** (-np.arange(half, dtype=np.float64) / half)
        cos = np.ones((nd, T)); sin = np.zeros((nd, T))
        for d_ in range(nd):
            b = d_ // h
            i = (d_ % h) % half
            pos = row if b == 0 else col
            ang = (pos.astype(np.float32)[:, None] * inv.astype(np.float32)[None, :])[:, i]
            cos[d_, NCTX:] = np.cos(ang.astype(np.float32))
            sin[d_, NCTX:] = np.sin(ang.astype(np.float32))
        R = np.zeros((nd, nd))
        for d_ in range(nd):
            e = d_ % h
            if e < half:
                R[d_, d_ + half] = -1.0
            else:
                R[d_, d_ - half] = 1.0
        return cos.astype(np.float32), sin.astype(np.float32), R.astype(np.float32)

    cA, sA, RA = tab(64)
    cB, sB, RB = tab(32)
    cosA = np.concatenate([cA, cA], 0); sinA = np.concatenate([sA, sA], 0)
    RA2 = np.zeros((128, 128), np.float32); RA2[:64, :64] = RA; RA2[64:, 64:] = RA
    cosB = np.ones((96, T), np.float32); sinB = np.zeros((96, T), np.float32)
    cosB[64:] = cB; sinB[64:] = sB
    RB96 = np.zeros((96, 96), np.float32); RB96[64:, 64:] = RB
    return cosA, sinA, RA2.T.copy(), cosB, sinB, RB96.T.copy()


def _consts():
    c = {}
    cosA, sinA, RAT, cosB, sinB, RBT = _rope_tables()
    c["cosA"] = cosA; c["sinA"] = sinA; c["cosB"] = cosB; c["sinB"] = sinB
    mats = {}
    mats["ident"] = np.eye(128, dtype=np.float32)
    mats["ones1024"] = np.full((128, 128), 1.0 / 1024, np.float32)
    mats["ones256"] = np.full((128, 128), 1.0 / 256, np.float32)
    o = np.zeros((128, 128), np.float32); o[:64, :64] = 1 / 64; o[64:, 64:] = 1 / 64
    mats["onesA"] = o
    o = np.zeros((128, 128), np.float32); o[:64, :64] = 1 / 64; o[64:96, 64:96] = 1 / 32
    mats["onesB"] = o
    mats["RAT"] = RAT
    r = np.zeros((128, 128), np.float32); r[:96, :96] = RBT
    mats["RBT"] = r
    a = np.arange(128)[:, None]; b = np.arange(128)[None, :]
    mats["maskPrevT"] = np.where(a <= b, 0.0, -30000.0).astype(np.float32)
    mats["maskNextT"] = np.where(b <= a, 0.0, -30000.0).astype(np.float32)
    names = list(mats.keys())
    c["cmats"] = np.stack([mats[n] for n in names], 1).astype(np.float32)
    return c, names


def _fm(v, p=128):
    v = np.asarray(v, np.float32)
    return np.ascontiguousarray(v.reshape(-1, p).T)


class Ctx:
    pass


def build_program(dbg=None, nlayers=2):
    nc = bass.Bass("TRN2", target_bir_lowering=False)
    st = ExitStack()
    K = Ctx()
    with st:
        S = Sched(nc, st)

        def din(name, shape, dt=F32):
            return nc.dram_tensor(name, list(shape), dt, kind="ExternalInput").ap()

        uid = [0]

        def sb(stack, name, shape, dt=F32):
            uid[0] += 1
            return stack.enter_context(nc.sbuf_tensor(f"{name}_s{uid[0]}", list(shape), dt))

        def ps(stack, name, shape, dt=F32):
            uid[0] += 1
            return stack.enter_context(nc.psum_tensor(f"{name}_p{uid[0]}", list(shape), dt))

        consts, cnames = _consts()
        NCM = len(cnames)
        x_d = din("x", [2048, D]); ctx_d = din("ctx", [NCTX, D])
        cv_d = din("cv", [128, 8, 2])
        adaw_d = din("ada_w", [2, D, 6 * D]); adab_d = din("ada_b_fm", [2, 128, 48])
        wup_d = din("ffn_w_up", [2, D, 2 * DFF]); wdn_d = din("ffn_w_down", [2, DFF, D])
        fconv_d = din("ffn_conv_fm", [2, 128, 44, 4])
        abin_d = din("ab_w_in", [D, 1312]); about_d = din("ab_w_out", [D, D])
        wuq_d = din("b_w_uq", [256, 768]); wuk_d = din("b_w_uk", [256, 512]); wuv_d = din("b_w_uv", [256, 512])
        vecs_d = din("vecs0", [128, 16])
        sink_d = din("sink_b", [64, 8])
        cmats_d = din("cmats", [128, NCM, 128])
        cosA_d = din("cosA", [128, T]); sinA_d = din("sinA", [128, T])
        cosB_d = din("cosB", [96, T]); sinB_d = din("sinB", [96, T])
        cdin_d = din("cd_w_in", [D, 3984]); cdout_d = din("cd_w_out", [D, D])
        cw2_d = din("c_w2r", [128, 512]); ca2_d = din("c_a2r", [128, 512]); cg2_d = din("c_g2", [128, 512])
        v1_d = din("vecs1", [128, 160])
        gsm_d = din("gsm", [16, 2])
        cm2_d = din("cm2", [64, 320])
        sel_d = din("sel16", [16, 16 * 128])
        xpark_d = nc.dram_tensor("xpark", [128, 8 * T], F32, kind="Internal").ap()
        out_d = nc.dram_tensor("out", [2048, D], F32, kind="ExternalOutput").ap()
        dbg_d = None
        if dbg is not None:
            dbg_d = nc.dram_tensor("dbg", [T, D], F32, kind="ExternalOutput").ap()

        X = sb(st, "X", [128, 8, T])
        identf = sb(st, "identf", [128, 128])
        cm_b = sb(st, "cm_b", [128, NCM, 128], BF16)
        id4 = sb(st, "id4", [128, 4, 128], BF16)
        modv = sb(st, "modv", [128, 2, 48, 2])
        onep = sb(st, "onep", [128, 2, 2, 8, 2])
        adab = sb(st, "adab", [128, 2, 48])
        cv = sb(st, "cv", [128, 8, 2])
        scv = sb(st, "scv", [128, 8, 2])
        vecs = sb(st, "vecs", [128, 16])
        zeros = sb(st, "zeros", [128, 128])

        def CM(name, bf=True):
            if not bf:
                assert name == "ident"
                return identf[:]
            i = cnames.index(name)
            return cm_b[:, i, :]

        rr = {"cast": 0, "ev": 0}

        def evac_eng():
            rr["ev"] ^= 1
            return "act" if rr["ev"] else "dve"

        def copy_op(e, out, in_):
            if e == "act":
                return lambda: nc.scalar.copy(out=out, in_=in_)
            if e == "dve":
                return lambda: nc.vector.tensor_copy(out=out, in_=in_)
            return lambda: nc.gpsimd.tensor_copy(out=out, in_=in_)

        with ExitStack() as ph:
            cm_f = sb(ph, "cm_f", [128, NCM, 128])
            S.dma("sp", cm_f[:], cmats_d, writes=["cm_f0"])
            S.op("dve", lambda: nc.vector.tensor_copy(out=cm_b[:], in_=cm_f[:]), reads=["cm_f0"], writes=["cm_b"])
            S.op("act", lambda: nc.scalar.copy(out=identf[:], in_=cm_f[:, 0, :]), reads=["cm_f0"], writes=["cm_f"])
            for r in range(4):
                S.op("pool", lambda: nc.gpsimd.tensor_copy(out=id4[:, r, :], in_=cm_f[:, 0, :]), reads=["cm_f0"], writes=["id4"])
            S.barrier()
        S.dma("sp", cv[:], cv_d, writes=["cv"])
        S.dma("sp", adab[:], adab_d.rearrange("l p m -> p l m"), writes=["adab"])
        S.dma("sp", vecs[:], vecs_d, writes=["vecs"])
        S.op("pool", lambda: nc.gpsimd.memset(zeros[:], 0.0), writes=["zeros"])
        S.op("act", lambda: nc.scalar.activation(out=scv[:], in_=cv[:], func=AF.Silu), reads=["cv"], writes=["scv"])

        with ExitStack() as ph:
            xin = [sb(ph, f"xin{i}", [128, D]) for i in range(2)]
            pT = [ps(ph, f"pT{i}", [128, 512]) for i in range(4)]
            for t in range(NT):
                src = ctx_d[t * 128:(t + 1) * 128, :] if t < 2 else x_d[(t - 2) * 128:(t - 1) * 128, :]
                xi = xin[t % 2]
                S.dma("sp" if t % 2 == 0 else "pool", xi[:], src, writes=[f"xin{t%2}"])
                for hh in range(2):
                    pt = pT[(t % 2) * 2 + hh]
                    pk = f"pT{(t%2)*2+hh}"
                    for c4 in range(4):
                        c = hh * 4 + c4
                        S.op("pe", lambda: nc.tensor.transpose(out=pt[:, c4 * 128:(c4 + 1) * 128],
                                                               in_=xi[:, c * 128:(c + 1) * 128],
                                                               identity=CM("ident", False)),
                             reads=[f"xin{t%2}", "cm_f"], writes=[pk])
                    e = evac_eng()
                    S.op(e, copy_op(e, X[:, hh * 4:(hh + 1) * 4, t * 128:(t + 1) * 128],
                                    pt[:].rearrange("p (c n) -> p c n", c=4)),
                         reads=[pk], writes=[f"X{t}"])
        S.barrier()

        with ExitStack() as ph:
            astg = [sb(ph, f"astg{i}", [128, 8, 768]) for i in range(2)]
            pm = ps(ph, "pm", [128, 48, 2])
            adv = adaw_d.rearrange("l (kc p) n -> l p kc n", p=128)
            it = 0
            for l in range(nlayers):
                for g in range(8):
                    a = astg[it % 2]
                    S.dma("sp" if it % 2 == 0 else "pool", a[:], adv[l, :, :, g * 768:(g + 1) * 768],
                          writes=[f"astg{it%2}"])
                    for mm in range(6):
                        m = g * 6 + mm
                        for k in range(8):
                            S.op("pe", lambda: nc.tensor.matmul(pm[:, m, :], lhsT=a[:, k, mm * 128:(mm + 1) * 128],
                                                                rhs=scv[:, k, :], start=(k == 0), stop=(k == 7)),
                                 reads=[f"astg{it%2}", "scv"], writes=["pm"])
                    it += 1
                for w in range(2):
                    S.op("dve", lambda: nc.vector.tensor_tensor(out=modv[:, l, :, w], in0=pm[:, :, w], in1=adab[:, l, :],
                                                                op=ALU.add),
                         reads=["pm", "adab"], writes=["modv"])
                for ji, j in enumerate((1, 4)):
                    S.op("dve", lambda: nc.vector.tensor_scalar_add(out=onep[:, l, ji, :, :],
                                                                    in0=modv[:, l, j * 8:(j + 1) * 8, :], scalar1=1.0),
                         reads=["modv"], writes=["onep"])
        S.barrier()

        def mod(l, j, c, w):
            return modv[:, l, j * 8 + c, w:w + 1]

        def modulate_chunk(ph_bufs, l, jshift, ji_scale, t0, n, w, dst_fn, dst_keys):
            sq, pms, sd, rstd, tmp = ph_bufs
            for c in range(8):
                S.op("act", lambda: nc.scalar.activation(out=sq[c % 2][:, :n], in_=X[:, c, t0:t0 + n], func=AF.Square),
                     reads=[f"X{tt}" for tt in range(t0 // 128, (t0 + n) // 128)], writes=[f"msq{c%2}"])
                S.op("pe", lambda: nc.tensor.matmul(pms[:, :n], lhsT=CM("ones1024"), rhs=sq[c % 2][:, :n],
                                                    start=(c == 0), stop=(c == 7)),
                     reads=[f"msq{c%2}", "cm_b"], writes=["pms"])
            S.op("act", lambda: nc.scalar.activation(out=sd[:, :n], in_=pms[:, :n], func=AF.Sqrt, bias=eps_t[:, 0:1]),
                 reads=["pms", "eps"], writes=["msd"])
            S.op("dve", lambda: nc.vector.reciprocal(out=rstd[:, :n], in_=sd[:, :n]), reads=["msd"], writes=["mrstd"])
            for c in range(8):
                S.op("dve", lambda: nc.vector.scalar_tensor_tensor(out=tmp[c % 2][:, :n], in0=X[:, c, t0:t0 + n],
                                                                   scalar=onep[:, l, ji_scale, c, w:w + 1],
                                                                   in1=rstd[:, :n], op0=ALU.mult, op1=ALU.mult),
                     reads=[f"X{tt}" for tt in range(t0 // 128, (t0 + n) // 128)] + ["mrstd", "onep"],
                     writes=[f"mtmp{c%2}"])
                S.op("act", lambda: nc.scalar.activation(out=dst_fn(c), in_=tmp[c % 2][:, :n], func=AF.Identity,
                                                         bias=mod(l, jshift, c, w), scale=1.0),
                     reads=[f"mtmp{c%2}", "modv"], writes=dst_keys)

        eps_t = sb(st, "eps_t", [128, 1])
        S.op("pool", lambda: nc.gpsimd.memset(eps_t[:], EPS), writes=["eps"])

        def mod_bufs(ph):
            sq = [sb(ph, f"msq{i}", [128, 512], BF16) for i in range(2)]
            pms = ps(ph, "pms", [128, 512])
            sd = sb(ph, "msd", [128, 512])
            rstd = sb(ph, "mrstd", [128, 512])
            tmp = [sb(ph, f"mtmp{i}", [128, 512]) for i in range(2)]
            return (sq, pms, sd, rstd, tmp)

        def wload(stg, stg_key, dst_ap, dst_keys, dram_ap, shape, q):
            view = stg[:, :int(np.prod(shape[1:]))]
            if len(shape) == 3:
                view = view.rearrange("p (a b) -> p a b", a=shape[1])
            view = view[:shape[0]] if shape[0] < 128 else view
            S.dma(q, view, dram_ap, writes=[stg_key])
            rr["cast"] = (rr["cast"] + 1) % 2
            e = ("dve", "pool")[rr["cast"]]
            S.op(e, copy_op(e, dst_ap, view), reads=[stg_key], writes=dst_keys)

        def xkeys(t0, n):
            return [f"X{tt}" for tt in range(t0 // 128, (t0 + n + 127) // 128)]

        def layer0():
            l = 0
            with ExitStack() as L:
                cqn = sb(L, "cqn", [128, 2, T], BF16)
                ckvn = sb(L, "ckvn", [128, 2, T], BF16)
                KR = sb(L, "KR", [96, T], BF16)
                LA = ExitStack()
                QA = sb(LA, "QA", [128, 4, T], BF16)
                KA = sb(LA, "KA", [128, 2, T], BF16)
                VA = sb(LA, "VA", [128, NT, 2, 128], BF16)
                S.op("pool", lambda: nc.gpsimd.memset(VA[:], 1.0), writes=["VA"])
                with ExitStack() as ph:
                    win = sb(ph, "win", [128, 8, 1312], BF16)
                    wkd = sb(ph, "wkd", [128, 8, 256], BF16)
                    wkr = sb(ph, "wkr", [128, 8, 96], BF16)
                    abv = abin_d.rearrange("(kc p) n -> p kc n", p=128)
                    with ExitStack() as phs:
                        stg = [sb(phs, f"stg{i}", [128, 4096]) for i in range(2)]
                        for i, (c0, c1) in enumerate([(0, 512), (512, 1024), (1024, 1312)]):
                            wload(stg[i % 2], f"stg{i%2}", win[:, :, c0:c1], ["win"], abv[:, :, c0:c1], [128, 8, c1 - c0],
                                  "sp" if i % 2 == 0 else "pool")
                        S.barrier()
                    for g in range(2):
                        for hf in range(2):
                            S.op("pool", lambda: nc.gpsimd.tensor_copy(out=wkd[:, :, g * 128 + hf * 64:g * 128 + hf * 64 + 64],
                                                                       in_=win[:, :, 512 + g * 64:512 + g * 64 + 64]),
                                 reads=["win"], writes=["wkd"])
                    S.op("pool", lambda: nc.gpsimd.memset(wkr[:], 0.0), writes=["wkr"])
                    S.op("pool", lambda: nc.gpsimd.tensor_copy(out=wkr[:, :, 64:96], in_=win[:, :, 1280:1312]),
                         reads=["win"], writes=["wkr"])
                    hbuf = sb(ph, "hbuf", [128, 8, 512], BF16)
                    mb = mod_bufs(ph)
                    tabs1 = [sb(ph, f"tab{j}", [128, 512]) for j in range(4)]
                    tabs = [tabs1, tabs1]
                    pin = [ps(ph, f"pin{i}", [128, 512]) for i in range(2)]
                    pms2 = ps(ph, "pms2", [128, 512])
                    prot = ps(ph, "prot", [128, 512])
                    sq2 = [sb(ph, f"sq2{i}", [128, 512], BF16) for i in range(2)]
                    sd2 = sb(ph, "sd2", [128, 512])
                    rs2 = sb(ph, "rs2", [128, 512])
                    qn = sb(ph, "qn", [128, 512], BF16)
                    t1 = sb(ph, "t1", [128, 512])
                    t2 = sb(ph, "t2", [128, 512])
                    cnt = {"pin": 0}

                    def pipeline(mm_list, M, n, ones_name, gain_ap, rope, dst_ap, dst_keys, tb):
                        pi = cnt["pin"] % 2
                        cnt["pin"] += 1
                        p_in = pin[pi]
                        for i, (lh, rh, rk) in enumerate(mm_list):
                            S.op("pe", lambda: nc.tensor.matmul(p_in[:M, :n], lhsT=lh, rhs=rh, start=(i == 0),
                                                                stop=(i == len(mm_list) - 1)),
                                 reads=rk, writes=[f"pin{pi}"])
                        S.op("act", lambda: nc.scalar.activation(out=sq2[pi][:M, :n], in_=p_in[:M, :n], func=AF.Square),
                             reads=[f"pin{pi}"], writes=[f"sq2{pi}"])
                        S.op("pe", lambda: nc.tensor.matmul(pms2[:M, :n], lhsT=CM(ones_name)[:M, :M], rhs=sq2[pi][:M, :n],
                                                            start=True, stop=True),
                             reads=[f"sq2{pi}", "cm_b"], writes=["pms2"])
                        S.op("act", lambda: nc.scalar.activation(out=sd2[:M, :n], in_=pms2[:M, :n], func=AF.Sqrt,
                                                                 bias=eps_t[:M, 0:1]),
                             reads=["pms2", "eps"], writes=["sd2"])
                        S.op("dve", lambda: nc.vector.reciprocal(out=rs2[:M, :n], in_=sd2[:M, :n]), reads=["sd2"], writes=["rs2"])
                        o1 = qn[:M, :n] if rope else dst_ap
                        S.op("dve", lambda: nc.vector.scalar_tensor_tensor(out=o1, in0=p_in[:M, :n], scalar=gain_ap,
                                                                           in1=rs2[:M, :n], op0=ALU.mult, op1=ALU.mult),
                             reads=[f"pin{pi}", "rs2", "vecs"], writes=(["qn"] if rope else dst_keys))
                        if rope:
                            rname, ci, si = rope
                            S.op("pe", lambda: nc.tensor.matmul(prot[:M, :n], lhsT=CM(rname)[:M, :M], rhs=qn[:M, :n],
                                                                start=True, stop=True),
                                 reads=["qn", "cm_b"], writes=["prot"])
                            S.op("dve", lambda: nc.vector.tensor_tensor(out=t1[:M, :n], in0=qn[:M, :n], in1=tb[ci][:M, :n],
                                                                        op=ALU.mult),
                                 reads=["qn", f"tab{ci}"], writes=["t1"])
                            S.op("dve", lambda: nc.vector.tensor_tensor(out=t2[:M, :n], in0=prot[:M, :n], in1=tb[si][:M, :n],
                                                                        op=ALU.mult),
                                 reads=["prot", f"tab{si}"], writes=["t2"])
                            S.op("pool", lambda: nc.gpsimd.tensor_tensor(out=dst_ap, in0=t1[:M, :n], in1=t2[:M, :n], op=ALU.add),
                                 reads=["t1", "t2"], writes=dst_keys)

                    K.pipeline = pipeline
                    for ci_, (t0, n) in enumerate(CHUNKS):
                        w = 1 if ci_ == 0 else 0
                        tb = tabs[ci_ % 2]
                        for j, src in enumerate((cosA_d, sinA_d, cosB_d, sinB_d)):
                            np_ = 128 if j < 2 else 96
                            S.dma("pool", tb[j][:np_, :n], src[:, t0:t0 + n], writes=[f"tab{j}"])
                        modulate_chunk(mb, l, 0, 0, t0, n, w, lambda c: hbuf[:, c, :n], ["hbuf"])
                        ck = [f"tk{tt}" for tt in range(t0 // 128, (t0 + n) // 128)]
                        for i in range(4):
                            pipeline([(win[:, k, i * 128:(i + 1) * 128], hbuf[:, k, :n], ["win", "hbuf"]) for k in range(8)],
                                     128, n, "onesA", vecs[:, 0:1], ("RAT", 0, 1), QA[:, i, t0:t0 + n],
                                     [f"QA{i}.{tt}" for tt in range(t0 // 128, (t0 + n) // 128)], tb)
                        for g in range(2):
                            pipeline([(wkd[:, k, g * 128:(g + 1) * 128], hbuf[:, k, :n], ["wkd", "hbuf"]) for k in range(8)],
                                     128, n, "onesA", vecs[:, 1:2], ("RAT", 0, 1), KA[:, g, t0:t0 + n], ["KA"], tb)
                        for tt in range(n // 128):
                            pi = cnt["pin"] % 2
                            cnt["pin"] += 1
                            for k in range(8):
                                S.op("pe", lambda: nc.tensor.matmul(pin[pi][:, :128], lhsT=hbuf[:, k, tt * 128:(tt + 1) * 128],
                                                                    rhs=win[:, k, 640:768], start=(k == 0), stop=(k == 7)),
                                     reads=["win", "hbuf"], writes=[f"pin{pi}"])
                            e = evac_eng()
                            S.op(e, copy_op(e, VA[:, t0 // 128 + tt, :, 0:64], pin[pi][:, :128].rearrange("p (g d) -> p g d", g=2)),
                                 reads=[f"pin{pi}"], writes=["VA"])
                        for (dst, c0, gcol, dk) in ((cqn, 768, 2, "cqn"), (ckvn, 1024, 4, "ckvn")):
                            for i in range(2):
                                for k in range(8):
                                    S.op("pe", lambda: nc.tensor.matmul(pin[i][:, :n], lhsT=win[:, k, c0 + i * 128:c0 + (i + 1) * 128],
                                                                        rhs=hbuf[:, k, :n], start=(k == 0), stop=(k == 7)),
                                         reads=["win", "hbuf"], writes=[f"pin{i}"])
                                S.op("act", lambda: nc.scalar.activation(out=sq2[i][:, :n], in_=pin[i][:, :n], func=AF.Square),
                                     reads=[f"pin{i}"], writes=[f"sq2{i}"])
                            for i in range(2):
                                S.op("pe", lambda: nc.tensor.matmul(pms2[:, :n], lhsT=CM("ones256"), rhs=sq2[i][:, :n],
                                                                    start=(i == 0), stop=(i == 1)),
                                     reads=[f"sq2{i}", "cm_b"], writes=["pms2"])
                            S.op("act", lambda: nc.scalar.activation(out=sd2[:, :n], in_=pms2[:, :n], func=AF.Sqrt, bias=eps_t[:, 0:1]),
                                 reads=["pms2", "eps"], writes=["sd2"])
                            S.op("dve", lambda: nc.vector.reciprocal(out=rs2[:, :n], in_=sd2[:, :n]), reads=["sd2"], writes=["rs2"])
                            for i in range(2):
                                S.op("dve", lambda: nc.vector.scalar_tensor_tensor(out=dst[:, i, t0:t0 + n], in0=pin[i][:, :n],
                                                                                   scalar=vecs[:, gcol + i:gcol + i + 1], in1=rs2[:, :n],
                                                                                   op0=ALU.mult, op1=ALU.mult),
                                     reads=[f"pin{i}", "rs2", "vecs"], writes=[dk])
                        pipeline([(wkr[:, k, :], hbuf[:, k, :n], ["wkr", "hbuf"]) for k in range(8)],
                                 96, n, "onesB", vecs[:96, 6:7], ("RBT", 2, 3), KR[:96, t0:t0 + n], ["KR"], tb)
                S.barrier()
                if dbg == "qa":
                    dump_fm(QA, 4, BF16)
                    LA.close()
                    return
                with ExitStack() as ph:
                    woA = sb(ph, "woA", [128, 4, D], BF16)
                    aov = about_d.rearrange("(kc p) n -> p kc n", p=128)
                    with ExitStack() as phs:
                        stg = [sb(phs, f"stg{i}", [128, 4096]) for i in range(2)]
                        for i in range(2):
                            wload(stg[i], f"stg{i}", woA[:, 2 * i:2 * i + 2, :], ["woA"], aov[:, 2 * i:2 * i + 2, :], [128, 2, D],
                                  "sp" if i == 0 else "pool")
                        S.barrier()
                    sk_raw = sb(ph, "sk_raw", [64, 8])
                    sk_exp = sb(ph, "sk_exp", [64, 8])
                    SE = sb(ph, "SE", [64, 2, 512])
                    S.dma("sp", sk_raw[:], sink_d, writes=["sk_raw"])
                    S.op("act", lambda: nc.scalar.activation(out=sk_exp[:], in_=sk_raw[:], func=AF.Exp), reads=["sk_raw"], writes=["sk_exp"])
                    for g in range(2):
                        for hb in range(4):
                            hf, j = hb // 2, hb % 2
                            hd = 4 * g + 2 * j + hf
                            S.op("dve", lambda: nc.vector.tensor_scalar(out=SE[:, g, hb * 128:(hb + 1) * 128], in0=zeros[:64, :],
                                                                        scalar1=sk_exp[:, hd:hd + 1], scalar2=None, op0=ALU.add),
                                 reads=["zeros", "sk_exp"], writes=["SE"])
                    pS = [[ps(ph, f"pS{i}{hf}", [128, 512]) for hf in range(2)] for i in range(2)]
                    pO = [ps(ph, f"pO{i}", [128, 512]) for i in range(2)]
                    PT = [sb(ph, f"PT{i}", [128, 512], BF16) for i in range(3)]
                    den = sb(ph, "den", [64, 512])
                    rden = sb(ph, "rden", [64, 512])
                    it = 0
                    nit = 0
                    for qb in range(NT):
                        q0 = qb * 128
                        if qb < 2:
                            kts = [(0, None), (1, None)]
                        else:
                            kts = [(0, None), (1, None)]
                            if qb - 1 >= 2:
                                kts.append((qb - 1, "maskPrevT"))
                            kts.append((qb, None))
                            if qb + 1 < NT:
                                kts.append((qb + 1, "maskNextT"))
                        for g in range(2):
                            po = pO[nit % 2]
                            pok = f"pO{nit%2}"
                            nit += 1
                            for ki, (kt, mk) in enumerate(kts):
                                ptb = PT[it % 3]; ptk = f"PT{it%3}"
                                pss = pS[it % 2]
                                it += 1
                                for hf in range(2):
                                    psb = pss[hf]; psk = f"pS{(it-1)%2}{hf}"
                                    pr = slice(hf * 64, (hf + 1) * 64)
                                    if mk is not None:
                                        S.op("pe", lambda: nc.tensor.matmul(psb[:, :256], lhsT=CM(mk),
                                                                            rhs=id4[:, 0:2, :].rearrange("p r n -> p (r n)"),
                                                                            start=True, stop=False),
                                             reads=["cm_b", "id4"], writes=[psk])
                                    S.op("pe", lambda: nc.tensor.matmul(psb[:, :256].rearrange("p (r n) -> p r n", r=2),
                                                                        lhsT=KA[pr, g, kt * 128:(kt + 1) * 128],
                                                                        rhs=QA[pr, 2 * g:2 * g + 2, q0:q0 + 128],
                                                                        start=(mk is None), stop=True),
                                         reads=["KA", f"QA{2*g}.{qb}", f"QA{2*g+1}.{qb}"], writes=[psk])
                                    S.op("act", lambda: nc.scalar.activation(out=ptb[:, hf * 256:(hf + 1) * 256], in_=psb[:, :256],
                                                                             func=AF.Exp, scale=0.125),
                                         reads=[psk], writes=[ptk])
                                S.op("pe", lambda: nc.tensor.matmul(po[:], lhsT=VA[:, kt, g, :], rhs=ptb[:], start=(ki == 0),
                                                                    stop=(ki == len(kts) - 1)),
                                     reads=["VA", ptk], writes=[pok])
                            S.op("dve", lambda: nc.vector.tensor_tensor(out=den[:], in0=po[64:128, :], in1=SE[:, g, :], op=ALU.add),
                                 reads=[pok, "SE"], writes=["den"])
                            S.op("dve", lambda: nc.vector.reciprocal(out=rden[:], in_=den[:]), reads=["den"], writes=["rden"])
                            for hf in range(2):
                                S.op("dve", lambda: nc.vector.tensor_tensor(
                                    out=QA[hf * 64:(hf + 1) * 64, 2 * g:2 * g + 2, q0:q0 + 128],
                                    in0=po[0:64, hf * 256:(hf + 1) * 256].rearrange("p (r n) -> p r n", r=2),
                                    in1=rden[:, hf * 256:(hf + 1) * 256].rearrange("p (r n) -> p r n", r=2), op=ALU.mult),
                                     reads=[pok, "rden"], writes=[f"QA{2*g}.{qb}", f"QA{2*g+1}.{qb}"])
                    if dbg == "oa":
                        S.barrier()
                        dump_fm(QA, 4, BF16)
                        K.stop = True
                    pd = [ps(ph, f"pd{i}", [128, 512]) for i in range(2)]
                    it = 0
                    for ci_, (t0, n) in enumerate([] if K.stop else CHUNKS):
                        w = 1 if ci_ == 0 else 0
                        for m in range(8):
                            p_ = pd[it % 2]; pk = f"pd{it%2}"; it += 1
                            for i in range(4):
                                S.op("pe", lambda: nc.tensor.matmul(p_[:, :n], lhsT=woA[:, i, m * 128:(m + 1) * 128], rhs=QA[:, i, t0:t0 + n],
                                                                    start=(i == 0), stop=(i == 3)),
                                     reads=["woA"] + [f"QA{i}.{tt}" for tt in range(t0 // 128, (t0 + n) // 128)], writes=[pk])
                            S.op("dve", lambda: nc.vector.scalar_tensor_tensor(out=X[:, m, t0:t0 + n], in0=p_[:, :n], scalar=mod(l, 2, m, w),
                                                                               in1=X[:, m, t0:t0 + n], op0=ALU.mult, op1=ALU.add),
                                 reads=[pk, "modv"] + xkeys(t0, n), writes=xkeys(t0, n))
                S.barrier()
                LA.close()
                if K.stop:
                    return
                with ExitStack() as ph:
                    wuq = sb(ph, "wuq", [128, 2, 768], BF16)
                    wuk = sb(ph, "wuk", [128, 2, 512], BF16)
                    wuv = sb(ph, "wuv", [128, 2, 512], BF16)
                    woB = sb(ph, "woB", [128, 4, D], BF16)
                    with ExitStack() as phs:
                        stg = [sb(phs, f"stg{i}", [128, 4096]) for i in range(2)]
                        wload(stg[0], "stg0", wuq[:], ["wuq"], wuq_d.rearrange("(kc p) n -> p kc n", p=128), [128, 2, 768], "sp")
                        wload(stg[1], "stg1", wuk[:], ["wuk"], wuk_d.rearrange("(kc p) n -> p kc n", p=128), [128, 2, 512], "pool")
                        wload(stg[0], "stg0", wuv[:], ["wuv"], wuv_d.rearrange("(kc p) n -> p kc n", p=128), [128, 2, 512], "sp")
                        aov = about_d.rearrange("(kc p) n -> p kc n", p=128)
                        for i in range(2):
                            wload(stg[(i + 1) % 2], f"stg{(i+1)%2}", woB[:, 2 * i:2 * i + 2, :], ["woB"], aov[:, 4 + 2 * i:4 + 2 * i + 2, :],
                                  [128, 2, D], "pool" if i == 0 else "sp")
                        S.barrier()
                    KB = sb(ph, "KB", [96, 4, T], BF16)
                    VB = sb(ph, "VB", [128, NT, 4, 128], BF16)
                    S.op("pool", lambda: nc.gpsimd.memset(VB[:], 1.0), writes=["VB"])
                    tabs1 = [None, None] + [sb(ph, f"tab{j}", [128, 512]) for j in (2, 3)]
                    tabs = [tabs1, tabs1]
                    pin = [ps(ph, f"pin{i}", [128, 512]) for i in range(2)]
                    pms2 = ps(ph, "pms2", [128, 512])
                    prot = ps(ph, "prot", [128, 512])
                    pS = [ps(ph, f"pS{i}", [128, 512]) for i in range(2)]
                    pO = ps(ph, "pO", [128, 512])
                    pd = ps(ph, "pd", [128, 512])
                    sq2 = [sb(ph, f"sq2{i}", [128, 512], BF16) for i in range(2)]
                    sd2 = sb(ph, "sd2", [128, 512])
                    rs2 = sb(ph, "rs2", [128, 512])
                    qn = sb(ph, "qn", [128, 512], BF16)
                    t1 = sb(ph, "t1", [128, 512])
                    t2 = sb(ph, "t2", [128, 512])
                    QBc = sb(ph, "QBc", [96, 4, 512], BF16)
                    Yc = sb(ph, "Yc", [128, 2, 512], BF16)
                    PT = [sb(ph, f"PT{i}", [128, 512], BF16) for i in range(3)]
                    rden = sb(ph, "rden", [64, 512])
                    cnt = {"pin": 0}

                    def pipeline(mm_list, M, n, ones_name, gain_ap, rope, dst_ap, dst_keys, tb):
                        pi = cnt["pin"] % 2
                        cnt["pin"] += 1
                        p_in = pin[pi]
                        for i, (lh, rh, rk) in enumerate(mm_list):
                            S.op("pe", lambda: nc.tensor.matmul(p_in[:M, :n], lhsT=lh, rhs=rh, start=(i == 0),
                                                                stop=(i == len(mm_list) - 1)),
                                 reads=rk, writes=[f"pin{pi}"])
                        S.op("act", lambda: nc.scalar.activation(out=sq2[pi][:M, :n], in_=p_in[:M, :n], func=AF.Square),
                             reads=[f"pin{pi}"], writes=[f"sq2{pi}"])
                        S.op("pe", lambda: nc.tensor.matmul(pms2[:M, :n], lhsT=CM(ones_name)[:M, :M], rhs=sq2[pi][:M, :n],
                                                            start=True, stop=True),
                             reads=[f"sq2{pi}", "cm_b"], writes=["pms2"])
                        S.op("act", lambda: nc.scalar.activation(out=sd2[:M, :n], in_=pms2[:M, :n], func=AF.Sqrt,
                                                                 bias=eps_t[:M, 0:1]),
                             reads=["pms2", "eps"], writes=["sd2"])
                        S.op("dve", lambda: nc.vector.reciprocal(out=rs2[:M, :n], in_=sd2[:M, :n]), reads=["sd2"], writes=["rs2"])
                        o1 = qn[:M, :n] if rope else dst_ap
                        S.op("dve", lambda: nc.vector.scalar_tensor_tensor(out=o1, in0=p_in[:M, :n], scalar=gain_ap,
                                                                           in1=rs2[:M, :n], op0=ALU.mult, op1=ALU.mult),
                             reads=[f"pin{pi}", "rs2", "vecs"], writes=(["qn"] if rope else dst_keys))
                        if rope:
                            rname, ci, si = rope
                            S.op("pe", lambda: nc.tensor.matmul(prot[:M, :n], lhsT=CM(rname)[:M, :M], rhs=qn[:M, :n],
                                                                start=True, stop=True),
                                 reads=["qn", "cm_b"], writes=["prot"])
                            S.op("dve", lambda: nc.vector.tensor_tensor(out=t1[:M, :n], in0=qn[:M, :n], in1=tb[ci][:M, :n],
                                                                        op=ALU.mult),
                                 reads=["qn", f"tab{ci}"], writes=["t1"])
                            S.op("dve", lambda: nc.vector.tensor_tensor(out=t2[:M, :n], in0=prot[:M, :n], in1=tb[si][:M, :n],
                                                                        op=ALU.mult),
                                 reads=["prot", f"tab{si}"], writes=["t2"])
                            S.op("pool", lambda: nc.gpsimd.tensor_tensor(out=dst_ap, in0=t1[:M, :n], in1=t2[:M, :n], op=ALU.add),
                                 reads=["t1", "t2"], writes=dst_keys)

                    it = 0
                    for p in range(2):
                        for hl in range(4):
                            h = 4 * p + hl
                            for (t0, n) in CHUNKS:
                                pipeline([(wuk[:, k, h * 64:(h + 1) * 64], ckvn[:, k, t0:t0 + n], ["wuk", "ckvn"]) for k in range(2)],
                                         64, n, "onesA", vecs[:64, 7:8], None, KB[0:64, hl, t0:t0 + n], ["KB"], None)
                            S.op("pool", lambda: nc.gpsimd.tensor_copy(out=KB[64:96, hl, :], in_=KR[64:96, :]), reads=["KR"], writes=["KB"])
                        for tt in range(NT):
                            pi = cnt["pin"] % 2
                            cnt["pin"] += 1
                            for k in range(2):
                                S.op("pe", lambda: nc.tensor.matmul(pin[pi][:, :256], lhsT=ckvn[:, k, tt * 128:(tt + 1) * 128],
                                                                    rhs=wuv[:, k, p * 256:(p + 1) * 256], start=(k == 0), stop=(k == 1)),
                                     reads=["wuv", "ckvn"], writes=[f"pin{pi}"])
                            e = evac_eng()
                            S.op(e, copy_op(e, VB[:, tt, :, 0:64], pin[pi][:, :256].rearrange("p (g d) -> p g d", g=4)),
                                 reads=[f"pin{pi}"], writes=["VB"])
                        for ci_, (t0, n) in enumerate(CHUNKS):
                            w = 1 if ci_ == 0 else 0
                            tb = tabs[ci_ % 2]
                            for j, src in ((2, cosB_d), (3, sinB_d)):
                                S.dma("pool", tb[j][:96, :n], src[:, t0:t0 + n], writes=[f"tab{j}"])
                            kts = [0, 1] if ci_ == 0 else list(range(NT))
                            for hl in range(4):
                                h = 4 * p + hl
                                pipeline([(wuq[:, k, h * 96:(h + 1) * 96], cqn[:, k, t0:t0 + n], ["wuq", "cqn"]) for k in range(2)],
                                         96, n, "onesB", vecs[:96, 8:9], ("RBT", 2, 3), QBc[:96, hl, :n], [f"QBc{hl}"], tb)
                            for hl in range(4):
                                for ki, kt in enumerate(kts):
                                    psb = pS[it % 2]; psk = f"pS{it%2}"
                                    ptb = PT[it % 3]; ptk = f"PT{it%3}"
                                    it += 1
                                    S.op("pe", lambda: nc.tensor.matmul(psb[:, :n], lhsT=KB[:96, hl, kt * 128:(kt + 1) * 128],
                                                                        rhs=QBc[:96, hl, :n], start=True, stop=True),
                                         reads=["KB", f"QBc{hl}"], writes=[psk])
                                    S.op("act", lambda: nc.scalar.activation(out=ptb[:, :n], in_=psb[:, :n], func=AF.Exp,
                                                                             scale=float(96 ** -0.5)),
                                         reads=[psk], writes=[ptk])
                                    S.op("pe", lambda: nc.tensor.matmul(pO[:, :n], lhsT=VB[:, kt, hl, :], rhs=ptb[:, :n],
                                                                        start=(ki == 0), stop=(ki == len(kts) - 1)),
                                         reads=["VB", ptk], writes=["pO"])
                                S.op("dve", lambda: nc.vector.reciprocal(out=rden[:, :n], in_=pO[64:128, :n]), reads=["pO"], writes=["rden"])
                                S.op("dve", lambda: nc.vector.tensor_tensor(out=Yc[(hl % 2) * 64:(hl % 2) * 64 + 64, hl // 2, :n],
                                                                            in0=pO[0:64, :n], in1=rden[:, :n], op=ALU.mult),
                                     reads=["pO", "rden"], writes=["Yc"])
                            for m in range(8):
                                for i in range(2):
                                    S.op("pe", lambda: nc.tensor.matmul(pd[:, :n], lhsT=woB[:, 2 * p + i, m * 128:(m + 1) * 128],
                                                                        rhs=Yc[:, i, :n], start=(i == 0), stop=(i == 1)),
                                         reads=["woB", "Yc"], writes=["pd"])
                                S.op("dve", lambda: nc.vector.scalar_tensor_tensor(out=X[:, m, t0:t0 + n], in0=pd[:, :n], scalar=mod(l, 2, m, w),
                                                                                   in1=X[:, m, t0:t0 + n], op0=ALU.mult, op1=ALU.add),
                                     reads=["pd", "modv"] + xkeys(t0, n), writes=xkeys(t0, n))
                S.barrier()

        def ffn(l, last):
            G = 4
            groups = [list(range(j, min(j + G, 22))) for j in range(0, 22, G)]
            tch = [(0, 256, 0, 0)]
            for i in range(8):
                tch.append((256 + i * 256, 256, 0 if i == 0 else 1, 0 if i == 7 else 1))
            if last:
                tch = tch[1:]
            with ExitStack() as ph:
                H2 = sb(ph, "H2", [128, 8, T], BF16)
                fcv = sb(ph, "fcv", [128, 44, 4])
                S.dma("sp", fcv[:], fconv_d[l], writes=["fcv"])
                with ExitStack() as ph2:
                    mb = mod_bufs(ph2)
                    for ci_, (t0, n) in enumerate(CHUNKS):
                        w = 1 if ci_ == 0 else 0
                        if last and ci_ == 0:
                            continue
                        modulate_chunk(mb, l, 3, 1, t0, n, w, lambda c: H2[:, c, t0:t0 + n], ["H2"])
                    S.barrier()
                stg1 = sb(ph, "fstg0", [128, 4096])
                stg = [stg1, stg1]
                wup = [sb(ph, f"wup{i}", [128, 8, 2 * G * 128], BF16) for i in range(2)]
                wdn = [sb(ph, f"wdn{i}", [128, G, D], BF16) for i in range(2)]
                pu = [ps(ph, f"pu{i}", [128, 512]) for i in range(4)]
                pdn = [ps(ph, f"pdn{i}", [128, 512]) for i in range(2)]
                uu = [sb(ph, f"uu{i}", [128, 256]) for i in range(4)]
                sg = sb(ph, "sg", [128, 256])
                actb = [sb(ph, f"actb{i}", [128, 256], BF16) for i in range(2 * G)]
                upv = wup_d[l].rearrange("(kc p) n -> p kc n", p=128)
                dnv = wdn_d[l].rearrange("(kc p) n -> p kc n", p=128)
                si = 0
                iu = 0
                ia = 0
                idn = 0
                for gi, js in enumerate(groups):
                    g_n = len(js)
                    wu = wup[gi % 2]; wd = wdn[gi % 2]
                    j0 = js[0]
                    for part in range(2):
                        wload(stg[si % 2], "fstg0", wu[:, :, part * G * 128:part * G * 128 + g_n * 128], [f"wup{gi%2}"],
                              upv[:, :, part * DFF + j0 * 128:part * DFF + (j0 + g_n) * 128], [128, 8, g_n * 128],
                              "sp" if si % 2 == 0 else "pool")
                        si += 1
                    wload(stg[si % 2], "fstg0", wd[:, :g_n, :], [f"wdn{gi%2}"], dnv[:, j0:j0 + g_n, :], [128, g_n, D],
                          "sp" if si % 2 == 0 else "pool")
                    si += 1
                    for (s0, n, lo, hi) in tch:
                        w = 1 if s0 == 0 else 0
                        e0 = s0 - lo
                        ne = n + lo + hi
                        abufs = []
                        for jj, j in enumerate(js):
                            us = []
                            for part in range(2):
                                p_ = pu[iu % 4]; pk = f"pu{iu%4}"
                                u_ = uu[iu % 4]; uk = f"uu{iu%4}"
                                iu += 1
                                fc = part * 22 + j
                                for k in range(8):
                                    S.op("pe", lambda: nc.tensor.matmul(p_[:, :ne], lhsT=wu[:, k, (part * G + jj) * 128:(part * G + jj + 1) * 128],
                                                                        rhs=H2[:, k, e0:e0 + ne], start=(k == 0), stop=(k == 7)),
                                         reads=[f"wup{gi%2}", "H2"], writes=[pk])
                                S.op("act", lambda: nc.scalar.activation(out=u_[:, :n], in_=p_[:, lo:lo + n], func=AF.Identity,
                                                                         bias=fcv[:, fc, 3:4], scale=fcv[:, fc, 1:2]),
                                     reads=[pk, "fcv"], writes=[uk])
                                a = 1 if lo == 0 else 0
                                S.op("dve", lambda: nc.vector.scalar_tensor_tensor(out=u_[:, a:n], in0=p_[:, lo - 1 + a:lo + n - 1],
                                                                                   scalar=fcv[:, fc, 0:1], in1=u_[:, a:n],
                                                                                   op0=ALU.mult, op1=ALU.add),
                                     reads=[pk, "fcv", uk], writes=[uk])
                                b = 1 if hi == 0 else 0
                                S.op("dve", lambda: nc.vector.scalar_tensor_tensor(out=u_[:, :n - b], in0=p_[:, lo + 1:lo + 1 + n - b],
                                                                                   scalar=fcv[:, fc, 2:3], in1=u_[:, :n - b],
                                                                                   op0=ALU.mult, op1=ALU.add),
                                     reads=[pk, "fcv", uk], writes=[uk])
                                us.append((u_, uk))
                            (uv, uvk), (ug, ugk) = us
                            S.op("act", lambda: nc.scalar.activation(out=sg[:, :n], in_=ug[:, :n], func=AF.Silu), reads=[ugk], writes=["sg"])
                            ab = actb[ia % (2 * G)]; abk = f"actb{ia%(2*G)}"
                            ia += 1
                            S.op("pool", lambda: nc.gpsimd.tensor_tensor(out=ab[:, :n], in0=sg[:, :n], in1=uv[:, :n], op=ALU.mult),
                                 reads=["sg", uvk], writes=[abk])
                            abufs.append((ab, abk))
                        for m in range(8):
                            p_ = pdn[idn % 2]; pk = f"pdn{idn%2}"; idn += 1
                            for jj in range(g_n):
                                S.op("pe", lambda: nc.tensor.matmul(p_[:, :n], lhsT=wd[:, jj, m * 128:(m + 1) * 128], rhs=abufs[jj][0][:, :n],
                                                                    start=(jj == 0), stop=(jj == g_n - 1)),
                                     reads=[f"wdn{gi%2}", abufs[jj][1]], writes=[pk])
                            S.op("dve", lambda: nc.vector.scalar_tensor_tensor(out=X[:, m, s0:s0 + n], in0=p_[:, :n], scalar=mod(l, 5, m, w),
                                                                               in1=X[:, m, s0:s0 + n], op0=ALU.mult, op1=ALU.add),
                                 reads=[pk, "modv"] + xkeys(s0, n), writes=xkeys(s0, n))
            S.barrier()

        def dump_fm(buf, nchunk, dt, ntile=NT):
            with ExitStack() as ph:
                tin = sb(ph, "d_tin", [128, 128])
                pt_ = ps(ph, "d_pt", [128, 128])
                ob = sb(ph, "d_ob", [128, D])
                for t in range(ntile):
                    for c in range(nchunk):
                        S.op("dve", lambda: nc.vector.tensor_copy(out=tin[:], in_=buf[:, c, t * 128:(t + 1) * 128]), reads=["*"], writes=["d_tin"])
                        S.op("pe", lambda: nc.tensor.transpose(out=pt_[:], in_=tin[:], identity=CM("ident", False)),
                             reads=["d_tin", "cm_f"], writes=["d_pt"])
                        S.op("dve", lambda: nc.vector.tensor_copy(out=ob[:, c * 128:(c + 1) * 128], in_=pt_[:]), reads=["d_pt"], writes=["d_ob"])
                    S.dma("sp", dbg_d[t * 128:(t + 1) * 128, :nchunk * 128], ob[:, :nchunk * 128], reads=["d_ob"])
                S.barrier()

        def write_out():
            with ExitStack() as ph:
                pt_ = [ps(ph, f"o_pt{i}", [128, 512]) for i in range(4)]
                ob = [sb(ph, f"o_ob{i}", [128, D]) for i in range(2)]
                for t in range(2, NT):
                    o_ = ob[t % 2]
                    for hh in range(2):
                        p_ = pt_[(t % 2) * 2 + hh]; pk = f"o_pt{(t%2)*2+hh}"
                        for c4 in range(4):
                            c = hh * 4 + c4
                            S.op("pe", lambda: nc.tensor.transpose(out=p_[:, c4 * 128:(c4 + 1) * 128], in_=X[:, c, t * 128:(t + 1) * 128],
                                                                   identity=CM("ident", False)),
                                 reads=[f"X{t}", "cm_f"], writes=[pk])
                        e = evac_eng()
                        S.op(e, copy_op(e, o_[:, hh * 512:(hh + 1) * 512], p_[:]), reads=[pk], writes=[f"o_ob{t%2}"])
                    S.dma("sp" if t % 2 == 0 else "pool", out_d[(t - 2) * 128:(t - 1) * 128, :], o_[:], reads=[f"o_ob{t%2}"])
                if dbg == "x":
                    for t in range(2):
                        o_ = ob[t % 2]
                        for hh in range(2):
                            p_ = pt_[(t % 2) * 2 + hh]; pk = f"o_pt{(t%2)*2+hh}"
                            for c4 in range(4):
                                c = hh * 4 + c4
                                S.op("pe", lambda: nc.tensor.transpose(out=p_[:, c4 * 128:(c4 + 1) * 128], in_=X[:, c, t * 128:(t + 1) * 128],
                                                                       identity=CM("ident", False)),
                                     reads=[f"X{t}", "cm_f"], writes=[pk])
                            e = evac_eng()
                            S.op(e, copy_op(e, o_[:, hh * 512:(hh + 1) * 512], p_[:]), reads=[pk], writes=[f"o_ob{t%2}"])
                        S.dma("sp", dbg_d[t * 128:(t + 1) * 128, :], o_[:], reads=[f"o_ob{t%2}"])


        def layer1():
            l = 1
            INC = 1920
            NCH = T // 64
            fwd_order = list(range(NCH))
            bwd_order = [3, 2, 1, 0] + list(range(NCH - 1, 3, -1))
            PCH = [(0, 256, 0, 256)] + [(256 + i * 256, 256, 256, T) for i in range(8)]
            with ExitStack() as L:
                L0 = L
                H1 = sb(L, "H1", [128, 8, T], BF16)
                Yall = sb(L, "Yall", [128, 8, 2048], BF16)
                v1 = sb(L, "v1", [128, 160])
                cm2 = sb(L, "cm2", [64, 320])
                ones64 = sb(L, "ones64", [128, 64])
                id64b = sb(L, "id64b", [64, 64], BF16)
                S.dma("sp", v1[:], v1_d, writes=["v1"])
                S.dma("sp", cm2[:], cm2_d, writes=["cm2"])
                S.op("pool", lambda: nc.gpsimd.memset(ones64[:], 1.0), writes=["ones64"])
                S.op("dve", lambda: nc.vector.tensor_copy(out=id64b[:], in_=cm2[:, 256:320]), reads=["cm2"], writes=["id64b"])
                S.op("dve", lambda: nc.vector.tensor_tensor(out=v1[:, 30:45], in0=v1[:, 0:15], in1=v1[:, 15:30], op=ALU.add), reads=["v1"], writes=["v1"])
                S.op("dve", lambda: nc.vector.tensor_scalar(out=v1[:, 30:45], in0=v1[:, 30:45], scalar1=-1.0, scalar2=1.0, op0=ALU.mult, op1=ALU.add),
                     reads=["v1"], writes=["v1"])
                with ExitStack() as ph2:
                    mb = mod_bufs(ph2)
                    for ci_, (t0, n) in enumerate(CHUNKS):
                        w = 1 if ci_ == 0 else 0
                        modulate_chunk(mb, l, 0, 0, t0, n, w, lambda c: H1[:, c, t0:t0 + n], ["H1"])
                    S.barrier()
                for c in range(8):
                    S.dma("sp" if c % 2 == 0 else "pool", xpark_d[:, c * T:(c + 1) * T], X[:, c, :], reads=[f"X{t}" for t in range(NT)])
                S.barrier()
                XB = X[:].rearrange("p c t -> p (c t)").bitcast(BF16)
                XF = X[:].rearrange("p c t -> p (c t)")

                def xb(i):
                    return XB[:, i * T:(i + 1) * T]

                def xf(i):
                    return XF[:, i * T:(i + 1) * T]

                LS = ExitStack()
                stg = sb(LS, "l1stg", [128, 1024])
                wg = sb(LS, "l1wg", [128, 8, 512], BF16)
                gneps = sb(LS, "gneps", [128, 1])
                S.op("pool", lambda: nc.gpsimd.memset(gneps[:], 64e-5), writes=["gneps"])
                L = LS
                pin = [ps(L, f"l1pin{i}", [128, 512]) for i in range(2)]
                pC = [ps(L, f"l1pC{i}", [128, 512]) for i in range(4)]
                pTr = [ps(L, f"l1pT{i}", [64, 1024], BF16) for i in range(2)]
                uw = [sb(L, f"l1u{i}", [128, 260]) for i in range(2)]
                cnt = {"pin": 0}
                cdv = cdin_d.rearrange("(kc p) n -> p kc n", p=128)

                def load_cols(cols_list):
                    off = 0
                    for (c0, ncol) in cols_list:
                        for b0 in range(0, ncol, 128):
                            nb = min(128, ncol - b0)
                            wload(stg, "l1stg", wg[:, :, off:off + nb], ["l1wg"], cdv[:, :, c0 + b0:c0 + b0 + nb], [128, 8, nb], "sp")
                            off += nb

                def proj_taps(woff, M, taps, func, out_fn, out_keys, bias_ap=None, scale_out=None):
                    hw = max(abs(o) for o, _ in taps)
                    for (s0, n, qlo, qhi) in PCH:
                        e0 = max(s0 - hw, qlo); e1 = min(s0 + n + hw, qhi)
                        ne = e1 - e0
                        pi = cnt["pin"] % 2; cnt["pin"] += 1
                        p_ = pin[pi]; u_ = uw[pi]
                        for k in range(8):
                            S.op("pe", lambda: nc.tensor.matmul(p_[:M, :ne], lhsT=wg[:, k, woff:woff + M], rhs=H1[:, k, e0:e1],
                                                                start=(k == 0), stop=(k == 7)),
                                 reads=["l1wg", "H1"], writes=[f"l1pin{pi}"])
                        src = p_
                        if len(taps) > 1:
                            base = s0 - e0
                            o0, c0_ = taps[0]
                            S.op("act", lambda: nc.scalar.activation(out=u_[:M, :n], in_=p_[:M, base:base + n], func=AF.Identity,
                                                                     scale=c0_),
                                 reads=[f"l1pin{pi}", "v1"], writes=[f"l1u{pi}"])
                            for (o, cf) in taps[1:]:
                                i0 = max(0, e0 - s0 - o); i1 = min(n, e1 - s0 - o)
                                S.op("dve", lambda: nc.vector.scalar_tensor_tensor(out=u_[:M, i0:i1], in0=p_[:M, base + i0 + o:base + i1 + o],
                                                                                   scalar=cf, in1=u_[:M, i0:i1], op0=ALU.mult, op1=ALU.add),
                                     reads=[f"l1pin{pi}", f"l1u{pi}", "v1"], writes=[f"l1u{pi}"])
                            src = u_
                            srck = f"l1u{pi}"
                            sl = slice(0, n)
                        else:
                            srck = f"l1pin{pi}"
                            sl = slice(s0 - e0, s0 - e0 + n)
                        kw = {}
                        if bias_ap is not None:
                            kw["bias"] = bias_ap
                        if scale_out is not None:
                            kw["scale"] = scale_out
                        S.op("act", lambda: nc.scalar.activation(out=out_fn(s0, n), in_=src[:M, sl], func=func, **kw),
                             reads=[srck, "v1"], writes=out_keys)

                def shift_taps(fc):
                    return [(0, v1[:, 30 + fc:31 + fc]), (-1, v1[:, fc:fc + 1]), (1, v1[:, 15 + fc:16 + fc])]

                wk = {}
                for dn in ("f", "b"):
                    wk[dn] = dict(
                        al=sb(L, f"al{dn}", [128, 64]), be=sb(L, f"be{dn}", [128, 64]), kd=sb(L, f"kd{dn}", [128, 64]),
                        rr=sb(L, f"rr{dn}", [128, 64]), lw=sb(L, f"lw{dn}", [128, 64]),
                        pfx=sb(L, f"pfx{dn}", [128, 64]), Gi=sb(L, f"Gi{dn}", [128, 64]), Ge=sb(L, f"Ge{dn}", [128, 64]),
                        E=sb(L, f"E{dn}", [128, 4, 64]), Es=sb(L, f"Es{dn}", [128, 3, 64]),
                        negm=sb(L, f"negm{dn}", [128, 1]),
                        BT=sb(L, f"BT{dn}", [128, 64], BF16), KT=sb(L, f"KT{dn}", [128, 64], BF16),
                        Hs=sb(L, f"Hs{dn}", [128, 128]), Hb=sb(L, f"Hb{dn}", [128, 128], BF16), ztmp=sb(L, f"ztmp{dn}", [64, 128]),
                    )
                    for j in range(2):
                        idt = BF16 if DBG['invbf'] else F32
                        wk[dn][f"N{j}"] = [sb(L, f"N{dn}{j}{i}", [64, 64], idt) for i in range(2)]
                        wk[dn][f"NT{j}"] = [sb(L, f"NT{dn}{j}{i}", [64, 64], idt) for i in range(2)]
                        wk[dn][f"IN{j}"] = sb(L, f"IN{dn}{j}", [64, 64], idt)
                        wk[dn][f"PT{j}"] = [sb(L, f"PTi{dn}{j}{i}", [64, 64], idt) for i in range(2)]
                        wk[dn][f"TT{j}"] = sb(L, f"TT{dn}{j}", [64, 64], BF16)
                        wk[dn][f"ARB{j}"] = sb(L, f"ARB{dn}{j}", [64, 64], BF16)
                        wk[dn][f"AKRK{j}"] = sb(L, f"AKRK{dn}{j}", [64, 128], BF16)
                        wk[dn][f"Zb{j}"] = sb(L, f"Zb{dn}{j}", [64, 128], BF16)
                        wk[dn][f"Ub{j}"] = sb(L, f"Ub{dn}{j}", [64, 128], BF16)

                wk2 = {}
                for dn in ("f", "b"):
                    wk2[dn] = []
                    for par in range(2):
                        wk2[dn].append(dict(
                            AR=sb(L, f"AR2{dn}{par}", [128, 2, 64], BF16), BH=sb(L, f"BH2{dn}{par}", [128, 64], BF16), KH=sb(L, f"KH2{dn}{par}", [128, 64], BF16),
                            ART=sb(L, f"ART2{dn}{par}", [128, 2, 64], BF16), BTt=sb(L, f"BTt2{dn}{par}", [64, 128], BF16),
                            KTt=sb(L, f"KTt2{dn}{par}", [64, 128], BF16), VT=sb(L, f"VT2{dn}{par}", [64, 128], BF16), Et=sb(L, f"Et2{dn}{par}", [128, 1])))

                def dm_aps(di, par):
                    if par == 0:
                        v = K.lw2flat[:, di * 648:di * 648 + 648].bitcast(F32)
                        return v[0:64, 0:256], v[0:64, 256:320], v[0:64, 320:322]
                    sqf = sq[:].bitcast(F32)
                    return t32[di][0:64, 0:256], sqf[0:64, di * 64:(di + 1) * 64], t32[2][0:64, di * 2:di * 2 + 2]

                def prefix_gen(dn, pre_ops, vsrc_ap, vkeys, sdec, par=0):
                    W = wk[dn]
                    W2 = wk2[dn][par]
                    k_ = lambda nm: f"{nm}{dn}"
                    k2 = lambda nm: f"{nm}{dn}p{par}"
                    fwd = dn == "f"
                    di = 0 if fwd else 1
                    ptr = pTr[di]; ptk = f"l1pT{di}"
                    for fn in pre_ops:
                        fn()
                        yield
                    S.op("dve", lambda: nc.vector.tensor_tensor_scan(out=W["pfx"][:], data0=ones64[:], data1=W["lw"][:], initial=0.0,
                                                                     op0=ALU.mult, op1=ALU.add),
                         reads=[k_("lw"), "ones64"], writes=[k_("pfx")])
                    yield
                    tot = W["pfx"][:, 63:64]
                    if fwd:
                        S.op("pool", lambda: nc.gpsimd.tensor_copy(out=W["Gi"][:], in_=W["pfx"][:]), reads=[k_("pfx")], writes=[k_("Gi")])
                        yield
                        S.op("dve", lambda: nc.vector.tensor_tensor(out=W["Ge"][:], in0=W["pfx"][:], in1=W["lw"][:], op=ALU.subtract),
                             reads=[k_("pfx"), k_("lw")], writes=[k_("Ge")])
                        yield
                    else:
                        S.op("dve", lambda: nc.vector.tensor_scalar(out=W["Ge"][:], in0=W["pfx"][:], scalar1=-1.0, scalar2=tot,
                                                                    op0=ALU.mult, op1=ALU.add),
                             reads=[k_("pfx")], writes=[k_("Ge")])
                        yield
                        S.op("dve", lambda: nc.vector.tensor_tensor(out=W["Gi"][:], in0=W["Ge"][:], in1=W["lw"][:], op=ALU.add),
                             reads=[k_("Ge"), k_("lw")], writes=[k_("Gi")])
                        yield
                    E = W["E"]; Es = W["Es"]; negm = W["negm"]
                    S.op("act", lambda: nc.scalar.activation(out=E[:, 0, :], in_=W["Ge"][:], func=AF.Exp), reads=[k_("Ge")], writes=[k_("E0")])
                    yield
                    S.op("act", lambda: nc.scalar.activation(out=E[:, 1, :], in_=W["Gi"][:], func=AF.Exp), reads=[k_("Gi")], writes=[k_("E1")])
                    yield
                    S.op("act", lambda: nc.scalar.activation(out=E[:, 3, :], in_=W["Gi"][:], func=AF.Exp, scale=-1.0, bias=tot),
                         reads=[k_("Gi"), k_("pfx")], writes=[k_("E3")])
                    yield
                    S.op("act", lambda: nc.scalar.activation(out=W2["Et"][:], in_=tot, func=AF.Exp), reads=[k_("pfx")], writes=[k2("Et")])
                    yield
                    if sdec:
                        dmA, dm4, gcol = dm_aps(di, par)
                        dk_ = f"dmw{di}p{par}"; gk_ = f"gcol{di}p{par}"
                        msk = (cm2[:, 0:64], cm2[:, 64:128], cm2[:, 128:192]) if fwd else (cm2[:, 128:192], cm2[:, 192:256], cm2[:, 0:64])
                        for ti, srcn in enumerate(("Ge", "Gi")):
                            po_ = di * 256 + ti * 128
                            S.op("pe", lambda: nc.tensor.transpose(out=pin[1][0:64, po_:po_ + 128], in_=W[srcn][:], identity=CM("ident", False)),
                                 reads=[k_(srcn), "cm_f"], writes=["l1pin1"])
                            yield
                            S.op("dve", lambda: nc.vector.tensor_copy(out=gcol[:, ti:ti + 1], in_=pin[1][0:64, po_:po_ + 1]),
                                 reads=["l1pin1"], writes=[gk_])
                            yield
                        rowGe = W["Ge"][0:64, :]; rowGi = W["Gi"][0:64, :]
                        for qd, (row, rk_, cidx) in enumerate(((rowGe, "Ge", 0), (rowGe, "Ge", 1), (rowGi, "Gi", 0), (rowGi, "Gi", 1))):
                            S.op("dve", lambda: nc.vector.tensor_scalar(out=dmA[:, qd * 64:(qd + 1) * 64], in0=row, scalar1=gcol[:, cidx:cidx + 1], scalar2=0.0,
                                                                        op0=ALU.subtract, op1=ALU.min),
                                 reads=[k_(rk_), gk_], writes=[dk_])
                            yield
                        S.op("dve", lambda: nc.vector.tensor_scalar(out=dm4, in0=rowGe, scalar1=-1.0, scalar2=gcol[:, 0:1], op0=ALU.mult, op1=ALU.add),
                             reads=[k_("Ge"), gk_], writes=[dk_])
                        yield
                        S.op("dve", lambda: nc.vector.tensor_scalar(out=dm4, in0=dm4, scalar1=0.0, scalar2=None, op0=ALU.min),
                             reads=[dk_], writes=[dk_])
                        yield
                        S.op("act", lambda: nc.scalar.activation(out=dmA, in_=dmA, func=AF.Exp), reads=[dk_], writes=[dk_])
                        yield
                        S.op("act", lambda: nc.scalar.activation(out=dm4, in_=dm4, func=AF.Exp), reads=[dk_], writes=[dk_])
                        yield
                        for qd, mi in enumerate((0, 0, 1, 1, 2)):
                            dsl = dmA[:, qd * 64:(qd + 1) * 64] if qd < 4 else dm4
                            S.op("pool", lambda: nc.gpsimd.tensor_tensor(out=dsl, in0=dsl, in1=msk[mi], op=ALU.mult),
                                 reads=[dk_, "cm2"], writes=[dk_])
                            yield
                        S.op("dve", lambda: nc.vector.tensor_copy(out=W2["AR"][:, 0, :], in_=W["al"][:]), reads=[k_("al")], writes=[k2("AR")])
                        yield
                        S.op("pool", lambda: nc.gpsimd.tensor_copy(out=W2["AR"][:, 1, :], in_=W["rr"][:]), reads=[k_("rr")], writes=[k2("AR")])
                        yield
                        S.op("pool", lambda: nc.gpsimd.tensor_copy(out=W2["KH"][:], in_=W["kd"][:]), reads=[k_("kd")], writes=[k2("KH")])
                        yield
                    else:
                        mcol = W["Gi"][:, 32:33]
                        S.op("dve", lambda: nc.vector.tensor_scalar(out=negm[:], in0=mcol, scalar1=-1.0, scalar2=None, op0=ALU.mult),
                             reads=[k_("Gi")], writes=[k_("negm")])
                        yield
                        S.op("dve", lambda: nc.vector.tensor_scalar(out=Es[:, 0, :], in0=W["Ge"][:], scalar1=negm[:, 0:1], scalar2=40.0, op0=ALU.add, op1=ALU.min),
                             reads=[k_("Ge"), k_("negm")], writes=[k_("Es0")])
                        yield
                        S.op("dve", lambda: nc.vector.tensor_scalar(out=Es[:, 1, :], in0=W["Gi"][:], scalar1=negm[:, 0:1], scalar2=40.0, op0=ALU.add, op1=ALU.min),
                             reads=[k_("Gi"), k_("negm")], writes=[k_("Es1")])
                        yield
                        S.op("dve", lambda: nc.vector.tensor_scalar(out=Es[:, 2, :], in0=W["Gi"][:], scalar1=-1.0, scalar2=mcol, op0=ALU.mult, op1=ALU.add),
                             reads=[k_("Gi")], writes=[k_("Es2")])
                        yield
                        S.op("dve", lambda: nc.vector.tensor_scalar(out=Es[:, 2, :], in0=Es[:, 2, :], scalar1=40.0, scalar2=None, op0=ALU.min),
                             reads=[k_("Es2")], writes=[k_("Es2")])
                        yield
                        S.op("act", lambda: nc.scalar.activation(out=Es[:, :, :], in_=Es[:, :, :], func=AF.Exp), reads=[k_("Es0"), k_("Es1"), k_("Es2")],
                             writes=[k_("Es0"), k_("Es1"), k_("Es2")])
                        yield
                        S.op("dve", lambda: nc.vector.tensor_tensor(out=W2["AR"][:, 0, :], in0=W["al"][:], in1=Es[:, 0, :], op=ALU.mult),
                             reads=[k_("al"), k_("Es0")], writes=[k2("AR")])
                        yield
                        S.op("pool", lambda: nc.gpsimd.tensor_tensor(out=W2["AR"][:, 1, :], in0=W["rr"][:], in1=Es[:, 1, :], op=ALU.mult),
                             reads=[k_("rr"), k_("Es1")], writes=[k2("AR")])
                        yield
                        S.op("dve", lambda: nc.vector.tensor_tensor(out=W2["BH"][:], in0=W["be"][:], in1=Es[:, 2, :], op=ALU.mult),
                             reads=[k_("be"), k_("Es2")], writes=[k2("BH")])
                        yield
                        S.op("pool", lambda: nc.gpsimd.tensor_tensor(out=W2["KH"][:], in0=W["kd"][:], in1=Es[:, 2, :], op=ALU.mult),
                             reads=[k_("kd"), k_("Es2")], writes=[k2("KH")])
                        yield
                    S.op("dve", lambda: nc.vector.tensor_tensor(out=W2["ART"][:, 0, :], in0=W["al"][:], in1=E[:, 0, :], op=ALU.mult),
                         reads=[k_("al"), k_("E0")], writes=[k2("ART")])
                    yield
                    S.op("pool", lambda: nc.gpsimd.tensor_tensor(out=W2["ART"][:, 1, :], in0=W["rr"][:], in1=E[:, 1, :], op=ALU.mult),
                         reads=[k_("rr"), k_("E1")], writes=[k2("ART")])
                    yield
                    S.op("dve", lambda: nc.vector.tensor_tensor(out=W["BT"][:], in0=W["be"][:], in1=E[:, 3, :], op=ALU.mult),
                         reads=[k_("be"), k_("E3")], writes=[k_("BT")])
                    yield
                    S.op("pool", lambda: nc.gpsimd.tensor_tensor(out=W["KT"][:], in0=W["kd"][:], in1=E[:, 3, :], op=ALU.mult),
                         reads=[k_("kd"), k_("E3")], writes=[k_("KT")])
                    yield
                    for ti, (src, srk, dst, dsk) in enumerate(((W["BT"][:], k_("BT"), W2["BTt"], k2("BTt")), (W["KT"][:], k_("KT"), W2["KTt"], k2("KTt")),
                                                               (vsrc_ap, vkeys, W2["VT"], k2("VT")))):
                        S.op("pe", lambda: nc.tensor.transpose(out=ptr[:, ti * 128:(ti + 1) * 128], in_=src, identity=CM("ident")),
                             reads=([srk] if isinstance(srk, str) else list(srk)) + ["cm_b"], writes=[ptk])
                        yield
                        e = "act" if di == 0 else "dve"
                        S.op(e, copy_op(e, dst[:], ptr[:, ti * 128:(ti + 1) * 128]), reads=[ptk], writes=[dsk])
                        yield

                def head_gen(dn, j, pr, dk, dv, c0, emit, sdec, par=0):
                    W = wk[dn]
                    W2 = wk2[dn][par]
                    k_ = lambda nm: f"{nm}{dn}"
                    k2 = lambda nm: f"{nm}{dn}p{par}"
                    kj = lambda nm: f"{nm}{dn}{j}"
                    fwd = dn == "f"
                    di = 0 if fwd else 1
                    ms_mi = cm2[:, 0:128] if fwd else cm2[:, 128:256]
                    ms_other = cm2[:, 128:192] if fwd else cm2[:, 0:64]
                    I64 = cm2[:, 256:320]
                    pc = pC[di * 2 + j]; pck = f"l1pC{di*2+j}"
                    ev = "dve" if not sdec else ("dve" if di == 0 else "act")
                    N = W[f"N{j}"]; NTt = W[f"NT{j}"]; PTm = W[f"PT{j}"]; TT = W[f"TT{j}"]
                    ARB = W[f"ARB{j}"]; AKRK = W[f"AKRK{j}"]; Zb = W[f"Zb{j}"]; Ub = W[f"Ub{j}"]
                    ar2 = W2["AR"][pr, :, :].rearrange("p a n -> p (a n)")
                    pa = pc[0:64, :]
                    if sdec:
                        dmA, dm4, _gc = dm_aps(di, par)
                        dk_ = f"dmw{di}p{par}"
                        S.op("pe", lambda: nc.tensor.matmul(pa[:, 0:128], lhsT=W2["KH"][pr, :], rhs=ar2, start=True, stop=True),
                             reads=[k2("KH"), k2("AR")], writes=[pck])
                        yield
                        S.op("pe", lambda: nc.tensor.matmul(pa[:, 256:320], lhsT=W2["AR"][pr, 0, :], rhs=W2["KH"][pr, :], start=True, stop=True),
                             reads=[k2("KH"), k2("AR")], writes=[pck])
                        yield
                        S.op("dve", lambda: nc.vector.scalar_tensor_tensor(out=NTt[0][:], in0=pa[:, 0:64], scalar=-1.0, in1=dmA[:, 0:64], op0=ALU.mult, op1=ALU.mult),
                             reads=[pck, dk_], writes=[kj("NT0")])
                        yield
                        S.op("dve", lambda: nc.vector.tensor_tensor(out=AKRK[:, 0:64], in0=pa[:, 0:64], in1=dmA[:, 64:128], op=ALU.mult),
                             reads=[pck, dk_], writes=[kj("AKRK")])
                        yield
                        S.op("dve", lambda: nc.vector.scalar_tensor_tensor(out=ARB[:], in0=pa[:, 64:128], scalar=-1.0, in1=dmA[:, 128:192], op0=ALU.mult, op1=ALU.mult),
                             reads=[pck, dk_], writes=[kj("ARB")])
                        yield
                        S.op("dve", lambda: nc.vector.tensor_tensor(out=AKRK[:, 64:128], in0=pa[:, 64:128], in1=dmA[:, 192:256], op=ALU.mult),
                             reads=[pck, dk_], writes=[kj("AKRK")])
                        yield
                        S.op("dve", lambda: nc.vector.scalar_tensor_tensor(out=N[0][:], in0=pa[:, 256:320], scalar=-1.0, in1=dm4, op0=ALU.mult, op1=ALU.mult),
                             reads=[pck, dk_], writes=[kj("N0")])
                        yield
                    else:
                        S.op("pe", lambda: nc.tensor.matmul(pa[:, 0:128], lhsT=W2["BH"][pr, :], rhs=ar2, start=True, stop=True),
                             reads=[k2("BH"), k2("AR")], writes=[pck])
                        yield
                        S.op("pe", lambda: nc.tensor.matmul(pa[:, 128:256], lhsT=W2["KH"][pr, :], rhs=ar2, start=True, stop=True),
                             reads=[k2("KH"), k2("AR")], writes=[pck])
                        yield
                        S.op("pe", lambda: nc.tensor.matmul(pa[:, 256:320], lhsT=W2["AR"][pr, 0, :], rhs=W2["BH"][pr, :], start=True, stop=True),
                             reads=[k2("BH"), k2("AR")], writes=[pck])
                        yield
                        S.op("dve", lambda: nc.vector.tensor_tensor(out=NTt[0][:], in0=pa[:, 0:64], in1=ms_mi[:, 0:64], op=ALU.mult),
                             reads=[pck, "cm2"], writes=[kj("NT0")])
                        yield
                        S.op("dve", lambda: nc.vector.tensor_tensor(out=ARB[:], in0=pa[:, 64:128], in1=ms_mi[:, 64:128], op=ALU.mult),
                             reads=[pck, "cm2"], writes=[kj("ARB")])
                        yield
                        S.op("dve", lambda: nc.vector.tensor_tensor(out=AKRK[:], in0=pa[:, 128:256], in1=ms_mi, op=ALU.mult),
                             reads=[pck, "cm2"], writes=[kj("AKRK")])
                        yield
                        S.op("dve", lambda: nc.vector.tensor_tensor(out=N[0][:], in0=pa[:, 256:320], in1=ms_other, op=ALU.mult),
                             reads=[pck, "cm2"], writes=[kj("N0")])
                        yield
                    S.op("pool", lambda: nc.gpsimd.tensor_tensor(out=PTm[0][:], in0=NTt[0][:], in1=I64, op=ALU.add),
                         reads=[kj("NT0"), "cm2"], writes=[kj("PT0")])
                    yield
                    for lv in range(1, 6):
                        a_, b_ = (lv - 1) % 2, lv % 2
                        S.op("pe", lambda: nc.tensor.matmul(pa[:, 320:384], lhsT=NTt[a_][:], rhs=N[a_][:], start=True, stop=True),
                             reads=[kj(f"NT{a_}"), kj(f"N{a_}")], writes=[pck])
                        yield
                        if lv < 5:
                            S.op("pe", lambda: nc.tensor.matmul(pa[:, 384:448], lhsT=N[a_][:], rhs=NTt[a_][:], start=True, stop=True),
                                 reads=[kj(f"NT{a_}"), kj(f"N{a_}")], writes=[pck])
                            yield
                        S.op(ev, copy_op(ev, N[b_][:], pa[:, 320:384]), reads=[pck], writes=[kj(f"N{b_}")])
                        yield
                        if lv < 5:
                            S.op(ev, copy_op(ev, NTt[b_][:], pa[:, 384:448]), reads=[pck], writes=[kj(f"NT{b_}")])
                            yield
                        if ev == "dve":
                            S.op("pe", lambda: nc.tensor.matmul(pa[:, 448:512], lhsT=N[b_][:], rhs=PTm[a_][:], start=True, stop=True),
                                 reads=[kj(f"N{b_}"), kj(f"PT{a_}")], writes=[pck])
                            yield
                            dst_ = PTm[b_][:] if lv < 5 else TT[:]
                            S.op("dve", lambda: nc.vector.tensor_tensor(out=dst_, in0=pa[:, 448:512], in1=PTm[a_][:], op=ALU.add),
                                 reads=[pck, kj(f"PT{a_}")], writes=[kj(f"PT{b_}") if lv < 5 else kj("TT")])
                            yield
                        else:
                            S.op("pe", lambda: nc.tensor.matmul(pa[:, 448:512], lhsT=I64, rhs=PTm[a_][:], start=True, stop=False),
                                 reads=["cm2", kj(f"PT{a_}")], writes=[pck])
                            yield
                            S.op("pe", lambda: nc.tensor.matmul(pa[:, 448:512], lhsT=N[b_][:], rhs=PTm[a_][:], start=False, stop=True),
                                 reads=[kj(f"N{b_}"), kj(f"PT{a_}")], writes=[pck])
                            yield
                            if lv < 5:
                                S.op(ev, copy_op(ev, PTm[b_][:], pa[:, 448:512]), reads=[pck], writes=[kj(f"PT{b_}")])
                            else:
                                S.op(ev, copy_op(ev, TT[:], pa[:, 448:512]), reads=[pck], writes=[kj("TT")])
                            yield
                    vt = W2["VT"][:, j * dv:(j + 1) * dv] if dv == 64 else W2["VT"][:, :]
                    cs = slice(j * 64, (j + 1) * 64) if dk == 64 else slice(0, 128)
                    split = (pr.start == 64)
                    psy = pin[0]; psyk = "l1pin0"
                    yo = di * 256
                    if split:
                        S.op("pe", lambda: nc.tensor.matmul(psy[0:64, yo:yo + dv], lhsT=W2["ART"][pr, 0, :], rhs=W["Hb"][pr, :dv], start=True, stop=True),
                             reads=[k2("ART"), kj("Hb")], writes=[psyk])
                        yield
                        S.op("pe", lambda: nc.tensor.matmul(pc[0:64, 0:dv], lhsT=AKRK[:, 0:64], rhs=vt, start=True, stop=True),
                             reads=[kj("AKRK"), k2("VT")], writes=[pck])
                        yield
                        S.op("act", lambda: nc.scalar.copy(out=W["ztmp"][:, :dv], in_=psy[0:64, yo:yo + dv]), reads=[psyk], writes=[k_("ztmp")])
                        yield
                        S.op("dve", lambda: nc.vector.tensor_tensor(out=Zb[:, :dv], in0=pc[0:64, 0:dv], in1=W["ztmp"][:, :dv], op=ALU.add),
                             reads=[pck, k_("ztmp")], writes=[kj("Zb")])
                        yield
                    else:
                        S.op("pe", lambda: nc.tensor.matmul(pc[0:64, 0:dv], lhsT=W2["ART"][pr, 0, :], rhs=W["Hb"][pr, :dv], start=True, stop=False),
                             reads=[k2("ART"), kj("Hb")], writes=[pck])
                        yield
                        S.op("pe", lambda: nc.tensor.matmul(pc[0:64, 0:dv], lhsT=AKRK[:, 0:64], rhs=vt, start=False, stop=True),
                             reads=[kj("AKRK"), k2("VT")], writes=[pck])
                        yield
                        S.op(ev, copy_op(ev, Zb[:, :dv], pc[0:64, 0:dv]), reads=[pck], writes=[kj("Zb")])
                        yield
                    S.op("pe", lambda: nc.tensor.matmul(pc[0:64, 128:128 + dv], lhsT=TT[:], rhs=Zb[:, :dv], start=True, stop=True),
                         reads=[kj("TT"), kj("Zb")], writes=[pck])
                    yield
                    S.op(ev, copy_op(ev, Ub[:, :dv], pc[0:64, 128:128 + dv]), reads=[pck], writes=[kj("Ub")])
                    yield
                    if emit:
                        t_lat = c0 - NCTX
                        yk_ = f"Yacc{t_lat//64}.{pr.start}"
                        if split:
                            S.op("pe", lambda: nc.tensor.matmul(psy[pr, yo + 64:yo + 128], lhsT=W["Hb"][pr, :dv], rhs=W2["ART"][pr, 1, :], start=True, stop=True),
                                 reads=[k2("ART"), kj("Hb")], writes=[psyk])
                            yield
                            S.op("dve", lambda: nc.vector.tensor_tensor(out=K.yacc[pr, t_lat:t_lat + 64], in0=psy[pr, yo + 64:yo + 128],
                                                                        in1=K.yacc[pr, t_lat:t_lat + 64], op=ALU.add),
                                 reads=[psyk, yk_], writes=[yk_])
                            yield
                        else:
                            S.op("pe", lambda: nc.tensor.matmul(pc[pr, 256:320], lhsT=W["Hb"][pr, :dv], rhs=W2["ART"][pr, 1, :], start=True, stop=False),
                                 reads=[k2("ART"), kj("Hb")], writes=[pck])
                            yield
                        S.op("pe", lambda: nc.tensor.matmul(pc[pr, 256:320], lhsT=Ub[:, :dv], rhs=ARB[:], start=split, stop=False),
                             reads=[kj("Ub"), kj("ARB")], writes=[pck])
                        yield
                        S.op("pe", lambda: nc.tensor.matmul(pc[pr, 256:320], lhsT=vt, rhs=AKRK[:, 64:128], start=False, stop=True),
                             reads=[kj("AKRK"), k2("VT")], writes=[pck])
                        yield
                        S.op("dve", lambda: nc.vector.tensor_tensor(out=K.yacc[pr, t_lat:t_lat + 64], in0=pc[pr, 256:320],
                                                                    in1=K.yacc[pr, t_lat:t_lat + 64], op=ALU.add),
                             reads=[pck, yk_], writes=[yk_])
                        yield
                    S.op("pe", lambda: nc.tensor.matmul(pc[pr, 320:320 + dv], lhsT=W2["BTt"][:, cs], rhs=Ub[:, :dv], start=True, stop=False),
                         reads=[k2("BTt"), kj("Ub")], writes=[pck])
                    yield
                    S.op("pe", lambda: nc.tensor.matmul(pc[pr, 320:320 + dv], lhsT=W2["KTt"][:, cs], rhs=vt, start=False, stop=True),
                         reads=[k2("KTt"), k2("VT")], writes=[pck])
                    yield
                    S.op("dve", lambda: nc.vector.scalar_tensor_tensor(out=W["Hs"][pr, :dv], in0=W["Hs"][pr, :dv], scalar=W2["Et"][pr, 0:1],
                                                                       in1=pc[pr, 320:320 + dv], op0=ALU.mult, op1=ALU.add),
                         reads=[pck, kj("Hs"), k2("Et")], writes=[kj("Hs")])
                    yield
                    S.op("act", lambda: nc.scalar.copy(out=W["Hb"][pr, :dv], in_=W["Hs"][pr, :dv]), reads=[kj("Hs")], writes=[kj("Hb")])
                    yield

                def round_robin(gens):
                    gens = list(gens)
                    while gens:
                        for g in list(gens):
                            try:
                                next(g)
                            except StopIteration:
                                gens.remove(g)

                def reset_state(insts):
                    for dn in ("f", "b"):
                        S.op("pool", lambda: nc.gpsimd.memset(wk[dn]["Hs"][:], 0.0), writes=[f"Hs{dn}{j}" for j in range(2)])
                        S.op("pool", lambda: nc.gpsimd.memset(wk[dn]["Hb"][:], 0.0), writes=[f"Hb{dn}{j}" for j in range(2)])
                    S.op("pool", lambda: nc.gpsimd.memset(K.yacc, 0.0), writes=[f"Yacc{i}.{p}" for i in range(32) for p in (0, 64)])

                TW = xb(10); AL = xb(11); SG = xb(12)
                if DBG['lora']:
                    load_cols([(1536, 384)])
                    proj_taps(0, 128, shift_taps(12), AF.Tanh, lambda s0, n: TW[:, s0:s0 + n], ["TW"])
                    proj_taps(128, 128, shift_taps(13), AF.Identity, lambda s0, n: AL[:, s0:s0 + n], ["AL"])
                    proj_taps(256, 128, shift_taps(14), AF.Sigmoid, lambda s0, n: SG[:, s0:s0 + n], ["SG"])
                lw2 = sb(L, "lw2", [128, 3, 512], BF16)
                K.lw2flat = lw2[:].rearrange("p a n -> p (a n)")
                for i, src in enumerate((cw2_d, ca2_d, cg2_d)):
                    wload(stg, "l1stg", lw2[:, i, :], ["lw2"], src, [128, 512], "sp")
                sq = sb(L, "l1sq", [128, 256], BF16)
                t32 = [sb(L, f"l1t{i}", [128, 256]) for i in range(3)]
                TCH = [(i * 256, 256) for i in range(9)]

                for gi in range(DBG['ng']):
                    rA, kA, vA, kkA, afA, abA = (xb(i) for i in range(6))
                    lwF = xf(3); lwB = xf(4)
                    gA = xb(13)
                    load_cols([(gi * 128, 128), (512 + gi * 128, 128), (1024 + gi * 128, 128)])
                    proj_taps(0, 128, shift_taps(gi), AF.Identity, lambda s0, n: rA[:, s0:s0 + n], ["rA"])
                    proj_taps(128, 128, shift_taps(4 + gi), AF.Identity, lambda s0, n: kA[:, s0:s0 + n], ["kA"])
                    proj_taps(256, 128, shift_taps(8 + gi), AF.Identity, lambda s0, n: vA[:, s0:s0 + n], ["vA"])
                    for (t0, n) in (TCH if DBG['prep'] else []):
                        S.op("dve", lambda: nc.vector.tensor_scalar(out=t32[0][:, :n], in0=kA[:, t0:t0 + n], scalar1=v1[:, 45 + gi:46 + gi], scalar2=None,
                                                                    op0=ALU.mult), reads=["kA", "v1"], writes=["l1t0"])
                        S.op("act", lambda: nc.scalar.activation(out=sq[:, :n], in_=t32[0][:, :n], func=AF.Square), reads=["l1t0"], writes=["l1sq"])
                        S.op("pe", lambda: nc.tensor.matmul(pin[0][:, :n], lhsT=CM("onesA"), rhs=sq[:, :n], start=True, stop=True),
                             reads=["l1sq", "cm_b"], writes=["l1pin0"])
                        S.op("act", lambda: nc.scalar.activation(out=t32[1][:, :n], in_=pin[0][:, :n], func=AF.Sqrt, scale=64.0, bias=eps_t[:, 0:1]),
                             reads=["l1pin0", "eps"], writes=["l1t1"])
                        S.op("dve", lambda: nc.vector.reciprocal(out=t32[1][:, :n], in_=t32[1][:, :n]), reads=["l1t1"], writes=["l1t1"])
                        S.op("dve", lambda: nc.vector.tensor_tensor(out=kkA[:, t0:t0 + n], in0=t32[0][:, :n], in1=t32[1][:, :n], op=ALU.mult),
                             reads=["l1t0", "l1t1"], writes=["kkA"])
                        for d_, (lwX, aX) in enumerate(((lwF, afA), (lwB, abA))):
                            prd = slice(d_ * 64, (d_ + 1) * 64)
                            S.op("pe", lambda: nc.tensor.matmul(pin[1][:, :n], lhsT=lw2[prd, 0, gi * 128:(gi + 1) * 128], rhs=TW[prd, t0:t0 + n],
                                                                start=True, stop=True), reads=["lw2", "TW"], writes=["l1pin1"])
                            S.op("act", lambda: nc.scalar.activation(out=t32[2][:, :n], in_=pin[1][:, :n], func=AF.Sigmoid,
                                                                     bias=v1[:, 49 + d_ * 4 + gi:50 + d_ * 4 + gi]),
                                 reads=["l1pin1", "v1"], writes=["l1t2"])
                            S.op("dve", lambda: nc.vector.tensor_scalar(out=lwX[:, t0:t0 + n], in0=t32[2][:, :n], scalar1=-0.6065306597126334,
                                                                        scalar2=None, op0=ALU.mult), reads=["l1t2"], writes=["lwX"])
                            S.op("pe", lambda: nc.tensor.matmul(pin[1][:, :n], lhsT=lw2[prd, 1, gi * 128:(gi + 1) * 128], rhs=AL[prd, t0:t0 + n],
                                                                start=True, stop=True), reads=["lw2", "AL"], writes=["l1pin1"])
                            S.op("act", lambda: nc.scalar.activation(out=aX[:, t0:t0 + n], in_=pin[1][:, :n], func=AF.Sigmoid,
                                                                     bias=v1[:, 57 + d_ * 4 + gi:58 + d_ * 4 + gi]),
                                 reads=["l1pin1", "v1"], writes=["aX"])
                        S.op("pe", lambda: nc.tensor.matmul(pin[1][:, :n], lhsT=lw2[:, 2, gi * 128:(gi + 1) * 128], rhs=SG[:, t0:t0 + n],
                                                            start=True, stop=True), reads=["lw2", "SG"], writes=["l1pin1"])
                        S.op("act", lambda: nc.scalar.copy(out=gA[:, t0:t0 + n], in_=pin[1][:, :n]), reads=["l1pin1"], writes=["gA"])
                    insts = [(slice(0, 64), 64, 64), (slice(64, 128), 64, 64)]
                    K.yacc = Yall[:, gi, :]
                    reset_state(insts)
                    def rw_pre(dn, aX, lwX, csl, gi_):
                        W = wk[dn]
                        return [
                            lambda: S.op("dve", lambda: nc.vector.tensor_scalar(out=W["al"][:], in0=kkA[:, csl], scalar1=-1.0, scalar2=None, op0=ALU.mult),
                                         reads=["kkA"], writes=[f"al{dn}"]),
                            lambda: S.op("pool", lambda: nc.gpsimd.tensor_tensor(out=W["be"][:], in0=kkA[:, csl], in1=aX[:, csl], op=ALU.mult),
                                         reads=["kkA", "aX"], writes=[f"be{dn}"]),
                            lambda: S.op("dve", lambda: nc.vector.tensor_scalar(out=W["kd"][:], in0=aX[:, csl], scalar1=-1.0, scalar2=v1[:, 65 + gi_:66 + gi_],
                                                                                op0=ALU.add, op1=ALU.mult), reads=["aX", "v1"], writes=[f"kd{dn}"]),
                            lambda: S.op("dve", lambda: nc.vector.scalar_tensor_tensor(out=W["kd"][:], in0=W["kd"][:], scalar=1.0, in1=kA[:, csl],
                                                                                       op0=ALU.add, op1=ALU.mult), reads=[f"kd{dn}", "kA"], writes=[f"kd{dn}"]),
                            lambda: S.op("pool", lambda: nc.gpsimd.tensor_copy(out=W["rr"][:], in_=rA[:, csl]), reads=["rA"], writes=[f"rr{dn}"]),
                            lambda: S.op("pool", lambda: nc.gpsimd.tensor_copy(out=W["lw"][:], in_=lwX[:, csl]), reads=["lwX"], writes=[f"lw{dn}"]),
                        ]
                    prev_h = []
                    for step in range(DBG['nstep'] + 1):
                        pg = []; hg = []
                        if step < DBG['nstep']:
                            for dn, order, aX, lwX in (("f", fwd_order, afA, lwF), ("b", bwd_order, abA, lwB)):
                                ch = order[step]; c0 = ch * 64
                                csl = slice(c0, c0 + 64)
                                pg.append(prefix_gen(dn, rw_pre(dn, aX, lwX, csl, gi), vA[:, csl], ["vA"], False, step % 2))
                                for j, (pr, dk, dv) in enumerate(insts):
                                    hg.append(head_gen(dn, j, pr, dk, dv, c0, ch >= 4, False, step % 2))
                        round_robin(prev_h + pg)
                        prev_h = hg
                    for qi in range(8 if DBG['rwout'] else 0):
                        t0 = NCTX + qi * 256; n = 256; y0 = qi * 256
                        yk = [f"Yacc{i}.{p}" for i in range(y0 // 64, y0 // 64 + 4) for p in (0, 64)]
                        S.op("pe", lambda: nc.tensor.matmul(pin[0][:, :n], lhsT=CM("onesA"), rhs=K.yacc[:, y0:y0 + n], start=True, stop=True),
                             reads=yk + ["cm_b"], writes=["l1pin0"])
                        S.op("dve", lambda: nc.vector.tensor_tensor(out=t32[0][:, :n], in0=K.yacc[:, y0:y0 + n], in1=pin[0][:, :n], op=ALU.subtract),
                             reads=yk + ["l1pin0"], writes=["l1t0"])
                        S.op("act", lambda: nc.scalar.activation(out=sq[:, :n], in_=t32[0][:, :n], func=AF.Square), reads=["l1t0"], writes=["l1sq"])
                        S.op("pe", lambda: nc.tensor.matmul(pin[0][:, :n], lhsT=CM("onesA"), rhs=sq[:, :n], start=True, stop=True),
                             reads=["l1sq", "cm_b"], writes=["l1pin0"])
                        S.op("act", lambda: nc.scalar.activation(out=t32[1][:, :n], in_=pin[0][:, :n], func=AF.Sqrt, bias=gneps[:, 0:1]),
                             reads=["l1pin0", "gneps"], writes=["l1t1"])
                        S.op("dve", lambda: nc.vector.reciprocal(out=t32[1][:, :n], in_=t32[1][:, :n]), reads=["l1t1"], writes=["l1t1"])
                        S.op("dve", lambda: nc.vector.tensor_tensor(out=t32[0][:, :n], in0=t32[0][:, :n], in1=t32[1][:, :n], op=ALU.mult),
                             reads=["l1t0", "l1t1"], writes=["l1t0"])
                        S.op("act", lambda: nc.scalar.activation(out=t32[0][:, :n], in_=t32[0][:, :n], func=AF.Identity,
                                                                 scale=v1[:, 69 + gi:70 + gi], bias=v1[:, 73 + gi:74 + gi]),
                             reads=["l1t0", "v1"], writes=["l1t0"])
                        S.op("dve", lambda: nc.vector.tensor_tensor(out=t32[1][:, :n], in0=afA[:, t0:t0 + n], in1=abA[:, t0:t0 + n], op=ALU.add),
                             reads=["aX"], writes=["l1t1"])
                        S.op("dve", lambda: nc.vector.tensor_scalar(out=t32[1][:, :n], in0=t32[1][:, :n], scalar1=-2.0, scalar2=v1[:, 65 + gi:66 + gi],
                                                                    op0=ALU.add, op1=ALU.mult), reads=["l1t1", "v1"], writes=["l1t1"])
                        S.op("dve", lambda: nc.vector.scalar_tensor_tensor(out=t32[1][:, :n], in0=t32[1][:, :n], scalar=2.0, in1=kA[:, t0:t0 + n],
                                                                           op0=ALU.add, op1=ALU.mult), reads=["l1t1", "kA"], writes=["l1t1"])
                        S.op("dve", lambda: nc.vector.scalar_tensor_tensor(out=sq[:, :n], in0=t32[1][:, :n], scalar=v1[:, 77 + gi:78 + gi],
                                                                           in1=rA[:, t0:t0 + n], op0=ALU.mult, op1=ALU.mult),
                             reads=["l1t1", "rA", "v1"], writes=["l1sq"])
                        S.op("pe", lambda: nc.tensor.matmul(pin[1][:, :n], lhsT=CM("onesA"), rhs=sq[:, :n], start=True, stop=True),
                             reads=["l1sq", "cm_b"], writes=["l1pin1"])
                        S.op("dve", lambda: nc.vector.scalar_tensor_tensor(out=t32[2][:, :n], in0=pin[1][:, :n], scalar=64.0, in1=vA[:, t0:t0 + n],
                                                                           op0=ALU.mult, op1=ALU.mult), reads=["l1pin1", "vA"], writes=["l1t2"])
                        S.op("pool", lambda: nc.gpsimd.tensor_tensor(out=t32[0][:, :n], in0=t32[0][:, :n], in1=t32[2][:, :n], op=ALU.add),
                             reads=["l1t0", "l1t2"], writes=["l1t0"])
                        S.op("dve", lambda: nc.vector.tensor_tensor(out=Yall[:, gi, y0:y0 + n], in0=t32[0][:, :n], in1=gA[:, t0:t0 + n], op=ALU.mult),
                             reads=["l1t0", "gA"], writes=["Yall"])
                    S.barrier()

                sm = XF[0:16, 7 * T:8 * T]
                smb = sb(L, "smb", [16, 256])
                smg = sb(L, "smg", [16, 256])
                gsm = sb(L, "gsmp", [16, 2])
                sel = sb(L, "sel", [16, 16 * 128])
                S.dma("sp", gsm[:], gsm_d, writes=["gsm"])
                S.dma("sp", sel[:], sel_d, writes=["sel"])
                S.op("act", lambda: nc.scalar.activation(out=gsm[:, 0:1], in_=gsm[:, 0:1], func=AF.Exp), reads=["gsm"], writes=["gsm"])
                S.op("dve", lambda: nc.vector.tensor_scalar(out=gsm[:, 0:1], in0=gsm[:, 0:1], scalar1=-1.0, scalar2=None, op0=ALU.mult),
                     reads=["gsm"], writes=["gsm"])
                load_cols([(INC + 2048, 16)])
                proj_taps(0, 16, [(0, None)], AF.Identity, lambda s0, n: sm[:, s0:s0 + n], ["sm"])
                t32q = xb(12); t32k = xb(13)
                one_t = sb(L, "one_t", [128, 1])
                S.op("pool", lambda: nc.gpsimd.memset(one_t[:], 1.0), writes=["one_t"])

                for hd in range(DBG['gdn']):
                    qA, kA, vA, zA = (xb(i) for i in range(4))
                    bmF = xf(2); bmB = xf(3); gmF = xf(4); gmB = xf(5)
                    load_cols([(INC + hd * 128, 128), (INC + 512 + hd * 128, 128), (INC + 1024 + hd * 128, 128), (INC + 1536 + hd * 128, 128)])

                    def ctaps(fc):
                        return [(0, v1[:, 81 + fc * 5 + 2:81 + fc * 5 + 3])] + [(o, v1[:, 81 + fc * 5 + 2 + o:81 + fc * 5 + 3 + o]) for o in (-2, -1, 1, 2)]
                    proj_taps(0, 128, ctaps(hd), AF.Silu, lambda s0, n: t32q[:, s0:s0 + n], ["t32q"])
                    proj_taps(128, 128, ctaps(4 + hd), AF.Silu, lambda s0, n: t32k[:, s0:s0 + n], ["t32k"])
                    proj_taps(256, 128, ctaps(8 + hd), AF.Silu, lambda s0, n: vA[:, s0:s0 + n], ["vA"])
                    proj_taps(384, 128, [(0, None)], AF.Silu, lambda s0, n: zA[:, s0:s0 + n], ["zA"])
                    for (t0, n) in (TCH if DBG['gprep'] >= 2 else []):
                        for (srcb, dstb, scl, dk_) in ((t32q, qA, 128.0 ** -0.5, "qA"), (t32k, kA, 1.0, "kA")):
                            S.op("act", lambda: nc.scalar.activation(out=sq[:, :n], in_=srcb[:, t0:t0 + n], func=AF.Square), reads=["t32q", "t32k"], writes=["l1sq"])
                            S.op("pe", lambda: nc.tensor.matmul(pin[0][:, :n], lhsT=CM("ones1024"), rhs=sq[:, :n], start=True, stop=True),
                                 reads=["l1sq", "cm_b"], writes=["l1pin0"])
                            S.op("act", lambda: nc.scalar.activation(out=t32[1][:, :n], in_=pin[0][:, :n], func=AF.Sqrt, scale=1024.0, bias=eps_t[:, 0:1]),
                                 reads=["l1pin0", "eps"], writes=["l1t1"])
                            S.op("dve", lambda: nc.vector.reciprocal(out=t32[1][:, :n], in_=t32[1][:, :n]), reads=["l1t1"], writes=["l1t1"])
                            S.op("dve", lambda: nc.vector.scalar_tensor_tensor(out=dstb[:, t0:t0 + n], in0=srcb[:, t0:t0 + n], scalar=float(scl),
                                                                               in1=t32[1][:, :n], op0=ALU.mult, op1=ALU.mult),
                                 reads=["t32q", "t32k", "l1t1"], writes=[dk_])
                        S.op("act", lambda: nc.scalar.activation(out=smb[:, :n], in_=sm[:, t0:t0 + n], func=AF.Sigmoid), reads=["sm"], writes=["smb"])
                        S.op("act", lambda: nc.scalar.activation(out=smg[:, :n], in_=sm[:, t0:t0 + n], func=AF.Exp, bias=gsm[:, 1:2]),
                             reads=["sm", "gsm"], writes=["smg"])
                        S.op("act", lambda: nc.scalar.activation(out=smg[:, :n], in_=smg[:, :n], func=AF.Ln, bias=one_t[0:16, 0:1]),
                             reads=["smg", "one_t"], writes=["smg"])
                        S.op("dve", lambda: nc.vector.tensor_scalar(out=smg[:, :n], in0=smg[:, :n], scalar1=gsm[:, 0:1], scalar2=None, op0=ALU.mult),
                             reads=["smg", "gsm"], writes=["smg"])
                        for (row, dst, srcm, dk_) in (((hd, bmF, smb, "bmF"), (4 + hd, bmB, smb, "bmB"), (8 + hd, gmF, smg, "gmF"), (12 + hd, gmB, smg, "gmB")) if DBG['gprep'] >= 3 else []):
                            S.op("pe", lambda: nc.tensor.matmul(pin[1][:, :n], lhsT=sel[:, row * 128:(row + 1) * 128], rhs=srcm[:, :n],
                                                                start=True, stop=True), reads=["sel", "smb", "smg"], writes=["l1pin1"])
                            e = evac_eng()
                            S.op(e, copy_op(e, dst[:, t0:t0 + n], pin[1][:, :n]), reads=["l1pin1"], writes=[dk_])
                    insts = [(slice(0, 128), 128, 128)]
                    K.yacc = Yall[:, 4 + hd, :]
                    reset_state(insts)
                    def gd_pre(dn, bmX, gmX, csl):
                        W = wk[dn]
                        return [
                            lambda: S.op("pool", lambda: nc.gpsimd.tensor_copy(out=W["al"][:], in_=kA[:, csl]), reads=["kA"], writes=[f"al{dn}"]),
                            lambda: S.op("dve", lambda: nc.vector.tensor_tensor(out=W["kd"][:], in0=kA[:, csl], in1=bmX[:, csl], op=ALU.mult),
                                         reads=["kA", "bmF", "bmB"], writes=[f"kd{dn}"]),
                            lambda: S.op("pool", lambda: nc.gpsimd.tensor_copy(out=W["lw"][:], in_=gmX[:, csl]), reads=["gmF", "gmB"], writes=[f"lw{dn}"]),
                            lambda: S.op("act", lambda: nc.scalar.activation(out=W["be"][:], in_=gmX[:, csl], func=AF.Exp), reads=["gmF", "gmB"], writes=[f"be{dn}"]),
                            lambda: S.op("dve", lambda: nc.vector.scalar_tensor_tensor(out=W["be"][:], in0=W["be"][:], scalar=-1.0, in1=W["kd"][:],
                                                                                       op0=ALU.mult, op1=ALU.mult), reads=[f"be{dn}", f"kd{dn}"], writes=[f"be{dn}"]),
                            lambda: S.op("pool", lambda: nc.gpsimd.tensor_copy(out=W["rr"][:], in_=qA[:, csl]), reads=["qA"], writes=[f"rr{dn}"]),
                        ]
                    prev_h = []
                    for step in range(DBG['nstep'] + 1):
                        pg = []; hg = []
                        for dn, order, bmX, gmX in ((("f", fwd_order, bmF, gmF), ("b", bwd_order, bmB, gmB)) if step < DBG['nstep'] else ()):
                            ch = order[step]; c0 = ch * 64
                            csl = slice(c0, c0 + 64)
                            pg.append(prefix_gen(dn, gd_pre(dn, bmX, gmX, csl), vA[:, csl], ["vA"], True, step % 2))
                            for j, (pr, dk, dv) in enumerate(insts):
                                hg.append(head_gen(dn, j, pr, dk, dv, c0, ch >= 4, True, step % 2))
                        round_robin(prev_h + pg)
                        prev_h = hg
                    for qi in range(8):
                        t0 = NCTX + qi * 256; n = 256; y0 = qi * 256
                        yk = [f"Yacc{i}.{p}" for i in range(y0 // 64, y0 // 64 + 4) for p in (0, 64)]
                        S.op("act", lambda: nc.scalar.activation(out=sq[:, :n], in_=K.yacc[:, y0:y0 + n], func=AF.Square), reads=yk, writes=["l1sq"])
                        S.op("pe", lambda: nc.tensor.matmul(pin[0][:, :n], lhsT=CM("ones1024"), rhs=sq[:, :n], start=True, stop=True),
                             reads=["l1sq", "cm_b"], writes=["l1pin0"])
                        S.op("act", lambda: nc.scalar.activation(out=t32[1][:, :n], in_=pin[0][:, :n], func=AF.Sqrt, scale=8.0, bias=eps_t[:, 0:1]),
                             reads=["l1pin0", "eps"], writes=["l1t1"])
                        S.op("dve", lambda: nc.vector.reciprocal(out=t32[1][:, :n], in_=t32[1][:, :n]), reads=["l1t1"], writes=["l1t1"])
                        S.op("dve", lambda: nc.vector.scalar_tensor_tensor(out=t32[0][:, :n], in0=K.yacc[:, y0:y0 + n], scalar=v1[:, 141:142],
                                                                           in1=t32[1][:, :n], op0=ALU.mult, op1=ALU.mult),
                             reads=yk + ["l1t1", "v1"], writes=["l1t0"])
                        S.op("dve", lambda: nc.vector.tensor_tensor(out=Yall[:, 4 + hd, y0:y0 + n], in0=t32[0][:, :n], in1=zA[:, t0:t0 + n], op=ALU.mult),
                             reads=["l1t0", "zA"], writes=["Yall"])
                    S.barrier()
                LS.close()
                L = L0
                if dbg == "y1":
                    dump_fm(Yall, 8, BF16, 16)
                stg2 = sb(L, "l1stg2", [128, 2048])
                wo = sb(L, "l1wo", [128, 8, D], BF16)
                pin = [ps(L, f"l1pinb{i}", [128, 512]) for i in range(2)]
                cov = cdout_d.rearrange("(kc p) n -> p kc n", p=128)
                for i in range(4):
                    wload(stg2, "l1stg2", wo[:, 2 * i:2 * i + 2, :], ["l1wo"], cov[:, 2 * i:2 * i + 2, :], [128, 2, D], "sp")
                for c in range(8):
                    S.dma("sp" if c % 2 == 0 else "pool", X[:, c, :], xpark_d[:, c * T:(c + 1) * T], writes=[f"X{t}" for t in range(NT)])
                S.barrier()
                it = 0
                for qi in range(4):
                    t0 = NCTX + qi * 512; n = 512; y0 = qi * 512
                    for m in range(8):
                        p_ = pin[it % 2]; pk = f"l1pinb{it%2}"; it += 1
                        for i in range(8):
                            S.op("pe", lambda: nc.tensor.matmul(p_[:, :n], lhsT=wo[:, i, m * 128:(m + 1) * 128], rhs=Yall[:, i, y0:y0 + n],
                                                                start=(i == 0), stop=(i == 7)), reads=["l1wo", "Yall"], writes=[pk])
                        S.op("dve", lambda: nc.vector.scalar_tensor_tensor(out=X[:, m, t0:t0 + n], in0=p_[:, :n], scalar=mod(l, 2, m, 0),
                                                                           in1=X[:, m, t0:t0 + n], op0=ALU.mult, op1=ALU.add),
                             reads=[pk, "modv"] + xkeys(t0, n), writes=xkeys(t0, n))
            S.barrier()

        K.stop = False
        if dbg in ("load", "mods"):
            write_out()
            S.finish()
            return nc, S
        layer0()
        if dbg in ("qa", "oa"):
            S.finish()
            return nc, S
        if dbg != "noffn":
            ffn(0, nlayers == 1)
        if nlayers == 2:
            layer1()
            if dbg not in ("noffn1", "y1"):
                ffn(1, True)
        write_out()
        S.finish()
    return nc, S


def make_in_maps(inputs, cores):
    consts, cnames = _consts()
    f = lambda k: np.asarray(inputs[k], np.float32)
    c_ctx = f("c_ctx")
    shared = {
        "ada_w": np.ascontiguousarray(f("ada_w")),
        "ada_b_fm": np.stack([_fm(f("ada_b")[l]) for l in range(2)], 0),
        "ffn_w_up": np.ascontiguousarray(f("ffn_w_up")),
        "ffn_w_down": np.ascontiguousarray(f("ffn_w_down")),
        "ab_w_in": np.ascontiguousarray(f("ab_w_in")[0]),
        "ab_w_out": np.ascontiguousarray(f("ab_w_out")[0]),
        "b_w_uq": np.ascontiguousarray(f("b_w_uq")[0]),
        "b_w_uk": np.ascontiguousarray(f("b_w_uk")[0]),
        "b_w_uv": np.ascontiguousarray(f("b_w_uv")[0]),
        "cmats": consts["cmats"], "cosA": consts["cosA"], "sinA": consts["sinA"],
        "cosB": consts["cosB"], "sinB": consts["sinB"],
    }
    fc = np.zeros((2, 128, 44, 4), np.float32)
    for l in range(2):
        for j in range(3):
            fc[l, :, :, j] = _fm(f("ffn_conv_w")[l, j])
        fc[l, :, :, 3] = _fm(f("ffn_conv_b")[l])
    shared["ffn_conv_fm"] = fc
    v = np.zeros((128, 16), np.float32)
    v[:, 0] = np.tile(f("a_q_norm")[0], 2)
    v[:, 1] = np.tile(f("a_k_norm")[0], 2)
    v[:, 2:4] = _fm(f("b_cq_norm")[0])
    v[:, 4:6] = _fm(f("b_ckv_norm")[0])
    v[64:96, 6] = f("b_kr_norm")[0]
    v[:64, 7] = f("b_kn_norm")[0]
    v[:64, 8] = f("b_qn_norm")[0]
    v[64:96, 8] = f("b_qr_norm")[0]
    shared["vecs0"] = v
    shared["sink_b"] = np.ascontiguousarray(np.broadcast_to(f("a_sink")[0][None, :], (64, 8)))
    shared["cd_w_in"] = np.ascontiguousarray(f("cd_w_in")[0])
    shared["cd_w_out"] = np.ascontiguousarray(f("cd_w_out")[0])
    shared["c_w2r"] = np.ascontiguousarray(f("c_w2")[0].reshape(128, 512))
    shared["c_a2r"] = np.ascontiguousarray(f("c_a2")[0].reshape(128, 512))
    shared["c_g2"] = np.ascontiguousarray(f("c_g2")[0])
    v1 = np.zeros((128, 160), np.float32)
    v1[:, 0:15] = _fm(f("c_mu_prev")[0]); v1[:, 15:30] = _fm(f("c_mu_next")[0])
    v1[:, 45:49] = _fm(f("c_k_k")[0])
    for d_ in range(2):
        v1[:, 49 + d_ * 4:53 + d_ * 4] = _fm(f("c_w0")[0, d_])
        v1[:, 57 + d_ * 4:61 + d_ * 4] = _fm(f("c_a0")[0, d_])
    v1[:, 65:69] = _fm(f("c_k_a")[0]); v1[:, 69:73] = _fm(f("c_ln_w")[0]); v1[:, 73:77] = _fm(f("c_ln_b")[0])
    v1[:, 77:81] = _fm(f("c_r_k")[0].reshape(-1))
    dcw = f("d_conv_w")[0]
    for fc in range(12):
        for j in range(5):
            v1[:, 81 + fc * 5 + j] = dcw[j, fc * 128:(fc + 1) * 128]
    v1[:, 141] = f("d_o_norm")[0]
    shared["vecs1"] = v1
    gsm = np.zeros((16, 2), np.float32)
    for d_ in range(2):
        gsm[8 + 4 * d_:12 + 4 * d_, 0] = f("d_A_log")[0, d_]
        gsm[8 + 4 * d_:12 + 4 * d_, 1] = f("d_dt_bias")[0, d_]
    shared["gsm"] = gsm
    a_ = np.arange(64)[:, None]; b_ = np.arange(64)[None, :]
    shared["cm2"] = np.concatenate([(a_ < b_), (a_ <= b_), (a_ > b_), (a_ >= b_), (a_ == b_)], 1).astype(np.float32)
    sel = np.zeros((16, 16 * 128), np.float32)
    for q in range(16):
        sel[q, q * 128:(q + 1) * 128] = 1.0
    shared["sel16"] = sel
    maps = []
    for b in cores:
        m = dict(shared)
        m["x"] = np.ascontiguousarray(f("x")[b])
        m["ctx"] = np.ascontiguousarray(f("ctx")[b])
        m["cv"] = np.ascontiguousarray(np.stack([_fm(f("c")[b]), _fm(c_ctx)], -1))
        maps.append(m)
    return maps


def kernel(**inputs):
    nc, S = build_program()
    maps = make_in_maps(inputs, list(range(8)))
    res = run_bass_kernel_spmd(nc, maps, core_ids=list(range(8)))
    return np.stack([r["out"] for r in res.results], 0).astype(np.float32)
```
